# Optimizing a Trainium2 kernel written in Bass

```python
import math
import jax, jax.numpy as jnp
from jax import lax
import numpy as np

D_MODEL = 2048
BATCH = 2
SEQ = 4096
DEPTH = 2
DEC_BATCH = 16
DEC_SEQ = 32
PAST_LEN = 4096

CHUNK = 64
NORM_EPS = 1e-6
A_HEADS = 4
A_DK = 128
A_DV = 256
A_GATE_RANK = 16
A_GATE_TAU = 16.0
B_HEADS = 8
B_DK = 128
B_DV = 128
B_CONV = 4
C_GROUP = 16
C_GROUPS = 64
C_STATE = 64
D_HEADS = 4
D_DK = 128
D_DV = 256

A_KW = A_HEADS * A_DK
A_VW = A_HEADS * A_DV
B_KW = B_HEADS * B_DK
B_VW = B_HEADS * B_DV
B_QKV = 2 * B_KW + B_VW
C_W = C_GROUPS * C_GROUP
D_KW = D_HEADS * D_DK
D_VW = D_HEADS * D_DV
MIX_AB = A_VW + B_VW
MIX_CD = C_W + D_VW
AB_SPLIT = (A_KW, A_KW, A_VW, A_VW, A_GATE_RANK, B_QKV, B_VW, B_HEADS, B_HEADS)
CD_SPLIT = (C_W, C_W, D_KW, D_KW, D_VW, D_VW, D_VW, D_HEADS, D_HEADS)
IN_AB = sum(AB_SPLIT)
IN_CD = sum(CD_SPLIT)

kernel_name = 'hybrid_gla_gdn_s5_mlstm_stream_step'


def _split(t, sizes):
    return jnp.split(t, [int(i) for i in np.cumsum(sizes)[:-1]], axis=-1)


def _rmsnorm(x, g):
    x32 = x.astype(jnp.float32)
    y = x32 * lax.rsqrt(jnp.mean(x32 * x32, axis=-1, keepdims=True) + NORM_EPS)
    return (y * g.astype(jnp.float32)).astype(x.dtype)


def _head_rms(o, g):
    o = o * lax.rsqrt(jnp.mean(o * o, axis=-1, keepdims=True) + NORM_EPS) * g
    return o.reshape(o.shape[0], o.shape[1], -1)


def _head_layernorm(o, g):
    mu = jnp.mean(o, axis=-1, keepdims=True)
    oc = o - mu
    o = oc * lax.rsqrt(jnp.mean(oc * oc, axis=-1, keepdims=True) + NORM_EPS) * g
    return o.reshape(o.shape[0], o.shape[1], -1)


def _l2norm(t):
    return t * lax.rsqrt(jnp.sum(t * t, axis=-1, keepdims=True) + NORM_EPS)


def _chunk(t, c):
    b, l, h, d = t.shape
    return t.reshape(b, l // c, c, h, d).transpose(0, 3, 1, 2, 4)


def _unchunk(t):
    b, h, n, c, d = t.shape
    return t.transpose(0, 2, 3, 1, 4).reshape(b, n * c, h, d)


def _gla(q, k, v, log_a, s0):
    c = min(CHUNK, q.shape[1])
    q, k, v, log_a = (_chunk(t, c) for t in (q, k, v, log_a))
    b = jnp.cumsum(log_a, axis=-2)
    b_last = b[..., -1, :]
    q_dec = q * jnp.exp(b)
    causal = jnp.tril(jnp.ones((c, c), dtype=bool))
    scores = jnp.einsum('bhnid,bhnjd->bhnij', q_dec, k * jnp.exp(-b))
    o_intra = jnp.einsum('bhnij,bhnjv->bhniv', jnp.where(causal, scores, 0.0), v)
    kv = jnp.einsum('bhnjd,bhnjv->bhndv', k * jnp.exp(b_last[..., None, :] - b), v)

    def step(s, inp):
        decay, kv_n = inp
        return decay[..., None] * s + kv_n, s

    s_fin, s_start = lax.scan(step, s0, (jnp.moveaxis(jnp.exp(b_last), 2, 0), jnp.moveaxis(kv, 2, 0)))
    o = o_intra + jnp.einsum('bhnid,bhndv->bhniv', q_dec, jnp.moveaxis(s_start, 0, 2))
    return _unchunk(o), s_fin


def _gated_delta(q, k, v, g, beta, s0):
    c = min(CHUNK, q.shape[1])
    dv = v.shape[-1]
    q, k, v = (_chunk(t, c) for t in (q, k, v))
    g, beta = (_chunk(t[..., None], c)[..., 0] for t in (g, beta))
    g = jnp.cumsum(g, axis=-1)
    causal = jnp.tril(jnp.ones((c, c), dtype=bool))
    strict = jnp.tril(jnp.ones((c, c), dtype=bool), k=-1)
    decay = jnp.exp(jnp.where(causal, g[..., :, None] - g[..., None, :], -jnp.inf))
    k_beta = k * beta[..., None]
    lower = jnp.where(strict, jnp.einsum('bhnid,bhnjd->bhnij', k_beta, k) * decay, 0.0)
    rhs = jnp.concatenate([v * beta[..., None], k_beta * jnp.exp(g)[..., None]], axis=-1)
    sol = lax.linalg.triangular_solve(lower + jnp.eye(c, dtype=lower.dtype), rhs,
                                      left_side=True, lower=True, unit_diagonal=True)
    u, w = sol[..., :dv], sol[..., dv:]
    qk = jnp.where(causal, jnp.einsum('bhnid,bhnjd->bhnij', q, k) * decay, 0.0)
    q_g = q * jnp.exp(g)[..., None]
    g_last = g[..., -1]
    k_g = k * jnp.exp(g_last[..., None] - g)[..., None]

    def step(s, inp):
        u_n, w_n, qg_n, qk_n, kg_n, gl_n = inp
        v_new = u_n - jnp.einsum('bhcd,bhdv->bhcv', w_n, s)
        o_n = jnp.einsum('bhcd,bhdv->bhcv', qg_n, s) + jnp.einsum('bhij,bhjv->bhiv', qk_n, v_new)
        s_new = jnp.exp(gl_n)[..., None, None] * s + jnp.einsum('bhcd,bhcv->bhdv', kg_n, v_new)
        return s_new, o_n

    xs = tuple(jnp.moveaxis(t, 2, 0) for t in (u, w, q_g, qk, k_g, g_last))
    s_fin, o = lax.scan(step, s0, xs)
    return _unchunk(jnp.moveaxis(o, 0, 2)), s_fin


def _s5(u, lam_re, lam_im, log_dt, b_re, b_im, c_re, c_im, d_skip, x0_re, x0_im):
    f32 = jnp.float32
    lam_re, lam_im, b_re, b_im, c_re, c_im, d_skip = (t.astype(f32) for t in (lam_re, lam_im, b_re, b_im, c_re, c_im, d_skip))
    dt = jnp.exp(log_dt.astype(f32))[:, None]
    mag = jnp.exp(lam_re * dt)
    ab_re, ab_im = mag * jnp.cos(lam_im * dt), mag * jnp.sin(lam_im * dt)
    den = lam_re * lam_re + lam_im * lam_im
    er = ab_re - 1.0
    zr = (er * lam_re + ab_im * lam_im) / den
    zi = (ab_im * lam_re - er * lam_im) / den
    bb_re = zr[..., None] * b_re - zi[..., None] * b_im
    bb_im = zr[..., None] * b_im + zi[..., None] * b_re
    bu_re = jnp.einsum('blgi,gpi->blgp', u, bb_re)
    bu_im = jnp.einsum('blgi,gpi->blgp', u, bb_im)
    bu_re = bu_re.at[:, 0].add(ab_re * x0_re - ab_im * x0_im)
    bu_im = bu_im.at[:, 0].add(ab_re * x0_im + ab_im * x0_re)
    a_re = jnp.broadcast_to(ab_re, bu_re.shape)
    a_im = jnp.broadcast_to(ab_im, bu_im.shape)

    def combine(e1, e2):
        a1r, a1i, b1r, b1i = e1
        a2r, a2i, b2r, b2i = e2
        return (a2r * a1r - a2i * a1i, a2r * a1i + a2i * a1r,
                a2r * b1r - a2i * b1i + b2r, a2r * b1i + a2i * b1r + b2i)

    _, _, xr, xi = lax.associative_scan(combine, (a_re, a_im, bu_re, bu_im), axis=1)
    y = jnp.einsum('blgp,gip->blgi', xr, c_re) - jnp.einsum('blgp,gip->blgi', xi, c_im) + d_skip * u
    return y, xr[:, -1], xi[:, -1]


def _mlstm(q, k, v, i_pre, log_f, c0, n0, m0):
    c = min(CHUNK, q.shape[1])
    q, k, v = (_chunk(t, c) for t in (q, k, v))
    i_pre, log_f = (_chunk(t[..., None], c)[..., 0] for t in (i_pre, log_f))
    b = jnp.cumsum(log_f, axis=-1)
    causal = jnp.tril(jnp.ones((c, c), dtype=bool))
    logw = jnp.where(causal, b[..., :, None] - b[..., None, :] + i_pre[..., None, :], -jnp.inf)
    m_intra = jnp.max(logw, axis=-1)
    p = jnp.exp(logw - m_intra[..., None]) * jnp.einsum('bhnid,bhnjd->bhnij', q, k)
    h_intra = jnp.einsum('bhnij,bhnjv->bhniv', p, v)
    n_intra = jnp.sum(p, axis=-1)
    m_chunk = m_intra[..., -1]
    k_w = k * jnp.exp(logw[..., -1, :] - m_chunk[..., None])[..., None]
    kv = jnp.einsum('bhnjd,bhnjv->bhndv', k_w, v)
    k_sum = jnp.sum(k_w, axis=-2)

    def step(carry, inp):
        c_s, n_s, m_s = carry
        b_l, m_c, kv_n, ks_n = inp
        m_new = jnp.maximum(b_l + m_s, m_c)
        w_old = jnp.exp(b_l + m_s - m_new)
        w_new = jnp.exp(m_c - m_new)
        c_new = w_old[..., None, None] * c_s + w_new[..., None, None] * kv_n
        n_new = w_old[..., None] * n_s + w_new[..., None] * ks_n
        return (c_new, n_new, m_new), (c_s, n_s, m_s)

    xs = tuple(jnp.moveaxis(t, 2, 0) for t in (b[..., -1], m_chunk, kv, k_sum))
    (c_f, n_f, m_f), starts = lax.scan(step, (c0, n0, m0), xs)
    c_st, n_st, m_st = (jnp.moveaxis(t, 0, 2) for t in starts)
    a = b + m_st[..., None]
    m_t = jnp.maximum(a, m_intra)
    w_a = jnp.exp(a - m_t)
    w_i = jnp.exp(m_intra - m_t)
    num = w_a[..., None] * jnp.einsum('bhnid,bhndv->bhniv', q, c_st) + w_i[..., None] * h_intra
    den = w_a * jnp.einsum('bhnid,bhnd->bhni', q, n_st) + w_i * n_intra
    h = num / jnp.maximum(jnp.abs(den), jnp.exp(-m_t))[..., None]
    return _unchunk(h), c_f, n_f, m_f


def _layer_ab(h, conv_prev, s_gla0, s_gdn0, w_in, a_gate_w, a_gate_b, a_norm_g, b_conv_w,
              b_a_log, b_dt_bias, b_norm_g, w_out):
    f32 = jnp.float32
    bsz, l, _ = h.shape
    heads = lambda t, n: t.reshape(bsz, l, n, -1)
    proj = jnp.einsum('bld,de->ble', h, w_in).astype(f32)
    qa, ka, va, za, ga, qkv, zb, beta_pre, a_pre = _split(proj, AB_SPLIT)
    log_alpha = jax.nn.log_sigmoid(ga @ a_gate_w.astype(f32) + a_gate_b) / A_GATE_TAU
    o_a, s_gla = _gla(heads(qa, A_HEADS) * A_DK ** -0.5, heads(ka, A_HEADS), heads(va, A_HEADS),
                      heads(log_alpha, A_HEADS), s_gla0.astype(f32))
    o_a = _head_rms(o_a, a_norm_g.reshape(A_HEADS, A_DV)) * jax.nn.silu(za)
    xp = jnp.concatenate([conv_prev.astype(f32), qkv], axis=1)
    conv = xp[:, 0:l] * b_conv_w[0]
    for j in range(1, B_CONV):
        conv = conv + xp[:, j:j + l] * b_conv_w[j]
    conv_new = xp[:, l:]
    qb, kb, vb = _split(jax.nn.silu(conv), (B_KW, B_KW, B_VW))
    g = -jnp.exp(b_a_log) * jax.nn.softplus(a_pre + b_dt_bias)
    beta = jax.nn.sigmoid(beta_pre)
    o_b, s_gdn = _gated_delta(_l2norm(heads(qb, B_HEADS)) * B_DK ** -0.5, _l2norm(heads(kb, B_HEADS)),
                              heads(vb, B_HEADS), g, beta, s_gdn0.astype(f32))
    o_b = _head_rms(o_b, b_norm_g) * jax.nn.silu(zb)
    out = jnp.einsum('ble,ed->bld', jnp.concatenate([o_a, o_b], axis=-1).astype(h.dtype), w_out)
    return out.astype(h.dtype), conv_new, s_gla, s_gdn


def _layer_cd(h, s5_re0, s5_im0, mc0, mn0, mm0, w_in, c_lam_re, c_lam_im, c_log_dt, c_b_re, c_b_im,
              c_c_re, c_c_im, c_d, c_glu_w, c_glu_b, d_i_bias, d_f_bias, d_norm_g, w_out):
    f32 = jnp.float32
    bsz, l, _ = h.shape
    heads = lambda t, n: t.reshape(bsz, l, n, -1)
    proj = jnp.einsum('bld,de->ble', h, w_in).astype(f32)
    uc, zc, qd, kd, vd, od, zd, i_pre, f_pre = _split(proj, CD_SPLIT)
    y, s5_re, s5_im = _s5(uc.reshape(bsz, l, C_GROUPS, C_GROUP), c_lam_re, c_lam_im, c_log_dt, c_b_re,
                          c_b_im, c_c_re, c_c_im, c_d, s5_re0.astype(f32), s5_im0.astype(f32))
    y = jax.nn.gelu(y.reshape(bsz, l, C_W))
    o_c = y * jax.nn.sigmoid(y @ c_glu_w.astype(f32) + c_glu_b) * jax.nn.silu(zc)
    h_d, mc, mn, mm = _mlstm(heads(qd, D_HEADS) * D_DK ** -0.5, heads(kd, D_HEADS), heads(vd, D_HEADS),
                             i_pre + d_i_bias, jax.nn.log_sigmoid(f_pre + d_f_bias),
                             mc0.astype(f32), mn0.astype(f32), mm0.astype(f32))
    h_d = jax.nn.sigmoid(od) * h_d.reshape(bsz, l, D_VW)
    o_d = _head_layernorm(heads(h_d, D_HEADS), d_norm_g.reshape(D_HEADS, D_DV)) * jax.nn.silu(zd)
    out = jnp.einsum('ble,ed->bld', jnp.concatenate([o_c, o_d], axis=-1).astype(h.dtype), w_out)
    return out.astype(h.dtype), s5_re, s5_im, mc, mn, mm


def _trunk(x, ab_state, cd_state, norm_g, final_norm_g, ab_w, cd_w):
    h = x
    for layer in range(DEPTH):
        hn = _rmsnorm(h, norm_g[layer])
        if layer % 2 == 0:
            out, *ab_state = _layer_ab(hn, *ab_state, *ab_w)
        else:
            out, *cd_state = _layer_cd(hn, *cd_state, *cd_w)
        h = h + out
    ab_state = [s.astype(x.dtype) for s in ab_state]
    cd_state = [s.astype(x.dtype) for s in cd_state]
    return _rmsnorm(h, final_norm_g), ab_state, cd_state


def setup_inputs(seed: int = 0) -> dict:
    key = jax.random.key(seed)
    keys = iter(jax.random.split(key, 48))

    def nrm(shape, scale):
        return jax.random.normal(next(keys), shape, jnp.float32) * scale

    def uni(shape, lo, hi):
        return jax.random.uniform(next(keys), shape, jnp.float32, lo, hi)

    x_prompt = nrm((BATCH, SEQ, D_MODEL), 1.0)
    x_sample = nrm((DEC_BATCH, DEC_SEQ, D_MODEL), 1.0)
    cache_gdn_conv = nrm((DEC_BATCH, B_CONV - 1, B_QKV), 1.0)
    state_gla = nrm((DEC_BATCH, A_HEADS, A_DK, A_DV), 0.5)
    state_gdn = nrm((DEC_BATCH, B_HEADS, B_DK, B_DV), 0.1)
    state_s5_re = nrm((DEC_BATCH, C_GROUPS, C_STATE), 0.3)
    state_s5_im = nrm((DEC_BATCH, C_GROUPS, C_STATE), 0.3)
    state_mlstm_c = nrm((DEC_BATCH, D_HEADS, D_DK, D_DV), 0.5)
    state_mlstm_n = nrm((DEC_BATCH, D_HEADS, D_DK), 0.5)
    state_mlstm_m = nrm((DEC_BATCH, D_HEADS), 1.0)
    norm_g = 1.0 + nrm((DEPTH, D_MODEL), 0.01)
    final_norm_g = 1.0 + nrm((D_MODEL,), 0.01)
    w_in_ab = nrm((D_MODEL, IN_AB), D_MODEL ** -0.5)
    a_gate_w = nrm((A_GATE_RANK, A_KW), A_GATE_RANK ** -0.5)
    a_gate_b = nrm((A_KW,), 0.01)
    a_norm_g = 1.0 + nrm((A_VW,), 0.01)
    b_conv_w = nrm((B_CONV, B_QKV), B_CONV ** -0.5)
    b_a_log = jnp.log(uni((B_HEADS,), 1.0, 16.0))
    b_dt = jnp.exp(uni((B_HEADS,), math.log(1e-3), math.log(1e-1)))
    b_dt_bias = b_dt + jnp.log(-jnp.expm1(-b_dt))
    b_norm_g = 1.0 + nrm((B_DV,), 0.01)
    w_out_ab = nrm((MIX_AB, D_MODEL), MIX_AB ** -0.5)
    w_in_cd = nrm((D_MODEL, IN_CD), D_MODEL ** -0.5)
    c_lam_re = -0.5 + nrm((C_GROUPS, C_STATE), 0.01)
    c_lam_im = math.pi * jnp.arange(C_STATE, dtype=jnp.float32) + nrm((C_GROUPS, C_STATE), 0.01)
    c_log_dt = uni((C_GROUPS,), math.log(1e-3), math.log(1e-1))
    c_b_re = nrm((C_GROUPS, C_STATE, C_GROUP), (2 * C_GROUP) ** -0.5)
    c_b_im = nrm((C_GROUPS, C_STATE, C_GROUP), (2 * C_GROUP) ** -0.5)
    c_c_re = nrm((C_GROUPS, C_GROUP, C_STATE), C_STATE ** -0.5)
    c_c_im = nrm((C_GROUPS, C_GROUP, C_STATE), C_STATE ** -0.5)
    c_d = nrm((C_GROUPS, C_GROUP), 1.0)
    c_glu_w = nrm((C_W, C_W), C_W ** -0.5)
    c_glu_b = nrm((C_W,), 0.01)
    d_i_bias = nrm((D_HEADS,), 0.1)
    d_f_bias = jnp.linspace(3.0, 6.0, D_HEADS, dtype=jnp.float32) + nrm((D_HEADS,), 0.01)
    d_norm_g = 1.0 + nrm((D_VW,), 0.01)
    w_out_cd = nrm((MIX_CD, D_MODEL), MIX_CD ** -0.5)
    return {'x_prompt': x_prompt, 'x_sample': x_sample, 'cache_gdn_conv': cache_gdn_conv,
            'state_gla': state_gla, 'state_gdn': state_gdn, 'state_s5_re': state_s5_re,
            'state_s5_im': state_s5_im, 'state_mlstm_c': state_mlstm_c, 'state_mlstm_n': state_mlstm_n,
            'state_mlstm_m': state_mlstm_m, 'norm_g': norm_g, 'final_norm_g': final_norm_g,
            'w_in_ab': w_in_ab, 'a_gate_w': a_gate_w, 'a_gate_b': a_gate_b, 'a_norm_g': a_norm_g,
            'b_conv_w': b_conv_w, 'b_a_log': b_a_log, 'b_dt_bias': b_dt_bias, 'b_norm_g': b_norm_g,
            'w_out_ab': w_out_ab, 'w_in_cd': w_in_cd, 'c_lam_re': c_lam_re, 'c_lam_im': c_lam_im,
            'c_log_dt': c_log_dt, 'c_b_re': c_b_re, 'c_b_im': c_b_im, 'c_c_re': c_c_re, 'c_c_im': c_c_im,
            'c_d': c_d, 'c_glu_w': c_glu_w, 'c_glu_b': c_glu_b, 'd_i_bias': d_i_bias, 'd_f_bias': d_f_bias,
            'd_norm_g': d_norm_g, 'w_out_cd': w_out_cd}


def reference(x_prompt, x_sample, cache_gdn_conv, state_gla, state_gdn, state_s5_re, state_s5_im,
              state_mlstm_c, state_mlstm_n, state_mlstm_m, norm_g, final_norm_g, w_in_ab, a_gate_w,
              a_gate_b, a_norm_g, b_conv_w, b_a_log, b_dt_bias, b_norm_g, w_out_ab, w_in_cd, c_lam_re,
              c_lam_im, c_log_dt, c_b_re, c_b_im, c_c_re, c_c_im, c_d, c_glu_w, c_glu_b, d_i_bias,
              d_f_bias, d_norm_g, w_out_cd):
    ab_w = (w_in_ab, a_gate_w, a_gate_b, a_norm_g, b_conv_w, b_a_log, b_dt_bias, b_norm_g, w_out_ab)
    cd_w = (w_in_cd, c_lam_re, c_lam_im, c_log_dt, c_b_re, c_b_im, c_c_re, c_c_im, c_d, c_glu_w,
            c_glu_b, d_i_bias, d_f_bias, d_norm_g, w_out_cd)
    nb = x_prompt.shape[0]
    zeros = lambda *shape: jnp.zeros(shape, jnp.float32)
    prompt_ab = (zeros(nb, B_CONV - 1, B_QKV), zeros(nb, A_HEADS, A_DK, A_DV), zeros(nb, B_HEADS, B_DK, B_DV))
    prompt_cd = (zeros(nb, C_GROUPS, C_STATE), zeros(nb, C_GROUPS, C_STATE), zeros(nb, D_HEADS, D_DK, D_DV),
                 zeros(nb, D_HEADS, D_DK), zeros(nb, D_HEADS))
    y_prompt, (p_conv, p_gla, p_gdn), (p_s5_re, p_s5_im, p_mc, p_mn, p_mm) = _trunk(
        x_prompt, prompt_ab, prompt_cd, norm_g, final_norm_g, ab_w, cd_w)
    y_sample, (s_conv, s_gla, s_gdn), (s_s5_re, s_s5_im, s_mc, s_mn, s_mm) = _trunk(
        x_sample, (cache_gdn_conv, state_gla, state_gdn),
        (state_s5_re, state_s5_im, state_mlstm_c, state_mlstm_n, state_mlstm_m),
        norm_g, final_norm_g, ab_w, cd_w)
    return (y_prompt, y_sample, p_conv, p_gla, p_gdn, p_s5_re, p_s5_im, p_mc, p_mn, p_mm,
            s_conv, s_gla, s_gdn, s_s5_re, s_s5_im, s_mc, s_mn, s_mm)
```

```python
import numpy as np
import concourse.bass as bass
import concourse.mybir as mybir
from contextlib import ExitStack
from concourse.bass_utils import run_bass_kernel_spmd

F32 = mybir.dt.float32
BF16 = mybir.dt.bfloat16
I32 = mybir.dt.int32
AF = mybir.ActivationFunctionType
ALU = mybir.AluOpType
AX = mybir.AxisListType

SEM_LIMIT = 24000


class KB:
    ENGS = ("pe", "act", "dve", "pool", "sp")

    def __init__(self, nc):
        self.nc = nc
        self.stack = ExitStack()
        self.ops = {e: [] for e in self.ENGS}
        self.cur_sem = {}
        self.cur_cnt = {}
        for e in self.ENGS:
            self.cur_sem[e] = None
            self.cur_cnt[e] = 0
        self.waited = {e: {} for e in self.ENGS}
        self.writers = {}
        self.readers = {}
        self.dma_pool = []
        self.dma_rr = 0
        self.n_dma_sems = 14
        self.sem_objs = {}
        self.nuid = 0
        self.psum_banks = []
        self.ps_rr = 0
        self.n_instr = 0
        self.scopes = []

    def _alloc_sem(self, name):
        s = self.nc.alloc_semaphore(name=name)
        self.nuid += 1
        sid = self.nuid
        self.sem_objs[sid] = s
        return sid

    def sb(self, name, shape, dtype=F32):
        st = self.scopes[-1] if self.scopes else self.stack
        self.nuid += 1
        t = st.enter_context(self.nc.sbuf_tensor(f"sb_{name}_{self.nuid}", list(shape), dtype))
        return t

    def barrier(self):
        for e in self.ENGS:
            waits = {}
            for o in self.ENGS:
                if o != e and self.cur_sem[o] is not None and self.waited[e].get(self.cur_sem[o], 0) < self.cur_cnt[o]:
                    waits[self.cur_sem[o]] = self.cur_cnt[o]
            for sid, val in self.dma_pool:
                if val > 0 and self.waited[e].get(sid, 0) < val:
                    waits[sid] = val
            self._emit_waits(e, waits)

    def open_scope(self):
        self.scopes.append(ExitStack())

    def close_scope(self):
        self.barrier()
        self.emit_block()
        self.scopes.pop().close()

    def psum_init(self):
        for i in range(8):
            t = self.stack.enter_context(self.nc.psum_tensor(f"psb{i}", [128, 512], F32))
            self.psum_banks.append(t)

    def ps(self):
        t = self.psum_banks[self.ps_rr % 8]
        self.ps_rr += 1
        return t

    @staticmethod
    def key(x):
        if isinstance(x, (str, tuple)):
            return x
        return x.tensor.name

    def _deps(self, eng, reads, writes, is_dma):
        deps = []
        for r in reads:
            for src, t in self.writers.get(r, {}).items():
                deps.append((src, t, "raw"))
        for w in writes:
            for src, t in self.writers.get(w, {}).items():
                deps.append((src, t, "waw"))
            for src, t in self.readers.get(w, {}).items():
                deps.append((src, t, "war"))
        out = {}
        for src, (sid, val), kind in deps:
            if src == eng and not is_dma and not str(src).startswith("dma"):
                if eng == "pe" or kind != "raw":
                    continue
            if self.waited[eng].get(sid, 0) >= val:
                continue
            if out.get(sid, 0) < val:
                out[sid] = val
        return out

    def _emit_waits(self, eng, waits):
        for sid, val in waits.items():
            s = self.sem_objs[sid]
            self.ops[eng].append(lambda h, s=s, val=val: h.wait_ge(s, val))
            self.waited[eng][sid] = val
            self.n_instr += 1

    def _record(self, src, ticket, reads, writes):
        for r in reads:
            self.readers.setdefault(r, {})[src] = ticket
        for w in writes:
            self.writers[w] = {src: ticket}
            self.readers[w] = {}

    def op(self, eng, fn, reads, writes):
        reads = [self.key(r) for r in reads if r is not None and not isinstance(r, (int, float))]
        writes = [self.key(w) for w in writes]
        waits = self._deps(eng, reads, writes, False)
        self._emit_waits(eng, waits)
        if self.cur_sem[eng] is None or self.cur_cnt[eng] >= SEM_LIMIT:
            self.cur_sem[eng] = self._alloc_sem(f"s_{eng}_{self.nuid}")
            self.cur_cnt[eng] = 0
        self.cur_cnt[eng] += 1
        sid, val = self.cur_sem[eng], self.cur_cnt[eng]
        s = self.sem_objs[sid]
        self.ops[eng].append(lambda h, s=s: fn(h).then_inc(s, 1))
        self.n_instr += 1
        self._record(eng, (sid, val), reads, writes)

    def dma(self, out, in_, reads=None, writes=None, eng="sp", **kw):
        reads = [self.key(r) for r in (reads if reads is not None else [in_])]
        writes = [self.key(w) for w in (writes if writes is not None else [out])]
        waits = self._deps(eng, reads, writes, True)
        if len(self.dma_pool) < self.n_dma_sems:
            self.dma_pool.append([self._alloc_sem(f"s_dma_{self.nuid}"), 0])
        slot = self.dma_pool[self.dma_rr % self.n_dma_sems]
        self.dma_rr += 1
        if slot[1] + 16 > SEM_LIMIT:
            if self.waited[eng].get(slot[0], 0) < slot[1]:
                waits[slot[0]] = max(waits.get(slot[0], 0), slot[1])
            self._emit_waits(eng, waits)
            waits = {}
            slot[0] = self._alloc_sem(f"s_dma_{self.nuid}")
            slot[1] = 0
        sid = slot[0]
        if slot[1] > 0 and self.waited[eng].get(sid, 0) < slot[1]:
            waits[sid] = max(waits.get(sid, 0), slot[1])
        self._emit_waits(eng, waits)
        slot[1] += 16
        val = slot[1]
        s = self.sem_objs[sid]
        self.ops[eng].append(lambda h, s=s: h.dma_start(out=out, in_=in_, allow_slow_non_contiguous=True, **kw).then_inc(s, 16))
        self.n_instr += 1
        self._record(f"dma{sid}", (sid, val), reads, writes)
        return (sid, val)

    def wait_all(self, eng, keys):
        waits = {}
        for k in keys:
            k = self.key(k)
            for src, (sid, val) in self.writers.get(k, {}).items():
                if self.waited[eng].get(sid, 0) < val:
                    waits[sid] = max(waits.get(sid, 0), val)
        self._emit_waits(eng, waits)

    def mm(self, out, lhsT, rhs, start=True, stop=True, extra_reads=()):
        self.op("pe", lambda h: h.matmul(out, lhsT, rhs, start=start, stop=stop),
                [lhsT, rhs] + list(extra_reads) + ([] if start else [out]), [out])

    def tr(self, out, in_, ident):
        self.op("pe", lambda h: h.transpose(out, in_, ident), [in_, ident], [out])

    def act(self, out, in_, func, bias=0.0, scale=1.0, eng="act"):
        rd = [in_]
        if not isinstance(bias, (int, float)):
            rd.append(bias)
        if not isinstance(scale, (int, float)):
            rd.append(scale)
        self.op(eng, lambda h: h.activation(out=out, in_=in_, func=func, bias=bias, scale=scale), rd, [out])

    def tt(self, out, a, b, op, eng="dve"):
        self.op(eng, lambda h: h.tensor_tensor(out=out, in0=a, in1=b, op=op), [a, b], [out])

    def ts(self, out, a, s1, op0, s2=None, op1=None, eng="dve"):
        rd = [a] + [s for s in (s1, s2) if s is not None and not isinstance(s, (int, float))]
        if op1 is None:
            self.op(eng, lambda h: h.tensor_scalar(out=out, in0=a, scalar1=s1, scalar2=None, op0=op0), rd, [out])
        else:
            self.op(eng, lambda h: h.tensor_scalar(out=out, in0=a, scalar1=s1, scalar2=s2, op0=op0, op1=op1), rd, [out])

    def stt(self, out, a, s, b, op0, op1, eng="dve"):
        rd = [a, b] + ([] if isinstance(s, (int, float)) else [s])
        self.op(eng, lambda h: h.scalar_tensor_tensor(out=out, in0=a, scalar=s, in1=b, op0=op0, op1=op1), rd, [out])

    def cp(self, out, in_, eng="dve"):
        if eng == "act":
            self.op("act", lambda h: h.copy(out=out, in_=in_), [in_], [out])
        else:
            self.op(eng, lambda h: h.tensor_copy(out=out, in_=in_), [in_], [out])

    def red(self, out, in_, op=None, eng="dve"):
        op = op or ALU.add
        self.op(eng, lambda h: h.tensor_reduce(out=out, in_=in_, axis=AX.X, op=op), [in_], [out])

    def recip(self, out, in_):
        self.op("dve", lambda h: h.reciprocal(out=out, in_=in_), [in_], [out])

    def memset(self, ap, val, eng="dve"):
        self.op(eng, lambda h: h.memset(ap, val), [], [ap])

    def rsqrt(self, out, in_, scale, eps):
        self.act(out, in_, AF.Sqrt, bias=eps, scale=scale)
        self.recip(out, out)

    def finish(self):
        waits = {}
        for sid, val in self.dma_pool:
            if val > 0 and self.waited["sp"].get(sid, 0) < val:
                waits[sid] = val
        self._emit_waits("sp", waits)

    def emit(self):
        self.emit_block()
        self.stack.close()

    def emit_block(self):
        nc = self.nc
        ops = self.ops
        self.ops = {e: [] for e in self.ENGS}
        with nc.Block() as block:
            @block.tensor
            def _(h):
                for f in ops["pe"]:
                    f(h)

            @block.scalar
            def _(h):
                for f in ops["act"]:
                    f(h)

            @block.vector
            def _(h):
                for f in ops["dve"]:
                    f(h)

            @block.gpsimd
            def _(h):
                for f in ops["pool"]:
                    f(h)

            @block.sync
            def _(h):
                for f in ops["sp"]:
                    f(h)


EPS = 1e-6
D = 2048
NEG = -30000.0


def build(L):
    nc = bass.Bass("TRN2", target_bir_lowering=False)
    NTOK = L + 64
    tiles = [(0, i * 128, 128, i == 0, i == L // 128 - 1) for i in range(L // 128)]
    tiles += [(1, L, 32, True, True), (2, L + 32, 32, True, True)]

    def din(name, shape, dt=F32):
        return nc.dram_tensor(name, list(shape), dt, kind="ExternalInput").ap()

    def dout(name, shape, dt=F32):
        return nc.dram_tensor(name, list(shape), dt, kind="ExternalOutput").ap()

    xT_d = din("xT", [D, NTOK]); xtok_d = din("xtok", [NTOK, D])
    w_in_ab = din("w_in_ab", [D, 7200]); w_out_ab = din("w_out_ab", [D, D])
    w_in_cd = din("w_in_cd", [D, 6152]); w_out_cd = din("w_out_cd", [D, D])
    gluw_d = din("glu_w", [1024, 1024])
    consts_d = din("consts", [128, 9, 128])
    g0T_d = din("g0T", [128, 16]); g1b_d = din("g1b", [128, D]); gfb_d = din("gfb", [128, D])
    agw_d = din("a_gate_w", [16, 512]); agb_d = din("a_gate_b", [1, 512]); ang_d = din("a_norm_g_b", [128, 1024])
    cw_d = din("conv_w", [128, 24, 4]); alog_d = din("a_log_b", [128, 8]); dtb_d = din("dt_bias_b", [128, 8])
    bng_d = din("b_norm_g_b", [128, 128])
    s5col_d = din("s5col", [128, 3, 32]); s5row_d = din("s5row", [128, 3, 4096])
    bdb_d = din("bd_b", [2, 8, 128, 512]); bdc_d = din("bd_c", [2, 4, 128, 8, 128]); bdd_d = din("bd_d", [4, 128, 2, 128])
    glub_d = din("glu_b_b", [128, 1024]); dib_d = din("d_i_b", [128, 4]); dfb_d = din("d_f_b", [128, 4])
    dng_d = din("d_norm_g_b", [128, 1024])
    si_conv = din("si_conv", [2, 128, 24, 3]); si_gla = din("si_gla", [2, 4, 128, 256]); si_gdn = din("si_gdn", [2, 8, 128, 128])
    si_s5 = din("si_s5", [2, 128, 2, 32]); si_mc = din("si_mc", [2, 4, 128, 256]); si_mn = din("si_mn", [2, 128, 4])
    si_mm = din("si_mm", [2, 128, 4])

    y_d = dout("y", [NTOK, D])
    so_conv = dout("so_conv", [3, 128, 24, 3]); so_gla = dout("so_gla", [3, 4, 128, 256]); so_gdn = dout("so_gdn", [3, 8, 128, 128])
    so_s5 = dout("so_s5", [3, 128, 2, 32]); so_mc = dout("so_mc", [3, 4, 128, 256]); so_mn = dout("so_mn", [3, 128, 4])
    so_mm = dout("so_mm", [3, 128, 4])

    oT_d = nc.dram_tensor("oT_s", [D, NTOK], BF16).ap()
    h1_d = nc.dram_tensor("h1_s", [NTOK, D], F32).ap()
    hnT_d = nc.dram_tensor("hnT_s", [D, NTOK], BF16).ap()
    yT_d = nc.dram_tensor("yT_s", [1024, NTOK], BF16).ap()
    yz_d = nc.dram_tensor("yz_s", [NTOK, 1024], F32).ap()

    k = KB(nc)
    k.psum_init()
    sb = k.sb
    cst = sb("cst", [128, 9, 128])
    k.dma(cst[:], consts_d[:, :, :])
    ident = cst[:, 0, :]; U = cst[:, 1, :]; SU = cst[:, 2, :]; NSL = cst[:, 3, :]; ones = cst[:, 4, :]
    NMU = cst[:, 5, :]; NML = cst[:, 6, :]; IOTA = cst[:, 7, :]; SELS = cst[:, 8, :]
    tcol = cst[:, 8, 0:1]
    sel128 = sb("sel128", [128, 128]); sel32 = sb("sel32", [128, 128])
    k.ts(sel128[:], ones, cst[:, 8, 1:2], ALU.mult)
    k.ts(sel32[:], ones, cst[:, 8, 2:3], ALU.mult)
    epsb = sb("epsb", [128, 1]); k.memset(epsb[:], EPS)
    wbuf = sb("wbuf", [128, 16, 2048], BF16)
    xg = sb("xg", [128, 16, 128], BF16)
    projT = sb("projT", [128, 1160])
    tbuf = [sb(f"tb{i}", [128, 512]) for i in range(12)]
    cols = sb("cols", [128, 64])
    obf = [sb(f"obf{i}", [128, 128], BF16) for i in range(4)]
    obc = [0]
    qT = sb("qT", [128, 128]); kT = sb("kT", [128, 128])
    k.open_scope()
    xt2 = [sb(f"xt{i}", [128, 16, 128]) for i in range(2)]
    xsq = sb("xsq", [128, 16, 128]); xs1 = sb("xs1", [128, 128])
    rbc = sb("rbc", [128, 128]); rcol = sb("rcol", [128, 1])
    g0T = sb("g0T", [128, 16]); k.dma(g0T[:], g0T_d[:, :])
    tctr = [0]

    def tmp():
        t = tbuf[tctr[0] % len(tbuf)]
        tctr[0] += 1
        return t

    def rs_eps(out, in_, scale):
        k.rsqrt(out, in_, scale, epsb[:out.shape[0], :])

    def load_w(src, pairs):
        for d0, s0, n in pairs:
            k.dma(wbuf[:, :, d0:d0 + n], src[:, s0:s0 + n].rearrange("(kc p) c -> p kc c", p=128), eng="pool")

    def norm_x(ti, t0, T):
        xt = xt2[ti % 2]
        k.dma(xt[:, :, :T], xT_d[:, t0:t0 + T].rearrange("(kc p) t -> p kc t", p=128))
        k.tt(xsq[:, :, :T], xt[:, :, :T], xt[:, :, :T], ALU.mult, eng="pool")
        k.red(xs1[:, :T], xsq[:, :, :T].rearrange("p kc t -> p t kc"))
        p = k.ps(); k.mm(p[:, :T], ones, xs1[:, :T]); rs_eps(rbc[:, :T], p[:, :T], 1.0 / D)
        p = k.ps(); k.mm(p[:T, 0:1], xs1[:, :T], ones[:, 0:1]); rs_eps(rcol[:T, :], p[:T, 0:1], 1.0 / D)
        k.tt(xg[:, :, :T], xt[:, :, :T], g0T[:, :].unsqueeze(2).to_broadcast([128, 16, T]), ALU.mult)

    def load_hn(t0, T):
        k.dma(xg[:, :, :T], hnT_d[:, t0:t0 + T].rearrange("(kc p) t -> p kc t", p=128), reads=[("hnT", t0)])

    def projF(dst, col0, T, ncols=128, scaled=True):
        p = k.ps()
        for kc in range(16):
            k.mm(p[:ncols, :T], wbuf[:, kc, col0:col0 + ncols], xg[:, kc, :T], start=(kc == 0), stop=(kc == 15))
        if scaled:
            k.tt(dst, p[:ncols, :T], rbc[:ncols, :T], ALU.mult)
        else:
            k.cp(dst, p[:ncols, :T], eng="act")

    def projTok(col0, ncols, T, scaled=True, dst0=0):
        for c0 in range(0, ncols, 512):
            n = min(512, ncols - c0)
            p = k.ps()
            for kc in range(16):
                k.mm(p[:T, :n], xg[:, kc, :T], wbuf[:, kc, col0 + c0:col0 + c0 + n], start=(kc == 0), stop=(kc == 15))
            if scaled:
                k.ts(projT[:T, dst0 + c0:dst0 + c0 + n], p[:T, :n], rcol[:T, :], ALU.mult)
            else:
                k.cp(projT[:T, dst0 + c0:dst0 + c0 + n], p[:T, :n], eng="act")

    def store_oT(src, ncols, frow, t0, T):
        for c0 in range(0, ncols, 128):
            p = k.ps(); k.tr(p[:, :T], src[:T, c0:c0 + 128], ident[:T, :T])
            ob = obf[obc[0] % 4]; obc[0] += 1
            k.cp(ob[:, :T], p[:, :T], eng="act")
            k.dma(oT_d[frow + c0:frow + c0 + 128, t0:t0 + T], ob[:, :T], writes=[("oT", frow + c0, t0)])

    def headnorm_rms(o_sb, T, n, gsz, dst):
        sq = tmp(); k.tt(sq[:T, :n], o_sb, o_sb, ALU.mult, eng="pool")
        c = cols[:, 60:61]; k.red(c[:T, :], sq[:T, :n])
        r = cols[:, 61:62]; rs_eps(r[:T, :], c[:T, :], 1.0 / n)
        k.stt(dst, o_sb, r[:T, :], gsz, ALU.mult, ALU.mult)

    S_gla = sb("S_gla", [128, 256]); S_gdn = [sb(f"S_gdn{i}", [128, 128]) for i in range(2)]
    cbuf = sb("cbuf", [128, 6, 131]); cw = sb("cw", [128, 6, 4])
    gw = sb("gw", [16, 128]); gb = sb("gb", [1, 128]); angb = sb("angb", [128, 256]); bngb = sb("bngb", [128, 128])
    alog = sb("alog", [128, 8]); dtb = sb("dtb", [128, 8]); na = sb("na", [128, 8])
    k.dma(alog[:], alog_d[:, :]); k.dma(dtb[:], dtb_d[:, :]); k.dma(bngb[:], bng_d[:, :])
    k.act(na[:], alog[:], AF.Exp); k.ts(na[:], na[:], -1.0, ALU.mult)
    gaT = sb("gaT", [16, 128])
    cvs = sb("cvs", [128, 6, 128]); cacc = sb("cacc", [128, 6, 128]); ctmp = sb("ctmp", [128, 6, 128])
    Rp = [sb(f"Rp{i}", [128, 128]) for i in range(7)]
    Qp = [sb(f"Qp{i}", [128, 128]) for i in range(2)]
    g_vk = sb("g_vk", [128, 256]); g_egrow = sb("g_egrow", [128, 128]); g_AT = sb("g_AT", [128, 128])
    g_X = [sb(f"g_X{i}", [128, 256]) for i in range(2)]
    SC = 128 ** -0.5

    def gla_tile(hg, seq, t0, T):
        kk = projT[:T, 0:128]; v = projT[:T, 128:384]; za = projT[:T, 384:640]
        p = k.ps()
        k.mm(p[:T, :128], gaT[:, :T], gw[:, :], start=True, stop=False)
        k.mm(p[:T, :128], ones[0:1, :T], gb[:, :], start=False, stop=True)
        e = tmp(); k.act(e[:T, :128], p[:T, :128], AF.Exp, scale=-1.0)
        spl = tmp(); k.act(spl[:T, :128], e[:T, :128], AF.Ln, bias=1.0)
        pc = k.ps(); k.mm(pc[:, :T], spl[:T, :128], U[:T, :T])
        ebT = tmp(); k.act(ebT[:, :T], pc[:, :T], AF.Exp, scale=-1.0 / 16)
        enbT = tmp(); k.act(enbT[:, :T], pc[:, :T], AF.Exp, scale=1.0 / 16)
        qd = tmp(); k.stt(qd[:, :T], qT[:, :T], SC, ebT[:, :T], ALU.mult, ALU.mult)
        kd = tmp(); k.tt(kd[:, :T], kT[:, :T], enbT[:, :T], ALU.mult, eng="pool")
        pd = k.ps(); k.mm(pd[:T, :128], NSL[:T, :T], spl[:T, :128])
        ekw = tmp(); k.act(ekw[:T, :128], pd[:T, :128], AF.Exp, scale=1.0 / 16)
        kw = tmp(); k.tt(kw[:T, :128], kk, ekw[:T, :128], ALU.mult, eng="pool")
        psc = k.ps(); k.mm(psc[:T, :T], kd[:, :T], qd[:, :T])
        PT = tmp(); k.tt(PT[:T, :T], psc[:T, :T], U[:T, :T], ALU.mult)
        po = k.ps()
        k.mm(po[:T, :256], PT[:T, :T], v, start=True, stop=False)
        k.mm(po[:T, :256], qd[:, :T], S_gla[:, :], start=False, stop=True)
        o_sb = tmp(); k.cp(o_sb[:T, :256], po[:T, :256], eng="act")
        pkv = k.ps(); k.mm(pkv[:, :256], kw[:T, :128], v)
        k.stt(S_gla[:, :], S_gla[:, :], ebT[:, T - 1:T], pkv[:, :256], ALU.mult, ALU.add)
        gsz = tmp(); k.act(gsz[:T, :256], za, AF.Silu)
        k.tt(gsz[:T, :256], gsz[:T, :256], angb[:T, :], ALU.mult, eng="pool")
        og = tmp(); headnorm_rms(o_sb[:T, :256], T, 256, gsz[:T, :256], og[:T, :256])
        store_oT(og, 256, hg * 256, t0, T)

    def gdn_pre(T):
        for j in range(4):
            wj = cw[:, :, j:j + 1].to_broadcast([128, 6, T])
            if j == 0:
                k.tt(cacc[:, :, :T], cbuf[:, :, 0:T], wj, ALU.mult)
            else:
                k.tt(ctmp[:, :, :T], cbuf[:, :, j:j + T], wj, ALU.mult, eng="pool")
                k.tt(cacc[:, :, :T], cacc[:, :, :T], ctmp[:, :, :T], ALU.add)
        k.act(cvs[:, :, :T], cacc[:, :, :T], AF.Silu)
        k.tt(ctmp[:, 0:4, :T], cvs[:, 0:4, :T], cvs[:, 0:4, :T], ALU.mult, eng="pool")
        for b in range(4):
            p = k.ps(); k.mm(p[:, :T], ones, ctmp[:, b, :T])
            rs_eps(cacc[:, b, :T], p[:, :T], 1.0)
        k.tt(cvs[:, 0:4, :T], cvs[:, 0:4, :T], cacc[:, 0:4, :T], ALU.mult)

    def gdn_head(hg, hh, seq, t0, T):
        Sg = S_gdn[hh]
        qTh = cvs[:, 0 + hh, :T]; kTh = cvs[:, 2 + hh, :T]; vTh = cvs[:, 4 + hh, :T]
        zb = projT[:T, 640 + 128 * hh:768 + 128 * hh]
        beta_pre = projT[:T, 896 + hh:897 + hh]; a_pre = projT[:T, 898 + hh:899 + hh]
        gh = 2 * hg + hh
        c = lambda i: cols[:, i:i + 1]
        beta = c(0); k.act(beta[:T, :], beta_pre, AF.Sigmoid)
        e = c(1); k.act(e[:T, :], a_pre, AF.Exp, bias=dtb[:T, gh:gh + 1])
        spg = c(2); k.act(spg[:T, :], e[:T, :], AF.Ln, bias=1.0)
        graw = c(3); k.tt(graw[:T, :], spg[:T, :], na[:T, gh:gh + 1], ALU.mult)
        vk = g_vk
        p = k.ps(); k.tr(p[:T, 0:128], vTh, ident); k.tr(p[:T, 128:256], kTh, ident)
        k.cp(vk[:T, :256], p[:T, :256], eng="act")
        p = k.ps(); k.mm(p[:T, 0:1], U[:T, :T], graw[:T, :])
        gc = c(4); k.cp(gc[:T, :], p[:T, 0:1]); ngc = c(5); k.ts(ngc[:T, :], p[:T, 0:1], -1.0, ALU.mult)
        Ug = tmp(); k.ts(Ug[:T, :T], U[:T, :T], graw[:T, :], ALU.mult)
        pg = k.ps(); k.mm(pg[:, :T], ones[:T, :], Ug[:T, :T])
        Ib = tmp(); k.ts(Ib[:T, :T], ident[:T, :T], beta[:T, :], ALU.mult)
        pb = k.ps(); k.mm(pb[:T, :T], ones[:T, :T], Ib[:T, :T])
        arg = tmp(); k.tt(arg[:T, :T], pg[:T, :T], NMU[:T, :T], ALU.add)
        decT = tmp(); k.act(decT[:T, :T], arg[:T, :T], AF.Exp, bias=ngc[:T, :])
        egrow = g_egrow; k.act(egrow[:, :T], pg[:, :T], AF.Exp)
        glast = c(6); k.cp(glast[:, :], pg[:, T - 1:T])
        eglast = c(7); k.cp(eglast[:, :], egrow[:, T - 1:T], eng="pool")
        egc = c(8); k.act(egc[:T, :], gc[:T, :], AF.Exp)
        ekl = c(9); k.act(ekl[:T, :], gc[:T, :], AF.Exp, scale=-1.0, bias=glast[:T, :])
        pkk = k.ps(); k.mm(pkk[:T, :T], kTh, kTh)
        pqk = k.ps(); k.mm(pqk[:T, :T], kTh, qTh)
        AT = g_AT; k.stt(AT[:T, :T], pqk[:T, :T], SC, decT[:T, :T], ALU.mult, ALU.mult)
        t1 = tmp(); k.tt(t1[:T, :T], pkk[:T, :T], decT[:T, :T], ALU.mult)
        t2 = tmp(); k.tt(t2[:T, :T], pb[:T, :T], SU[:T, :T], ALU.mult)
        LT = Rp[0]; k.tt(LT[:T, :T], t1[:T, :T], t2[:T, :T], ALU.mult, eng="pool")
        p = k.ps(); k.tr(p[:T, :T], LT[:T, :T], ident[:T, :T]); k.cp(Qp[0][:T, :T], p[:T, :T], eng="act")
        nsq = {128: 6, 32: 4}[T]
        for n in range(nsq):
            Q = Qp[n % 2]; R = Rp[n]
            p2 = k.ps(); k.mm(p2[:T, :T], Q[:T, :T], R[:T, :T]); k.cp(Rp[n + 1][:T, :T], p2[:T, :T], eng="act")
            if n < nsq - 1:
                p1 = k.ps(); k.mm(p1[:T, :T], R[:T, :T], Q[:T, :T]); k.cp(Qp[(n + 1) % 2][:T, :T], p1[:T, :T])
        X = g_X[0]; xi_ = 0
        k.ts(X[:T, 0:128], vk[:T, 0:128], beta[:T, :], ALU.mult)
        k.ts(X[:T, 128:256], vk[:T, 128:256], beta[:T, :], ALU.mult, egc[:T, :], ALU.mult)
        for n in range(nsq, -1, -1):
            p = k.ps(); k.mm(p[:T, :256], Rp[n][:T, :T], X[:T, :256])
            xi_ ^= 1; Xn = g_X[xi_]; k.tt(Xn[:T, :256], X[:T, :256], p[:T, :256], ALU.add if n > 0 else ALU.subtract)
            X = Xn
        p = k.ps(); k.tr(p[:, :T], X[:T, 128:256], ident[:T, :T])
        wT = tmp(); k.cp(wT[:, :T], p[:, :T], eng="act")
        p = k.ps(); k.mm(p[:T, :128], wT[:, :T], Sg[:, :])
        vnew = tmp(); k.tt(vnew[:T, :128], X[:T, 0:128], p[:T, :128], ALU.subtract)
        qg = tmp(); k.stt(qg[:, :T], qTh, SC, egrow[:, :T], ALU.mult, ALU.mult)
        po = k.ps()
        k.mm(po[:T, :128], qg[:, :T], Sg[:, :], start=True, stop=False)
        k.mm(po[:T, :128], AT[:T, :T], vnew[:T, :128], start=False, stop=True)
        o_sb = tmp(); k.cp(o_sb[:T, :128], po[:T, :128], eng="act")
        kg = tmp(); k.ts(kg[:T, :128], vk[:T, 128:256], ekl[:T, :], ALU.mult)
        pkv = k.ps(); k.mm(pkv[:, :128], kg[:T, :128], vnew[:T, :128])
        k.stt(Sg[:, :], Sg[:, :], eglast[:, :], pkv[:, :128], ALU.mult, ALU.add)
        gsz = tmp(); k.act(gsz[:T, :128], zb, AF.Silu)
        k.tt(gsz[:T, :128], gsz[:T, :128], bngb[:T, :], ALU.mult, eng="pool")
        og = tmp(); headnorm_rms(o_sb[:T, :128], T, 128, gsz[:T, :128], og[:T, :128])
        store_oT(og, 128, 1024 + gh * 128, t0, T)

    for hg in range(4):
        TB = 1040
        load_w(w_in_ab, [(0, 128 * hg, 128), (128, 512 + 128 * hg, 128),
                         (256, 3088 + 256 * hg, 256), (512, 4112 + 256 * hg, 256), (768, 5136 + 256 * hg, 256),
                         (1024, 3072, 16),
                         (TB, 512 + 128 * hg, 128), (TB + 128, 1024 + 256 * hg, 256), (TB + 384, 2048 + 256 * hg, 256),
                         (TB + 640, 6160 + 256 * hg, 256), (TB + 896, 7184 + 2 * hg, 2), (TB + 898, 7192 + 2 * hg, 2)])
        k.dma(gw[:], agw_d[:, 128 * hg:128 * hg + 128]); k.dma(gb[:], agb_d[:, 128 * hg:128 * hg + 128])
        k.dma(angb[:], ang_d[:, 256 * hg:256 * hg + 256])
        for typ in range(3):
            k.dma(cw[:, 2 * typ:2 * typ + 2, :], cw_d[:, typ * 8 + 2 * hg:typ * 8 + 2 * hg + 2, :])
        for ti, (seq, t0, T, first, last) in enumerate(tiles):
            if first:
                if seq == 0:
                    k.memset(S_gla[:], 0.0); k.memset(S_gdn[0][:], 0.0); k.memset(S_gdn[1][:], 0.0)
                    k.memset(cbuf[:, :, 0:3], 0.0)
                else:
                    k.dma(S_gla[:], si_gla[seq - 1, hg, :, :])
                    for hh in range(2):
                        k.dma(S_gdn[hh][:], si_gdn[seq - 1, 2 * hg + hh, :, :])
                    for typ in range(3):
                        k.dma(cbuf[:, 2 * typ:2 * typ + 2, 0:3], si_conv[seq - 1, :, typ * 8 + 2 * hg:typ * 8 + 2 * hg + 2, :])
            else:
                k.cp(cbuf[:, :, 0:3], cbuf[:, :, 128:131])
            norm_x(ti, t0, T)
            projF(qT[:, :T], 0, T); projF(kT[:, :T], 128, T)
            for b in range(6):
                projF(cbuf[:, b, 3:3 + T], 256 + 128 * b, T)
            projF(gaT[:, :T], 1024, T, ncols=16)
            projTok(TB, 900, T)
            gla_tile(hg, seq, t0, T)
            gdn_pre(T)
            for hh in range(2):
                gdn_head(hg, hh, seq, t0, T)
            if last:
                k.dma(so_gla[seq, hg, :, :], S_gla[:])
                for hh in range(2):
                    k.dma(so_gdn[seq, 2 * hg + hh, :, :], S_gdn[hh][:])
                for typ in range(3):
                    k.dma(so_conv[seq, :, typ * 8 + 2 * hg:typ * 8 + 2 * hg + 2, :], cbuf[:, 2 * typ:2 * typ + 2, T:T + 3])

    k.close_scope()
    def outproj_alloc():
        return (sb("ot", [128, 16, 128], BF16), sb("resid", [128, D]), sb("hbuf", [128, D]),
                sb("gnb", [128, D]), sb("hsq", [128, D]), sb("hnT_sb", [128, 16, 128], BF16))

    def outproj_pass(w_out, layer):
        k.open_scope()
        ot, resid, hbuf, gnb, hsq, hnT_sb = outproj_alloc()
        load_w(w_out, [(0, 0, 2048)])
        k.dma(gnb[:], (g1b_d if layer == 0 else gfb_d)[:, :])
        for ti, (seq, t0, T, first, last) in enumerate(tiles):
            k.dma(ot[:, :, :T], oT_d[:, t0:t0 + T].rearrange("(kc p) t -> p kc t", p=128),
                  reads=[("oT", f, t0) for f in range(0, D, 128)])
            if layer == 0:
                k.dma(resid[:T, :], xtok_d[t0:t0 + T, :])
            else:
                k.dma(resid[:T, :], h1_d[t0:t0 + T, :], reads=[("h1", t0)])
            for nb in range(4):
                p = k.ps()
                for kc in range(16):
                    k.mm(p[:T, :512], ot[:, kc, :T], wbuf[:, kc, nb * 512:(nb + 1) * 512], start=(kc == 0), stop=(kc == 15))
                k.tt(hbuf[:T, nb * 512:(nb + 1) * 512], p[:T, :512], resid[:T, nb * 512:(nb + 1) * 512], ALU.add)
            if layer == 0:
                k.dma(h1_d[t0:t0 + T, :], hbuf[:T, :], writes=[("h1", t0)])
            k.tt(hsq[:T, :], hbuf[:T, :], hbuf[:T, :], ALU.mult, eng="pool")
            c = cols[:, 62:63]; k.red(c[:T, :], hsq[:T, :])
            r = cols[:, 63:64]; rs_eps(r[:T, :], c[:T, :], 1.0 / D)
            k.stt(hsq[:T, :], hbuf[:T, :], r[:T, :], gnb[:T, :], ALU.mult, ALU.mult)
            if layer == 0:
                for q4 in range(4):
                    p = k.ps()
                    for j in range(4):
                        kc = q4 * 4 + j
                        k.tr(p[:, j * 128:j * 128 + T], hsq[:T, kc * 128:(kc + 1) * 128], ident[:T, :T])
                    if T == 128:
                        k.cp(hnT_sb[:, q4 * 4:q4 * 4 + 4, :], p[:, :].rearrange("p (a b) -> p a b", a=4), eng="act")
                    else:
                        for j in range(4):
                            k.cp(hnT_sb[:, q4 * 4 + j, :T], p[:, j * 128:j * 128 + T], eng="act")
                k.dma(hnT_d[:, t0:t0 + T].rearrange("(kc p) t -> p kc t", p=128), hnT_sb[:, :, :T], writes=[("hnT", t0)])
            else:
                k.dma(y_d[t0:t0 + T, :], hsq[:T, :])
        k.close_scope()

    outproj_pass(w_out_ab, 0)

    k.open_scope()
    s5c = sb("s5c", [128, 3, 8]); s5r = sb("s5r", [128, 3, 1024])
    ETr = sb("ETr", [128, 8, 128]); ETi = sb("ETi", [128, 8, 128]); EIr = sb("EIr", [128, 1024]); EIi = sb("EIi", [128, 1024])
    big = [sb(f"big{i}", [128, 1024]) for i in range(6)]
    bigi = sb("bigi", [128, 1024], I32)
    BB = sb("BB", [128, 4, 512]); WB = sb("WB", [128, 4, 512])
    CB = sb("CB", [128, 2, 8, 128]); DD = sb("DD", [128, 2, 128])
    uT = sb("uT", [128, 2, 128])
    car = sb("car", [128, 2, 8]); xl = sb("xl", [128, 2, 8]); A1 = sb("A1", [128, 2, 8])
    TWO_PI = 2.0 * np.pi

    def sincos(dst_sin, dst_cos, ang, n, shp=None):
        for dst, off in ((dst_sin, 0.0), (dst_cos, 0.25)):
            tn = big[4][:, :n]; tf = big[5][:, :n]
            k.ts(tn, ang, 1.0 / TWO_PI, ALU.mult, off, ALU.add)
            k.cp(bigi[:, :n], tn)
            k.cp(tf, bigi[:, :n])
            k.tt(tn, tn, tf, ALU.subtract)
            k.act(dst, tn, AF.Sin, scale=TWO_PI)

    def cmul(outr, outi, ar, ai, br, bi, n, t1, t2):
        k.tt(t1, ar, br, ALU.mult); k.tt(t2, ai, bi, ALU.mult, eng="pool"); k.tt(outr, t1, t2, ALU.subtract)
        k.tt(t1, ar, bi, ALU.mult); k.tt(t2, ai, br, ALU.mult, eng="pool"); k.tt(outi, t1, t2, ALU.add)

    def s5_setup(hg):
        k.dma(s5c[:], s5col_d[:, :, 8 * hg:8 * hg + 8]); k.dma(s5r[:], s5row_d[:, :, 1024 * hg:1024 * hg + 1024])
        dtc = cols[:, 16:24]; arc = cols[:, 24:32]; aic = cols[:, 32:40]
        k.act(dtc, s5c[:, 2, :], AF.Exp)
        k.tt(arc, s5c[:, 0, :], dtc, ALU.mult); k.tt(aic, s5c[:, 1, :], dtc, ALU.mult)
        ang = big[0][:, :].rearrange("p (a b) -> p a b", a=8)
        io = IOTA.unsqueeze(1).to_broadcast([128, 8, 128])
        k.tt(ang, io, aic.unsqueeze(2).to_broadcast([128, 8, 128]), ALU.mult)
        sincos(big[1][:, :], big[2][:, :], big[0][:, :], 1024)
        mg = big[3][:, :].rearrange("p (a b) -> p a b", a=8)
        k.tt(mg, io, arc.unsqueeze(2).to_broadcast([128, 8, 128]), ALU.mult)
        k.act(big[3][:, :], big[3][:, :], AF.Exp)
        k.tt(ETi[:, :, :].rearrange("p a b -> p (a b)"), big[3][:, :], big[1][:, :], ALU.mult)
        k.tt(ETr[:, :, :].rearrange("p a b -> p (a b)"), big[3][:, :], big[2][:, :], ALU.mult)
        k.cp(A1[:, 0, :], ETr[:, :, 1]); k.cp(A1[:, 1, :], ETi[:, :, 1])
        dtr = big[0][:, :]; k.act(dtr, s5r[:, 2, :], AF.Exp)
        arr = sb_arr[:, :]; air = sb_air[:, :]
        k.tt(arr, s5r[:, 0, :], dtr, ALU.mult); k.tt(air, s5r[:, 1, :], dtr, ALU.mult)
        sincos(big[1][:, :], big[2][:, :], air, 1024)
        k.act(big[3][:, :], arr, AF.Exp)
        abi = big[1][:, :]; abr = big[2][:, :]
        k.tt(abi, abi, big[3][:, :], ALU.mult); k.tt(abr, abr, big[3][:, :], ALU.mult)
        k.ts(abr, abr, -1.0, ALU.add)
        den = big[3][:, :]; t = big[0][:, :]
        k.tt(den, s5r[:, 0, :], s5r[:, 0, :], ALU.mult); k.tt(t, s5r[:, 1, :], s5r[:, 1, :], ALU.mult)
        k.tt(den, den, t, ALU.add); k.recip(den, den)
        zr = big[4][:, :]; zi = big[5][:, :]
        k.tt(zr, abr, s5r[:, 0, :], ALU.mult); k.tt(t, abi, s5r[:, 1, :], ALU.mult); k.tt(zr, zr, t, ALU.add); k.tt(zr, zr, den, ALU.mult)
        k.tt(zi, abi, s5r[:, 0, :], ALU.mult); k.tt(t, abr, s5r[:, 1, :], ALU.mult); k.tt(zi, zi, t, ALU.subtract); k.tt(zi, zi, den, ALU.mult)
        for h in range(2):
            k.dma(WB[:, 2 * h, :], bdb_d[0, 2 * hg + h, :, :]); k.dma(WB[:, 2 * h + 1, :], bdb_d[1, 2 * hg + h, :, :])
            sl = slice(512 * h, 512 * h + 512)
            t1 = big[0][:, 0:512]; t2 = big[0][:, 512:1024]
            cmul(BB[:, 2 * h, :], BB[:, 2 * h + 1, :], zr[:, sl], zi[:, sl], WB[:, 2 * h, :], WB[:, 2 * h + 1, :], 512, t1, t2)
        k.ts(big[0][:, :], air, tcol, ALU.mult)
        sincos(big[1][:, :], big[2][:, :], big[0][:, :], 1024)
        k.ts(big[3][:, :], arr, tcol, ALU.mult); k.act(big[3][:, :], big[3][:, :], AF.Exp, scale=-1.0)
        k.tt(EIr[:, :], big[3][:, :], big[2][:, :], ALU.mult)
        k.stt(EIi[:, :], big[3][:, :], -1.0, big[1][:, :], ALU.mult, ALU.mult)
        k.dma(CB[:, 0, :, :], bdc_d[0, hg, :, :, :]); k.dma(CB[:, 1, :, :], bdc_d[1, hg, :, :, :])
        k.ts(CB[:, 1, :, :], CB[:, 1, :, :], -1.0, ALU.mult)
        k.dma(DD[:], bdd_d[hg, :, :, :])

    sb_arr = sb("sb_arr", [128, 1024]); sb_air = sb("sb_air", [128, 1024])

    def s5_tile(hg, seq, t0, T):
        zc = projT[:T, 0:256]
        Bu = [big[0], big[1]]
        for h in range(2):
            for cpx in range(2):
                p = k.ps(); k.mm(p[:T, :512], uT[:, h, :T], BB[:, 2 * h + cpx, :])
                k.cp(Bu[cpx][:T, 512 * h:512 * h + 512], p[:T, :512], eng="act")
        Zr = big[2]; Zi = big[3]
        cmul(Zr[:T, :], Zi[:T, :], EIr[:T, :], EIi[:T, :], Bu[0][:T, :], Bu[1][:T, :], 1024, big[4][:T, :], big[5][:T, :])
        Wc = [big[0], big[1]]
        for cpx, Z in enumerate((Zr, Zi)):
            for half in range(2):
                p = k.ps()
                for j in range(4):
                    ch = half * 4 + j
                    k.mm(p[:, j * 128:j * 128 + T], Z[:T, ch * 128:(ch + 1) * 128], U[:T, :T])
                src = p[:, :].rearrange("p (a b) -> p a b", a=4)[:, :, :T]
                dst = Wc[cpx][:, :].rearrange("p (a b) -> p a b", a=8)[:, half * 4:half * 4 + 4, :T]
                k.tt(dst, src, car[:, cpx, half * 4:half * 4 + 4].unsqueeze(2).to_broadcast([128, 4, T]), ALU.add)
        XTr = big[2][:, :].rearrange("p (a b) -> p a b", a=8); XTi = big[3][:, :].rearrange("p (a b) -> p a b", a=8)
        W3 = [w[:, :].rearrange("p (a b) -> p a b", a=8) for w in Wc]
        t1 = big[4][:, :].rearrange("p (a b) -> p a b", a=8); t2 = big[5][:, :].rearrange("p (a b) -> p a b", a=8)
        cmul(XTr[:, :, :T], XTi[:, :, :T], ETr[:, :, :T], ETi[:, :, :T], W3[0][:, :, :T], W3[1][:, :, :T], 0, t1[:, :, :T], t2[:, :, :T])
        k.cp(xl[:, 0, :], XTr[:, :, T - 1]); k.cp(xl[:, 1, :], XTi[:, :, T - 1])
        c1 = cols[:, 40:48]; c2 = cols[:, 48:56]
        cmul(car[:, 0, :], car[:, 1, :], A1[:, 0, :], A1[:, 1, :], xl[:, 0, :], xl[:, 1, :], 8, c1, c2)
        py = k.ps()
        for h in range(2):
            o = py[:T, 128 * h:128 * h + 128]
            k.mm(o, uT[:, h, :T], DD[:, h, :], start=True, stop=False)
            for pc in range(4):
                ch = 4 * h + pc
                k.mm(o, XTr[:, ch, :T], CB[:, 0, ch, :], start=False, stop=False)
                k.mm(o, XTi[:, ch, :T], CB[:, 1, ch, :], start=False, stop=(pc == 3))
        yg = tmp(); k.act(yg[:T, :256], py[:T, :256], AF.Gelu)
        sz = tmp(); k.act(sz[:T, :256], zc, AF.Silu)
        yz = tmp(); k.tt(yz[:T, :256], yg[:T, :256], sz[:T, :256], ALU.mult)
        k.dma(yz_d[t0:t0 + T, 256 * hg:256 * hg + 256], yz[:T, :256], writes=[("yz", hg, t0)])
        for c0 in range(0, 256, 128):
            p = k.ps(); k.tr(p[:, :T], yg[:T, c0:c0 + 128], ident[:T, :T])
            ob = obf[obc[0] % 4]; obc[0] += 1
            k.cp(ob[:, :T], p[:, :T], eng="act")
            k.dma(yT_d[256 * hg + c0:256 * hg + c0 + 128, t0:t0 + T], ob[:, :T], writes=[("yT", 256 * hg + c0, t0)])

    for hg in range(4):
        load_w(w_in_cd, [(0, 256 * hg, 256), (256, 1024 + 256 * hg, 256)])
        s5_setup(hg)
        for ti, (seq, t0, T, first, last) in enumerate(tiles):
            if first:
                if seq == 0:
                    k.memset(car[:], 0.0)
                else:
                    k.dma(xl[:], si_s5[seq - 1, :, :, 8 * hg:8 * hg + 8])
                    cmul(car[:, 0, :], car[:, 1, :], A1[:, 0, :], A1[:, 1, :], xl[:, 0, :], xl[:, 1, :], 8, cols[:, 40:48], cols[:, 48:56])
            load_hn(t0, T)
            projF(uT[:, 0, :T], 0, T, scaled=False); projF(uT[:, 1, :T], 128, T, scaled=False)
            projTok(256, 256, T, scaled=False)
            s5_tile(hg, seq, t0, T)
            if last:
                k.dma(so_s5[seq, :, :, 8 * hg:8 * hg + 8], xl[:])

    k.close_scope()
    k.open_scope()
    Cx = sb("Cx", [128, 257]); mst = sb("mst", [128, 1]); vx = sb("vx", [128, 257])
    gluw = sb("gluw", [128, 8, 256], BF16); ytl = sb("ytl", [128, 8, 128], BF16)
    glub = sb("glub", [128, 256]); dngb = sb("dngb", [128, 256]); dib = sb("dib", [128, 4]); dfb = sb("dfb", [128, 4])
    yzt = sb("yzt", [128, 256])
    k.dma(dib[:], dib_d[:, :]); k.dma(dfb[:], dfb_d[:, :]); k.ts(dfb[:], dfb[:], -1.0, ALU.mult)
    k.memset(vx[:, 256:257], 1.0)

    def mlstm_tile(hg, seq, t0, T):
        kk = projT[:T, 0:128]; od = projT[:T, 384:640]; zd = projT[:T, 640:896]
        c = lambda i: cols[:, i:i + 1]
        k.cp(vx[:T, 0:256], projT[:T, 128:384], eng="pool")
        ip = c(0); k.tt(ip[:T, :], projT[:T, 896:897], dib[:T, hg:hg + 1], ALU.add)
        e = c(1); k.act(e[:T, :], projT[:T, 897:898], AF.Exp, scale=-1.0, bias=dfb[:T, hg:hg + 1])
        sp = c(2); k.act(sp[:T, :], e[:T, :], AF.Ln, bias=1.0)
        p = k.ps(); k.mm(p[:T, 0:1], U[:T, :T], sp[:T, :]); bcol = c(3); k.ts(bcol[:T, :], p[:T, 0:1], -1.0, ALU.mult)
        p = k.ps(); k.mm(p[:, 0:1], ones[:T, :], sp[:T, :]); blast = c(4); k.ts(blast[:, :], p[:, 0:1], -1.0, ALU.mult)
        d = c(5); k.tt(d[:T, :], ip[:T, :], bcol[:T, :], ALU.subtract)
        Dm = tmp(); k.ts(Dm[:T, :T], ident[:T, :T], d[:T, :], ALU.mult)
        pr = k.ps(); k.mm(pr[:T, :T], ones[:T, :T], Dm[:T, :T])
        lw = tmp(); k.stt(lw[:T, :T], pr[:T, :T], bcol[:T, :], NML[:T, :T], ALU.add, ALU.add)
        mi = c(6); k.red(mi[:T, :], lw[:T, :T], ALU.max)
        nmi = c(7); k.ts(nmi[:T, :], mi[:T, :], -1.0, ALU.mult)
        e1 = tmp(); k.act(e1[:T, :T], lw[:T, :T], AF.Exp, bias=nmi[:T, :])
        pqk = k.ps(); k.mm(pqk[:T, :T], qT[:, :T], kT[:, :T])
        pm = tmp(); k.stt(pm[:T, :T], pqk[:T, :T], SC, e1[:T, :T], ALU.mult, ALU.mult)
        p = k.ps(); k.tr(p[:T, :T], pm[:T, :T], ident[:T, :T]); pT = tmp(); k.cp(pT[:T, :T], p[:T, :T], eng="act")
        sel = sel128 if T == 128 else sel32
        p = k.ps(); k.mm(p[:, 0:1], sel[:T, :], mi[:T, :]); mch = c(8); k.cp(mch[:, :], p[:, 0:1])
        bm = c(9); k.tt(bm[:, :], blast[:, :], mch[:, :], ALU.subtract)
        kws = c(10); k.act(kws[:T, :], d[:T, :], AF.Exp, bias=bm[:T, :])
        kw = tmp(); k.ts(kw[:T, :128], kk, kws[:T, :], ALU.mult)
        a = c(11); k.tt(a[:T, :], bcol[:T, :], mst[:T, :], ALU.add)
        mt = c(12); k.tt(mt[:T, :], a[:T, :], mi[:T, :], ALU.max)
        nmt = c(13); k.ts(nmt[:T, :], mt[:T, :], -1.0, ALU.mult)
        wa = c(14); k.act(wa[:T, :], a[:T, :], AF.Exp, bias=nmt[:T, :]); k.ts(wa[:T, :], wa[:T, :], SC, ALU.mult)
        wi = c(15); k.act(wi[:T, :], mi[:T, :], AF.Exp, bias=nmt[:T, :])
        emt = c(56); k.act(emt[:T, :], nmt[:T, :], AF.Exp)
        pA = k.ps(); k.mm(pA[:T, :257], qT[:, :T], Cx[:, :])
        pB = k.ps(); k.mm(pB[:T, :257], pT[:T, :T], vx[:T, :])
        r1 = tmp(); k.ts(r1[:T, :257], pA[:T, :257], wa[:T, :], ALU.mult)
        res = tmp(); k.stt(res[:T, :257], pB[:T, :257], wi[:T, :], r1[:T, :257], ALU.mult, ALU.add)
        dn = c(57); k.act(dn[:T, :], res[:T, 256:257], AF.Abs); k.tt(dn[:T, :], dn[:T, :], emt[:T, :], ALU.max)
        k.recip(dn[:T, :], dn[:T, :])
        sg = tmp(); k.act(sg[:T, :256], od, AF.Sigmoid)
        hd = tmp(); k.stt(hd[:T, :256], res[:T, :256], dn[:T, :], sg[:T, :256], ALU.mult, ALU.mult)
        mu = c(58); k.red(mu[:T, :], hd[:T, :256]); k.ts(mu[:T, :], mu[:T, :], -1.0 / 256, ALU.mult)
        xc = tmp(); k.ts(xc[:T, :256], hd[:T, :256], mu[:T, :], ALU.add)
        gsz = tmp(); k.act(gsz[:T, :256], zd, AF.Silu); k.tt(gsz[:T, :256], gsz[:T, :256], dngb[:T, :], ALU.mult, eng="pool")
        od_ = tmp(); headnorm_rms(xc[:T, :256], T, 256, gsz[:T, :256], od_[:T, :256])
        store_oT(od_, 256, 1024 + 256 * hg, t0, T)
        pkv = k.ps(); k.mm(pkv[:, :257], kw[:T, :128], vx[:T, :])
        bms = c(59); k.tt(bms[:, :], blast[:, :], mst[:, :], ALU.add)
        mnew = c(56); k.tt(mnew[:, :], bms[:, :], mch[:, :], ALU.max)
        nmn = c(57); k.ts(nmn[:, :], mnew[:, :], -1.0, ALU.mult)
        wold = c(58); k.act(wold[:, :], bms[:, :], AF.Exp, bias=nmn[:, :])
        wnew = c(12); k.act(wnew[:, :], mch[:, :], AF.Exp, bias=nmn[:, :])
        t = tmp(); k.ts(t[:, :257], pkv[:, :257], wnew[:, :], ALU.mult)
        k.stt(Cx[:, :], Cx[:, :], wold[:, :], t[:, :257], ALU.mult, ALU.add)
        k.cp(mst[:, :], mnew[:, :])

    def glu_tile(hg, seq, t0, T):
        k.dma(ytl[:, :, :T], yT_d[:, t0:t0 + T].rearrange("(kc p) t -> p kc t", p=128),
              reads=[("yT", f, t0) for f in range(0, 1024, 128)])
        k.dma(yzt[:T, :], yz_d[t0:t0 + T, 256 * hg:256 * hg + 256], reads=[("yz", hg, t0)])
        p = k.ps()
        for kc in range(8):
            k.mm(p[:T, :256], ytl[:, kc, :T], gluw[:, kc, :], start=(kc == 0), stop=(kc == 7))
        g = tmp(); k.tt(g[:T, :256], p[:T, :256], glub[:T, :], ALU.add)
        k.act(g[:T, :256], g[:T, :256], AF.Sigmoid)
        oc = tmp(); k.tt(oc[:T, :256], g[:T, :256], yzt[:T, :], ALU.mult)
        store_oT(oc, 256, 256 * hg, t0, T)

    for hg in range(4):
        load_w(w_in_cd, [(0, 2048 + 128 * hg, 128), (128, 2560 + 128 * hg, 128),
                         (256, 2560 + 128 * hg, 128), (384, 3072 + 256 * hg, 256), (640, 4096 + 256 * hg, 256),
                         (896, 5120 + 256 * hg, 256), (1152, 6144 + hg, 1), (1153, 6148 + hg, 1)])
        k.dma(gluw[:, :, :], gluw_d[:, 256 * hg:256 * hg + 256].rearrange("(kc p) c -> p kc c", p=128), eng="pool")
        k.dma(glub[:], glub_d[:, 256 * hg:256 * hg + 256]); k.dma(dngb[:], dng_d[:, 256 * hg:256 * hg + 256])
        for ti, (seq, t0, T, first, last) in enumerate(tiles):
            if first:
                if seq == 0:
                    k.memset(Cx[:], 0.0); k.memset(mst[:], 0.0)
                else:
                    k.dma(Cx[:, 0:256], si_mc[seq - 1, hg, :, :]); k.dma(Cx[:, 256:257], si_mn[seq - 1, :, hg:hg + 1])
                    k.dma(mst[:], si_mm[seq - 1, :, hg:hg + 1])
            load_hn(t0, T)
            projF(qT[:, :T], 0, T, scaled=False); projF(kT[:, :T], 128, T, scaled=False)
            projTok(256, 898, T, scaled=False)
            mlstm_tile(hg, seq, t0, T)
            glu_tile(hg, seq, t0, T)
            if last:
                k.dma(so_mc[seq, hg, :, :], Cx[:, 0:256]); k.dma(so_mn[seq, :, hg:hg + 1], Cx[:, 256:257])
                k.dma(so_mm[seq, :, hg:hg + 1], mst[:])

    k.close_scope()
    outproj_pass(w_out_cd, 1)
    k.finish()
    k.emit()
    return nc, k


def _consts():
    c = np.zeros((128, 9, 128), np.float32)
    idx = np.arange(128)
    kk, ii = idx[:, None], idx[None, :]
    c[:, 0] = (kk == ii); c[:, 1] = (kk <= ii); c[:, 2] = (kk < ii); c[:, 3] = -(kk > ii).astype(np.float32)
    c[:, 4] = 1.0; c[:, 5] = np.where(kk <= ii, 0.0, NEG); c[:, 6] = np.where(ii <= kk, 0.0, NEG)
    c[:, 7] = np.broadcast_to(idx[None, :], (128, 128))
    c[:, 8, 0] = idx; c[127, 8, 1] = 1.0; c[31, 8, 2] = 1.0
    return c


def _bc(v, n=128):
    v = np.asarray(v, np.float32).reshape(1, -1)
    return np.ascontiguousarray(np.broadcast_to(v, (n, v.shape[1])))


def _s5col(a):
    return np.ascontiguousarray(np.asarray(a, np.float32).reshape(32, 128).T)


def _shared_inputs(inp):
    f = lambda a: np.ascontiguousarray(np.asarray(a, np.float32))
    d = {}
    d["w_in_ab"] = f(inp["w_in_ab"]); d["w_out_ab"] = f(inp["w_out_ab"])
    d["w_in_cd"] = f(inp["w_in_cd"]); d["w_out_cd"] = f(inp["w_out_cd"]); d["glu_w"] = f(inp["c_glu_w"])
    d["consts"] = _consts()
    ng = f(inp["norm_g"])
    d["g0T"] = np.ascontiguousarray(ng[0].reshape(16, 128).T); d["g1b"] = _bc(ng[1]); d["gfb"] = _bc(inp["final_norm_g"])
    d["a_gate_w"] = f(inp["a_gate_w"]); d["a_gate_b"] = f(inp["a_gate_b"]).reshape(1, 512); d["a_norm_g_b"] = _bc(inp["a_norm_g"])
    d["conv_w"] = np.ascontiguousarray(f(inp["b_conv_w"]).reshape(4, 24, 128).transpose(2, 1, 0))
    d["a_log_b"] = _bc(inp["b_a_log"]); d["dt_bias_b"] = _bc(inp["b_dt_bias"]); d["b_norm_g_b"] = _bc(inp["b_norm_g"])
    lre, lim = f(inp["c_lam_re"]), f(inp["c_lam_im"])
    ldt = np.ascontiguousarray(np.broadcast_to(f(inp["c_log_dt"])[:, None], (64, 64)))
    d["s5col"] = np.ascontiguousarray(np.stack([_s5col(lre), _s5col(lim), _s5col(ldt)], axis=1))
    d["s5row"] = np.ascontiguousarray(np.stack([_bc(lre.reshape(-1)), _bc(lim.reshape(-1)), _bc(ldt.reshape(-1))], axis=1))
    bd_b = np.zeros((2, 8, 128, 512), np.float32)
    for ci, b in enumerate((f(inp["c_b_re"]), f(inp["c_b_im"]))):
        for H in range(8):
            for gl in range(8):
                bd_b[ci, H, gl * 16:(gl + 1) * 16, gl * 64:(gl + 1) * 64] = b[8 * H + gl].T
    d["bd_b"] = bd_b
    bd_c = np.zeros((2, 4, 128, 8, 128), np.float32)
    for ci, c in enumerate((f(inp["c_c_re"]), f(inp["c_c_im"]))):
        for hg in range(4):
            for ch in range(8):
                for g2 in range(2):
                    g = 16 * hg + 2 * ch + g2
                    col = (2 * (ch % 4) + g2) * 16
                    bd_c[ci, hg, g2 * 64:(g2 + 1) * 64, ch, col:col + 16] = c[g].T
    d["bd_c"] = bd_c
    bd_d = np.zeros((4, 128, 2, 128), np.float32)
    cd = f(inp["c_d"])
    for hg in range(4):
        for h in range(2):
            for gl in range(8):
                for i in range(16):
                    bd_d[hg, gl * 16 + i, h, gl * 16 + i] = cd[16 * hg + 8 * h + gl, i]
    d["bd_d"] = bd_d
    d["glu_b_b"] = _bc(inp["c_glu_b"]); d["d_i_b"] = _bc(inp["d_i_bias"]); d["d_f_b"] = _bc(inp["d_f_bias"])
    d["d_norm_g_b"] = _bc(inp["d_norm_g"])
    return d


_NC_CACHE = {}


def kernel(**inp):
    f = lambda a: np.ascontiguousarray(np.asarray(a, np.float32))
    xp = f(inp["x_prompt"]); xs = f(inp["x_sample"])
    NB, L, _ = xp.shape
    shared = _shared_inputs(inp)
    conv = f(inp["cache_gdn_conv"]); sgla = f(inp["state_gla"]); sgdn = f(inp["state_gdn"])
    s5re = f(inp["state_s5_re"]); s5im = f(inp["state_s5_im"]); smc = f(inp["state_mlstm_c"])
    smn = f(inp["state_mlstm_n"]); smm = f(inp["state_mlstm_m"])
    in_maps = []
    for c in range(8):
        m = dict(shared)
        sq = [2 * c, 2 * c + 1]
        xtok = np.concatenate([xp[c % NB]] + [xs[s] for s in sq], axis=0)
        m["xtok"] = np.ascontiguousarray(xtok); m["xT"] = np.ascontiguousarray(xtok.T)
        m["si_conv"] = np.ascontiguousarray(np.stack([conv[s].reshape(3, 24, 128).transpose(2, 1, 0) for s in sq]))
        m["si_gla"] = np.ascontiguousarray(sgla[sq]); m["si_gdn"] = np.ascontiguousarray(sgdn[sq])
        m["si_s5"] = np.ascontiguousarray(np.stack([np.stack([_s5col(s5re[s]), _s5col(s5im[s])], axis=1) for s in sq]))
        m["si_mc"] = np.ascontiguousarray(smc[sq])
        m["si_mn"] = np.ascontiguousarray(np.stack([smn[s].T for s in sq]))
        m["si_mm"] = np.ascontiguousarray(np.stack([_bc(smm[s]) for s in sq]))
        in_maps.append(m)
    if L not in _NC_CACHE:
        _NC_CACHE[L] = build(L)[0]
    nc = _NC_CACHE[L]
    res = run_bass_kernel_spmd(nc, in_maps, core_ids=list(range(8)))
    R = res.results
    NS = xs.shape[0]
    y_p = np.stack([R[b]["y"][:L] for b in range(NB)])
    y_s = np.stack([R[s // 2]["y"][L + 32 * (s % 2):L + 32 * (s % 2) + 32] for s in range(NS)])

    def gather(fn):
        p = np.stack([fn(R[b], 0) for b in range(NB)])
        s = np.stack([fn(R[s // 2], 1 + s % 2) for s in range(NS)])
        return p, s

    def s5o(r, i, ci):
        return np.ascontiguousarray(r["so_s5"][i][:, ci, :].T).reshape(64, 64)

    pc, sc = gather(lambda r, i: np.ascontiguousarray(r["so_conv"][i].transpose(2, 1, 0)).reshape(3, 3072))
    pg, sg = gather(lambda r, i: r["so_gla"][i]); pd, sd = gather(lambda r, i: r["so_gdn"][i])
    pr, sr = gather(lambda r, i: s5o(r, i, 0)); pi, si = gather(lambda r, i: s5o(r, i, 1))
    pmc, smc_ = gather(lambda r, i: r["so_mc"][i]); pmn, smn_ = gather(lambda r, i: np.ascontiguousarray(r["so_mn"][i].T))
    pmm, smm_ = gather(lambda r, i: np.ascontiguousarray(r["so_mm"][i][0, :]))
    outs = (y_p, y_s, pc, pg, pd, pr, pi, pmc, pmn, pmm, sc, sg, sd, sr, si, smc_, smn_, smm_)
    return tuple(np.ascontiguousarray(o, dtype=np.float32) for o in outs)
```

```python
import numpy as np
import concourse.bass as bass
import concourse.mybir as mybir
from contextlib import ExitStack
from concourse.bass_utils import run_bass_kernel_spmd

F32 = mybir.dt.float32
BF16 = mybir.dt.bfloat16
I32 = mybir.dt.int32
AF = mybir.ActivationFunctionType
ALU = mybir.AluOpType
AX = mybir.AxisListType

SEM_LIMIT = 24000


class KB:
    ENGS = ("pe", "act", "dve", "pool", "sp")

    def __init__(self, nc):
        self.nc = nc
        self.stack = ExitStack()
        self.ops = {e: [] for e in self.ENGS}
        self.cur_sem = {}
        self.cur_cnt = {}
        for e in self.ENGS:
            self.cur_sem[e] = None
            self.cur_cnt[e] = 0
        self.waited = {e: {} for e in self.ENGS}
        self.writers = {}
        self.readers = {}
        self.dma_pool = []
        self.dma_rr = 0
        self.n_dma_sems = 14
        self.sem_objs = {}
        self.nuid = 0
        self.psum_banks = []
        self.ps_rr = 0
        self.n_instr = 0
        self.scopes = []

    def _alloc_sem(self, name):
        s = self.nc.alloc_semaphore(name=name)
        self.nuid += 1
        sid = self.nuid
        self.sem_objs[sid] = s
        return sid

    def sb(self, name, shape, dtype=F32):
        st = self.scopes[-1] if self.scopes else self.stack
        self.nuid += 1
        t = st.enter_context(self.nc.sbuf_tensor(f"sb_{name}_{self.nuid}", list(shape), dtype))
        return t

    def barrier(self):
        for e in self.ENGS:
            waits = {}
            for o in self.ENGS:
                if o != e and self.cur_sem[o] is not None and self.waited[e].get(self.cur_sem[o], 0) < self.cur_cnt[o]:
                    waits[self.cur_sem[o]] = self.cur_cnt[o]
            for sid, val in self.dma_pool:
                if val > 0 and self.waited[e].get(sid, 0) < val:
                    waits[sid] = val
            self._emit_waits(e, waits)

    def open_scope(self):
        self.scopes.append(ExitStack())

    def close_scope(self):
        self.barrier()
        self.emit_block()
        self.scopes.pop().close()

    def psum_init(self):
        for i in range(8):
            t = self.stack.enter_context(self.nc.psum_tensor(f"psb{i}", [128, 512], F32))
            self.psum_banks.append(t)

    def ps(self):
        t = self.psum_banks[self.ps_rr % 8]
        self.ps_rr += 1
        return t

    @staticmethod
    def key(x):
        if isinstance(x, (str, tuple)):
            return x
        return x.tensor.name

    def _deps(self, eng, reads, writes, is_dma):
        deps = []
        for r in reads:
            for src, t in self.writers.get(r, {}).items():
                deps.append((src, t, "raw"))
        for w in writes:
            for src, t in self.writers.get(w, {}).items():
                deps.append((src, t, "waw"))
            for src, t in self.readers.get(w, {}).items():
                deps.append((src, t, "war"))
        out = {}
        for src, (sid, val), kind in deps:
            if src == eng and not is_dma and not str(src).startswith("dma"):
                if eng == "pe" or kind != "raw":
                    continue
            if self.waited[eng].get(sid, 0) >= val:
                continue
            if out.get(sid, 0) < val:
                out[sid] = val
        return out

    def _emit_waits(self, eng, waits):
        for sid, val in waits.items():
            s = self.sem_objs[sid]
            self.ops[eng].append(lambda h, s=s, val=val: h.wait_ge(s, val))
            self.waited[eng][sid] = val
            self.n_instr += 1

    def _record(self, src, ticket, reads, writes):
        for r in reads:
            self.readers.setdefault(r, {})[src] = ticket
        for w in writes:
            self.writers[w] = {src: ticket}
            self.readers[w] = {}

    def op(self, eng, fn, reads, writes):
        reads = [self.key(r) for r in reads if r is not None and not isinstance(r, (int, float))]
        writes = [self.key(w) for w in writes]
        waits = self._deps(eng, reads, writes, False)
        self._emit_waits(eng, waits)
        if self.cur_sem[eng] is None or self.cur_cnt[eng] >= SEM_LIMIT:
            self.cur_sem[eng] = self._alloc_sem(f"s_{eng}_{self.nuid}")
            self.cur_cnt[eng] = 0
        self.cur_cnt[eng] += 1
        sid, val = self.cur_sem[eng], self.cur_cnt[eng]
        s = self.sem_objs[sid]
        self.ops[eng].append(lambda h, s=s: fn(h).then_inc(s, 1))
        self.n_instr += 1
        self._record(eng, (sid, val), reads, writes)

    def dma(self, out, in_, reads=None, writes=None, eng="sp", **kw):
        reads = [self.key(r) for r in (reads if reads is not None else [in_])]
        writes = [self.key(w) for w in (writes if writes is not None else [out])]
        waits = self._deps(eng, reads, writes, True)
        if len(self.dma_pool) < self.n_dma_sems:
            self.dma_pool.append([self._alloc_sem(f"s_dma_{self.nuid}"), 0])
        slot = self.dma_pool[self.dma_rr % self.n_dma_sems]
        self.dma_rr += 1
        if slot[1] + 16 > SEM_LIMIT:
            if self.waited[eng].get(slot[0], 0) < slot[1]:
                waits[slot[0]] = max(waits.get(slot[0], 0), slot[1])
            self._emit_waits(eng, waits)
            waits = {}
            slot[0] = self._alloc_sem(f"s_dma_{self.nuid}")
            slot[1] = 0
        sid = slot[0]
        if slot[1] > 0 and self.waited[eng].get(sid, 0) < slot[1]:
            waits[sid] = max(waits.get(sid, 0), slot[1])
        self._emit_waits(eng, waits)
        slot[1] += 16
        val = slot[1]
        s = self.sem_objs[sid]
        self.ops[eng].append(lambda h, s=s: h.dma_start(out=out, in_=in_, allow_slow_non_contiguous=True, **kw).then_inc(s, 16))
        self.n_instr += 1
        self._record(f"dma{sid}", (sid, val), reads, writes)
        return (sid, val)

    def wait_all(self, eng, keys):
        waits = {}
        for k in keys:
            k = self.key(k)
            for src, (sid, val) in self.writers.get(k, {}).items():
                if self.waited[eng].get(sid, 0) < val:
                    waits[sid] = max(waits.get(sid, 0), val)
        self._emit_waits(eng, waits)

    def mm(self, out, lhsT, rhs, start=True, stop=True, extra_reads=()):
        self.op("pe", lambda h: h.matmul(out, lhsT, rhs, start=start, stop=stop),
                [lhsT, rhs] + list(extra_reads) + ([] if start else [out]), [out])

    def tr(self, out, in_, ident):
        self.op("pe", lambda h: h.transpose(out, in_, ident), [in_, ident], [out])

    def act(self, out, in_, func, bias=0.0, scale=1.0, eng="act"):
        rd = [in_]
        if not isinstance(bias, (int, float)):
            rd.append(bias)
        if not isinstance(scale, (int, float)):
            rd.append(scale)
        self.op(eng, lambda h: h.activation(out=out, in_=in_, func=func, bias=bias, scale=scale), rd, [out])

    def tt(self, out, a, b, op, eng="dve"):
        self.op(eng, lambda h: h.tensor_tensor(out=out, in0=a, in1=b, op=op), [a, b], [out])

    def ts(self, out, a, s1, op0, s2=None, op1=None, eng="dve"):
        rd = [a] + [s for s in (s1, s2) if s is not None and not isinstance(s, (int, float))]
        if op1 is None:
            self.op(eng, lambda h: h.tensor_scalar(out=out, in0=a, scalar1=s1, scalar2=None, op0=op0), rd, [out])
        else:
            self.op(eng, lambda h: h.tensor_scalar(out=out, in0=a, scalar1=s1, scalar2=s2, op0=op0, op1=op1), rd, [out])

    def stt(self, out, a, s, b, op0, op1, eng="dve"):
        rd = [a, b] + ([] if isinstance(s, (int, float)) else [s])
        self.op(eng, lambda h: h.scalar_tensor_tensor(out=out, in0=a, scalar=s, in1=b, op0=op0, op1=op1), rd, [out])

    def cp(self, out, in_, eng="dve"):
        if eng == "act":
            self.op("act", lambda h: h.copy(out=out, in_=in_), [in_], [out])
        else:
            self.op(eng, lambda h: h.tensor_copy(out=out, in_=in_), [in_], [out])

    def red(self, out, in_, op=None, eng="dve"):
        op = op or ALU.add
        self.op(eng, lambda h: h.tensor_reduce(out=out, in_=in_, axis=AX.X, op=op), [in_], [out])

    def recip(self, out, in_):
        self.op("dve", lambda h: h.reciprocal(out=out, in_=in_), [in_], [out])

    def memset(self, ap, val, eng="dve"):
        self.op(eng, lambda h: h.memset(ap, val), [], [ap])

    def rsqrt(self, out, in_, scale, eps):
        self.act(out, in_, AF.Sqrt, bias=eps, scale=scale)
        self.recip(out, out)

    def finish(self):
        waits = {}
        for sid, val in self.dma_pool:
            if val > 0 and self.waited["sp"].get(sid, 0) < val:
                waits[sid] = val
        self._emit_waits("sp", waits)

    def emit(self):
        self.emit_block()
        self.stack.close()

    def emit_block(self):
        nc = self.nc
        ops = self.ops
        self.ops = {e: [] for e in self.ENGS}
        with nc.Block() as block:
            @block.tensor
            def _(h):
                for f in ops["pe"]:
                    f(h)

            @block.scalar
            def _(h):
                for f in ops["act"]:
                    f(h)

            @block.vector
            def _(h):
                for f in ops["dve"]:
                    f(h)

            @block.gpsimd
            def _(h):
                for f in ops["pool"]:
                    f(h)

            @block.sync
            def _(h):
                for f in ops["sp"]:
                    f(h)


EPS = 1e-6
D = 2048
NEG = -30000.0


def build(L):
    nc = bass.Bass("TRN2", target_bir_lowering=False)
    NTOK = L + 64
    tiles = [(0, i * 128, 128, i == 0, i == L // 128 - 1) for i in range(L // 128)]
    tiles += [(1, L, 32, True, True), (2, L + 32, 32, True, True)]

    def din(name, shape, dt=F32):
        return nc.dram_tensor(name, list(shape), dt, kind="ExternalInput").ap()

    def dout(name, shape, dt=F32):
        return nc.dram_tensor(name, list(shape), dt, kind="ExternalOutput").ap()

    xT_d = din("xT", [D, NTOK]); xtok_d = din("xtok", [NTOK, D])
    w_in_ab = din("w_in_ab", [D, 7200]); w_out_ab = din("w_out_ab", [D, D])
    w_in_cd = din("w_in_cd", [D, 6152]); w_out_cd = din("w_out_cd", [D, D])
    gluw_d = din("glu_w", [1024, 1024])
    consts_d = din("consts", [128, 9, 128])
    g0T_d = din("g0T", [128, 16]); g1b_d = din("g1b", [128, D]); gfb_d = din("gfb", [128, D])
    agw_d = din("a_gate_w", [16, 512]); agb_d = din("a_gate_b", [1, 512]); ang_d = din("a_norm_g_b", [128, 1024])
    cw_d = din("conv_w", [128, 24, 4]); alog_d = din("a_log_b", [128, 8]); dtb_d = din("dt_bias_b", [128, 8])
    bng_d = din("b_norm_g_b", [128, 128])
    s5col_d = din("s5col", [128, 3, 32]); s5row_d = din("s5row", [128, 3, 4096])
    bdb_d = din("bd_b", [2, 8, 128, 512]); bdc_d = din("bd_c", [2, 4, 128, 8, 128]); bdd_d = din("bd_d", [4, 128, 2, 128])
    glub_d = din("glu_b_b", [128, 1024]); dib_d = din("d_i_b", [128, 4]); dfb_d = din("d_f_b", [128, 4])
    dng_d = din("d_norm_g_b", [128, 1024])
    si_conv = din("si_conv", [2, 128, 24, 3]); si_gla = din("si_gla", [2, 4, 128, 256]); si_gdn = din("si_gdn", [2, 8, 128, 128])
    si_s5 = din("si_s5", [2, 128, 2, 32]); si_mc = din("si_mc", [2, 4, 128, 256]); si_mn = din("si_mn", [2, 128, 4])
    si_mm = din("si_mm", [2, 128, 4])

    y_d = dout("y", [NTOK, D])
    so_conv = dout("so_conv", [3, 128, 24, 3]); so_gla = dout("so_gla", [3, 4, 128, 256]); so_gdn = dout("so_gdn", [3, 8, 128, 128])
    so_s5 = dout("so_s5", [3, 128, 2, 32]); so_mc = dout("so_mc", [3, 4, 128, 256]); so_mn = dout("so_mn", [3, 128, 4])
    so_mm = dout("so_mm", [3, 128, 4])

    oT_d = nc.dram_tensor("oT_s", [D, NTOK], BF16).ap()
    h1_d = nc.dram_tensor("h1_s", [NTOK, D], F32).ap()
    hnT_d = nc.dram_tensor("hnT_s", [D, NTOK], BF16).ap()
    yT_d = nc.dram_tensor("yT_s", [1024, NTOK], BF16).ap()
    yz_d = nc.dram_tensor("yz_s", [NTOK, 1024], F32).ap()

    k = KB(nc)
    k.psum_init()
    sb = k.sb
    cst = sb("cst", [128, 9, 128])
    k.dma(cst[:], consts_d[:, :, :])
    ident = cst[:, 0, :]; U = cst[:, 1, :]; SU = cst[:, 2, :]; NSL = cst[:, 3, :]; ones = cst[:, 4, :]
    NMU = cst[:, 5, :]; NML = cst[:, 6, :]; IOTA = cst[:, 7, :]; SELS = cst[:, 8, :]
    tcol = cst[:, 8, 0:1]
    sel128 = sb("sel128", [128, 128]); sel32 = sb("sel32", [128, 128])
    k.ts(sel128[:], ones, cst[:, 8, 1:2], ALU.mult)
    k.ts(sel32[:], ones, cst[:, 8, 2:3], ALU.mult)
    epsb = sb("epsb", [128, 1]); k.memset(epsb[:], EPS)
    wbuf = sb("wbuf", [128, 16, 2048], BF16)
    xg = sb("xg", [128, 16, 128], BF16)
    projT = sb("projT", [128, 1160])
    tbuf = [sb(f"tb{i}", [128, 512]) for i in range(12)]
    cols = sb("cols", [128, 64])
    obf = [sb(f"obf{i}", [128, 128], BF16) for i in range(4)]
    obc = [0]
    qT = sb("qT", [128, 128]); kT = sb("kT", [128, 128])
    k.open_scope()
    xt2 = [sb(f"xt{i}", [128, 16, 128]) for i in range(2)]
    xsq = sb("xsq", [128, 16, 128]); xs1 = sb("xs1", [128, 128])
    rbc = sb("rbc", [128, 128]); rcol = sb("rcol", [128, 1])
    g0T = sb("g0T", [128, 16]); k.dma(g0T[:], g0T_d[:, :])
    tctr = [0]

    def tmp():
        t = tbuf[tctr[0] % len(tbuf)]
        tctr[0] += 1
        return t

    def rs_eps(out, in_, scale):
        k.rsqrt(out, in_, scale, epsb[:out.shape[0], :])

    def load_w(src, pairs):
        for d0, s0, n in pairs:
            k.dma(wbuf[:, :, d0:d0 + n], src[:, s0:s0 + n].rearrange("(kc p) c -> p kc c", p=128), eng="pool")

    def norm_x(ti, t0, T):
        xt = xt2[ti % 2]
        k.dma(xt[:, :, :T], xT_d[:, t0:t0 + T].rearrange("(kc p) t -> p kc t", p=128))
        k.tt(xsq[:, :, :T], xt[:, :, :T], xt[:, :, :T], ALU.mult, eng="pool")
        k.red(xs1[:, :T], xsq[:, :, :T].rearrange("p kc t -> p t kc"))
        p = k.ps(); k.mm(p[:, :T], ones, xs1[:, :T]); rs_eps(rbc[:, :T], p[:, :T], 1.0 / D)
        p = k.ps(); k.mm(p[:T, 0:1], xs1[:, :T], ones[:, 0:1]); rs_eps(rcol[:T, :], p[:T, 0:1], 1.0 / D)
        k.tt(xg[:, :, :T], xt[:, :, :T], g0T[:, :].unsqueeze(2).to_broadcast([128, 16, T]), ALU.mult)

    def load_hn(t0, T):
        k.dma(xg[:, :, :T], hnT_d[:, t0:t0 + T].rearrange("(kc p) t -> p kc t", p=128), reads=[("hnT", t0)])

    def projF(dst, col0, T, ncols=128, scaled=True):
        p = k.ps()
        for kc in range(16):
            k.mm(p[:ncols, :T], wbuf[:, kc, col0:col0 + ncols], xg[:, kc, :T], start=(kc == 0), stop=(kc == 15))
        if scaled:
            k.tt(dst, p[:ncols, :T], rbc[:ncols, :T], ALU.mult)
        else:
            k.cp(dst, p[:ncols, :T], eng="act")

    def projTok(col0, ncols, T, scaled=True, dst0=0):
        for c0 in range(0, ncols, 512):
            n = min(512, ncols - c0)
            p = k.ps()
            for kc in range(16):
                k.mm(p[:T, :n], xg[:, kc, :T], wbuf[:, kc, col0 + c0:col0 + c0 + n], start=(kc == 0), stop=(kc == 15))
            if scaled:
                k.ts(projT[:T, dst0 + c0:dst0 + c0 + n], p[:T, :n], rcol[:T, :], ALU.mult)
            else:
                k.cp(projT[:T, dst0 + c0:dst0 + c0 + n], p[:T, :n], eng="act")

    def store_oT(src, ncols, frow, t0, T):
        for c0 in range(0, ncols, 128):
            p = k.ps(); k.tr(p[:, :T], src[:T, c0:c0 + 128], ident[:T, :T])
            ob = obf[obc[0] % 4]; obc[0] += 1
            k.cp(ob[:, :T], p[:, :T], eng="act")
            k.dma(oT_d[frow + c0:frow + c0 + 128, t0:t0 + T], ob[:, :T], writes=[("oT", frow + c0, t0)])

    def headnorm_rms(o_sb, T, n, gsz, dst):
        sq = tmp(); k.tt(sq[:T, :n], o_sb, o_sb, ALU.mult, eng="pool")
        c = cols[:, 60:61]; k.red(c[:T, :], sq[:T, :n])
        r = cols[:, 61:62]; rs_eps(r[:T, :], c[:T, :], 1.0 / n)
        k.stt(dst, o_sb, r[:T, :], gsz, ALU.mult, ALU.mult)

    S_gla = sb("S_gla", [128, 256]); S_gdn = [sb(f"S_gdn{i}", [128, 128]) for i in range(2)]
    cbuf = sb("cbuf", [128, 6, 131]); cw = sb("cw", [128, 6, 4])
    gw = sb("gw", [16, 128]); gb = sb("gb", [1, 128]); angb = sb("angb", [128, 256]); bngb = sb("bngb", [128, 128])
    alog = sb("alog", [128, 8]); dtb = sb("dtb", [128, 8]); na = sb("na", [128, 8])
    k.dma(alog[:], alog_d[:, :]); k.dma(dtb[:], dtb_d[:, :]); k.dma(bngb[:], bng_d[:, :])
    k.act(na[:], alog[:], AF.Exp); k.ts(na[:], na[:], -1.0, ALU.mult)
    gaT = sb("gaT", [16, 128])
    cvs = sb("cvs", [128, 6, 128]); cacc = sb("cacc", [128, 6, 128]); ctmp = sb("ctmp", [128, 6, 128])
    Rp = [sb(f"Rp{i}", [128, 128]) for i in range(7)]
    Qp = [sb(f"Qp{i}", [128, 128]) for i in range(2)]
    g_vk = sb("g_vk", [128, 256]); g_egrow = sb("g_egrow", [128, 128]); g_AT = sb("g_AT", [128, 128])
    g_X = [sb(f"g_X{i}", [128, 256]) for i in range(2)]
    SC = 128 ** -0.5

    def gla_tile(hg, seq, t0, T):
        kk = projT[:T, 0:128]; v = projT[:T, 128:384]; za = projT[:T, 384:640]
        p = k.ps()
        k.mm(p[:T, :128], gaT[:, :T], gw[:, :], start=True, stop=False)
        k.mm(p[:T, :128], ones[0:1, :T], gb[:, :], start=False, stop=True)
        e = tmp(); k.act(e[:T, :128], p[:T, :128], AF.Exp, scale=-1.0)
        spl = tmp(); k.act(spl[:T, :128], e[:T, :128], AF.Ln, bias=1.0)
        pc = k.ps(); k.mm(pc[:, :T], spl[:T, :128], U[:T, :T])
        ebT = tmp(); k.act(ebT[:, :T], pc[:, :T], AF.Exp, scale=-1.0 / 16)
        enbT = tmp(); k.act(enbT[:, :T], pc[:, :T], AF.Exp, scale=1.0 / 16)
        qd = tmp(); k.stt(qd[:, :T], qT[:, :T], SC, ebT[:, :T], ALU.mult, ALU.mult)
        kd = tmp(); k.tt(kd[:, :T], kT[:, :T], enbT[:, :T], ALU.mult, eng="pool")
        pd = k.ps(); k.mm(pd[:T, :128], NSL[:T, :T], spl[:T, :128])
        ekw = tmp(); k.act(ekw[:T, :128], pd[:T, :128], AF.Exp, scale=1.0 / 16)
        kw = tmp(); k.tt(kw[:T, :128], kk, ekw[:T, :128], ALU.mult, eng="pool")
        psc = k.ps(); k.mm(psc[:T, :T], kd[:, :T], qd[:, :T])
        PT = tmp(); k.tt(PT[:T, :T], psc[:T, :T], U[:T, :T], ALU.mult)
        po = k.ps()
        k.mm(po[:T, :256], PT[:T, :T], v, start=True, stop=False)
        k.mm(po[:T, :256], qd[:, :T], S_gla[:, :], start=False, stop=True)
        o_sb = tmp(); k.cp(o_sb[:T, :256], po[:T, :256], eng="act")
        pkv = k.ps(); k.mm(pkv[:, :256], kw[:T, :128], v)
        k.stt(S_gla[:, :], S_gla[:, :], ebT[:, T - 1:T], pkv[:, :256], ALU.mult, ALU.add)
        gsz = tmp(); k.act(gsz[:T, :256], za, AF.Silu)
        k.tt(gsz[:T, :256], gsz[:T, :256], angb[:T, :], ALU.mult, eng="pool")
        og = tmp(); headnorm_rms(o_sb[:T, :256], T, 256, gsz[:T, :256], og[:T, :256])
        store_oT(og, 256, hg * 256, t0, T)

    def gdn_pre(T):
        for j in range(4):
            wj = cw[:, :, j:j + 1].to_broadcast([128, 6, T])
            if j == 0:
                k.tt(cacc[:, :, :T], cbuf[:, :, 0:T], wj, ALU.mult)
            else:
                k.tt(ctmp[:, :, :T], cbuf[:, :, j:j + T], wj, ALU.mult, eng="pool")
                k.tt(cacc[:, :, :T], cacc[:, :, :T], ctmp[:, :, :T], ALU.add)
        k.act(cvs[:, :, :T], cacc[:, :, :T], AF.Silu)
        k.tt(ctmp[:, 0:4, :T], cvs[:, 0:4, :T], cvs[:, 0:4, :T], ALU.mult, eng="pool")
        for b in range(4):
            p = k.ps(); k.mm(p[:, :T], ones, ctmp[:, b, :T])
            rs_eps(cacc[:, b, :T], p[:, :T], 1.0)
        k.tt(cvs[:, 0:4, :T], cvs[:, 0:4, :T], cacc[:, 0:4, :T], ALU.mult)

    def gdn_head(hg, hh, seq, t0, T):
        Sg = S_gdn[hh]
        qTh = cvs[:, 0 + hh, :T]; kTh = cvs[:, 2 + hh, :T]; vTh = cvs[:, 4 + hh, :T]
        zb = projT[:T, 640 + 128 * hh:768 + 128 * hh]
        beta_pre = projT[:T, 896 + hh:897 + hh]; a_pre = projT[:T, 898 + hh:899 + hh]
        gh = 2 * hg + hh
        c = lambda i: cols[:, i:i + 1]
        beta = c(0); k.act(beta[:T, :], beta_pre, AF.Sigmoid)
        e = c(1); k.act(e[:T, :], a_pre, AF.Exp, bias=dtb[:T, gh:gh + 1])
        spg = c(2); k.act(spg[:T, :], e[:T, :], AF.Ln, bias=1.0)
        graw = c(3); k.tt(graw[:T, :], spg[:T, :], na[:T, gh:gh + 1], ALU.mult)
        vk = g_vk
        p = k.ps(); k.tr(p[:T, 0:128], vTh, ident); k.tr(p[:T, 128:256], kTh, ident)
        k.cp(vk[:T, :256], p[:T, :256], eng="act")
        p = k.ps(); k.mm(p[:T, 0:1], U[:T, :T], graw[:T, :])
        gc = c(4); k.cp(gc[:T, :], p[:T, 0:1]); ngc = c(5); k.ts(ngc[:T, :], p[:T, 0:1], -1.0, ALU.mult)
        Ug = tmp(); k.ts(Ug[:T, :T], U[:T, :T], graw[:T, :], ALU.mult)
        pg = k.ps(); k.mm(pg[:, :T], ones[:T, :], Ug[:T, :T])
        Ib = tmp(); k.ts(Ib[:T, :T], ident[:T, :T], beta[:T, :], ALU.mult)
        pb = k.ps(); k.mm(pb[:T, :T], ones[:T, :T], Ib[:T, :T])
        arg = tmp(); k.tt(arg[:T, :T], pg[:T, :T], NMU[:T, :T], ALU.add)
        decT = tmp(); k.act(decT[:T, :T], arg[:T, :T], AF.Exp, bias=ngc[:T, :])
        egrow = g_egrow; k.act(egrow[:, :T], pg[:, :T], AF.Exp)
        glast = c(6); k.cp(glast[:, :], pg[:, T - 1:T])
        eglast = c(7); k.cp(eglast[:, :], egrow[:, T - 1:T], eng="pool")
        egc = c(8); k.act(egc[:T, :], gc[:T, :], AF.Exp)
        ekl = c(9); k.act(ekl[:T, :], gc[:T, :], AF.Exp, scale=-1.0, bias=glast[:T, :])
        pkk = k.ps(); k.mm(pkk[:T, :T], kTh, kTh)
        pqk = k.ps(); k.mm(pqk[:T, :T], kTh, qTh)
        AT = g_AT; k.stt(AT[:T, :T], pqk[:T, :T], SC, decT[:T, :T], ALU.mult, ALU.mult)
        t1 = tmp(); k.tt(t1[:T, :T], pkk[:T, :T], decT[:T, :T], ALU.mult)
        t2 = tmp(); k.tt(t2[:T, :T], pb[:T, :T], SU[:T, :T], ALU.mult)
        LT = Rp[0]; k.tt(LT[:T, :T], t1[:T, :T], t2[:T, :T], ALU.mult, eng="pool")
        X = g_X[0]; xi_ = 0
        k.ts(X[:T, 0:128], vk[:T, 0:128], beta[:T, :], ALU.mult)
        k.ts(X[:T, 128:256], vk[:T, 128:256], beta[:T, :], ALU.mult, egc[:T, :], ALU.mult)
        p = k.ps(); k.tr(p[:T, :T], LT[:T, :T], ident[:T, :T]); k.cp(Qp[0][:T, :T], p[:T, :T], eng="act")
        p = k.ps(); k.mm(p[:T, :256], LT[:T, :T], X[:T, :256])
        xi_ ^= 1; Xn = g_X[xi_]; k.tt(Xn[:T, :256], X[:T, :256], p[:T, :256], ALU.subtract); X = Xn
        nsq = {128: 6, 32: 4}[T]
        for n in range(nsq):
            Q = Qp[n % 2]; R = Rp[n]
            p2 = k.ps(); k.mm(p2[:T, :T], Q[:T, :T], R[:T, :T]); k.cp(Rp[n + 1][:T, :T], p2[:T, :T], eng="act")
            if n < nsq - 1:
                p1 = k.ps(); k.mm(p1[:T, :T], R[:T, :T], Q[:T, :T]); k.cp(Qp[(n + 1) % 2][:T, :T], p1[:T, :T])
            p = k.ps(); k.mm(p[:T, :256], Rp[n + 1][:T, :T], X[:T, :256])
            xi_ ^= 1; Xn = g_X[xi_]; k.tt(Xn[:T, :256], X[:T, :256], p[:T, :256], ALU.add); X = Xn
        p = k.ps(); k.tr(p[:, :T], X[:T, 128:256], ident[:T, :T])
        wT = tmp(); k.cp(wT[:, :T], p[:, :T], eng="act")
        p = k.ps(); k.mm(p[:T, :128], wT[:, :T], Sg[:, :])
        vnew = tmp(); k.tt(vnew[:T, :128], X[:T, 0:128], p[:T, :128], ALU.subtract)
        qg = tmp(); k.stt(qg[:, :T], qTh, SC, egrow[:, :T], ALU.mult, ALU.mult)
        po = k.ps()
        k.mm(po[:T, :128], qg[:, :T], Sg[:, :], start=True, stop=False)
        k.mm(po[:T, :128], AT[:T, :T], vnew[:T, :128], start=False, stop=True)
        o_sb = tmp(); k.cp(o_sb[:T, :128], po[:T, :128], eng="act")
        kg = tmp(); k.ts(kg[:T, :128], vk[:T, 128:256], ekl[:T, :], ALU.mult)
        pkv = k.ps(); k.mm(pkv[:, :128], kg[:T, :128], vnew[:T, :128])
        k.stt(Sg[:, :], Sg[:, :], eglast[:, :], pkv[:, :128], ALU.mult, ALU.add)
        gsz = tmp(); k.act(gsz[:T, :128], zb, AF.Silu)
        k.tt(gsz[:T, :128], gsz[:T, :128], bngb[:T, :], ALU.mult, eng="pool")
        og = tmp(); headnorm_rms(o_sb[:T, :128], T, 128, gsz[:T, :128], og[:T, :128])
        store_oT(og, 128, 1024 + gh * 128, t0, T)

    for hg in range(4):
        TB = 1040
        load_w(w_in_ab, [(0, 128 * hg, 128), (128, 512 + 128 * hg, 128),
                         (256, 3088 + 256 * hg, 256), (512, 4112 + 256 * hg, 256), (768, 5136 + 256 * hg, 256),
                         (1024, 3072, 16),
                         (TB, 512 + 128 * hg, 128), (TB + 128, 1024 + 256 * hg, 256), (TB + 384, 2048 + 256 * hg, 256),
                         (TB + 640, 6160 + 256 * hg, 256), (TB + 896, 7184 + 2 * hg, 2), (TB + 898, 7192 + 2 * hg, 2)])
        k.dma(gw[:], agw_d[:, 128 * hg:128 * hg + 128]); k.dma(gb[:], agb_d[:, 128 * hg:128 * hg + 128])
        k.dma(angb[:], ang_d[:, 256 * hg:256 * hg + 256])
        for typ in range(3):
            k.dma(cw[:, 2 * typ:2 * typ + 2, :], cw_d[:, typ * 8 + 2 * hg:typ * 8 + 2 * hg + 2, :])
        for ti, (seq, t0, T, first, last) in enumerate(tiles):
            if first:
                if seq == 0:
                    k.memset(S_gla[:], 0.0); k.memset(S_gdn[0][:], 0.0); k.memset(S_gdn[1][:], 0.0)
                    k.memset(cbuf[:, :, 0:3], 0.0)
                else:
                    k.dma(S_gla[:], si_gla[seq - 1, hg, :, :])
                    for hh in range(2):
                        k.dma(S_gdn[hh][:], si_gdn[seq - 1, 2 * hg + hh, :, :])
                    for typ in range(3):
                        k.dma(cbuf[:, 2 * typ:2 * typ + 2, 0:3], si_conv[seq - 1, :, typ * 8 + 2 * hg:typ * 8 + 2 * hg + 2, :])
            else:
                k.cp(cbuf[:, :, 0:3], cbuf[:, :, 128:131])
            norm_x(ti, t0, T)
            projF(qT[:, :T], 0, T); projF(kT[:, :T], 128, T)
            for b in range(6):
                projF(cbuf[:, b, 3:3 + T], 256 + 128 * b, T)
            projF(gaT[:, :T], 1024, T, ncols=16)
            projTok(TB, 900, T)
            gla_tile(hg, seq, t0, T)
            gdn_pre(T)
            for hh in range(2):
                gdn_head(hg, hh, seq, t0, T)
            if last:
                k.dma(so_gla[seq, hg, :, :], S_gla[:])
                for hh in range(2):
                    k.dma(so_gdn[seq, 2 * hg + hh, :, :], S_gdn[hh][:])
                for typ in range(3):
                    k.dma(so_conv[seq, :, typ * 8 + 2 * hg:typ * 8 + 2 * hg + 2, :], cbuf[:, 2 * typ:2 * typ + 2, T:T + 3])

    k.close_scope()
    def outproj_alloc():
        return (sb("ot", [128, 16, 128], BF16), sb("resid", [128, D]), sb("hbuf", [128, D]),
                sb("gnb", [128, D]), sb("hsq", [128, D]), sb("hnT_sb", [128, 16, 128], BF16))

    def outproj_pass(w_out, layer):
        k.open_scope()
        ot, resid, hbuf, gnb, hsq, hnT_sb = outproj_alloc()
        load_w(w_out, [(0, 0, 2048)])
        k.dma(gnb[:], (g1b_d if layer == 0 else gfb_d)[:, :])
        for ti, (seq, t0, T, first, last) in enumerate(tiles):
            k.dma(ot[:, :, :T], oT_d[:, t0:t0 + T].rearrange("(kc p) t -> p kc t", p=128),
                  reads=[("oT", f, t0) for f in range(0, D, 128)])
            if layer == 0:
                k.dma(resid[:T, :], xtok_d[t0:t0 + T, :])
            else:
                k.dma(resid[:T, :], h1_d[t0:t0 + T, :], reads=[("h1", t0)])
            for nb in range(4):
                p = k.ps()
                for kc in range(16):
                    k.mm(p[:T, :512], ot[:, kc, :T], wbuf[:, kc, nb * 512:(nb + 1) * 512], start=(kc == 0), stop=(kc == 15))
                k.tt(hbuf[:T, nb * 512:(nb + 1) * 512], p[:T, :512], resid[:T, nb * 512:(nb + 1) * 512], ALU.add)
            if layer == 0:
                k.dma(h1_d[t0:t0 + T, :], hbuf[:T, :], writes=[("h1", t0)])
            k.tt(hsq[:T, :], hbuf[:T, :], hbuf[:T, :], ALU.mult, eng="pool")
            c = cols[:, 62:63]; k.red(c[:T, :], hsq[:T, :])
            r = cols[:, 63:64]; rs_eps(r[:T, :], c[:T, :], 1.0 / D)
            k.stt(hsq[:T, :], hbuf[:T, :], r[:T, :], gnb[:T, :], ALU.mult, ALU.mult)
            if layer == 0:
                for q4 in range(4):
                    p = k.ps()
                    for j in range(4):
                        kc = q4 * 4 + j
                        k.tr(p[:, j * 128:j * 128 + T], hsq[:T, kc * 128:(kc + 1) * 128], ident[:T, :T])
                    if T == 128:
                        k.cp(hnT_sb[:, q4 * 4:q4 * 4 + 4, :], p[:, :].rearrange("p (a b) -> p a b", a=4), eng="act")
                    else:
                        for j in range(4):
                            k.cp(hnT_sb[:, q4 * 4 + j, :T], p[:, j * 128:j * 128 + T], eng="act")
                k.dma(hnT_d[:, t0:t0 + T].rearrange("(kc p) t -> p kc t", p=128), hnT_sb[:, :, :T], writes=[("hnT", t0)])
            else:
                k.dma(y_d[t0:t0 + T, :], hsq[:T, :])
        k.close_scope()

    outproj_pass(w_out_ab, 0)

    k.open_scope()
    s5c = sb("s5c", [128, 3, 8]); s5r = sb("s5r", [128, 3, 1024])
    ETr = sb("ETr", [128, 8, 128]); ETi = sb("ETi", [128, 8, 128]); EIr = sb("EIr", [128, 1024]); EIi = sb("EIi", [128, 1024])
    big = [sb(f"big{i}", [128, 1024]) for i in range(6)]
    bigi = sb("bigi", [128, 1024], I32)
    BB = sb("BB", [128, 4, 512]); WB = sb("WB", [128, 4, 512])
    CB = sb("CB", [128, 2, 8, 128]); DD = sb("DD", [128, 2, 128])
    uT = sb("uT", [128, 2, 128])
    car = sb("car", [128, 2, 8]); xl = sb("xl", [128, 2, 8]); A1 = sb("A1", [128, 2, 8])
    TWO_PI = 2.0 * np.pi

    def sincos(dst_sin, dst_cos, ang, n, shp=None):
        for dst, off in ((dst_sin, 0.0), (dst_cos, 0.25)):
            tn = big[4][:, :n]; tf = big[5][:, :n]
            k.ts(tn, ang, 1.0 / TWO_PI, ALU.mult, off, ALU.add)
            k.cp(bigi[:, :n], tn)
            k.cp(tf, bigi[:, :n])
            k.tt(tn, tn, tf, ALU.subtract)
            k.act(dst, tn, AF.Sin, scale=TWO_PI)

    def cmul(outr, outi, ar, ai, br, bi, n, t1, t2):
        k.tt(t1, ar, br, ALU.mult); k.tt(t2, ai, bi, ALU.mult, eng="pool"); k.tt(outr, t1, t2, ALU.subtract)
        k.tt(t1, ar, bi, ALU.mult); k.tt(t2, ai, br, ALU.mult, eng="pool"); k.tt(outi, t1, t2, ALU.add)

    def s5_setup(hg):
        k.dma(s5c[:], s5col_d[:, :, 8 * hg:8 * hg + 8]); k.dma(s5r[:], s5row_d[:, :, 1024 * hg:1024 * hg + 1024])
        dtc = cols[:, 16:24]; arc = cols[:, 24:32]; aic = cols[:, 32:40]
        k.act(dtc, s5c[:, 2, :], AF.Exp)
        k.tt(arc, s5c[:, 0, :], dtc, ALU.mult); k.tt(aic, s5c[:, 1, :], dtc, ALU.mult)
        ang = big[0][:, :].rearrange("p (a b) -> p a b", a=8)
        io = IOTA.unsqueeze(1).to_broadcast([128, 8, 128])
        k.tt(ang, io, aic.unsqueeze(2).to_broadcast([128, 8, 128]), ALU.mult)
        sincos(big[1][:, :], big[2][:, :], big[0][:, :], 1024)
        mg = big[3][:, :].rearrange("p (a b) -> p a b", a=8)
        k.tt(mg, io, arc.unsqueeze(2).to_broadcast([128, 8, 128]), ALU.mult)
        k.act(big[3][:, :], big[3][:, :], AF.Exp)
        k.tt(ETi[:, :, :].rearrange("p a b -> p (a b)"), big[3][:, :], big[1][:, :], ALU.mult)
        k.tt(ETr[:, :, :].rearrange("p a b -> p (a b)"), big[3][:, :], big[2][:, :], ALU.mult)
        k.cp(A1[:, 0, :], ETr[:, :, 1]); k.cp(A1[:, 1, :], ETi[:, :, 1])
        dtr = big[0][:, :]; k.act(dtr, s5r[:, 2, :], AF.Exp)
        arr = sb_arr[:, :]; air = sb_air[:, :]
        k.tt(arr, s5r[:, 0, :], dtr, ALU.mult); k.tt(air, s5r[:, 1, :], dtr, ALU.mult)
        sincos(big[1][:, :], big[2][:, :], air, 1024)
        k.act(big[3][:, :], arr, AF.Exp)
        abi = big[1][:, :]; abr = big[2][:, :]
        k.tt(abi, abi, big[3][:, :], ALU.mult); k.tt(abr, abr, big[3][:, :], ALU.mult)
        k.ts(abr, abr, -1.0, ALU.add)
        den = big[3][:, :]; t = big[0][:, :]
        k.tt(den, s5r[:, 0, :], s5r[:, 0, :], ALU.mult); k.tt(t, s5r[:, 1, :], s5r[:, 1, :], ALU.mult)
        k.tt(den, den, t, ALU.add); k.recip(den, den)
        zr = big[4][:, :]; zi = big[5][:, :]
        k.tt(zr, abr, s5r[:, 0, :], ALU.mult); k.tt(t, abi, s5r[:, 1, :], ALU.mult); k.tt(zr, zr, t, ALU.add); k.tt(zr, zr, den, ALU.mult)
        k.tt(zi, abi, s5r[:, 0, :], ALU.mult); k.tt(t, abr, s5r[:, 1, :], ALU.mult); k.tt(zi, zi, t, ALU.subtract); k.tt(zi, zi, den, ALU.mult)
        for h in range(2):
            k.dma(WB[:, 2 * h, :], bdb_d[0, 2 * hg + h, :, :]); k.dma(WB[:, 2 * h + 1, :], bdb_d[1, 2 * hg + h, :, :])
            sl = slice(512 * h, 512 * h + 512)
            t1 = big[0][:, 0:512]; t2 = big[0][:, 512:1024]
            cmul(BB[:, 2 * h, :], BB[:, 2 * h + 1, :], zr[:, sl], zi[:, sl], WB[:, 2 * h, :], WB[:, 2 * h + 1, :], 512, t1, t2)
        k.ts(big[0][:, :], air, tcol, ALU.mult)
        sincos(big[1][:, :], big[2][:, :], big[0][:, :], 1024)
        k.ts(big[3][:, :], arr, tcol, ALU.mult); k.act(big[3][:, :], big[3][:, :], AF.Exp, scale=-1.0)
        k.tt(EIr[:, :], big[3][:, :], big[2][:, :], ALU.mult)
        k.stt(EIi[:, :], big[3][:, :], -1.0, big[1][:, :], ALU.mult, ALU.mult)
        k.dma(CB[:, 0, :, :], bdc_d[0, hg, :, :, :]); k.dma(CB[:, 1, :, :], bdc_d[1, hg, :, :, :])
        k.ts(CB[:, 1, :, :], CB[:, 1, :, :], -1.0, ALU.mult)
        k.dma(DD[:], bdd_d[hg, :, :, :])

    sb_arr = sb("sb_arr", [128, 1024]); sb_air = sb("sb_air", [128, 1024])

    def s5_tile(hg, seq, t0, T):
        zc = projT[:T, 0:256]
        Bu = [big[0], big[1]]
        for h in range(2):
            for cpx in range(2):
                p = k.ps(); k.mm(p[:T, :512], uT[:, h, :T], BB[:, 2 * h + cpx, :])
                k.cp(Bu[cpx][:T, 512 * h:512 * h + 512], p[:T, :512], eng="act")
        Zr = big[2]; Zi = big[3]
        cmul(Zr[:T, :], Zi[:T, :], EIr[:T, :], EIi[:T, :], Bu[0][:T, :], Bu[1][:T, :], 1024, big[4][:T, :], big[5][:T, :])
        Wc = [big[0], big[1]]
        for cpx, Z in enumerate((Zr, Zi)):
            for half in range(2):
                p = k.ps()
                for j in range(4):
                    ch = half * 4 + j
                    k.mm(p[:, j * 128:j * 128 + T], Z[:T, ch * 128:(ch + 1) * 128], U[:T, :T])
                src = p[:, :].rearrange("p (a b) -> p a b", a=4)[:, :, :T]
                dst = Wc[cpx][:, :].rearrange("p (a b) -> p a b", a=8)[:, half * 4:half * 4 + 4, :T]
                k.tt(dst, src, car[:, cpx, half * 4:half * 4 + 4].unsqueeze(2).to_broadcast([128, 4, T]), ALU.add)
        XTr = big[2][:, :].rearrange("p (a b) -> p a b", a=8); XTi = big[3][:, :].rearrange("p (a b) -> p a b", a=8)
        W3 = [w[:, :].rearrange("p (a b) -> p a b", a=8) for w in Wc]
        t1 = big[4][:, :].rearrange("p (a b) -> p a b", a=8); t2 = big[5][:, :].rearrange("p (a b) -> p a b", a=8)
        cmul(XTr[:, :, :T], XTi[:, :, :T], ETr[:, :, :T], ETi[:, :, :T], W3[0][:, :, :T], W3[1][:, :, :T], 0, t1[:, :, :T], t2[:, :, :T])
        k.cp(xl[:, 0, :], XTr[:, :, T - 1]); k.cp(xl[:, 1, :], XTi[:, :, T - 1])
        c1 = cols[:, 40:48]; c2 = cols[:, 48:56]
        cmul(car[:, 0, :], car[:, 1, :], A1[:, 0, :], A1[:, 1, :], xl[:, 0, :], xl[:, 1, :], 8, c1, c2)
        py = k.ps()
        for h in range(2):
            o = py[:T, 128 * h:128 * h + 128]
            k.mm(o, uT[:, h, :T], DD[:, h, :], start=True, stop=False)
            for pc in range(4):
                ch = 4 * h + pc
                k.mm(o, XTr[:, ch, :T], CB[:, 0, ch, :], start=False, stop=False)
                k.mm(o, XTi[:, ch, :T], CB[:, 1, ch, :], start=False, stop=(pc == 3))
        yg = tmp(); k.act(yg[:T, :256], py[:T, :256], AF.Gelu)
        sz = tmp(); k.act(sz[:T, :256], zc, AF.Silu)
        yz = tmp(); k.tt(yz[:T, :256], yg[:T, :256], sz[:T, :256], ALU.mult)
        k.dma(yz_d[t0:t0 + T, 256 * hg:256 * hg + 256], yz[:T, :256], writes=[("yz", hg, t0)])
        for c0 in range(0, 256, 128):
            p = k.ps(); k.tr(p[:, :T], yg[:T, c0:c0 + 128], ident[:T, :T])
            ob = obf[obc[0] % 4]; obc[0] += 1
            k.cp(ob[:, :T], p[:, :T], eng="act")
            k.dma(yT_d[256 * hg + c0:256 * hg + c0 + 128, t0:t0 + T], ob[:, :T], writes=[("yT", 256 * hg + c0, t0)])

    for hg in range(4):
        load_w(w_in_cd, [(0, 256 * hg, 256), (256, 1024 + 256 * hg, 256)])
        s5_setup(hg)
        for ti, (seq, t0, T, first, last) in enumerate(tiles):
            if first:
                if seq == 0:
                    k.memset(car[:], 0.0)
                else:
                    k.dma(xl[:], si_s5[seq - 1, :, :, 8 * hg:8 * hg + 8])
                    cmul(car[:, 0, :], car[:, 1, :], A1[:, 0, :], A1[:, 1, :], xl[:, 0, :], xl[:, 1, :], 8, cols[:, 40:48], cols[:, 48:56])
            load_hn(t0, T)
            projF(uT[:, 0, :T], 0, T, scaled=False); projF(uT[:, 1, :T], 128, T, scaled=False)
            projTok(256, 256, T, scaled=False)
            s5_tile(hg, seq, t0, T)
            if last:
                k.dma(so_s5[seq, :, :, 8 * hg:8 * hg + 8], xl[:])

    k.close_scope()
    k.open_scope()
    Cx = sb("Cx", [128, 257]); mst = sb("mst", [128, 1]); vx = sb("vx", [128, 257])
    gluw = sb("gluw", [128, 8, 256], BF16); ytl = sb("ytl", [128, 8, 128], BF16)
    glub = sb("glub", [128, 256]); dngb = sb("dngb", [128, 256]); dib = sb("dib", [128, 4]); dfb = sb("dfb", [128, 4])
    yzt = sb("yzt", [128, 256])
    k.dma(dib[:], dib_d[:, :]); k.dma(dfb[:], dfb_d[:, :]); k.ts(dfb[:], dfb[:], -1.0, ALU.mult)
    k.memset(vx[:, 256:257], 1.0)

    def mlstm_tile(hg, seq, t0, T):
        kk = projT[:T, 0:128]; od = projT[:T, 384:640]; zd = projT[:T, 640:896]
        c = lambda i: cols[:, i:i + 1]
        k.cp(vx[:T, 0:256], projT[:T, 128:384], eng="pool")
        ip = c(0); k.tt(ip[:T, :], projT[:T, 896:897], dib[:T, hg:hg + 1], ALU.add)
        e = c(1); k.act(e[:T, :], projT[:T, 897:898], AF.Exp, scale=-1.0, bias=dfb[:T, hg:hg + 1])
        sp = c(2); k.act(sp[:T, :], e[:T, :], AF.Ln, bias=1.0)
        p = k.ps(); k.mm(p[:T, 0:1], U[:T, :T], sp[:T, :]); bcol = c(3); k.ts(bcol[:T, :], p[:T, 0:1], -1.0, ALU.mult)
        p = k.ps(); k.mm(p[:, 0:1], ones[:T, :], sp[:T, :]); blast = c(4); k.ts(blast[:, :], p[:, 0:1], -1.0, ALU.mult)
        d = c(5); k.tt(d[:T, :], ip[:T, :], bcol[:T, :], ALU.subtract)
        Dm = tmp(); k.ts(Dm[:T, :T], ident[:T, :T], d[:T, :], ALU.mult)
        pr = k.ps(); k.mm(pr[:T, :T], ones[:T, :T], Dm[:T, :T])
        lw = tmp(); k.stt(lw[:T, :T], pr[:T, :T], bcol[:T, :], NML[:T, :T], ALU.add, ALU.add)
        mi = c(6); k.red(mi[:T, :], lw[:T, :T], ALU.max)
        nmi = c(7); k.ts(nmi[:T, :], mi[:T, :], -1.0, ALU.mult)
        e1 = tmp(); k.act(e1[:T, :T], lw[:T, :T], AF.Exp, bias=nmi[:T, :])
        pqk = k.ps(); k.mm(pqk[:T, :T], qT[:, :T], kT[:, :T])
        pm = tmp(); k.stt(pm[:T, :T], pqk[:T, :T], SC, e1[:T, :T], ALU.mult, ALU.mult)
        p = k.ps(); k.tr(p[:T, :T], pm[:T, :T], ident[:T, :T]); pT = tmp(); k.cp(pT[:T, :T], p[:T, :T], eng="act")
        sel = sel128 if T == 128 else sel32
        p = k.ps(); k.mm(p[:, 0:1], sel[:T, :], mi[:T, :]); mch = c(8); k.cp(mch[:, :], p[:, 0:1])
        bm = c(9); k.tt(bm[:, :], blast[:, :], mch[:, :], ALU.subtract)
        kws = c(10); k.act(kws[:T, :], d[:T, :], AF.Exp, bias=bm[:T, :])
        kw = tmp(); k.ts(kw[:T, :128], kk, kws[:T, :], ALU.mult)
        a = c(11); k.tt(a[:T, :], bcol[:T, :], mst[:T, :], ALU.add)
        mt = c(12); k.tt(mt[:T, :], a[:T, :], mi[:T, :], ALU.max)
        nmt = c(13); k.ts(nmt[:T, :], mt[:T, :], -1.0, ALU.mult)
        wa = c(14); k.act(wa[:T, :], a[:T, :], AF.Exp, bias=nmt[:T, :]); k.ts(wa[:T, :], wa[:T, :], SC, ALU.mult)
        wi = c(15); k.act(wi[:T, :], mi[:T, :], AF.Exp, bias=nmt[:T, :])
        emt = c(56); k.act(emt[:T, :], nmt[:T, :], AF.Exp)
        pA = k.ps(); k.mm(pA[:T, :257], qT[:, :T], Cx[:, :])
        pB = k.ps(); k.mm(pB[:T, :257], pT[:T, :T], vx[:T, :])
        r1 = tmp(); k.ts(r1[:T, :257], pA[:T, :257], wa[:T, :], ALU.mult)
        res = tmp(); k.stt(res[:T, :257], pB[:T, :257], wi[:T, :], r1[:T, :257], ALU.mult, ALU.add)
        dn = c(57); k.act(dn[:T, :], res[:T, 256:257], AF.Abs); k.tt(dn[:T, :], dn[:T, :], emt[:T, :], ALU.max)
        k.recip(dn[:T, :], dn[:T, :])
        sg = tmp(); k.act(sg[:T, :256], od, AF.Sigmoid)
        hd = tmp(); k.stt(hd[:T, :256], res[:T, :256], dn[:T, :], sg[:T, :256], ALU.mult, ALU.mult)
        mu = c(58); k.red(mu[:T, :], hd[:T, :256]); k.ts(mu[:T, :], mu[:T, :], -1.0 / 256, ALU.mult)
        xc = tmp(); k.ts(xc[:T, :256], hd[:T, :256], mu[:T, :], ALU.add)
        gsz = tmp(); k.act(gsz[:T, :256], zd, AF.Silu); k.tt(gsz[:T, :256], gsz[:T, :256], dngb[:T, :], ALU.mult, eng="pool")
        od_ = tmp(); headnorm_rms(xc[:T, :256], T, 256, gsz[:T, :256], od_[:T, :256])
        store_oT(od_, 256, 1024 + 256 * hg, t0, T)
        pkv = k.ps(); k.mm(pkv[:, :257], kw[:T, :128], vx[:T, :])
        bms = c(59); k.tt(bms[:, :], blast[:, :], mst[:, :], ALU.add)
        mnew = c(56); k.tt(mnew[:, :], bms[:, :], mch[:, :], ALU.max)
        nmn = c(57); k.ts(nmn[:, :], mnew[:, :], -1.0, ALU.mult)
        wold = c(58); k.act(wold[:, :], bms[:, :], AF.Exp, bias=nmn[:, :])
        wnew = c(12); k.act(wnew[:, :], mch[:, :], AF.Exp, bias=nmn[:, :])
        t = tmp(); k.ts(t[:, :257], pkv[:, :257], wnew[:, :], ALU.mult)
        k.stt(Cx[:, :], Cx[:, :], wold[:, :], t[:, :257], ALU.mult, ALU.add)
        k.cp(mst[:, :], mnew[:, :])

    def glu_tile(hg, seq, t0, T):
        k.dma(ytl[:, :, :T], yT_d[:, t0:t0 + T].rearrange("(kc p) t -> p kc t", p=128),
              reads=[("yT", f, t0) for f in range(0, 1024, 128)])
        k.dma(yzt[:T, :], yz_d[t0:t0 + T, 256 * hg:256 * hg + 256], reads=[("yz", hg, t0)])
        p = k.ps()
        for kc in range(8):
            k.mm(p[:T, :256], ytl[:, kc, :T], gluw[:, kc, :], start=(kc == 0), stop=(kc == 7))
        g = tmp(); k.tt(g[:T, :256], p[:T, :256], glub[:T, :], ALU.add)
        k.act(g[:T, :256], g[:T, :256], AF.Sigmoid)
        oc = tmp(); k.tt(oc[:T, :256], g[:T, :256], yzt[:T, :], ALU.mult)
        store_oT(oc, 256, 256 * hg, t0, T)

    for hg in range(4):
        load_w(w_in_cd, [(0, 2048 + 128 * hg, 128), (128, 2560 + 128 * hg, 128),
                         (256, 2560 + 128 * hg, 128), (384, 3072 + 256 * hg, 256), (640, 4096 + 256 * hg, 256),
                         (896, 5120 + 256 * hg, 256), (1152, 6144 + hg, 1), (1153, 6148 + hg, 1)])
        k.dma(gluw[:, :, :], gluw_d[:, 256 * hg:256 * hg + 256].rearrange("(kc p) c -> p kc c", p=128), eng="pool")
        k.dma(glub[:], glub_d[:, 256 * hg:256 * hg + 256]); k.dma(dngb[:], dng_d[:, 256 * hg:256 * hg + 256])
        for ti, (seq, t0, T, first, last) in enumerate(tiles):
            if first:
                if seq == 0:
                    k.memset(Cx[:], 0.0); k.memset(mst[:], 0.0)
                else:
                    k.dma(Cx[:, 0:256], si_mc[seq - 1, hg, :, :]); k.dma(Cx[:, 256:257], si_mn[seq - 1, :, hg:hg + 1])
                    k.dma(mst[:], si_mm[seq - 1, :, hg:hg + 1])
            load_hn(t0, T)
            projF(qT[:, :T], 0, T, scaled=False); projF(kT[:, :T], 128, T, scaled=False)
            projTok(256, 898, T, scaled=False)
            mlstm_tile(hg, seq, t0, T)
            glu_tile(hg, seq, t0, T)
            if last:
                k.dma(so_mc[seq, hg, :, :], Cx[:, 0:256]); k.dma(so_mn[seq, :, hg:hg + 1], Cx[:, 256:257])
                k.dma(so_mm[seq, :, hg:hg + 1], mst[:])

    k.close_scope()
    outproj_pass(w_out_cd, 1)
    k.finish()
    k.emit()
    return nc, k


def _consts():
    c = np.zeros((128, 9, 128), np.float32)
    idx = np.arange(128)
    kk, ii = idx[:, None], idx[None, :]
    c[:, 0] = (kk == ii); c[:, 1] = (kk <= ii); c[:, 2] = (kk < ii); c[:, 3] = -(kk > ii).astype(np.float32)
    c[:, 4] = 1.0; c[:, 5] = np.where(kk <= ii, 0.0, NEG); c[:, 6] = np.where(ii <= kk, 0.0, NEG)
    c[:, 7] = np.broadcast_to(idx[None, :], (128, 128))
    c[:, 8, 0] = idx; c[127, 8, 1] = 1.0; c[31, 8, 2] = 1.0
    return c


def _bc(v, n=128):
    v = np.asarray(v, np.float32).reshape(1, -1)
    return np.ascontiguousarray(np.broadcast_to(v, (n, v.shape[1])))


def _s5col(a):
    return np.ascontiguousarray(np.asarray(a, np.float32).reshape(32, 128).T)


def _shared_inputs(inp):
    f = lambda a: np.ascontiguousarray(np.asarray(a, np.float32))
    d = {}
    d["w_in_ab"] = f(inp["w_in_ab"]); d["w_out_ab"] = f(inp["w_out_ab"])
    d["w_in_cd"] = f(inp["w_in_cd"]); d["w_out_cd"] = f(inp["w_out_cd"]); d["glu_w"] = f(inp["c_glu_w"])
    d["consts"] = _consts()
    ng = f(inp["norm_g"])
    d["g0T"] = np.ascontiguousarray(ng[0].reshape(16, 128).T); d["g1b"] = _bc(ng[1]); d["gfb"] = _bc(inp["final_norm_g"])
    d["a_gate_w"] = f(inp["a_gate_w"]); d["a_gate_b"] = f(inp["a_gate_b"]).reshape(1, 512); d["a_norm_g_b"] = _bc(inp["a_norm_g"])
    d["conv_w"] = np.ascontiguousarray(f(inp["b_conv_w"]).reshape(4, 24, 128).transpose(2, 1, 0))
    d["a_log_b"] = _bc(inp["b_a_log"]); d["dt_bias_b"] = _bc(inp["b_dt_bias"]); d["b_norm_g_b"] = _bc(inp["b_norm_g"])
    lre, lim = f(inp["c_lam_re"]), f(inp["c_lam_im"])
    ldt = np.ascontiguousarray(np.broadcast_to(f(inp["c_log_dt"])[:, None], (64, 64)))
    d["s5col"] = np.ascontiguousarray(np.stack([_s5col(lre), _s5col(lim), _s5col(ldt)], axis=1))
    d["s5row"] = np.ascontiguousarray(np.stack([_bc(lre.reshape(-1)), _bc(lim.reshape(-1)), _bc(ldt.reshape(-1))], axis=1))
    bd_b = np.zeros((2, 8, 128, 512), np.float32)
    for ci, b in enumerate((f(inp["c_b_re"]), f(inp["c_b_im"]))):
        for H in range(8):
            for gl in range(8):
                bd_b[ci, H, gl * 16:(gl + 1) * 16, gl * 64:(gl + 1) * 64] = b[8 * H + gl].T
    d["bd_b"] = bd_b
    bd_c = np.zeros((2, 4, 128, 8, 128), np.float32)
    for ci, c in enumerate((f(inp["c_c_re"]), f(inp["c_c_im"]))):
        for hg in range(4):
            for ch in range(8):
                for g2 in range(2):
                    g = 16 * hg + 2 * ch + g2
                    col = (2 * (ch % 4) + g2) * 16
                    bd_c[ci, hg, g2 * 64:(g2 + 1) * 64, ch, col:col + 16] = c[g].T
    d["bd_c"] = bd_c
    bd_d = np.zeros((4, 128, 2, 128), np.float32)
    cd = f(inp["c_d"])
    for hg in range(4):
        for h in range(2):
            for gl in range(8):
                for i in range(16):
                    bd_d[hg, gl * 16 + i, h, gl * 16 + i] = cd[16 * hg + 8 * h + gl, i]
    d["bd_d"] = bd_d
    d["glu_b_b"] = _bc(inp["c_glu_b"]); d["d_i_b"] = _bc(inp["d_i_bias"]); d["d_f_b"] = _bc(inp["d_f_bias"])
    d["d_norm_g_b"] = _bc(inp["d_norm_g"])
    return d


_NC_CACHE = {}


def kernel(**inp):
    f = lambda a: np.ascontiguousarray(np.asarray(a, np.float32))
    xp = f(inp["x_prompt"]); xs = f(inp["x_sample"])
    NB, L, _ = xp.shape
    shared = _shared_inputs(inp)
    conv = f(inp["cache_gdn_conv"]); sgla = f(inp["state_gla"]); sgdn = f(inp["state_gdn"])
    s5re = f(inp["state_s5_re"]); s5im = f(inp["state_s5_im"]); smc = f(inp["state_mlstm_c"])
    smn = f(inp["state_mlstm_n"]); smm = f(inp["state_mlstm_m"])
    in_maps = []
    for c in range(8):
        m = dict(shared)
        sq = [2 * c, 2 * c + 1]
        xtok = np.concatenate([xp[c % NB]] + [xs[s] for s in sq], axis=0)
        m["xtok"] = np.ascontiguousarray(xtok); m["xT"] = np.ascontiguousarray(xtok.T)
        m["si_conv"] = np.ascontiguousarray(np.stack([conv[s].reshape(3, 24, 128).transpose(2, 1, 0) for s in sq]))
        m["si_gla"] = np.ascontiguousarray(sgla[sq]); m["si_gdn"] = np.ascontiguousarray(sgdn[sq])
        m["si_s5"] = np.ascontiguousarray(np.stack([np.stack([_s5col(s5re[s]), _s5col(s5im[s])], axis=1) for s in sq]))
        m["si_mc"] = np.ascontiguousarray(smc[sq])
        m["si_mn"] = np.ascontiguousarray(np.stack([smn[s].T for s in sq]))
        m["si_mm"] = np.ascontiguousarray(np.stack([_bc(smm[s]) for s in sq]))
        in_maps.append(m)
    if L not in _NC_CACHE:
        _NC_CACHE[L] = build(L)[0]
    nc = _NC_CACHE[L]
    res = run_bass_kernel_spmd(nc, in_maps, core_ids=list(range(8)))
    R = res.results
    NS = xs.shape[0]
    y_p = np.stack([R[b]["y"][:L] for b in range(NB)])
    y_s = np.stack([R[s // 2]["y"][L + 32 * (s % 2):L + 32 * (s % 2) + 32] for s in range(NS)])

    def gather(fn):
        p = np.stack([fn(R[b], 0) for b in range(NB)])
        s = np.stack([fn(R[s // 2], 1 + s % 2) for s in range(NS)])
        return p, s

    def s5o(r, i, ci):
        return np.ascontiguousarray(r["so_s5"][i][:, ci, :].T).reshape(64, 64)

    pc, sc = gather(lambda r, i: np.ascontiguousarray(r["so_conv"][i].transpose(2, 1, 0)).reshape(3, 3072))
    pg, sg = gather(lambda r, i: r["so_gla"][i]); pd, sd = gather(lambda r, i: r["so_gdn"][i])
    pr, sr = gather(lambda r, i: s5o(r, i, 0)); pi, si = gather(lambda r, i: s5o(r, i, 1))
    pmc, smc_ = gather(lambda r, i: r["so_mc"][i]); pmn, smn_ = gather(lambda r, i: np.ascontiguousarray(r["so_mn"][i].T))
    pmm, smm_ = gather(lambda r, i: np.ascontiguousarray(r["so_mm"][i][0, :]))
    outs = (y_p, y_s, pc, pg, pd, pr, pi, pmc, pmn, pmm, sc, sg, sd, sr, si, smc_, smn_, smm_)
    return tuple(np.ascontiguousarray(o, dtype=np.float32) for o in outs)
```

```python
import numpy as np
import concourse.bass as bass
import concourse.mybir as mybir
from contextlib import ExitStack
from concourse.bass_utils import run_bass_kernel_spmd

F32 = mybir.dt.float32
BF16 = mybir.dt.bfloat16
I32 = mybir.dt.int32
AF = mybir.ActivationFunctionType
ALU = mybir.AluOpType
AX = mybir.AxisListType

SEM_LIMIT = 24000


class KB:
    ENGS = ("pe", "act", "dve", "pool", "sp")

    def __init__(self, nc):
        self.nc = nc
        self.stack = ExitStack()
        self.ops = {e: [] for e in self.ENGS}
        self.cur_sem = {}
        self.cur_cnt = {}
        for e in self.ENGS:
            self.cur_sem[e] = None
            self.cur_cnt[e] = 0
        self.waited = {e: {} for e in self.ENGS}
        self.writers = {}
        self.readers = {}
        self.dma_pool = []
        self.dma_rr = 0
        self.n_dma_sems = 14
        self.sem_objs = {}
        self.nuid = 0
        self.psum_banks = []
        self.ps_rr = 0
        self.n_instr = 0
        self.scopes = []

    def _alloc_sem(self, name):
        s = self.nc.alloc_semaphore(name=name)
        self.nuid += 1
        sid = self.nuid
        self.sem_objs[sid] = s
        return sid

    def sb(self, name, shape, dtype=F32):
        st = self.scopes[-1] if self.scopes else self.stack
        self.nuid += 1
        t = st.enter_context(self.nc.sbuf_tensor(f"sb_{name}_{self.nuid}", list(shape), dtype))
        return t

    def barrier(self):
        for e in self.ENGS:
            waits = {}
            for o in self.ENGS:
                if o != e and self.cur_sem[o] is not None and self.waited[e].get(self.cur_sem[o], 0) < self.cur_cnt[o]:
                    waits[self.cur_sem[o]] = self.cur_cnt[o]
            for sid, val in self.dma_pool:
                if val > 0 and self.waited[e].get(sid, 0) < val:
                    waits[sid] = val
            self._emit_waits(e, waits)

    def open_scope(self):
        self.scopes.append(ExitStack())

    def close_scope(self):
        self.barrier()
        self.emit_block()
        self.scopes.pop().close()

    def psum_init(self):
        for i in range(8):
            t = self.stack.enter_context(self.nc.psum_tensor(f"psb{i}", [128, 512], F32))
            self.psum_banks.append(t)

    def ps(self):
        t = self.psum_banks[self.ps_rr % 8]
        self.ps_rr += 1
        return t

    @staticmethod
    def key(x):
        if isinstance(x, (str, tuple)):
            return x
        return x.tensor.name

    def _deps(self, eng, reads, writes, is_dma):
        deps = []
        for r in reads:
            for src, t in self.writers.get(r, {}).items():
                deps.append((src, t, "raw"))
        for w in writes:
            for src, t in self.writers.get(w, {}).items():
                deps.append((src, t, "waw"))
            for src, t in self.readers.get(w, {}).items():
                deps.append((src, t, "war"))
        out = {}
        for src, (sid, val), kind in deps:
            if src == eng and not is_dma and not str(src).startswith("dma"):
                if eng == "pe" or kind != "raw":
                    continue
            if self.waited[eng].get(sid, 0) >= val:
                continue
            if out.get(sid, 0) < val:
                out[sid] = val
        return out

    def _emit_waits(self, eng, waits):
        for sid, val in waits.items():
            s = self.sem_objs[sid]
            self.ops[eng].append(lambda h, s=s, val=val: h.wait_ge(s, val))
            self.waited[eng][sid] = val
            self.n_instr += 1

    def _record(self, src, ticket, reads, writes):
        for r in reads:
            self.readers.setdefault(r, {})[src] = ticket
        for w in writes:
            self.writers[w] = {src: ticket}
            self.readers[w] = {}

    def op(self, eng, fn, reads, writes):
        reads = [self.key(r) for r in reads if r is not None and not isinstance(r, (int, float))]
        writes = [self.key(w) for w in writes]
        waits = self._deps(eng, reads, writes, False)
        self._emit_waits(eng, waits)
        if self.cur_sem[eng] is None or self.cur_cnt[eng] >= SEM_LIMIT:
            self.cur_sem[eng] = self._alloc_sem(f"s_{eng}_{self.nuid}")
            self.cur_cnt[eng] = 0
        self.cur_cnt[eng] += 1
        sid, val = self.cur_sem[eng], self.cur_cnt[eng]
        s = self.sem_objs[sid]
        self.ops[eng].append(lambda h, s=s: fn(h).then_inc(s, 1))
        self.n_instr += 1
        self._record(eng, (sid, val), reads, writes)

    def dma(self, out, in_, reads=None, writes=None, eng="sp", **kw):
        reads = [self.key(r) for r in (reads if reads is not None else [in_])]
        writes = [self.key(w) for w in (writes if writes is not None else [out])]
        waits = self._deps(eng, reads, writes, True)
        if len(self.dma_pool) < self.n_dma_sems:
            self.dma_pool.append([self._alloc_sem(f"s_dma_{self.nuid}"), 0])
        slot = self.dma_pool[self.dma_rr % self.n_dma_sems]
        self.dma_rr += 1
        if slot[1] + 16 > SEM_LIMIT:
            if self.waited[eng].get(slot[0], 0) < slot[1]:
                waits[slot[0]] = max(waits.get(slot[0], 0), slot[1])
            self._emit_waits(eng, waits)
            waits = {}
            slot[0] = self._alloc_sem(f"s_dma_{self.nuid}")
            slot[1] = 0
        sid = slot[0]
        if slot[1] > 0 and self.waited[eng].get(sid, 0) < slot[1]:
            waits[sid] = max(waits.get(sid, 0), slot[1])
        self._emit_waits(eng, waits)
        slot[1] += 16
        val = slot[1]
        s = self.sem_objs[sid]
        self.ops[eng].append(lambda h, s=s: h.dma_start(out=out, in_=in_, allow_slow_non_contiguous=True, **kw).then_inc(s, 16))
        self.n_instr += 1
        self._record(f"dma{sid}", (sid, val), reads, writes)
        return (sid, val)

    def wait_all(self, eng, keys):
        waits = {}
        for k in keys:
            k = self.key(k)
            for src, (sid, val) in self.writers.get(k, {}).items():
                if self.waited[eng].get(sid, 0) < val:
                    waits[sid] = max(waits.get(sid, 0), val)
        self._emit_waits(eng, waits)

    def mm(self, out, lhsT, rhs, start=True, stop=True, extra_reads=()):
        self.op("pe", lambda h: h.matmul(out, lhsT, rhs, start=start, stop=stop),
                [lhsT, rhs] + list(extra_reads) + ([] if start else [out]), [out])

    def tr(self, out, in_, ident):
        self.op("pe", lambda h: h.transpose(out, in_, ident), [in_, ident], [out])

    def act(self, out, in_, func, bias=0.0, scale=1.0, eng="act"):
        rd = [in_]
        if not isinstance(bias, (int, float)):
            rd.append(bias)
        if not isinstance(scale, (int, float)):
            rd.append(scale)
        self.op(eng, lambda h: h.activation(out=out, in_=in_, func=func, bias=bias, scale=scale), rd, [out])

    def tt(self, out, a, b, op, eng="dve"):
        self.op(eng, lambda h: h.tensor_tensor(out=out, in0=a, in1=b, op=op), [a, b], [out])

    def ts(self, out, a, s1, op0, s2=None, op1=None, eng="dve"):
        rd = [a] + [s for s in (s1, s2) if s is not None and not isinstance(s, (int, float))]
        if op1 is None:
            self.op(eng, lambda h: h.tensor_scalar(out=out, in0=a, scalar1=s1, scalar2=None, op0=op0), rd, [out])
        else:
            self.op(eng, lambda h: h.tensor_scalar(out=out, in0=a, scalar1=s1, scalar2=s2, op0=op0, op1=op1), rd, [out])

    def stt(self, out, a, s, b, op0, op1, eng="dve"):
        rd = [a, b] + ([] if isinstance(s, (int, float)) else [s])
        self.op(eng, lambda h: h.scalar_tensor_tensor(out=out, in0=a, scalar=s, in1=b, op0=op0, op1=op1), rd, [out])

    def cp(self, out, in_, eng="dve"):
        if eng == "act":
            self.op("act", lambda h: h.copy(out=out, in_=in_), [in_], [out])
        else:
            self.op(eng, lambda h: h.tensor_copy(out=out, in_=in_), [in_], [out])

    def red(self, out, in_, op=None, eng="dve"):
        op = op or ALU.add
        self.op(eng, lambda h: h.tensor_reduce(out=out, in_=in_, axis=AX.X, op=op), [in_], [out])

    def recip(self, out, in_):
        self.op("dve", lambda h: h.reciprocal(out=out, in_=in_), [in_], [out])

    def memset(self, ap, val, eng="dve"):
        self.op(eng, lambda h: h.memset(ap, val), [], [ap])

    def rsqrt(self, out, in_, scale, eps):
        self.act(out, in_, AF.Sqrt, bias=eps, scale=scale)
        self.recip(out, out)

    def finish(self):
        waits = {}
        for sid, val in self.dma_pool:
            if val > 0 and self.waited["sp"].get(sid, 0) < val:
                waits[sid] = val
        self._emit_waits("sp", waits)

    def emit(self):
        self.emit_block()
        self.stack.close()

    def emit_block(self):
        nc = self.nc
        ops = self.ops
        self.ops = {e: [] for e in self.ENGS}
        with nc.Block() as block:
            @block.tensor
            def _(h):
                for f in ops["pe"]:
                    f(h)

            @block.scalar
            def _(h):
                for f in ops["act"]:
                    f(h)

            @block.vector
            def _(h):
                for f in ops["dve"]:
                    f(h)

            @block.gpsimd
            def _(h):
                for f in ops["pool"]:
                    f(h)

            @block.sync
            def _(h):
                for f in ops["sp"]:
                    f(h)


EPS = 1e-6
D = 2048
NEG = -30000.0


def build(L):
    nc = bass.Bass("TRN2", target_bir_lowering=False)
    NTOK = L + 64
    tiles = [(0, i * 128, 128, i == 0, i == L // 128 - 1) for i in range(L // 128)]
    tiles += [(1, L, 32, True, True), (2, L + 32, 32, True, True)]

    def din(name, shape, dt=F32):
        return nc.dram_tensor(name, list(shape), dt, kind="ExternalInput").ap()

    def dout(name, shape, dt=F32):
        return nc.dram_tensor(name, list(shape), dt, kind="ExternalOutput").ap()

    xT_d = din("xT", [D, NTOK]); xtok_d = din("xtok", [NTOK, D])
    w_in_ab = din("w_in_ab", [D, 7200]); w_out_ab = din("w_out_ab", [D, D])
    w_in_cd = din("w_in_cd", [D, 6152]); w_out_cd = din("w_out_cd", [D, D])
    gluw_d = din("glu_w", [1024, 1024])
    consts_d = din("consts", [128, 9, 128])
    g0T_d = din("g0T", [128, 16]); g1b_d = din("g1b", [128, D]); gfb_d = din("gfb", [128, D])
    agw_d = din("a_gate_w", [16, 512]); agb_d = din("a_gate_b", [1, 512]); ang_d = din("a_norm_g_b", [128, 1024])
    cw_d = din("conv_w", [128, 24, 4]); alog_d = din("a_log_b", [128, 8]); dtb_d = din("dt_bias_b", [128, 8])
    bng_d = din("b_norm_g_b", [128, 128])
    s5col_d = din("s5col", [128, 3, 32]); s5row_d = din("s5row", [128, 3, 4096])
    bdb_d = din("bd_b", [2, 8, 128, 512]); bdc_d = din("bd_c", [2, 4, 128, 8, 128]); bdd_d = din("bd_d", [4, 128, 2, 128])
    glub_d = din("glu_b_b", [128, 1024]); dib_d = din("d_i_b", [128, 4]); dfb_d = din("d_f_b", [128, 4])
    dng_d = din("d_norm_g_b", [128, 1024])
    si_conv = din("si_conv", [2, 128, 24, 3]); si_gla = din("si_gla", [2, 4, 128, 256]); si_gdn = din("si_gdn", [2, 8, 128, 128])
    si_s5 = din("si_s5", [2, 128, 2, 32]); si_mc = din("si_mc", [2, 4, 128, 256]); si_mn = din("si_mn", [2, 128, 4])
    si_mm = din("si_mm", [2, 128, 4])

    y_d = dout("y", [NTOK, D])
    so_conv = dout("so_conv", [3, 128, 24, 3]); so_gla = dout("so_gla", [3, 4, 128, 256]); so_gdn = dout("so_gdn", [3, 8, 128, 128])
    so_s5 = dout("so_s5", [3, 128, 2, 32]); so_mc = dout("so_mc", [3, 4, 128, 256]); so_mn = dout("so_mn", [3, 128, 4])
    so_mm = dout("so_mm", [3, 128, 4])

    oT_d = nc.dram_tensor("oT_s", [D, NTOK], BF16).ap()
    h1_d = nc.dram_tensor("h1_s", [NTOK, D], F32).ap()
    hnT_d = nc.dram_tensor("hnT_s", [D, NTOK], BF16).ap()
    yT_d = nc.dram_tensor("yT_s", [1024, NTOK], BF16).ap()
    yz_d = nc.dram_tensor("yz_s", [NTOK, 1024], F32).ap()

    k = KB(nc)
    k.psum_init()
    sb = k.sb
    cst = sb("cst", [128, 9, 128])
    k.dma(cst[:], consts_d[:, :, :])
    ident = cst[:, 0, :]; U = cst[:, 1, :]; SU = cst[:, 2, :]; NSL = cst[:, 3, :]; ones = cst[:, 4, :]
    NMU = cst[:, 5, :]; NML = cst[:, 6, :]; IOTA = cst[:, 7, :]; SELS = cst[:, 8, :]
    tcol = cst[:, 8, 0:1]
    sel128 = sb("sel128", [128, 128]); sel32 = sb("sel32", [128, 128])
    k.ts(sel128[:], ones, cst[:, 8, 1:2], ALU.mult)
    k.ts(sel32[:], ones, cst[:, 8, 2:3], ALU.mult)
    epsb = sb("epsb", [128, 1]); k.memset(epsb[:], EPS)
    wbuf = sb("wbuf", [128, 16, 2048], BF16)
    xg = sb("xg", [128, 16, 128], BF16)
    projT = sb("projT", [128, 1160])
    tbuf = [sb(f"tb{i}", [128, 512]) for i in range(12)]
    cols = sb("cols", [128, 64])
    obf = [sb(f"obf{i}", [128, 128], BF16) for i in range(4)]
    obc = [0]
    qT = sb("qT", [128, 128]); kT = sb("kT", [128, 128])
    k.open_scope()
    xt2 = [sb(f"xt{i}", [128, 16, 128]) for i in range(2)]
    xsq = sb("xsq", [128, 16, 128]); xs1 = sb("xs1", [128, 128])
    rbc = sb("rbc", [128, 128]); rcol = sb("rcol", [128, 1])
    g0T = sb("g0T", [128, 16]); k.dma(g0T[:], g0T_d[:, :])
    tctr = [0]

    def tmp():
        t = tbuf[tctr[0] % len(tbuf)]
        tctr[0] += 1
        return t

    def rs_eps(out, in_, scale):
        k.act(out, in_, AF.Ln, bias=epsb[:out.shape[0], :], scale=scale)
        k.act(out, out, AF.Exp, scale=-0.5)

    def load_w(src, pairs):
        for d0, s0, n in pairs:
            k.dma(wbuf[:, :, d0:d0 + n], src[:, s0:s0 + n].rearrange("(kc p) c -> p kc c", p=128), eng="pool")

    def norm_x(ti, t0, T):
        xt = xt2[ti % 2]
        k.dma(xt[:, :, :T], xT_d[:, t0:t0 + T].rearrange("(kc p) t -> p kc t", p=128))
        k.tt(xsq[:, :, :T], xt[:, :, :T], xt[:, :, :T], ALU.mult, eng="pool")
        k.red(xs1[:, :T], xsq[:, :, :T].rearrange("p kc t -> p t kc"))
        p = k.ps(); k.mm(p[:, :T], ones, xs1[:, :T]); rs_eps(rbc[:, :T], p[:, :T], 1.0 / D)
        p = k.ps(); k.mm(p[:T, 0:1], xs1[:, :T], ones[:, 0:1]); rs_eps(rcol[:T, :], p[:T, 0:1], 1.0 / D)
        k.tt(xg[:, :, :T], xt[:, :, :T], g0T[:, :].unsqueeze(2).to_broadcast([128, 16, T]), ALU.mult)

    def load_hn(t0, T):
        k.dma(xg[:, :, :T], hnT_d[:, t0:t0 + T].rearrange("(kc p) t -> p kc t", p=128), reads=[("hnT", t0)])

    def projF(dst, col0, T, ncols=128, scaled=True):
        p = k.ps()
        for kc in range(16):
            k.mm(p[:ncols, :T], wbuf[:, kc, col0:col0 + ncols], xg[:, kc, :T], start=(kc == 0), stop=(kc == 15))
        if scaled:
            k.tt(dst, p[:ncols, :T], rbc[:ncols, :T], ALU.mult)
        else:
            k.cp(dst, p[:ncols, :T], eng="act")

    def projTok(col0, ncols, T, scaled=True, dst0=0):
        for c0 in range(0, ncols, 512):
            n = min(512, ncols - c0)
            p = k.ps()
            for kc in range(16):
                k.mm(p[:T, :n], xg[:, kc, :T], wbuf[:, kc, col0 + c0:col0 + c0 + n], start=(kc == 0), stop=(kc == 15))
            if scaled:
                k.ts(projT[:T, dst0 + c0:dst0 + c0 + n], p[:T, :n], rcol[:T, :], ALU.mult)
            else:
                k.cp(projT[:T, dst0 + c0:dst0 + c0 + n], p[:T, :n], eng="act")

    def store_oT(src, ncols, frow, t0, T):
        for c0 in range(0, ncols, 128):
            p = k.ps(); k.tr(p[:, :T], src[:T, c0:c0 + 128], ident[:T, :T])
            ob = obf[obc[0] % 4]; obc[0] += 1
            k.cp(ob[:, :T], p[:, :T], eng="act")
            k.dma(oT_d[frow + c0:frow + c0 + 128, t0:t0 + T], ob[:, :T], writes=[("oT", frow + c0, t0)])

    def headnorm_rms(o_sb, T, n, gsz, dst):
        sq = tmp(); k.tt(sq[:T, :n], o_sb, o_sb, ALU.mult, eng="pool")
        c = cols[:, 60:61]; k.red(c[:T, :], sq[:T, :n])
        r = cols[:, 61:62]; rs_eps(r[:T, :], c[:T, :], 1.0 / n)
        k.stt(dst, o_sb, r[:T, :], gsz, ALU.mult, ALU.mult)

    S_gla = sb("S_gla", [128, 256]); S_gdn = [sb(f"S_gdn{i}", [128, 128]) for i in range(2)]
    cbuf = sb("cbuf", [128, 6, 131]); cw = sb("cw", [128, 6, 4])
    gw = sb("gw", [16, 128]); gb = sb("gb", [1, 128]); angb = sb("angb", [128, 256]); bngb = sb("bngb", [128, 128])
    alog = sb("alog", [128, 8]); dtb = sb("dtb", [128, 8]); na = sb("na", [128, 8])
    k.dma(alog[:], alog_d[:, :]); k.dma(dtb[:], dtb_d[:, :]); k.dma(bngb[:], bng_d[:, :])
    k.act(na[:], alog[:], AF.Exp); k.ts(na[:], na[:], -1.0, ALU.mult)
    gaT = sb("gaT", [16, 128])
    cvs = sb("cvs", [128, 6, 128]); cacc = sb("cacc", [128, 6, 128]); ctmp = sb("ctmp", [128, 6, 128])
    Rp = [sb(f"Rp{i}", [128, 128]) for i in range(7)]
    Qp = [sb(f"Qp{i}", [128, 128]) for i in range(2)]
    g_vk = sb("g_vk", [128, 256]); g_egrow = sb("g_egrow", [128, 128]); g_AT = sb("g_AT", [128, 128])
    g_X = [sb(f"g_X{i}", [128, 256]) for i in range(2)]
    SC = 128 ** -0.5

    def gla_tile(hg, seq, t0, T):
        kk = projT[:T, 0:128]; v = projT[:T, 128:384]; za = projT[:T, 384:640]
        p = k.ps()
        k.mm(p[:T, :128], gaT[:, :T], gw[:, :], start=True, stop=False)
        k.mm(p[:T, :128], ones[0:1, :T], gb[:, :], start=False, stop=True)
        e = tmp(); k.act(e[:T, :128], p[:T, :128], AF.Exp, scale=-1.0)
        spl = tmp(); k.act(spl[:T, :128], e[:T, :128], AF.Ln, bias=1.0)
        pc = k.ps(); k.mm(pc[:, :T], spl[:T, :128], U[:T, :T])
        ebT = tmp(); k.act(ebT[:, :T], pc[:, :T], AF.Exp, scale=-1.0 / 16)
        enbT = tmp(); k.act(enbT[:, :T], pc[:, :T], AF.Exp, scale=1.0 / 16)
        qd = tmp(); k.stt(qd[:, :T], qT[:, :T], SC, ebT[:, :T], ALU.mult, ALU.mult)
        kd = tmp(); k.tt(kd[:, :T], kT[:, :T], enbT[:, :T], ALU.mult, eng="pool")
        pd = k.ps(); k.mm(pd[:T, :128], NSL[:T, :T], spl[:T, :128])
        ekw = tmp(); k.act(ekw[:T, :128], pd[:T, :128], AF.Exp, scale=1.0 / 16)
        kw = tmp(); k.tt(kw[:T, :128], kk, ekw[:T, :128], ALU.mult, eng="pool")
        psc = k.ps(); k.mm(psc[:T, :T], kd[:, :T], qd[:, :T])
        PT = tmp(); k.tt(PT[:T, :T], psc[:T, :T], U[:T, :T], ALU.mult)
        po = k.ps()
        k.mm(po[:T, :256], PT[:T, :T], v, start=True, stop=False)
        k.mm(po[:T, :256], qd[:, :T], S_gla[:, :], start=False, stop=True)
        o_sb = tmp(); k.cp(o_sb[:T, :256], po[:T, :256], eng="act")
        pkv = k.ps(); k.mm(pkv[:, :256], kw[:T, :128], v)
        k.stt(S_gla[:, :], S_gla[:, :], ebT[:, T - 1:T], pkv[:, :256], ALU.mult, ALU.add)
        gsz = tmp(); k.act(gsz[:T, :256], za, AF.Silu)
        k.tt(gsz[:T, :256], gsz[:T, :256], angb[:T, :], ALU.mult, eng="pool")
        og = tmp(); headnorm_rms(o_sb[:T, :256], T, 256, gsz[:T, :256], og[:T, :256])
        store_oT(og, 256, hg * 256, t0, T)

    def gdn_pre(T):
        for j in range(4):
            wj = cw[:, :, j:j + 1].to_broadcast([128, 6, T])
            if j == 0:
                k.tt(cacc[:, :, :T], cbuf[:, :, 0:T], wj, ALU.mult)
            else:
                k.tt(ctmp[:, :, :T], cbuf[:, :, j:j + T], wj, ALU.mult, eng="pool")
                k.tt(cacc[:, :, :T], cacc[:, :, :T], ctmp[:, :, :T], ALU.add)
        k.act(cvs[:, :, :T], cacc[:, :, :T], AF.Silu)
        k.tt(ctmp[:, 0:4, :T], cvs[:, 0:4, :T], cvs[:, 0:4, :T], ALU.mult, eng="pool")
        for b in range(4):
            p = k.ps(); k.mm(p[:, :T], ones, ctmp[:, b, :T])
            rs_eps(cacc[:, b, :T], p[:, :T], 1.0)
        k.tt(cvs[:, 0:4, :T], cvs[:, 0:4, :T], cacc[:, 0:4, :T], ALU.mult)

    def gdn_head(hg, hh, seq, t0, T):
        Sg = S_gdn[hh]
        qTh = cvs[:, 0 + hh, :T]; kTh = cvs[:, 2 + hh, :T]; vTh = cvs[:, 4 + hh, :T]
        zb = projT[:T, 640 + 128 * hh:768 + 128 * hh]
        beta_pre = projT[:T, 896 + hh:897 + hh]; a_pre = projT[:T, 898 + hh:899 + hh]
        gh = 2 * hg + hh
        c = lambda i: cols[:, i:i + 1]
        beta = c(0); k.act(beta[:T, :], beta_pre, AF.Sigmoid)
        e = c(1); k.act(e[:T, :], a_pre, AF.Exp, bias=dtb[:T, gh:gh + 1])
        spg = c(2); k.act(spg[:T, :], e[:T, :], AF.Ln, bias=1.0)
        graw = c(3); k.tt(graw[:T, :], spg[:T, :], na[:T, gh:gh + 1], ALU.mult)
        vk = g_vk
        p = k.ps(); k.tr(p[:T, 0:128], vTh, ident); k.tr(p[:T, 128:256], kTh, ident)
        k.cp(vk[:T, :256], p[:T, :256], eng="act")
        p = k.ps(); k.mm(p[:T, 0:1], U[:T, :T], graw[:T, :])
        gc = c(4); k.cp(gc[:T, :], p[:T, 0:1]); ngc = c(5); k.ts(ngc[:T, :], p[:T, 0:1], -1.0, ALU.mult)
        Ug = tmp(); k.ts(Ug[:T, :T], U[:T, :T], graw[:T, :], ALU.mult)
        pg = k.ps(); k.mm(pg[:, :T], ones[:T, :], Ug[:T, :T])
        Ib = tmp(); k.ts(Ib[:T, :T], ident[:T, :T], beta[:T, :], ALU.mult)
        pb = k.ps(); k.mm(pb[:T, :T], ones[:T, :T], Ib[:T, :T])
        arg = tmp(); k.tt(arg[:T, :T], pg[:T, :T], NMU[:T, :T], ALU.add)
        decT = tmp(); k.act(decT[:T, :T], arg[:T, :T], AF.Exp, bias=ngc[:T, :])
        egrow = g_egrow; k.act(egrow[:, :T], pg[:, :T], AF.Exp)
        glast = c(6); k.cp(glast[:, :], pg[:, T - 1:T])
        eglast = c(7); k.cp(eglast[:, :], egrow[:, T - 1:T], eng="pool")
        egc = c(8); k.act(egc[:T, :], gc[:T, :], AF.Exp)
        ekl = c(9); k.act(ekl[:T, :], gc[:T, :], AF.Exp, scale=-1.0, bias=glast[:T, :])
        pkk = k.ps(); k.mm(pkk[:T, :T], kTh, kTh)
        pqk = k.ps(); k.mm(pqk[:T, :T], kTh, qTh)
        AT = g_AT; k.stt(AT[:T, :T], pqk[:T, :T], SC, decT[:T, :T], ALU.mult, ALU.mult)
        t1 = tmp(); k.tt(t1[:T, :T], pkk[:T, :T], decT[:T, :T], ALU.mult)
        t2 = tmp(); k.tt(t2[:T, :T], pb[:T, :T], SU[:T, :T], ALU.mult)
        LT = Rp[0]; k.tt(LT[:T, :T], t1[:T, :T], t2[:T, :T], ALU.mult, eng="pool")
        X = g_X[0]; xi_ = 0
        k.ts(X[:T, 0:128], vk[:T, 0:128], beta[:T, :], ALU.mult)
        k.ts(X[:T, 128:256], vk[:T, 128:256], beta[:T, :], ALU.mult, egc[:T, :], ALU.mult)
        p = k.ps(); k.tr(p[:T, :T], LT[:T, :T], ident[:T, :T]); k.cp(Qp[0][:T, :T], p[:T, :T], eng="act")
        p = k.ps(); k.mm(p[:T, :256], LT[:T, :T], X[:T, :256])
        xi_ ^= 1; Xn = g_X[xi_]; k.tt(Xn[:T, :256], X[:T, :256], p[:T, :256], ALU.subtract); X = Xn
        nsq = {128: 6, 32: 4}[T]
        for n in range(nsq):
            Q = Qp[n % 2]; R = Rp[n]
            p2 = k.ps(); k.mm(p2[:T, :T], Q[:T, :T], R[:T, :T]); k.cp(Rp[n + 1][:T, :T], p2[:T, :T], eng="act")
            if n < nsq - 1:
                p1 = k.ps(); k.mm(p1[:T, :T], R[:T, :T], Q[:T, :T]); k.cp(Qp[(n + 1) % 2][:T, :T], p1[:T, :T])
            p = k.ps(); k.mm(p[:T, :256], Rp[n + 1][:T, :T], X[:T, :256])
            xi_ ^= 1; Xn = g_X[xi_]; k.tt(Xn[:T, :256], X[:T, :256], p[:T, :256], ALU.add); X = Xn
        p = k.ps(); k.tr(p[:, :T], X[:T, 128:256], ident[:T, :T])
        wT = tmp(); k.cp(wT[:, :T], p[:, :T], eng="act")
        p = k.ps(); k.mm(p[:T, :128], wT[:, :T], Sg[:, :])
        vnew = tmp(); k.tt(vnew[:T, :128], X[:T, 0:128], p[:T, :128], ALU.subtract)
        qg = tmp(); k.stt(qg[:, :T], qTh, SC, egrow[:, :T], ALU.mult, ALU.mult)
        po = k.ps()
        k.mm(po[:T, :128], qg[:, :T], Sg[:, :], start=True, stop=False)
        k.mm(po[:T, :128], AT[:T, :T], vnew[:T, :128], start=False, stop=True)
        o_sb = tmp(); k.cp(o_sb[:T, :128], po[:T, :128], eng="act")
        kg = tmp(); k.ts(kg[:T, :128], vk[:T, 128:256], ekl[:T, :], ALU.mult)
        pkv = k.ps(); k.mm(pkv[:, :128], kg[:T, :128], vnew[:T, :128])
        k.stt(Sg[:, :], Sg[:, :], eglast[:, :], pkv[:, :128], ALU.mult, ALU.add)
        gsz = tmp(); k.act(gsz[:T, :128], zb, AF.Silu)
        k.tt(gsz[:T, :128], gsz[:T, :128], bngb[:T, :], ALU.mult, eng="pool")
        og = tmp(); headnorm_rms(o_sb[:T, :128], T, 128, gsz[:T, :128], og[:T, :128])
        store_oT(og, 128, 1024 + gh * 128, t0, T)

    for hg in range(4):
        TB = 1040
        load_w(w_in_ab, [(0, 128 * hg, 128), (128, 512 + 128 * hg, 128),
                         (256, 3088 + 256 * hg, 256), (512, 4112 + 256 * hg, 256), (768, 5136 + 256 * hg, 256),
                         (1024, 3072, 16),
                         (TB, 512 + 128 * hg, 128), (TB + 128, 1024 + 256 * hg, 256), (TB + 384, 2048 + 256 * hg, 256),
                         (TB + 640, 6160 + 256 * hg, 256), (TB + 896, 7184 + 2 * hg, 2), (TB + 898, 7192 + 2 * hg, 2)])
        k.dma(gw[:], agw_d[:, 128 * hg:128 * hg + 128]); k.dma(gb[:], agb_d[:, 128 * hg:128 * hg + 128])
        k.dma(angb[:], ang_d[:, 256 * hg:256 * hg + 256])
        for typ in range(3):
            k.dma(cw[:, 2 * typ:2 * typ + 2, :], cw_d[:, typ * 8 + 2 * hg:typ * 8 + 2 * hg + 2, :])
        for ti, (seq, t0, T, first, last) in enumerate(tiles):
            if first:
                if seq == 0:
                    k.memset(S_gla[:], 0.0); k.memset(S_gdn[0][:], 0.0); k.memset(S_gdn[1][:], 0.0)
                    k.memset(cbuf[:, :, 0:3], 0.0)
                else:
                    k.dma(S_gla[:], si_gla[seq - 1, hg, :, :])
                    for hh in range(2):
                        k.dma(S_gdn[hh][:], si_gdn[seq - 1, 2 * hg + hh, :, :])
                    for typ in range(3):
                        k.dma(cbuf[:, 2 * typ:2 * typ + 2, 0:3], si_conv[seq - 1, :, typ * 8 + 2 * hg:typ * 8 + 2 * hg + 2, :])
            else:
                k.cp(cbuf[:, :, 0:3], cbuf[:, :, 128:131])
            norm_x(ti, t0, T)
            projF(qT[:, :T], 0, T); projF(kT[:, :T], 128, T)
            for b in range(6):
                projF(cbuf[:, b, 3:3 + T], 256 + 128 * b, T)
            projF(gaT[:, :T], 1024, T, ncols=16)
            projTok(TB, 900, T)
            gla_tile(hg, seq, t0, T)
            gdn_pre(T)
            for hh in range(2):
                gdn_head(hg, hh, seq, t0, T)
            if last:
                k.dma(so_gla[seq, hg, :, :], S_gla[:])
                for hh in range(2):
                    k.dma(so_gdn[seq, 2 * hg + hh, :, :], S_gdn[hh][:])
                for typ in range(3):
                    k.dma(so_conv[seq, :, typ * 8 + 2 * hg:typ * 8 + 2 * hg + 2, :], cbuf[:, 2 * typ:2 * typ + 2, T:T + 3])

    k.close_scope()
    def outproj_alloc():
        return (sb("ot", [128, 16, 128], BF16), sb("resid", [128, D]), sb("hbuf", [128, D]),
                sb("gnb", [128, D]), sb("hsq", [128, D]), sb("hnT_sb", [128, 16, 128], BF16))

    def outproj_pass(w_out, layer):
        k.open_scope()
        ot, resid, hbuf, gnb, hsq, hnT_sb = outproj_alloc()
        load_w(w_out, [(0, 0, 2048)])
        k.dma(gnb[:], (g1b_d if layer == 0 else gfb_d)[:, :])
        for ti, (seq, t0, T, first, last) in enumerate(tiles):
            k.dma(ot[:, :, :T], oT_d[:, t0:t0 + T].rearrange("(kc p) t -> p kc t", p=128),
                  reads=[("oT", f, t0) for f in range(0, D, 128)])
            if layer == 0:
                k.dma(resid[:T, :], xtok_d[t0:t0 + T, :])
            else:
                k.dma(resid[:T, :], h1_d[t0:t0 + T, :], reads=[("h1", t0)])
            for nb in range(4):
                p = k.ps()
                for kc in range(16):
                    k.mm(p[:T, :512], ot[:, kc, :T], wbuf[:, kc, nb * 512:(nb + 1) * 512], start=(kc == 0), stop=(kc == 15))
                k.tt(hbuf[:T, nb * 512:(nb + 1) * 512], p[:T, :512], resid[:T, nb * 512:(nb + 1) * 512], ALU.add)
            if layer == 0:
                k.dma(h1_d[t0:t0 + T, :], hbuf[:T, :], writes=[("h1", t0)])
            k.tt(hsq[:T, :], hbuf[:T, :], hbuf[:T, :], ALU.mult, eng="pool")
            c = cols[:, 62:63]; k.red(c[:T, :], hsq[:T, :])
            r = cols[:, 63:64]; rs_eps(r[:T, :], c[:T, :], 1.0 / D)
            k.stt(hsq[:T, :], hbuf[:T, :], r[:T, :], gnb[:T, :], ALU.mult, ALU.mult)
            if layer == 0:
                for q4 in range(4):
                    p = k.ps()
                    for j in range(4):
                        kc = q4 * 4 + j
                        k.tr(p[:, j * 128:j * 128 + T], hsq[:T, kc * 128:(kc + 1) * 128], ident[:T, :T])
                    if T == 128:
                        k.cp(hnT_sb[:, q4 * 4:q4 * 4 + 4, :], p[:, :].rearrange("p (a b) -> p a b", a=4), eng="act")
                    else:
                        for j in range(4):
                            k.cp(hnT_sb[:, q4 * 4 + j, :T], p[:, j * 128:j * 128 + T], eng="act")
                k.dma(hnT_d[:, t0:t0 + T].rearrange("(kc p) t -> p kc t", p=128), hnT_sb[:, :, :T], writes=[("hnT", t0)])
            else:
                k.dma(y_d[t0:t0 + T, :], hsq[:T, :])
        k.close_scope()

    outproj_pass(w_out_ab, 0)

    k.open_scope()
    s5c = sb("s5c", [128, 3, 8]); s5r = sb("s5r", [128, 3, 1024])
    ETr = sb("ETr", [128, 8, 128]); ETi = sb("ETi", [128, 8, 128]); EIr = sb("EIr", [128, 1024]); EIi = sb("EIi", [128, 1024])
    big = [sb(f"big{i}", [128, 1024]) for i in range(6)]
    bigi = sb("bigi", [128, 1024], I32)
    BB = sb("BB", [128, 4, 512]); WB = sb("WB", [128, 4, 512])
    CB = sb("CB", [128, 2, 8, 128]); DD = sb("DD", [128, 2, 128])
    uT = sb("uT", [128, 2, 128])
    car = sb("car", [128, 2, 8]); xl = sb("xl", [128, 2, 8]); A1 = sb("A1", [128, 2, 8])
    TWO_PI = 2.0 * np.pi

    def sincos(dst_sin, dst_cos, ang, n, shp=None):
        for dst, off in ((dst_sin, 0.0), (dst_cos, 0.25)):
            tn = big[4][:, :n]; tf = big[5][:, :n]
            k.ts(tn, ang, 1.0 / TWO_PI, ALU.mult, off, ALU.add)
            k.cp(bigi[:, :n], tn)
            k.cp(tf, bigi[:, :n])
            k.tt(tn, tn, tf, ALU.subtract)
            k.act(dst, tn, AF.Sin, scale=TWO_PI)

    def cmul(outr, outi, ar, ai, br, bi, n, t1, t2):
        k.tt(t1, ar, br, ALU.mult); k.tt(t2, ai, bi, ALU.mult, eng="pool"); k.tt(outr, t1, t2, ALU.subtract)
        k.tt(t1, ar, bi, ALU.mult); k.tt(t2, ai, br, ALU.mult, eng="pool"); k.tt(outi, t1, t2, ALU.add)

    def s5_setup(hg):
        k.dma(s5c[:], s5col_d[:, :, 8 * hg:8 * hg + 8]); k.dma(s5r[:], s5row_d[:, :, 1024 * hg:1024 * hg + 1024])
        dtc = cols[:, 16:24]; arc = cols[:, 24:32]; aic = cols[:, 32:40]
        k.act(dtc, s5c[:, 2, :], AF.Exp)
        k.tt(arc, s5c[:, 0, :], dtc, ALU.mult); k.tt(aic, s5c[:, 1, :], dtc, ALU.mult)
        ang = big[0][:, :].rearrange("p (a b) -> p a b", a=8)
        io = IOTA.unsqueeze(1).to_broadcast([128, 8, 128])
        k.tt(ang, io, aic.unsqueeze(2).to_broadcast([128, 8, 128]), ALU.mult)
        sincos(big[1][:, :], big[2][:, :], big[0][:, :], 1024)
        mg = big[3][:, :].rearrange("p (a b) -> p a b", a=8)
        k.tt(mg, io, arc.unsqueeze(2).to_broadcast([128, 8, 128]), ALU.mult)
        k.act(big[3][:, :], big[3][:, :], AF.Exp)
        k.tt(ETi[:, :, :].rearrange("p a b -> p (a b)"), big[3][:, :], big[1][:, :], ALU.mult)
        k.tt(ETr[:, :, :].rearrange("p a b -> p (a b)"), big[3][:, :], big[2][:, :], ALU.mult)
        k.cp(A1[:, 0, :], ETr[:, :, 1]); k.cp(A1[:, 1, :], ETi[:, :, 1])
        dtr = big[0][:, :]; k.act(dtr, s5r[:, 2, :], AF.Exp)
        arr = sb_arr[:, :]; air = sb_air[:, :]
        k.tt(arr, s5r[:, 0, :], dtr, ALU.mult); k.tt(air, s5r[:, 1, :], dtr, ALU.mult)
        sincos(big[1][:, :], big[2][:, :], air, 1024)
        k.act(big[3][:, :], arr, AF.Exp)
        abi = big[1][:, :]; abr = big[2][:, :]
        k.tt(abi, abi, big[3][:, :], ALU.mult); k.tt(abr, abr, big[3][:, :], ALU.mult)
        k.ts(abr, abr, -1.0, ALU.add)
        den = big[3][:, :]; t = big[0][:, :]
        k.tt(den, s5r[:, 0, :], s5r[:, 0, :], ALU.mult); k.tt(t, s5r[:, 1, :], s5r[:, 1, :], ALU.mult)
        k.tt(den, den, t, ALU.add); k.recip(den, den)
        zr = big[4][:, :]; zi = big[5][:, :]
        k.tt(zr, abr, s5r[:, 0, :], ALU.mult); k.tt(t, abi, s5r[:, 1, :], ALU.mult); k.tt(zr, zr, t, ALU.add); k.tt(zr, zr, den, ALU.mult)
        k.tt(zi, abi, s5r[:, 0, :], ALU.mult); k.tt(t, abr, s5r[:, 1, :], ALU.mult); k.tt(zi, zi, t, ALU.subtract); k.tt(zi, zi, den, ALU.mult)
        for h in range(2):
            k.dma(WB[:, 2 * h, :], bdb_d[0, 2 * hg + h, :, :]); k.dma(WB[:, 2 * h + 1, :], bdb_d[1, 2 * hg + h, :, :])
            sl = slice(512 * h, 512 * h + 512)
            t1 = big[0][:, 0:512]; t2 = big[0][:, 512:1024]
            cmul(BB[:, 2 * h, :], BB[:, 2 * h + 1, :], zr[:, sl], zi[:, sl], WB[:, 2 * h, :], WB[:, 2 * h + 1, :], 512, t1, t2)
        k.ts(big[0][:, :], air, tcol, ALU.mult)
        sincos(big[1][:, :], big[2][:, :], big[0][:, :], 1024)
        k.ts(big[3][:, :], arr, tcol, ALU.mult); k.act(big[3][:, :], big[3][:, :], AF.Exp, scale=-1.0)
        k.tt(EIr[:, :], big[3][:, :], big[2][:, :], ALU.mult)
        k.stt(EIi[:, :], big[3][:, :], -1.0, big[1][:, :], ALU.mult, ALU.mult)
        k.dma(CB[:, 0, :, :], bdc_d[0, hg, :, :, :]); k.dma(CB[:, 1, :, :], bdc_d[1, hg, :, :, :])
        k.ts(CB[:, 1, :, :], CB[:, 1, :, :], -1.0, ALU.mult)
        k.dma(DD[:], bdd_d[hg, :, :, :])

    sb_arr = sb("sb_arr", [128, 1024]); sb_air = sb("sb_air", [128, 1024])

    def s5_tile(hg, seq, t0, T):
        zc = projT[:T, 0:256]
        Bu = [big[0], big[1]]
        for h in range(2):
            for cpx in range(2):
                p = k.ps(); k.mm(p[:T, :512], uT[:, h, :T], BB[:, 2 * h + cpx, :])
                k.cp(Bu[cpx][:T, 512 * h:512 * h + 512], p[:T, :512], eng="act")
        Zr = big[2]; Zi = big[3]
        cmul(Zr[:T, :], Zi[:T, :], EIr[:T, :], EIi[:T, :], Bu[0][:T, :], Bu[1][:T, :], 1024, big[4][:T, :], big[5][:T, :])
        Wc = [big[0], big[1]]
        for cpx, Z in enumerate((Zr, Zi)):
            for half in range(2):
                p = k.ps()
                for j in range(4):
                    ch = half * 4 + j
                    k.mm(p[:, j * 128:j * 128 + T], Z[:T, ch * 128:(ch + 1) * 128], U[:T, :T])
                src = p[:, :].rearrange("p (a b) -> p a b", a=4)[:, :, :T]
                dst = Wc[cpx][:, :].rearrange("p (a b) -> p a b", a=8)[:, half * 4:half * 4 + 4, :T]
                k.tt(dst, src, car[:, cpx, half * 4:half * 4 + 4].unsqueeze(2).to_broadcast([128, 4, T]), ALU.add)
        XTr = big[2][:, :].rearrange("p (a b) -> p a b", a=8); XTi = big[3][:, :].rearrange("p (a b) -> p a b", a=8)
        W3 = [w[:, :].rearrange("p (a b) -> p a b", a=8) for w in Wc]
        t1 = big[4][:, :].rearrange("p (a b) -> p a b", a=8); t2 = big[5][:, :].rearrange("p (a b) -> p a b", a=8)
        cmul(XTr[:, :, :T], XTi[:, :, :T], ETr[:, :, :T], ETi[:, :, :T], W3[0][:, :, :T], W3[1][:, :, :T], 0, t1[:, :, :T], t2[:, :, :T])
        k.cp(xl[:, 0, :], XTr[:, :, T - 1]); k.cp(xl[:, 1, :], XTi[:, :, T - 1])
        c1 = cols[:, 40:48]; c2 = cols[:, 48:56]
        cmul(car[:, 0, :], car[:, 1, :], A1[:, 0, :], A1[:, 1, :], xl[:, 0, :], xl[:, 1, :], 8, c1, c2)
        py = k.ps()
        for h in range(2):
            o = py[:T, 128 * h:128 * h + 128]
            k.mm(o, uT[:, h, :T], DD[:, h, :], start=True, stop=False)
            for pc in range(4):
                ch = 4 * h + pc
                k.mm(o, XTr[:, ch, :T], CB[:, 0, ch, :], start=False, stop=False)
                k.mm(o, XTi[:, ch, :T], CB[:, 1, ch, :], start=False, stop=(pc == 3))
        yg = tmp(); k.act(yg[:T, :256], py[:T, :256], AF.Gelu)
        sz = tmp(); k.act(sz[:T, :256], zc, AF.Silu)
        yz = tmp(); k.tt(yz[:T, :256], yg[:T, :256], sz[:T, :256], ALU.mult)
        k.dma(yz_d[t0:t0 + T, 256 * hg:256 * hg + 256], yz[:T, :256], writes=[("yz", hg, t0)])
        for c0 in range(0, 256, 128):
            p = k.ps(); k.tr(p[:, :T], yg[:T, c0:c0 + 128], ident[:T, :T])
            ob = obf[obc[0] % 4]; obc[0] += 1
            k.cp(ob[:, :T], p[:, :T], eng="act")
            k.dma(yT_d[256 * hg + c0:256 * hg + c0 + 128, t0:t0 + T], ob[:, :T], writes=[("yT", 256 * hg + c0, t0)])

    for hg in range(4):
        load_w(w_in_cd, [(0, 256 * hg, 256), (256, 1024 + 256 * hg, 256)])
        s5_setup(hg)
        for ti, (seq, t0, T, first, last) in enumerate(tiles):
            if first:
                if seq == 0:
                    k.memset(car[:], 0.0)
                else:
                    k.dma(xl[:], si_s5[seq - 1, :, :, 8 * hg:8 * hg + 8])
                    cmul(car[:, 0, :], car[:, 1, :], A1[:, 0, :], A1[:, 1, :], xl[:, 0, :], xl[:, 1, :], 8, cols[:, 40:48], cols[:, 48:56])
            load_hn(t0, T)
            projF(uT[:, 0, :T], 0, T, scaled=False); projF(uT[:, 1, :T], 128, T, scaled=False)
            projTok(256, 256, T, scaled=False)
            s5_tile(hg, seq, t0, T)
            if last:
                k.dma(so_s5[seq, :, :, 8 * hg:8 * hg + 8], xl[:])

    k.close_scope()
    k.open_scope()
    Cx = sb("Cx", [128, 257]); mst = sb("mst", [128, 1]); vx = sb("vx", [128, 257])
    gluw = sb("gluw", [128, 8, 256], BF16); ytl = sb("ytl", [128, 8, 128], BF16)
    glub = sb("glub", [128, 256]); dngb = sb("dngb", [128, 256]); dib = sb("dib", [128, 4]); dfb = sb("dfb", [128, 4])
    yzt = sb("yzt", [128, 256])
    k.dma(dib[:], dib_d[:, :]); k.dma(dfb[:], dfb_d[:, :]); k.ts(dfb[:], dfb[:], -1.0, ALU.mult)
    k.memset(vx[:, 256:257], 1.0)

    def mlstm_tile(hg, seq, t0, T):
        kk = projT[:T, 0:128]; od = projT[:T, 384:640]; zd = projT[:T, 640:896]
        c = lambda i: cols[:, i:i + 1]
        k.cp(vx[:T, 0:256], projT[:T, 128:384], eng="pool")
        ip = c(0); k.tt(ip[:T, :], projT[:T, 896:897], dib[:T, hg:hg + 1], ALU.add)
        e = c(1); k.act(e[:T, :], projT[:T, 897:898], AF.Exp, scale=-1.0, bias=dfb[:T, hg:hg + 1])
        sp = c(2); k.act(sp[:T, :], e[:T, :], AF.Ln, bias=1.0)
        p = k.ps(); k.mm(p[:T, 0:1], U[:T, :T], sp[:T, :]); bcol = c(3); k.ts(bcol[:T, :], p[:T, 0:1], -1.0, ALU.mult)
        p = k.ps(); k.mm(p[:, 0:1], ones[:T, :], sp[:T, :]); blast = c(4); k.ts(blast[:, :], p[:, 0:1], -1.0, ALU.mult)
        d = c(5); k.tt(d[:T, :], ip[:T, :], bcol[:T, :], ALU.subtract)
        Dm = tmp(); k.ts(Dm[:T, :T], ident[:T, :T], d[:T, :], ALU.mult)
        pr = k.ps(); k.mm(pr[:T, :T], ones[:T, :T], Dm[:T, :T])
        lw = tmp(); k.stt(lw[:T, :T], pr[:T, :T], bcol[:T, :], NML[:T, :T], ALU.add, ALU.add)
        mi = c(6); k.red(mi[:T, :], lw[:T, :T], ALU.max)
        nmi = c(7); k.ts(nmi[:T, :], mi[:T, :], -1.0, ALU.mult)
        e1 = tmp(); k.act(e1[:T, :T], lw[:T, :T], AF.Exp, bias=nmi[:T, :])
        pqk = k.ps(); k.mm(pqk[:T, :T], qT[:, :T], kT[:, :T])
        pm = tmp(); k.stt(pm[:T, :T], pqk[:T, :T], SC, e1[:T, :T], ALU.mult, ALU.mult)
        p = k.ps(); k.tr(p[:T, :T], pm[:T, :T], ident[:T, :T]); pT = tmp(); k.cp(pT[:T, :T], p[:T, :T], eng="act")
        sel = sel128 if T == 128 else sel32
        p = k.ps(); k.mm(p[:, 0:1], sel[:T, :], mi[:T, :]); mch = c(8); k.cp(mch[:, :], p[:, 0:1])
        bm = c(9); k.tt(bm[:, :], blast[:, :], mch[:, :], ALU.subtract)
        kws = c(10); k.act(kws[:T, :], d[:T, :], AF.Exp, bias=bm[:T, :])
        kw = tmp(); k.ts(kw[:T, :128], kk, kws[:T, :], ALU.mult)
        a = c(11); k.tt(a[:T, :], bcol[:T, :], mst[:T, :], ALU.add)
        mt = c(12); k.tt(mt[:T, :], a[:T, :], mi[:T, :], ALU.max)
        nmt = c(13); k.ts(nmt[:T, :], mt[:T, :], -1.0, ALU.mult)
        wa = c(14); k.act(wa[:T, :], a[:T, :], AF.Exp, bias=nmt[:T, :]); k.ts(wa[:T, :], wa[:T, :], SC, ALU.mult)
        wi = c(15); k.act(wi[:T, :], mi[:T, :], AF.Exp, bias=nmt[:T, :])
        emt = c(56); k.act(emt[:T, :], nmt[:T, :], AF.Exp)
        pA = k.ps(); k.mm(pA[:T, :257], qT[:, :T], Cx[:, :])
        pB = k.ps(); k.mm(pB[:T, :257], pT[:T, :T], vx[:T, :])
        r1 = tmp(); k.ts(r1[:T, :257], pA[:T, :257], wa[:T, :], ALU.mult)
        res = tmp(); k.stt(res[:T, :257], pB[:T, :257], wi[:T, :], r1[:T, :257], ALU.mult, ALU.add)
        dn = c(57); k.act(dn[:T, :], res[:T, 256:257], AF.Abs); k.tt(dn[:T, :], dn[:T, :], emt[:T, :], ALU.max)
        k.recip(dn[:T, :], dn[:T, :])
        sg = tmp(); k.act(sg[:T, :256], od, AF.Sigmoid)
        hd = tmp(); k.stt(hd[:T, :256], res[:T, :256], dn[:T, :], sg[:T, :256], ALU.mult, ALU.mult)
        mu = c(58); k.red(mu[:T, :], hd[:T, :256]); k.ts(mu[:T, :], mu[:T, :], -1.0 / 256, ALU.mult)
        xc = tmp(); k.ts(xc[:T, :256], hd[:T, :256], mu[:T, :], ALU.add)
        gsz = tmp(); k.act(gsz[:T, :256], zd, AF.Silu); k.tt(gsz[:T, :256], gsz[:T, :256], dngb[:T, :], ALU.mult, eng="pool")
        od_ = tmp(); headnorm_rms(xc[:T, :256], T, 256, gsz[:T, :256], od_[:T, :256])
        store_oT(od_, 256, 1024 + 256 * hg, t0, T)
        pkv = k.ps(); k.mm(pkv[:, :257], kw[:T, :128], vx[:T, :])
        bms = c(59); k.tt(bms[:, :], blast[:, :], mst[:, :], ALU.add)
        mnew = c(56); k.tt(mnew[:, :], bms[:, :], mch[:, :], ALU.max)
        nmn = c(57); k.ts(nmn[:, :], mnew[:, :], -1.0, ALU.mult)
        wold = c(58); k.act(wold[:, :], bms[:, :], AF.Exp, bias=nmn[:, :])
        wnew = c(12); k.act(wnew[:, :], mch[:, :], AF.Exp, bias=nmn[:, :])
        t = tmp(); k.ts(t[:, :257], pkv[:, :257], wnew[:, :], ALU.mult)
        k.stt(Cx[:, :], Cx[:, :], wold[:, :], t[:, :257], ALU.mult, ALU.add)
        k.cp(mst[:, :], mnew[:, :])

    def glu_tile(hg, seq, t0, T):
        k.dma(ytl[:, :, :T], yT_d[:, t0:t0 + T].rearrange("(kc p) t -> p kc t", p=128),
              reads=[("yT", f, t0) for f in range(0, 1024, 128)])
        k.dma(yzt[:T, :], yz_d[t0:t0 + T, 256 * hg:256 * hg + 256], reads=[("yz", hg, t0)])
        p = k.ps()
        for kc in range(8):
            k.mm(p[:T, :256], ytl[:, kc, :T], gluw[:, kc, :], start=(kc == 0), stop=(kc == 7))
        g = tmp(); k.tt(g[:T, :256], p[:T, :256], glub[:T, :], ALU.add)
        k.act(g[:T, :256], g[:T, :256], AF.Sigmoid)
        oc = tmp(); k.tt(oc[:T, :256], g[:T, :256], yzt[:T, :], ALU.mult)
        store_oT(oc, 256, 256 * hg, t0, T)

    for hg in range(4):
        load_w(w_in_cd, [(0, 2048 + 128 * hg, 128), (128, 2560 + 128 * hg, 128),
                         (256, 2560 + 128 * hg, 128), (384, 3072 + 256 * hg, 256), (640, 4096 + 256 * hg, 256),
                         (896, 5120 + 256 * hg, 256), (1152, 6144 + hg, 1), (1153, 6148 + hg, 1)])
        k.dma(gluw[:, :, :], gluw_d[:, 256 * hg:256 * hg + 256].rearrange("(kc p) c -> p kc c", p=128), eng="pool")
        k.dma(glub[:], glub_d[:, 256 * hg:256 * hg + 256]); k.dma(dngb[:], dng_d[:, 256 * hg:256 * hg + 256])
        for ti, (seq, t0, T, first, last) in enumerate(tiles):
            if first:
                if seq == 0:
                    k.memset(Cx[:], 0.0); k.memset(mst[:], 0.0)
                else:
                    k.dma(Cx[:, 0:256], si_mc[seq - 1, hg, :, :]); k.dma(Cx[:, 256:257], si_mn[seq - 1, :, hg:hg + 1])
                    k.dma(mst[:], si_mm[seq - 1, :, hg:hg + 1])
            load_hn(t0, T)
            projF(qT[:, :T], 0, T, scaled=False); projF(kT[:, :T], 128, T, scaled=False)
            projTok(256, 898, T, scaled=False)
            mlstm_tile(hg, seq, t0, T)
            glu_tile(hg, seq, t0, T)
            if last:
                k.dma(so_mc[seq, hg, :, :], Cx[:, 0:256]); k.dma(so_mn[seq, :, hg:hg + 1], Cx[:, 256:257])
                k.dma(so_mm[seq, :, hg:hg + 1], mst[:])

    k.close_scope()
    outproj_pass(w_out_cd, 1)
    k.finish()
    k.emit()
    return nc, k


def _consts():
    c = np.zeros((128, 9, 128), np.float32)
    idx = np.arange(128)
    kk, ii = idx[:, None], idx[None, :]
    c[:, 0] = (kk == ii); c[:, 1] = (kk <= ii); c[:, 2] = (kk < ii); c[:, 3] = -(kk > ii).astype(np.float32)
    c[:, 4] = 1.0; c[:, 5] = np.where(kk <= ii, 0.0, NEG); c[:, 6] = np.where(ii <= kk, 0.0, NEG)
    c[:, 7] = np.broadcast_to(idx[None, :], (128, 128))
    c[:, 8, 0] = idx; c[127, 8, 1] = 1.0; c[31, 8, 2] = 1.0
    return c


def _bc(v, n=128):
    v = np.asarray(v, np.float32).reshape(1, -1)
    return np.ascontiguousarray(np.broadcast_to(v, (n, v.shape[1])))


def _s5col(a):
    return np.ascontiguousarray(np.asarray(a, np.float32).reshape(32, 128).T)


def _shared_inputs(inp):
    f = lambda a: np.ascontiguousarray(np.asarray(a, np.float32))
    d = {}
    d["w_in_ab"] = f(inp["w_in_ab"]); d["w_out_ab"] = f(inp["w_out_ab"])
    d["w_in_cd"] = f(inp["w_in_cd"]); d["w_out_cd"] = f(inp["w_out_cd"]); d["glu_w"] = f(inp["c_glu_w"])
    d["consts"] = _consts()
    ng = f(inp["norm_g"])
    d["g0T"] = np.ascontiguousarray(ng[0].reshape(16, 128).T); d["g1b"] = _bc(ng[1]); d["gfb"] = _bc(inp["final_norm_g"])
    d["a_gate_w"] = f(inp["a_gate_w"]); d["a_gate_b"] = f(inp["a_gate_b"]).reshape(1, 512); d["a_norm_g_b"] = _bc(inp["a_norm_g"])
    d["conv_w"] = np.ascontiguousarray(f(inp["b_conv_w"]).reshape(4, 24, 128).transpose(2, 1, 0))
    d["a_log_b"] = _bc(inp["b_a_log"]); d["dt_bias_b"] = _bc(inp["b_dt_bias"]); d["b_norm_g_b"] = _bc(inp["b_norm_g"])
    lre, lim = f(inp["c_lam_re"]), f(inp["c_lam_im"])
    ldt = np.ascontiguousarray(np.broadcast_to(f(inp["c_log_dt"])[:, None], (64, 64)))
    d["s5col"] = np.ascontiguousarray(np.stack([_s5col(lre), _s5col(lim), _s5col(ldt)], axis=1))
    d["s5row"] = np.ascontiguousarray(np.stack([_bc(lre.reshape(-1)), _bc(lim.reshape(-1)), _bc(ldt.reshape(-1))], axis=1))
    bd_b = np.zeros((2, 8, 128, 512), np.float32)
    for ci, b in enumerate((f(inp["c_b_re"]), f(inp["c_b_im"]))):
        for H in range(8):
            for gl in range(8):
                bd_b[ci, H, gl * 16:(gl + 1) * 16, gl * 64:(gl + 1) * 64] = b[8 * H + gl].T
    d["bd_b"] = bd_b
    bd_c = np.zeros((2, 4, 128, 8, 128), np.float32)
    for ci, c in enumerate((f(inp["c_c_re"]), f(inp["c_c_im"]))):
        for hg in range(4):
            for ch in range(8):
                for g2 in range(2):
                    g = 16 * hg + 2 * ch + g2
                    col = (2 * (ch % 4) + g2) * 16
                    bd_c[ci, hg, g2 * 64:(g2 + 1) * 64, ch, col:col + 16] = c[g].T
    d["bd_c"] = bd_c
    bd_d = np.zeros((4, 128, 2, 128), np.float32)
    cd = f(inp["c_d"])
    for hg in range(4):
        for h in range(2):
            for gl in range(8):
                for i in range(16):
                    bd_d[hg, gl * 16 + i, h, gl * 16 + i] = cd[16 * hg + 8 * h + gl, i]
    d["bd_d"] = bd_d
    d["glu_b_b"] = _bc(inp["c_glu_b"]); d["d_i_b"] = _bc(inp["d_i_bias"]); d["d_f_b"] = _bc(inp["d_f_bias"])
    d["d_norm_g_b"] = _bc(inp["d_norm_g"])
    return d


_NC_CACHE = {}


def kernel(**inp):
    f = lambda a: np.ascontiguousarray(np.asarray(a, np.float32))
    xp = f(inp["x_prompt"]); xs = f(inp["x_sample"])
    NB, L, _ = xp.shape
    shared = _shared_inputs(inp)
    conv = f(inp["cache_gdn_conv"]); sgla = f(inp["state_gla"]); sgdn = f(inp["state_gdn"])
    s5re = f(inp["state_s5_re"]); s5im = f(inp["state_s5_im"]); smc = f(inp["state_mlstm_c"])
    smn = f(inp["state_mlstm_n"]); smm = f(inp["state_mlstm_m"])
    in_maps = []
    for c in range(8):
        m = dict(shared)
        sq = [2 * c, 2 * c + 1]
        xtok = np.concatenate([xp[c % NB]] + [xs[s] for s in sq], axis=0)
        m["xtok"] = np.ascontiguousarray(xtok); m["xT"] = np.ascontiguousarray(xtok.T)
        m["si_conv"] = np.ascontiguousarray(np.stack([conv[s].reshape(3, 24, 128).transpose(2, 1, 0) for s in sq]))
        m["si_gla"] = np.ascontiguousarray(sgla[sq]); m["si_gdn"] = np.ascontiguousarray(sgdn[sq])
        m["si_s5"] = np.ascontiguousarray(np.stack([np.stack([_s5col(s5re[s]), _s5col(s5im[s])], axis=1) for s in sq]))
        m["si_mc"] = np.ascontiguousarray(smc[sq])
        m["si_mn"] = np.ascontiguousarray(np.stack([smn[s].T for s in sq]))
        m["si_mm"] = np.ascontiguousarray(np.stack([_bc(smm[s]) for s in sq]))
        in_maps.append(m)
    if L not in _NC_CACHE:
        _NC_CACHE[L] = build(L)[0]
    nc = _NC_CACHE[L]
    res = run_bass_kernel_spmd(nc, in_maps, core_ids=list(range(8)))
    R = res.results
    NS = xs.shape[0]
    y_p = np.stack([R[b]["y"][:L] for b in range(NB)])
    y_s = np.stack([R[s // 2]["y"][L + 32 * (s % 2):L + 32 * (s % 2) + 32] for s in range(NS)])

    def gather(fn):
        p = np.stack([fn(R[b], 0) for b in range(NB)])
        s = np.stack([fn(R[s // 2], 1 + s % 2) for s in range(NS)])
        return p, s

    def s5o(r, i, ci):
        return np.ascontiguousarray(r["so_s5"][i][:, ci, :].T).reshape(64, 64)

    pc, sc = gather(lambda r, i: np.ascontiguousarray(r["so_conv"][i].transpose(2, 1, 0)).reshape(3, 3072))
    pg, sg = gather(lambda r, i: r["so_gla"][i]); pd, sd = gather(lambda r, i: r["so_gdn"][i])
    pr, sr = gather(lambda r, i: s5o(r, i, 0)); pi, si = gather(lambda r, i: s5o(r, i, 1))
    pmc, smc_ = gather(lambda r, i: r["so_mc"][i]); pmn, smn_ = gather(lambda r, i: np.ascontiguousarray(r["so_mn"][i].T))
    pmm, smm_ = gather(lambda r, i: np.ascontiguousarray(r["so_mm"][i][0, :]))
    outs = (y_p, y_s, pc, pg, pd, pr, pi, pmc, pmn, pmm, sc, sg, sd, sr, si, smc_, smn_, smm_)
    return tuple(np.ascontiguousarray(o, dtype=np.float32) for o in outs)
```

```python
import numpy as np
import concourse.bass as bass
import concourse.mybir as mybir
from contextlib import ExitStack
from concourse.bass_utils import run_bass_kernel_spmd

F32 = mybir.dt.float32
BF16 = mybir.dt.bfloat16
I32 = mybir.dt.int32
AF = mybir.ActivationFunctionType
ALU = mybir.AluOpType
AX = mybir.AxisListType

SEM_LIMIT = 24000


class KB:
    ENGS = ("pe", "act", "dve", "pool", "sp")

    def __init__(self, nc):
        self.nc = nc
        self.stack = ExitStack()
        self.ops = {e: [] for e in self.ENGS}
        self.cur_sem = {}
        self.cur_cnt = {}
        for e in self.ENGS:
            self.cur_sem[e] = None
            self.cur_cnt[e] = 0
        self.waited = {e: {} for e in self.ENGS}
        self.writers = {}
        self.readers = {}
        self.dma_pool = []
        self.dma_rr = 0
        self.n_dma_sems = 14
        self.sem_objs = {}
        self.nuid = 0
        self.psum_banks = []
        self.ps_rr = 0
        self.n_instr = 0
        self.scopes = []

    def _alloc_sem(self, name):
        s = self.nc.alloc_semaphore(name=name)
        self.nuid += 1
        sid = self.nuid
        self.sem_objs[sid] = s
        return sid

    def sb(self, name, shape, dtype=F32):
        st = self.scopes[-1] if self.scopes else self.stack
        self.nuid += 1
        t = st.enter_context(self.nc.sbuf_tensor(f"sb_{name}_{self.nuid}", list(shape), dtype))
        return t

    def barrier(self):
        for e in self.ENGS:
            waits = {}
            for o in self.ENGS:
                if o != e and self.cur_sem[o] is not None and self.waited[e].get(self.cur_sem[o], 0) < self.cur_cnt[o]:
                    waits[self.cur_sem[o]] = self.cur_cnt[o]
            for sid, val in self.dma_pool:
                if val > 0 and self.waited[e].get(sid, 0) < val:
                    waits[sid] = val
            self._emit_waits(e, waits)

    def open_scope(self):
        self.scopes.append(ExitStack())

    def close_scope(self):
        self.barrier()
        self.emit_block()
        self.scopes.pop().close()

    def psum_init(self):
        for i in range(8):
            t = self.stack.enter_context(self.nc.psum_tensor(f"psb{i}", [128, 512], F32))
            self.psum_banks.append(t)

    def ps(self):
        t = self.psum_banks[self.ps_rr % 8]
        self.ps_rr += 1
        return t

    @staticmethod
    def key(x):
        if isinstance(x, (str, tuple)):
            return x
        return x.tensor.name

    def _deps(self, eng, reads, writes, is_dma):
        deps = []
        for r in reads:
            for src, t in self.writers.get(r, {}).items():
                deps.append((src, t, "raw"))
        for w in writes:
            for src, t in self.writers.get(w, {}).items():
                deps.append((src, t, "waw"))
            for src, t in self.readers.get(w, {}).items():
                deps.append((src, t, "war"))
        out = {}
        for src, (sid, val), kind in deps:
            if src == eng and not is_dma and not str(src).startswith("dma"):
                if eng == "pe" or kind != "raw":
                    continue
            if self.waited[eng].get(sid, 0) >= val:
                continue
            if out.get(sid, 0) < val:
                out[sid] = val
        return out

    def _emit_waits(self, eng, waits):
        for sid, val in waits.items():
            s = self.sem_objs[sid]
            self.ops[eng].append(lambda h, s=s, val=val: h.wait_ge(s, val))
            self.waited[eng][sid] = val
            self.n_instr += 1

    def _record(self, src, ticket, reads, writes):
        for r in reads:
            self.readers.setdefault(r, {})[src] = ticket
        for w in writes:
            self.writers[w] = {src: ticket}
            self.readers[w] = {}

    def op(self, eng, fn, reads, writes):
        reads = [self.key(r) for r in reads if r is not None and not isinstance(r, (int, float))]
        writes = [self.key(w) for w in writes]
        waits = self._deps(eng, reads, writes, False)
        self._emit_waits(eng, waits)
        if self.cur_sem[eng] is None or self.cur_cnt[eng] >= SEM_LIMIT:
            self.cur_sem[eng] = self._alloc_sem(f"s_{eng}_{self.nuid}")
            self.cur_cnt[eng] = 0
        self.cur_cnt[eng] += 1
        sid, val = self.cur_sem[eng], self.cur_cnt[eng]
        s = self.sem_objs[sid]
        self.ops[eng].append(lambda h, s=s: fn(h).then_inc(s, 1))
        self.n_instr += 1
        self._record(eng, (sid, val), reads, writes)

    def dma(self, out, in_, reads=None, writes=None, eng="sp", **kw):
        reads = [self.key(r) for r in (reads if reads is not None else [in_])]
        writes = [self.key(w) for w in (writes if writes is not None else [out])]
        waits = self._deps(eng, reads, writes, True)
        if len(self.dma_pool) < self.n_dma_sems:
            self.dma_pool.append([self._alloc_sem(f"s_dma_{self.nuid}"), 0])
        slot = self.dma_pool[self.dma_rr % self.n_dma_sems]
        self.dma_rr += 1
        if slot[1] + 16 > SEM_LIMIT:
            if self.waited[eng].get(slot[0], 0) < slot[1]:
                waits[slot[0]] = max(waits.get(slot[0], 0), slot[1])
            self._emit_waits(eng, waits)
            waits = {}
            slot[0] = self._alloc_sem(f"s_dma_{self.nuid}")
            slot[1] = 0
        sid = slot[0]
        if slot[1] > 0 and self.waited[eng].get(sid, 0) < slot[1]:
            waits[sid] = max(waits.get(sid, 0), slot[1])
        self._emit_waits(eng, waits)
        slot[1] += 16
        val = slot[1]
        s = self.sem_objs[sid]
        self.ops[eng].append(lambda h, s=s: h.dma_start(out=out, in_=in_, allow_slow_non_contiguous=True, **kw).then_inc(s, 16))
        self.n_instr += 1
        self._record(f"dma{sid}", (sid, val), reads, writes)
        return (sid, val)

    def wait_all(self, eng, keys):
        waits = {}
        for k in keys:
            k = self.key(k)
            for src, (sid, val) in self.writers.get(k, {}).items():
                if self.waited[eng].get(sid, 0) < val:
                    waits[sid] = max(waits.get(sid, 0), val)
        self._emit_waits(eng, waits)

    def mm(self, out, lhsT, rhs, start=True, stop=True, extra_reads=()):
        self.op("pe", lambda h: h.matmul(out, lhsT, rhs, start=start, stop=stop),
                [lhsT, rhs] + list(extra_reads) + ([] if start else [out]), [out])

    def tr(self, out, in_, ident):
        self.op("pe", lambda h: h.transpose(out, in_, ident), [in_, ident], [out])

    def act(self, out, in_, func, bias=0.0, scale=1.0, eng="act"):
        rd = [in_]
        if not isinstance(bias, (int, float)):
            rd.append(bias)
        if not isinstance(scale, (int, float)):
            rd.append(scale)
        self.op(eng, lambda h: h.activation(out=out, in_=in_, func=func, bias=bias, scale=scale), rd, [out])

    def tt(self, out, a, b, op, eng="dve"):
        self.op(eng, lambda h: h.tensor_tensor(out=out, in0=a, in1=b, op=op), [a, b], [out])

    def ts(self, out, a, s1, op0, s2=None, op1=None, eng="dve"):
        rd = [a] + [s for s in (s1, s2) if s is not None and not isinstance(s, (int, float))]
        if op1 is None:
            self.op(eng, lambda h: h.tensor_scalar(out=out, in0=a, scalar1=s1, scalar2=None, op0=op0), rd, [out])
        else:
            self.op(eng, lambda h: h.tensor_scalar(out=out, in0=a, scalar1=s1, scalar2=s2, op0=op0, op1=op1), rd, [out])

    def stt(self, out, a, s, b, op0, op1, eng="dve"):
        rd = [a, b] + ([] if isinstance(s, (int, float)) else [s])
        self.op(eng, lambda h: h.scalar_tensor_tensor(out=out, in0=a, scalar=s, in1=b, op0=op0, op1=op1), rd, [out])

    def cp(self, out, in_, eng="dve"):
        if eng == "act":
            self.op("act", lambda h: h.copy(out=out, in_=in_), [in_], [out])
        else:
            self.op(eng, lambda h: h.tensor_copy(out=out, in_=in_), [in_], [out])

    def red(self, out, in_, op=None, eng="dve"):
        op = op or ALU.add
        self.op(eng, lambda h: h.tensor_reduce(out=out, in_=in_, axis=AX.X, op=op), [in_], [out])

    def recip(self, out, in_):
        self.op("dve", lambda h: h.reciprocal(out=out, in_=in_), [in_], [out])

    def memset(self, ap, val, eng="dve"):
        self.op(eng, lambda h: h.memset(ap, val), [], [ap])

    def rsqrt(self, out, in_, scale, eps):
        self.act(out, in_, AF.Sqrt, bias=eps, scale=scale)
        self.recip(out, out)

    def finish(self):
        waits = {}
        for sid, val in self.dma_pool:
            if val > 0 and self.waited["sp"].get(sid, 0) < val:
                waits[sid] = val
        self._emit_waits("sp", waits)

    def emit(self):
        self.emit_block()
        self.stack.close()

    def emit_block(self):
        nc = self.nc
        ops = self.ops
        self.ops = {e: [] for e in self.ENGS}
        with nc.Block() as block:
            @block.tensor
            def _(h):
                for f in ops["pe"]:
                    f(h)

            @block.scalar
            def _(h):
                for f in ops["act"]:
                    f(h)

            @block.vector
            def _(h):
                for f in ops["dve"]:
                    f(h)

            @block.gpsimd
            def _(h):
                for f in ops["pool"]:
                    f(h)

            @block.sync
            def _(h):
                for f in ops["sp"]:
                    f(h)


EPS = 1e-6
D = 2048
NEG = -30000.0


def build(L):
    nc = bass.Bass("TRN2", target_bir_lowering=False)
    NTOK = L + 64
    tiles = [(0, i * 128, 128, i == 0, i == L // 128 - 1) for i in range(L // 128)]
    tiles += [(1, L, 32, True, True), (2, L + 32, 32, True, True)]

    def din(name, shape, dt=F32):
        return nc.dram_tensor(name, list(shape), dt, kind="ExternalInput").ap()

    def dout(name, shape, dt=F32):
        return nc.dram_tensor(name, list(shape), dt, kind="ExternalOutput").ap()

    xT_d = din("xT", [D, NTOK]); xtok_d = din("xtok", [NTOK, D])
    w_in_ab = din("w_in_ab", [D, 7200]); w_out_ab = din("w_out_ab", [D, D])
    w_in_cd = din("w_in_cd", [D, 6152]); w_out_cd = din("w_out_cd", [D, D])
    gluw_d = din("glu_w", [1024, 1024])
    consts_d = din("consts", [128, 9, 128])
    g0T_d = din("g0T", [128, 16]); g1b_d = din("g1b", [128, D]); gfb_d = din("gfb", [128, D])
    agw_d = din("a_gate_w", [16, 512]); agb_d = din("a_gate_b", [1, 512]); ang_d = din("a_norm_g_b", [128, 1024])
    cw_d = din("conv_w", [128, 24, 4]); alog_d = din("a_log_b", [128, 8]); dtb_d = din("dt_bias_b", [128, 8])
    bng_d = din("b_norm_g_b", [128, 128])
    s5col_d = din("s5col", [128, 3, 32]); s5row_d = din("s5row", [128, 3, 4096])
    bdb_d = din("bd_b", [2, 8, 128, 512]); bdc_d = din("bd_c", [2, 4, 128, 8, 128]); bdd_d = din("bd_d", [4, 128, 2, 128])
    glub_d = din("glu_b_b", [128, 1024]); dib_d = din("d_i_b", [128, 4]); dfb_d = din("d_f_b", [128, 4])
    dng_d = din("d_norm_g_b", [128, 1024])
    si_conv = din("si_conv", [2, 128, 24, 3]); si_gla = din("si_gla", [2, 4, 128, 256]); si_gdn = din("si_gdn", [2, 8, 128, 128])
    si_s5 = din("si_s5", [2, 128, 2, 32]); si_mc = din("si_mc", [2, 4, 128, 256]); si_mn = din("si_mn", [2, 128, 4])
    si_mm = din("si_mm", [2, 128, 4])

    y_d = dout("y", [NTOK, D])
    so_conv = dout("so_conv", [3, 128, 24, 3]); so_gla = dout("so_gla", [3, 4, 128, 256]); so_gdn = dout("so_gdn", [3, 8, 128, 128])
    so_s5 = dout("so_s5", [3, 128, 2, 32]); so_mc = dout("so_mc", [3, 4, 128, 256]); so_mn = dout("so_mn", [3, 128, 4])
    so_mm = dout("so_mm", [3, 128, 4])

    oT_d = nc.dram_tensor("oT_s", [D, NTOK], BF16).ap()
    h1_d = nc.dram_tensor("h1_s", [NTOK, D], F32).ap()
    hnT_d = nc.dram_tensor("hnT_s", [D, NTOK], BF16).ap()
    yT_d = nc.dram_tensor("yT_s", [1024, NTOK], BF16).ap()
    yz_d = nc.dram_tensor("yz_s", [NTOK, 1024], F32).ap()

    k = KB(nc)
    k.psum_init()
    sb = k.sb
    cst = sb("cst", [128, 9, 128])
    k.dma(cst[:], consts_d[:, :, :])
    ident = cst[:, 0, :]; U = cst[:, 1, :]; SU = cst[:, 2, :]; NSL = cst[:, 3, :]; ones = cst[:, 4, :]
    NMU = cst[:, 5, :]; NML = cst[:, 6, :]; IOTA = cst[:, 7, :]; SELS = cst[:, 8, :]
    tcol = cst[:, 8, 0:1]
    sel128 = sb("sel128", [128, 128]); sel32 = sb("sel32", [128, 128])
    k.ts(sel128[:], ones, cst[:, 8, 1:2], ALU.mult)
    k.ts(sel32[:], ones, cst[:, 8, 2:3], ALU.mult)
    epsb = sb("epsb", [128, 1]); k.memset(epsb[:], EPS)
    wbuf = sb("wbuf", [128, 16, 2048], BF16)
    xg = sb("xg", [128, 16, 128], BF16)
    projT = sb("projT", [128, 1160])
    tbuf = [sb(f"tb{i}", [128, 512]) for i in range(12)]
    cols = sb("cols", [128, 64])
    obf = [sb(f"obf{i}", [128, 128], BF16) for i in range(4)]
    obc = [0]
    qT = sb("qT", [128, 128]); kT = sb("kT", [128, 128])
    k.open_scope()
    xt2 = [sb(f"xt{i}", [128, 16, 128]) for i in range(2)]
    xsq = sb("xsq", [128, 16, 128]); xs1 = sb("xs1", [128, 128])
    rbc = sb("rbc", [128, 128]); rcol = sb("rcol", [128, 1])
    g0T = sb("g0T", [128, 16]); k.dma(g0T[:], g0T_d[:, :])
    tctr = [0]

    def tmp():
        t = tbuf[tctr[0] % len(tbuf)]
        tctr[0] += 1
        return t

    def rs_eps(out, in_, scale):
        k.act(out, in_, AF.Ln, bias=epsb[:out.shape[0], :], scale=scale)
        k.act(out, out, AF.Exp, scale=-0.5)

    def load_w(src, pairs):
        for d0, s0, n in pairs:
            k.dma(wbuf[:, :, d0:d0 + n], src[:, s0:s0 + n].rearrange("(kc p) c -> p kc c", p=128), eng="pool")

    def norm_x(ti, t0, T):
        xt = xt2[ti % 2]
        k.dma(xt[:, :, :T], xT_d[:, t0:t0 + T].rearrange("(kc p) t -> p kc t", p=128))
        k.tt(xsq[:, :, :T], xt[:, :, :T], xt[:, :, :T], ALU.mult, eng="pool")
        k.red(xs1[:, :T], xsq[:, :, :T].rearrange("p kc t -> p t kc"))
        p = k.ps(); k.mm(p[:, :T], ones, xs1[:, :T]); rs_eps(rbc[:, :T], p[:, :T], 1.0 / D)
        p = k.ps(); k.mm(p[:T, 0:1], xs1[:, :T], ones[:, 0:1]); rs_eps(rcol[:T, :], p[:T, 0:1], 1.0 / D)
        k.tt(xg[:, :, :T], xt[:, :, :T], g0T[:, :].unsqueeze(2).to_broadcast([128, 16, T]), ALU.mult)

    def load_hn(t0, T):
        k.dma(xg[:, :, :T], hnT_d[:, t0:t0 + T].rearrange("(kc p) t -> p kc t", p=128), reads=[("hnT", t0)])

    def projF(dst, col0, T, ncols=128, scaled=True):
        p = k.ps()
        for kc in range(16):
            k.mm(p[:ncols, :T], wbuf[:, kc, col0:col0 + ncols], xg[:, kc, :T], start=(kc == 0), stop=(kc == 15))
        if scaled:
            k.tt(dst, p[:ncols, :T], rbc[:ncols, :T], ALU.mult)
        else:
            k.cp(dst, p[:ncols, :T], eng="act")

    def projTok(col0, ncols, T, scaled=True, dst0=0):
        for c0 in range(0, ncols, 512):
            n = min(512, ncols - c0)
            p = k.ps()
            for kc in range(16):
                k.mm(p[:T, :n], xg[:, kc, :T], wbuf[:, kc, col0 + c0:col0 + c0 + n], start=(kc == 0), stop=(kc == 15))
            if scaled:
                k.ts(projT[:T, dst0 + c0:dst0 + c0 + n], p[:T, :n], rcol[:T, :], ALU.mult)
            else:
                k.cp(projT[:T, dst0 + c0:dst0 + c0 + n], p[:T, :n], eng="act")

    def store_oT(src, ncols, frow, t0, T):
        for c0 in range(0, ncols, 128):
            p = k.ps(); k.tr(p[:, :T], src[:T, c0:c0 + 128], ident[:T, :T])
            ob = obf[obc[0] % 4]; obc[0] += 1
            k.cp(ob[:, :T], p[:, :T], eng="act")
            k.dma(oT_d[frow + c0:frow + c0 + 128, t0:t0 + T], ob[:, :T], writes=[("oT", frow + c0, t0)])

    def headnorm_rms(o_sb, T, n, gsz, dst):
        sq = tmp(); k.tt(sq[:T, :n], o_sb, o_sb, ALU.mult, eng="pool")
        c = cols[:, 60:61]; k.red(c[:T, :], sq[:T, :n])
        r = cols[:, 61:62]; rs_eps(r[:T, :], c[:T, :], 1.0 / n)
        k.stt(dst, o_sb, r[:T, :], gsz, ALU.mult, ALU.mult)

    S_gla = sb("S_gla", [128, 256]); S_gdn = [sb(f"S_gdn{i}", [128, 128]) for i in range(2)]
    cbuf = sb("cbuf", [128, 6, 131]); cw = sb("cw", [128, 6, 4])
    gw = sb("gw", [16, 128]); gb = sb("gb", [1, 128]); angb = sb("angb", [128, 256]); bngb = sb("bngb", [128, 128])
    alog = sb("alog", [128, 8]); dtb = sb("dtb", [128, 8]); na = sb("na", [128, 8])
    k.dma(alog[:], alog_d[:, :]); k.dma(dtb[:], dtb_d[:, :]); k.dma(bngb[:], bng_d[:, :])
    k.act(na[:], alog[:], AF.Exp); k.ts(na[:], na[:], -1.0, ALU.mult)
    gaT = sb("gaT", [16, 128])
    cvs = sb("cvs", [128, 6, 128]); cacc = sb("cacc", [128, 6, 128]); ctmp = sb("ctmp", [128, 6, 128])
    Rp = [sb(f"Rp{i}", [128, 128]) for i in range(7)]
    Qp = [sb(f"Qp{i}", [128, 128]) for i in range(2)]
    g_vk = sb("g_vk", [128, 256]); g_egrow = sb("g_egrow", [128, 128]); g_AT = sb("g_AT", [128, 128])
    g_X = [sb(f"g_X{i}", [128, 256]) for i in range(2)]
    SC = 128 ** -0.5

    def gla_tile(hg, seq, t0, T):
        kk = projT[:T, 0:128]; v = projT[:T, 128:384]; za = projT[:T, 384:640]
        p = k.ps()
        k.mm(p[:T, :128], gaT[:, :T], gw[:, :], start=True, stop=False)
        k.mm(p[:T, :128], ones[0:1, :T], gb[:, :], start=False, stop=True)
        e = tmp(); k.act(e[:T, :128], p[:T, :128], AF.Exp, scale=-1.0)
        spl = tmp(); k.act(spl[:T, :128], e[:T, :128], AF.Ln, bias=1.0)
        pc = k.ps(); k.mm(pc[:, :T], spl[:T, :128], U[:T, :T])
        ebT = tmp(); k.act(ebT[:, :T], pc[:, :T], AF.Exp, scale=-1.0 / 16)
        enbT = tmp(); k.act(enbT[:, :T], pc[:, :T], AF.Exp, scale=1.0 / 16)
        qd = tmp(); k.stt(qd[:, :T], qT[:, :T], SC, ebT[:, :T], ALU.mult, ALU.mult)
        kd = tmp(); k.tt(kd[:, :T], kT[:, :T], enbT[:, :T], ALU.mult, eng="pool")
        pd = k.ps(); k.mm(pd[:T, :128], NSL[:T, :T], spl[:T, :128])
        ekw = tmp(); k.act(ekw[:T, :128], pd[:T, :128], AF.Exp, scale=1.0 / 16)
        kw = tmp(); k.tt(kw[:T, :128], kk, ekw[:T, :128], ALU.mult, eng="pool")
        psc = k.ps(); k.mm(psc[:T, :T], kd[:, :T], qd[:, :T])
        PT = tmp(); k.tt(PT[:T, :T], psc[:T, :T], U[:T, :T], ALU.mult)
        po = k.ps()
        k.mm(po[:T, :256], PT[:T, :T], v, start=True, stop=False)
        k.mm(po[:T, :256], qd[:, :T], S_gla[:, :], start=False, stop=True)
        o_sb = tmp(); k.cp(o_sb[:T, :256], po[:T, :256], eng="act")
        pkv = k.ps(); k.mm(pkv[:, :256], kw[:T, :128], v)
        k.stt(S_gla[:, :], S_gla[:, :], ebT[:, T - 1:T], pkv[:, :256], ALU.mult, ALU.add)
        gsz = tmp(); k.act(gsz[:T, :256], za, AF.Silu)
        k.tt(gsz[:T, :256], gsz[:T, :256], angb[:T, :], ALU.mult, eng="pool")
        og = tmp(); headnorm_rms(o_sb[:T, :256], T, 256, gsz[:T, :256], og[:T, :256])
        store_oT(og, 256, hg * 256, t0, T)

    def gdn_pre(T):
        for j in range(4):
            wj = cw[:, :, j:j + 1].to_broadcast([128, 6, T])
            if j == 0:
                k.tt(cacc[:, :, :T], cbuf[:, :, 0:T], wj, ALU.mult)
            else:
                k.tt(ctmp[:, :, :T], cbuf[:, :, j:j + T], wj, ALU.mult, eng="pool")
                k.tt(cacc[:, :, :T], cacc[:, :, :T], ctmp[:, :, :T], ALU.add)
        k.act(cvs[:, :, :T], cacc[:, :, :T], AF.Silu)
        k.tt(ctmp[:, 0:4, :T], cvs[:, 0:4, :T], cvs[:, 0:4, :T], ALU.mult, eng="pool")
        p = k.ps()
        for b in range(4):
            k.mm(p[:, b * T:(b + 1) * T], ones, ctmp[:, b, :T])
        rs_eps(cacc[:, 0:4, :T], p[:, :4 * T].rearrange("p (a b) -> p a b", a=4), 1.0)
        k.tt(cvs[:, 0:4, :T], cvs[:, 0:4, :T], cacc[:, 0:4, :T], ALU.mult)

    def gdn_head(hg, hh, seq, t0, T):
        Sg = S_gdn[hh]
        qTh = cvs[:, 0 + hh, :T]; kTh = cvs[:, 2 + hh, :T]; vTh = cvs[:, 4 + hh, :T]
        zb = projT[:T, 640 + 128 * hh:768 + 128 * hh]
        beta_pre = projT[:T, 896 + hh:897 + hh]; a_pre = projT[:T, 898 + hh:899 + hh]
        gh = 2 * hg + hh
        c = lambda i: cols[:, i:i + 1]
        beta = c(0); k.act(beta[:T, :], beta_pre, AF.Sigmoid)
        e = c(1); k.act(e[:T, :], a_pre, AF.Exp, bias=dtb[:T, gh:gh + 1])
        spg = c(2); k.act(spg[:T, :], e[:T, :], AF.Ln, bias=1.0)
        graw = c(3); k.tt(graw[:T, :], spg[:T, :], na[:T, gh:gh + 1], ALU.mult)
        vk = g_vk
        p = k.ps(); k.tr(p[:T, 0:128], vTh, ident); k.tr(p[:T, 128:256], kTh, ident)
        k.cp(vk[:T, :256], p[:T, :256], eng="act")
        p = k.ps(); k.mm(p[:T, 0:1], U[:T, :T], graw[:T, :])
        gc = c(4); k.cp(gc[:T, :], p[:T, 0:1]); ngc = c(5); k.ts(ngc[:T, :], p[:T, 0:1], -1.0, ALU.mult)
        Ug = tmp(); k.ts(Ug[:T, :T], U[:T, :T], graw[:T, :], ALU.mult)
        pg = k.ps(); k.mm(pg[:, :T], ones[:T, :], Ug[:T, :T])
        Ib = tmp(); k.ts(Ib[:T, :T], ident[:T, :T], beta[:T, :], ALU.mult)
        pb = k.ps(); k.mm(pb[:T, :T], ones[:T, :T], Ib[:T, :T])
        arg = tmp(); k.tt(arg[:T, :T], pg[:T, :T], NMU[:T, :T], ALU.add)
        decT = tmp(); k.act(decT[:T, :T], arg[:T, :T], AF.Exp, bias=ngc[:T, :])
        egrow = g_egrow; k.act(egrow[:, :T], pg[:, :T], AF.Exp)
        glast = c(6); k.cp(glast[:, :], pg[:, T - 1:T])
        eglast = c(7); k.cp(eglast[:, :], egrow[:, T - 1:T], eng="pool")
        egc = c(8); k.act(egc[:T, :], gc[:T, :], AF.Exp)
        ekl = c(9); k.act(ekl[:T, :], gc[:T, :], AF.Exp, scale=-1.0, bias=glast[:T, :])
        pkk = k.ps(); k.mm(pkk[:T, :T], kTh, kTh)
        pqk = k.ps(); k.mm(pqk[:T, :T], kTh, qTh)
        AT = g_AT; k.stt(AT[:T, :T], pqk[:T, :T], SC, decT[:T, :T], ALU.mult, ALU.mult)
        t1 = tmp(); k.tt(t1[:T, :T], pkk[:T, :T], decT[:T, :T], ALU.mult)
        t2 = tmp(); k.tt(t2[:T, :T], pb[:T, :T], SU[:T, :T], ALU.mult)
        LT = Rp[0]; k.tt(LT[:T, :T], t1[:T, :T], t2[:T, :T], ALU.mult, eng="pool")
        X = g_X[0]; xi_ = 0
        k.ts(X[:T, 0:128], vk[:T, 0:128], beta[:T, :], ALU.mult)
        k.ts(X[:T, 128:256], vk[:T, 128:256], beta[:T, :], ALU.mult, egc[:T, :], ALU.mult)
        p = k.ps(); k.tr(p[:T, :T], LT[:T, :T], ident[:T, :T]); k.cp(Qp[0][:T, :T], p[:T, :T], eng="act")
        p = k.ps(); k.mm(p[:T, :256], LT[:T, :T], X[:T, :256])
        xi_ ^= 1; Xn = g_X[xi_]; k.tt(Xn[:T, :256], X[:T, :256], p[:T, :256], ALU.subtract); X = Xn
        nsq = {128: 6, 32: 4}[T]
        for n in range(nsq):
            Q = Qp[n % 2]; R = Rp[n]
            p2 = k.ps(); k.mm(p2[:T, :T], Q[:T, :T], R[:T, :T]); k.cp(Rp[n + 1][:T, :T], p2[:T, :T], eng="act")
            if n < nsq - 1:
                p1 = k.ps(); k.mm(p1[:T, :T], R[:T, :T], Q[:T, :T]); k.cp(Qp[(n + 1) % 2][:T, :T], p1[:T, :T])
            p = k.ps(); k.mm(p[:T, :256], Rp[n + 1][:T, :T], X[:T, :256])
            xi_ ^= 1; Xn = g_X[xi_]; k.tt(Xn[:T, :256], X[:T, :256], p[:T, :256], ALU.add); X = Xn
        p = k.ps(); k.tr(p[:, :T], X[:T, 128:256], ident[:T, :T])
        wT = tmp(); k.cp(wT[:, :T], p[:, :T], eng="act")
        p = k.ps(); k.mm(p[:T, :128], wT[:, :T], Sg[:, :])
        vnew = tmp(); k.tt(vnew[:T, :128], X[:T, 0:128], p[:T, :128], ALU.subtract)
        qg = tmp(); k.stt(qg[:, :T], qTh, SC, egrow[:, :T], ALU.mult, ALU.mult)
        po = k.ps()
        k.mm(po[:T, :128], qg[:, :T], Sg[:, :], start=True, stop=False)
        k.mm(po[:T, :128], AT[:T, :T], vnew[:T, :128], start=False, stop=True)
        o_sb = tmp(); k.cp(o_sb[:T, :128], po[:T, :128], eng="act")
        kg = tmp(); k.ts(kg[:T, :128], vk[:T, 128:256], ekl[:T, :], ALU.mult)
        pkv = k.ps(); k.mm(pkv[:, :128], kg[:T, :128], vnew[:T, :128])
        k.stt(Sg[:, :], Sg[:, :], eglast[:, :], pkv[:, :128], ALU.mult, ALU.add)
        gsz = tmp(); k.act(gsz[:T, :128], zb, AF.Silu)
        k.tt(gsz[:T, :128], gsz[:T, :128], bngb[:T, :], ALU.mult, eng="pool")
        og = tmp(); headnorm_rms(o_sb[:T, :128], T, 128, gsz[:T, :128], og[:T, :128])
        store_oT(og, 128, 1024 + gh * 128, t0, T)

    for hg in range(4):
        TB = 1040
        load_w(w_in_ab, [(0, 128 * hg, 128), (128, 512 + 128 * hg, 128),
                         (256, 3088 + 256 * hg, 256), (512, 4112 + 256 * hg, 256), (768, 5136 + 256 * hg, 256),
                         (1024, 3072, 16),
                         (TB, 512 + 128 * hg, 128), (TB + 128, 1024 + 256 * hg, 256), (TB + 384, 2048 + 256 * hg, 256),
                         (TB + 640, 6160 + 256 * hg, 256), (TB + 896, 7184 + 2 * hg, 2), (TB + 898, 7192 + 2 * hg, 2)])
        k.dma(gw[:], agw_d[:, 128 * hg:128 * hg + 128]); k.dma(gb[:], agb_d[:, 128 * hg:128 * hg + 128])
        k.dma(angb[:], ang_d[:, 256 * hg:256 * hg + 256])
        for typ in range(3):
            k.dma(cw[:, 2 * typ:2 * typ + 2, :], cw_d[:, typ * 8 + 2 * hg:typ * 8 + 2 * hg + 2, :])
        for ti, (seq, t0, T, first, last) in enumerate(tiles):
            if first:
                if seq == 0:
                    k.memset(S_gla[:], 0.0); k.memset(S_gdn[0][:], 0.0); k.memset(S_gdn[1][:], 0.0)
                    k.memset(cbuf[:, :, 0:3], 0.0)
                else:
                    k.dma(S_gla[:], si_gla[seq - 1, hg, :, :])
                    for hh in range(2):
                        k.dma(S_gdn[hh][:], si_gdn[seq - 1, 2 * hg + hh, :, :])
                    for typ in range(3):
                        k.dma(cbuf[:, 2 * typ:2 * typ + 2, 0:3], si_conv[seq - 1, :, typ * 8 + 2 * hg:typ * 8 + 2 * hg + 2, :])
            else:
                k.cp(cbuf[:, :, 0:3], cbuf[:, :, 128:131])
            norm_x(ti, t0, T)
            projF(qT[:, :T], 0, T); projF(kT[:, :T], 128, T)
            for b in range(6):
                projF(cbuf[:, b, 3:3 + T], 256 + 128 * b, T)
            projF(gaT[:, :T], 1024, T, ncols=16)
            projTok(TB, 900, T)
            gla_tile(hg, seq, t0, T)
            gdn_pre(T)
            for hh in range(2):
                gdn_head(hg, hh, seq, t0, T)
            if last:
                k.dma(so_gla[seq, hg, :, :], S_gla[:])
                for hh in range(2):
                    k.dma(so_gdn[seq, 2 * hg + hh, :, :], S_gdn[hh][:])
                for typ in range(3):
                    k.dma(so_conv[seq, :, typ * 8 + 2 * hg:typ * 8 + 2 * hg + 2, :], cbuf[:, 2 * typ:2 * typ + 2, T:T + 3])

    k.close_scope()
    def outproj_alloc():
        return (sb("ot", [128, 16, 128], BF16), sb("resid", [128, D]), sb("hbuf", [128, D]),
                sb("gnb", [128, D]), sb("hsq", [128, D]), sb("hnT_sb", [128, 16, 128], BF16))

    def outproj_pass(w_out, layer):
        k.open_scope()
        ot, resid, hbuf, gnb, hsq, hnT_sb = outproj_alloc()
        load_w(w_out, [(0, 0, 2048)])
        k.dma(gnb[:], (g1b_d if layer == 0 else gfb_d)[:, :])
        for ti, (seq, t0, T, first, last) in enumerate(tiles):
            k.dma(ot[:, :, :T], oT_d[:, t0:t0 + T].rearrange("(kc p) t -> p kc t", p=128),
                  reads=[("oT", f, t0) for f in range(0, D, 128)])
            if layer == 0:
                k.dma(resid[:T, :], xtok_d[t0:t0 + T, :])
            else:
                k.dma(resid[:T, :], h1_d[t0:t0 + T, :], reads=[("h1", t0)])
            for nb in range(4):
                p = k.ps()
                for kc in range(16):
                    k.mm(p[:T, :512], ot[:, kc, :T], wbuf[:, kc, nb * 512:(nb + 1) * 512], start=(kc == 0), stop=(kc == 15))
                k.tt(hbuf[:T, nb * 512:(nb + 1) * 512], p[:T, :512], resid[:T, nb * 512:(nb + 1) * 512], ALU.add)
            if layer == 0:
                k.dma(h1_d[t0:t0 + T, :], hbuf[:T, :], writes=[("h1", t0)])
            k.tt(hsq[:T, :], hbuf[:T, :], hbuf[:T, :], ALU.mult, eng="pool")
            c = cols[:, 62:63]; k.red(c[:T, :], hsq[:T, :])
            r = cols[:, 63:64]; rs_eps(r[:T, :], c[:T, :], 1.0 / D)
            k.stt(hsq[:T, :], hbuf[:T, :], r[:T, :], gnb[:T, :], ALU.mult, ALU.mult)
            if layer == 0:
                for q4 in range(4):
                    p = k.ps()
                    for j in range(4):
                        kc = q4 * 4 + j
                        k.tr(p[:, j * 128:j * 128 + T], hsq[:T, kc * 128:(kc + 1) * 128], ident[:T, :T])
                    if T == 128:
                        k.cp(hnT_sb[:, q4 * 4:q4 * 4 + 4, :], p[:, :].rearrange("p (a b) -> p a b", a=4), eng="act")
                    else:
                        for j in range(4):
                            k.cp(hnT_sb[:, q4 * 4 + j, :T], p[:, j * 128:j * 128 + T], eng="act")
                k.dma(hnT_d[:, t0:t0 + T].rearrange("(kc p) t -> p kc t", p=128), hnT_sb[:, :, :T], writes=[("hnT", t0)])
            else:
                k.dma(y_d[t0:t0 + T, :], hsq[:T, :])
        k.close_scope()

    outproj_pass(w_out_ab, 0)

    k.open_scope()
    s5c = sb("s5c", [128, 3, 8]); s5r = sb("s5r", [128, 3, 1024])
    ETr = sb("ETr", [128, 8, 128]); ETi = sb("ETi", [128, 8, 128]); EIr = sb("EIr", [128, 1024]); EIi = sb("EIi", [128, 1024])
    big = [sb(f"big{i}", [128, 1024]) for i in range(6)]
    bigi = sb("bigi", [128, 1024], I32)
    BB = sb("BB", [128, 4, 512]); WB = sb("WB", [128, 4, 512])
    CB = sb("CB", [128, 2, 8, 128]); DD = sb("DD", [128, 2, 128])
    uT = sb("uT", [128, 2, 128])
    car = sb("car", [128, 2, 8]); xl = sb("xl", [128, 2, 8]); A1 = sb("A1", [128, 2, 8])
    TWO_PI = 2.0 * np.pi

    def sincos(dst_sin, dst_cos, ang, n, shp=None):
        for dst, off in ((dst_sin, 0.0), (dst_cos, 0.25)):
            tn = big[4][:, :n]; tf = big[5][:, :n]
            k.ts(tn, ang, 1.0 / TWO_PI, ALU.mult, off, ALU.add)
            k.cp(bigi[:, :n], tn)
            k.cp(tf, bigi[:, :n])
            k.tt(tn, tn, tf, ALU.subtract)
            k.act(dst, tn, AF.Sin, scale=TWO_PI)

    def cmul(outr, outi, ar, ai, br, bi, n, t1, t2):
        k.tt(t1, ar, br, ALU.mult); k.tt(t2, ai, bi, ALU.mult, eng="pool"); k.tt(outr, t1, t2, ALU.subtract)
        k.tt(t1, ar, bi, ALU.mult); k.tt(t2, ai, br, ALU.mult, eng="pool"); k.tt(outi, t1, t2, ALU.add)

    def s5_setup(hg):
        k.dma(s5c[:], s5col_d[:, :, 8 * hg:8 * hg + 8]); k.dma(s5r[:], s5row_d[:, :, 1024 * hg:1024 * hg + 1024])
        dtc = cols[:, 16:24]; arc = cols[:, 24:32]; aic = cols[:, 32:40]
        k.act(dtc, s5c[:, 2, :], AF.Exp)
        k.tt(arc, s5c[:, 0, :], dtc, ALU.mult); k.tt(aic, s5c[:, 1, :], dtc, ALU.mult)
        ang = big[0][:, :].rearrange("p (a b) -> p a b", a=8)
        io = IOTA.unsqueeze(1).to_broadcast([128, 8, 128])
        k.tt(ang, io, aic.unsqueeze(2).to_broadcast([128, 8, 128]), ALU.mult)
        sincos(big[1][:, :], big[2][:, :], big[0][:, :], 1024)
        mg = big[3][:, :].rearrange("p (a b) -> p a b", a=8)
        k.tt(mg, io, arc.unsqueeze(2).to_broadcast([128, 8, 128]), ALU.mult)
        k.act(big[3][:, :], big[3][:, :], AF.Exp)
        k.tt(ETi[:, :, :].rearrange("p a b -> p (a b)"), big[3][:, :], big[1][:, :], ALU.mult)
        k.tt(ETr[:, :, :].rearrange("p a b -> p (a b)"), big[3][:, :], big[2][:, :], ALU.mult)
        k.cp(A1[:, 0, :], ETr[:, :, 1]); k.cp(A1[:, 1, :], ETi[:, :, 1])
        dtr = big[0][:, :]; k.act(dtr, s5r[:, 2, :], AF.Exp)
        arr = sb_arr[:, :]; air = sb_air[:, :]
        k.tt(arr, s5r[:, 0, :], dtr, ALU.mult); k.tt(air, s5r[:, 1, :], dtr, ALU.mult)
        sincos(big[1][:, :], big[2][:, :], air, 1024)
        k.act(big[3][:, :], arr, AF.Exp)
        abi = big[1][:, :]; abr = big[2][:, :]
        k.tt(abi, abi, big[3][:, :], ALU.mult); k.tt(abr, abr, big[3][:, :], ALU.mult)
        k.ts(abr, abr, -1.0, ALU.add)
        den = big[3][:, :]; t = big[0][:, :]
        k.tt(den, s5r[:, 0, :], s5r[:, 0, :], ALU.mult); k.tt(t, s5r[:, 1, :], s5r[:, 1, :], ALU.mult)
        k.tt(den, den, t, ALU.add); k.recip(den, den)
        zr = big[4][:, :]; zi = big[5][:, :]
        k.tt(zr, abr, s5r[:, 0, :], ALU.mult); k.tt(t, abi, s5r[:, 1, :], ALU.mult); k.tt(zr, zr, t, ALU.add); k.tt(zr, zr, den, ALU.mult)
        k.tt(zi, abi, s5r[:, 0, :], ALU.mult); k.tt(t, abr, s5r[:, 1, :], ALU.mult); k.tt(zi, zi, t, ALU.subtract); k.tt(zi, zi, den, ALU.mult)
        for h in range(2):
            k.dma(WB[:, 2 * h, :], bdb_d[0, 2 * hg + h, :, :]); k.dma(WB[:, 2 * h + 1, :], bdb_d[1, 2 * hg + h, :, :])
            sl = slice(512 * h, 512 * h + 512)
            t1 = big[0][:, 0:512]; t2 = big[0][:, 512:1024]
            cmul(BB[:, 2 * h, :], BB[:, 2 * h + 1, :], zr[:, sl], zi[:, sl], WB[:, 2 * h, :], WB[:, 2 * h + 1, :], 512, t1, t2)
        k.ts(big[0][:, :], air, tcol, ALU.mult)
        sincos(big[1][:, :], big[2][:, :], big[0][:, :], 1024)
        k.ts(big[3][:, :], arr, tcol, ALU.mult); k.act(big[3][:, :], big[3][:, :], AF.Exp, scale=-1.0)
        k.tt(EIr[:, :], big[3][:, :], big[2][:, :], ALU.mult)
        k.stt(EIi[:, :], big[3][:, :], -1.0, big[1][:, :], ALU.mult, ALU.mult)
        k.dma(CB[:, 0, :, :], bdc_d[0, hg, :, :, :]); k.dma(CB[:, 1, :, :], bdc_d[1, hg, :, :, :])
        k.ts(CB[:, 1, :, :], CB[:, 1, :, :], -1.0, ALU.mult)
        k.dma(DD[:], bdd_d[hg, :, :, :])

    sb_arr = sb("sb_arr", [128, 1024]); sb_air = sb("sb_air", [128, 1024])

    def s5_tile(hg, seq, t0, T):
        zc = projT[:T, 0:256]
        Bu = [big[0], big[1]]
        for h in range(2):
            for cpx in range(2):
                p = k.ps(); k.mm(p[:T, :512], uT[:, h, :T], BB[:, 2 * h + cpx, :])
                k.cp(Bu[cpx][:T, 512 * h:512 * h + 512], p[:T, :512], eng="act")
        Zr = big[2]; Zi = big[3]
        cmul(Zr[:T, :], Zi[:T, :], EIr[:T, :], EIi[:T, :], Bu[0][:T, :], Bu[1][:T, :], 1024, big[4][:T, :], big[5][:T, :])
        Wc = [big[0], big[1]]
        for cpx, Z in enumerate((Zr, Zi)):
            for half in range(2):
                p = k.ps()
                for j in range(4):
                    ch = half * 4 + j
                    k.mm(p[:, j * 128:j * 128 + T], Z[:T, ch * 128:(ch + 1) * 128], U[:T, :T])
                src = p[:, :].rearrange("p (a b) -> p a b", a=4)[:, :, :T]
                dst = Wc[cpx][:, :].rearrange("p (a b) -> p a b", a=8)[:, half * 4:half * 4 + 4, :T]
                k.tt(dst, src, car[:, cpx, half * 4:half * 4 + 4].unsqueeze(2).to_broadcast([128, 4, T]), ALU.add)
        XTr = big[2][:, :].rearrange("p (a b) -> p a b", a=8); XTi = big[3][:, :].rearrange("p (a b) -> p a b", a=8)
        W3 = [w[:, :].rearrange("p (a b) -> p a b", a=8) for w in Wc]
        t1 = big[4][:, :].rearrange("p (a b) -> p a b", a=8); t2 = big[5][:, :].rearrange("p (a b) -> p a b", a=8)
        cmul(XTr[:, :, :T], XTi[:, :, :T], ETr[:, :, :T], ETi[:, :, :T], W3[0][:, :, :T], W3[1][:, :, :T], 0, t1[:, :, :T], t2[:, :, :T])
        k.cp(xl[:, 0, :], XTr[:, :, T - 1]); k.cp(xl[:, 1, :], XTi[:, :, T - 1])
        c1 = cols[:, 40:48]; c2 = cols[:, 48:56]
        cmul(car[:, 0, :], car[:, 1, :], A1[:, 0, :], A1[:, 1, :], xl[:, 0, :], xl[:, 1, :], 8, c1, c2)
        py = k.ps()
        for h in range(2):
            o = py[:T, 128 * h:128 * h + 128]
            k.mm(o, uT[:, h, :T], DD[:, h, :], start=True, stop=False)
            for pc in range(4):
                ch = 4 * h + pc
                k.mm(o, XTr[:, ch, :T], CB[:, 0, ch, :], start=False, stop=False)
                k.mm(o, XTi[:, ch, :T], CB[:, 1, ch, :], start=False, stop=(pc == 3))
        yg = tmp(); k.act(yg[:T, :256], py[:T, :256], AF.Gelu)
        sz = tmp(); k.act(sz[:T, :256], zc, AF.Silu)
        yz = tmp(); k.tt(yz[:T, :256], yg[:T, :256], sz[:T, :256], ALU.mult)
        k.dma(yz_d[t0:t0 + T, 256 * hg:256 * hg + 256], yz[:T, :256], writes=[("yz", hg, t0)])
        for c0 in range(0, 256, 128):
            p = k.ps(); k.tr(p[:, :T], yg[:T, c0:c0 + 128], ident[:T, :T])
            ob = obf[obc[0] % 4]; obc[0] += 1
            k.cp(ob[:, :T], p[:, :T], eng="act")
            k.dma(yT_d[256 * hg + c0:256 * hg + c0 + 128, t0:t0 + T], ob[:, :T], writes=[("yT", 256 * hg + c0, t0)])

    for hg in range(4):
        load_w(w_in_cd, [(0, 256 * hg, 256), (256, 1024 + 256 * hg, 256)])
        s5_setup(hg)
        for ti, (seq, t0, T, first, last) in enumerate(tiles):
            if first:
                if seq == 0:
                    k.memset(car[:], 0.0)
                else:
                    k.dma(xl[:], si_s5[seq - 1, :, :, 8 * hg:8 * hg + 8])
                    cmul(car[:, 0, :], car[:, 1, :], A1[:, 0, :], A1[:, 1, :], xl[:, 0, :], xl[:, 1, :], 8, cols[:, 40:48], cols[:, 48:56])
            load_hn(t0, T)
            projF(uT[:, 0, :T], 0, T, scaled=False); projF(uT[:, 1, :T], 128, T, scaled=False)
            projTok(256, 256, T, scaled=False)
            s5_tile(hg, seq, t0, T)
            if last:
                k.dma(so_s5[seq, :, :, 8 * hg:8 * hg + 8], xl[:])

    k.close_scope()
    k.open_scope()
    Cx = sb("Cx", [128, 257]); mst = sb("mst", [128, 1]); vx = sb("vx", [128, 257])
    gluw = sb("gluw", [128, 8, 256], BF16); ytl = sb("ytl", [128, 8, 128], BF16)
    glub = sb("glub", [128, 256]); dngb = sb("dngb", [128, 256]); dib = sb("dib", [128, 4]); dfb = sb("dfb", [128, 4])
    yzt = sb("yzt", [128, 256])
    k.dma(dib[:], dib_d[:, :]); k.dma(dfb[:], dfb_d[:, :]); k.ts(dfb[:], dfb[:], -1.0, ALU.mult)
    k.memset(vx[:, 256:257], 1.0)

    def mlstm_tile(hg, seq, t0, T):
        kk = projT[:T, 0:128]; od = projT[:T, 384:640]; zd = projT[:T, 640:896]
        c = lambda i: cols[:, i:i + 1]
        k.cp(vx[:T, 0:256], projT[:T, 128:384], eng="pool")
        ip = c(0); k.tt(ip[:T, :], projT[:T, 896:897], dib[:T, hg:hg + 1], ALU.add)
        e = c(1); k.act(e[:T, :], projT[:T, 897:898], AF.Exp, scale=-1.0, bias=dfb[:T, hg:hg + 1])
        sp = c(2); k.act(sp[:T, :], e[:T, :], AF.Ln, bias=1.0)
        p = k.ps(); k.mm(p[:T, 0:1], U[:T, :T], sp[:T, :]); bcol = c(3); k.ts(bcol[:T, :], p[:T, 0:1], -1.0, ALU.mult)
        p = k.ps(); k.mm(p[:, 0:1], ones[:T, :], sp[:T, :]); blast = c(4); k.ts(blast[:, :], p[:, 0:1], -1.0, ALU.mult)
        d = c(5); k.tt(d[:T, :], ip[:T, :], bcol[:T, :], ALU.subtract)
        Dm = tmp(); k.ts(Dm[:T, :T], ident[:T, :T], d[:T, :], ALU.mult)
        pr = k.ps(); k.mm(pr[:T, :T], ones[:T, :T], Dm[:T, :T])
        lw = tmp(); k.stt(lw[:T, :T], pr[:T, :T], bcol[:T, :], NML[:T, :T], ALU.add, ALU.add)
        mi = c(6); k.red(mi[:T, :], lw[:T, :T], ALU.max)
        nmi = c(7); k.ts(nmi[:T, :], mi[:T, :], -1.0, ALU.mult)
        e1 = tmp(); k.act(e1[:T, :T], lw[:T, :T], AF.Exp, bias=nmi[:T, :])
        pqk = k.ps(); k.mm(pqk[:T, :T], qT[:, :T], kT[:, :T])
        pm = tmp(); k.stt(pm[:T, :T], pqk[:T, :T], SC, e1[:T, :T], ALU.mult, ALU.mult)
        p = k.ps(); k.tr(p[:T, :T], pm[:T, :T], ident[:T, :T]); pT = tmp(); k.cp(pT[:T, :T], p[:T, :T], eng="act")
        sel = sel128 if T == 128 else sel32
        p = k.ps(); k.mm(p[:, 0:1], sel[:T, :], mi[:T, :]); mch = c(8); k.cp(mch[:, :], p[:, 0:1])
        bm = c(9); k.tt(bm[:, :], blast[:, :], mch[:, :], ALU.subtract)
        kws = c(10); k.act(kws[:T, :], d[:T, :], AF.Exp, bias=bm[:T, :])
        kw = tmp(); k.ts(kw[:T, :128], kk, kws[:T, :], ALU.mult)
        a = c(11); k.tt(a[:T, :], bcol[:T, :], mst[:T, :], ALU.add)
        mt = c(12); k.tt(mt[:T, :], a[:T, :], mi[:T, :], ALU.max)
        nmt = c(13); k.ts(nmt[:T, :], mt[:T, :], -1.0, ALU.mult)
        wa = c(14); k.act(wa[:T, :], a[:T, :], AF.Exp, bias=nmt[:T, :]); k.ts(wa[:T, :], wa[:T, :], SC, ALU.mult)
        wi = c(15); k.act(wi[:T, :], mi[:T, :], AF.Exp, bias=nmt[:T, :])
        emt = c(56); k.act(emt[:T, :], nmt[:T, :], AF.Exp)
        pA = k.ps(); k.mm(pA[:T, :257], qT[:, :T], Cx[:, :])
        pB = k.ps(); k.mm(pB[:T, :257], pT[:T, :T], vx[:T, :])
        r1 = tmp(); k.ts(r1[:T, :257], pA[:T, :257], wa[:T, :], ALU.mult)
        res = tmp(); k.stt(res[:T, :257], pB[:T, :257], wi[:T, :], r1[:T, :257], ALU.mult, ALU.add)
        dn = c(57); k.act(dn[:T, :], res[:T, 256:257], AF.Abs); k.tt(dn[:T, :], dn[:T, :], emt[:T, :], ALU.max)
        k.recip(dn[:T, :], dn[:T, :])
        sg = tmp(); k.act(sg[:T, :256], od, AF.Sigmoid)
        hd = tmp(); k.stt(hd[:T, :256], res[:T, :256], dn[:T, :], sg[:T, :256], ALU.mult, ALU.mult)
        mu = c(58); k.red(mu[:T, :], hd[:T, :256]); k.ts(mu[:T, :], mu[:T, :], -1.0 / 256, ALU.mult)
        xc = tmp(); k.ts(xc[:T, :256], hd[:T, :256], mu[:T, :], ALU.add)
        gsz = tmp(); k.act(gsz[:T, :256], zd, AF.Silu); k.tt(gsz[:T, :256], gsz[:T, :256], dngb[:T, :], ALU.mult, eng="pool")
        od_ = tmp(); headnorm_rms(xc[:T, :256], T, 256, gsz[:T, :256], od_[:T, :256])
        store_oT(od_, 256, 1024 + 256 * hg, t0, T)
        pkv = k.ps(); k.mm(pkv[:, :257], kw[:T, :128], vx[:T, :])
        bms = c(59); k.tt(bms[:, :], blast[:, :], mst[:, :], ALU.add)
        mnew = c(56); k.tt(mnew[:, :], bms[:, :], mch[:, :], ALU.max)
        nmn = c(57); k.ts(nmn[:, :], mnew[:, :], -1.0, ALU.mult)
        wold = c(58); k.act(wold[:, :], bms[:, :], AF.Exp, bias=nmn[:, :])
        wnew = c(12); k.act(wnew[:, :], mch[:, :], AF.Exp, bias=nmn[:, :])
        t = tmp(); k.ts(t[:, :257], pkv[:, :257], wnew[:, :], ALU.mult)
        k.stt(Cx[:, :], Cx[:, :], wold[:, :], t[:, :257], ALU.mult, ALU.add)
        k.cp(mst[:, :], mnew[:, :])

    def glu_tile(hg, seq, t0, T):
        k.dma(ytl[:, :, :T], yT_d[:, t0:t0 + T].rearrange("(kc p) t -> p kc t", p=128),
              reads=[("yT", f, t0) for f in range(0, 1024, 128)])
        k.dma(yzt[:T, :], yz_d[t0:t0 + T, 256 * hg:256 * hg + 256], reads=[("yz", hg, t0)])
        p = k.ps()
        for kc in range(8):
            k.mm(p[:T, :256], ytl[:, kc, :T], gluw[:, kc, :], start=(kc == 0), stop=(kc == 7))
        g = tmp(); k.tt(g[:T, :256], p[:T, :256], glub[:T, :], ALU.add)
        k.act(g[:T, :256], g[:T, :256], AF.Sigmoid)
        oc = tmp(); k.tt(oc[:T, :256], g[:T, :256], yzt[:T, :], ALU.mult)
        store_oT(oc, 256, 256 * hg, t0, T)

    for hg in range(4):
        load_w(w_in_cd, [(0, 2048 + 128 * hg, 128), (128, 2560 + 128 * hg, 128),
                         (256, 2560 + 128 * hg, 128), (384, 3072 + 256 * hg, 256), (640, 4096 + 256 * hg, 256),
                         (896, 5120 + 256 * hg, 256), (1152, 6144 + hg, 1), (1153, 6148 + hg, 1)])
        k.dma(gluw[:, :, :], gluw_d[:, 256 * hg:256 * hg + 256].rearrange("(kc p) c -> p kc c", p=128), eng="pool")
        k.dma(glub[:], glub_d[:, 256 * hg:256 * hg + 256]); k.dma(dngb[:], dng_d[:, 256 * hg:256 * hg + 256])
        for ti, (seq, t0, T, first, last) in enumerate(tiles):
            if first:
                if seq == 0:
                    k.memset(Cx[:], 0.0); k.memset(mst[:], 0.0)
                else:
                    k.dma(Cx[:, 0:256], si_mc[seq - 1, hg, :, :]); k.dma(Cx[:, 256:257], si_mn[seq - 1, :, hg:hg + 1])
                    k.dma(mst[:], si_mm[seq - 1, :, hg:hg + 1])
            load_hn(t0, T)
            projF(qT[:, :T], 0, T, scaled=False); projF(kT[:, :T], 128, T, scaled=False)
            projTok(256, 898, T, scaled=False)
            mlstm_tile(hg, seq, t0, T)
            glu_tile(hg, seq, t0, T)
            if last:
                k.dma(so_mc[seq, hg, :, :], Cx[:, 0:256]); k.dma(so_mn[seq, :, hg:hg + 1], Cx[:, 256:257])
                k.dma(so_mm[seq, :, hg:hg + 1], mst[:])

    k.close_scope()
    outproj_pass(w_out_cd, 1)
    k.finish()
    k.emit()
    return nc, k


def _consts():
    c = np.zeros((128, 9, 128), np.float32)
    idx = np.arange(128)
    kk, ii = idx[:, None], idx[None, :]
    c[:, 0] = (kk == ii); c[:, 1] = (kk <= ii); c[:, 2] = (kk < ii); c[:, 3] = -(kk > ii).astype(np.float32)
    c[:, 4] = 1.0; c[:, 5] = np.where(kk <= ii, 0.0, NEG); c[:, 6] = np.where(ii <= kk, 0.0, NEG)
    c[:, 7] = np.broadcast_to(idx[None, :], (128, 128))
    c[:, 8, 0] = idx; c[127, 8, 1] = 1.0; c[31, 8, 2] = 1.0
    return c


def _bc(v, n=128):
    v = np.asarray(v, np.float32).reshape(1, -1)
    return np.ascontiguousarray(np.broadcast_to(v, (n, v.shape[1])))


def _s5col(a):
    return np.ascontiguousarray(np.asarray(a, np.float32).reshape(32, 128).T)


def _shared_inputs(inp):
    f = lambda a: np.ascontiguousarray(np.asarray(a, np.float32))
    d = {}
    d["w_in_ab"] = f(inp["w_in_ab"]); d["w_out_ab"] = f(inp["w_out_ab"])
    d["w_in_cd"] = f(inp["w_in_cd"]); d["w_out_cd"] = f(inp["w_out_cd"]); d["glu_w"] = f(inp["c_glu_w"])
    d["consts"] = _consts()
    ng = f(inp["norm_g"])
    d["g0T"] = np.ascontiguousarray(ng[0].reshape(16, 128).T); d["g1b"] = _bc(ng[1]); d["gfb"] = _bc(inp["final_norm_g"])
    d["a_gate_w"] = f(inp["a_gate_w"]); d["a_gate_b"] = f(inp["a_gate_b"]).reshape(1, 512); d["a_norm_g_b"] = _bc(inp["a_norm_g"])
    d["conv_w"] = np.ascontiguousarray(f(inp["b_conv_w"]).reshape(4, 24, 128).transpose(2, 1, 0))
    d["a_log_b"] = _bc(inp["b_a_log"]); d["dt_bias_b"] = _bc(inp["b_dt_bias"]); d["b_norm_g_b"] = _bc(inp["b_norm_g"])
    lre, lim = f(inp["c_lam_re"]), f(inp["c_lam_im"])
    ldt = np.ascontiguousarray(np.broadcast_to(f(inp["c_log_dt"])[:, None], (64, 64)))
    d["s5col"] = np.ascontiguousarray(np.stack([_s5col(lre), _s5col(lim), _s5col(ldt)], axis=1))
    d["s5row"] = np.ascontiguousarray(np.stack([_bc(lre.reshape(-1)), _bc(lim.reshape(-1)), _bc(ldt.reshape(-1))], axis=1))
    bd_b = np.zeros((2, 8, 128, 512), np.float32)
    for ci, b in enumerate((f(inp["c_b_re"]), f(inp["c_b_im"]))):
        for H in range(8):
            for gl in range(8):
                bd_b[ci, H, gl * 16:(gl + 1) * 16, gl * 64:(gl + 1) * 64] = b[8 * H + gl].T
    d["bd_b"] = bd_b
    bd_c = np.zeros((2, 4, 128, 8, 128), np.float32)
    for ci, c in enumerate((f(inp["c_c_re"]), f(inp["c_c_im"]))):
        for hg in range(4):
            for ch in range(8):
                for g2 in range(2):
                    g = 16 * hg + 2 * ch + g2
                    col = (2 * (ch % 4) + g2) * 16
                    bd_c[ci, hg, g2 * 64:(g2 + 1) * 64, ch, col:col + 16] = c[g].T
    d["bd_c"] = bd_c
    bd_d = np.zeros((4, 128, 2, 128), np.float32)
    cd = f(inp["c_d"])
    for hg in range(4):
        for h in range(2):
            for gl in range(8):
                for i in range(16):
                    bd_d[hg, gl * 16 + i, h, gl * 16 + i] = cd[16 * hg + 8 * h + gl, i]
    d["bd_d"] = bd_d
    d["glu_b_b"] = _bc(inp["c_glu_b"]); d["d_i_b"] = _bc(inp["d_i_bias"]); d["d_f_b"] = _bc(inp["d_f_bias"])
    d["d_norm_g_b"] = _bc(inp["d_norm_g"])
    return d


_NC_CACHE = {}


def kernel(**inp):
    f = lambda a: np.ascontiguousarray(np.asarray(a, np.float32))
    xp = f(inp["x_prompt"]); xs = f(inp["x_sample"])
    NB, L, _ = xp.shape
    shared = _shared_inputs(inp)
    conv = f(inp["cache_gdn_conv"]); sgla = f(inp["state_gla"]); sgdn = f(inp["state_gdn"])
    s5re = f(inp["state_s5_re"]); s5im = f(inp["state_s5_im"]); smc = f(inp["state_mlstm_c"])
    smn = f(inp["state_mlstm_n"]); smm = f(inp["state_mlstm_m"])
    in_maps = []
    for c in range(8):
        m = dict(shared)
        sq = [2 * c, 2 * c + 1]
        xtok = np.concatenate([xp[c % NB]] + [xs[s] for s in sq], axis=0)
        m["xtok"] = np.ascontiguousarray(xtok); m["xT"] = np.ascontiguousarray(xtok.T)
        m["si_conv"] = np.ascontiguousarray(np.stack([conv[s].reshape(3, 24, 128).transpose(2, 1, 0) for s in sq]))
        m["si_gla"] = np.ascontiguousarray(sgla[sq]); m["si_gdn"] = np.ascontiguousarray(sgdn[sq])
        m["si_s5"] = np.ascontiguousarray(np.stack([np.stack([_s5col(s5re[s]), _s5col(s5im[s])], axis=1) for s in sq]))
        m["si_mc"] = np.ascontiguousarray(smc[sq])
        m["si_mn"] = np.ascontiguousarray(np.stack([smn[s].T for s in sq]))
        m["si_mm"] = np.ascontiguousarray(np.stack([_bc(smm[s]) for s in sq]))
        in_maps.append(m)
    if L not in _NC_CACHE:
        _NC_CACHE[L] = build(L)[0]
    nc = _NC_CACHE[L]
    res = run_bass_kernel_spmd(nc, in_maps, core_ids=list(range(8)))
    R = res.results
    NS = xs.shape[0]
    y_p = np.stack([R[b]["y"][:L] for b in range(NB)])
    y_s = np.stack([R[s // 2]["y"][L + 32 * (s % 2):L + 32 * (s % 2) + 32] for s in range(NS)])

    def gather(fn):
        p = np.stack([fn(R[b], 0) for b in range(NB)])
        s = np.stack([fn(R[s // 2], 1 + s % 2) for s in range(NS)])
        return p, s

    def s5o(r, i, ci):
        return np.ascontiguousarray(r["so_s5"][i][:, ci, :].T).reshape(64, 64)

    pc, sc = gather(lambda r, i: np.ascontiguousarray(r["so_conv"][i].transpose(2, 1, 0)).reshape(3, 3072))
    pg, sg = gather(lambda r, i: r["so_gla"][i]); pd, sd = gather(lambda r, i: r["so_gdn"][i])
    pr, sr = gather(lambda r, i: s5o(r, i, 0)); pi, si = gather(lambda r, i: s5o(r, i, 1))
    pmc, smc_ = gather(lambda r, i: r["so_mc"][i]); pmn, smn_ = gather(lambda r, i: np.ascontiguousarray(r["so_mn"][i].T))
    pmm, smm_ = gather(lambda r, i: np.ascontiguousarray(r["so_mm"][i][0, :]))
    outs = (y_p, y_s, pc, pg, pd, pr, pi, pmc, pmn, pmm, sc, sg, sd, sr, si, smc_, smn_, smm_)
    return tuple(np.ascontiguousarray(o, dtype=np.float32) for o in outs)
```

```python
import numpy as np
import concourse.bass as bass
import concourse.mybir as mybir
from contextlib import ExitStack
from concourse.bass_utils import run_bass_kernel_spmd

F32 = mybir.dt.float32
BF16 = mybir.dt.bfloat16
I32 = mybir.dt.int32
AF = mybir.ActivationFunctionType
ALU = mybir.AluOpType
AX = mybir.AxisListType

SEM_LIMIT = 24000


class KB:
    ENGS = ("pe", "act", "dve", "pool", "sp")

    def __init__(self, nc):
        self.nc = nc
        self.stack = ExitStack()
        self.ops = {e: [] for e in self.ENGS}
        self.cur_sem = {}
        self.cur_cnt = {}
        for e in self.ENGS:
            self.cur_sem[e] = None
            self.cur_cnt[e] = 0
        self.waited = {e: {} for e in self.ENGS}
        self.writers = {}
        self.readers = {}
        self.dma_pool = []
        self.dma_rr = 0
        self.n_dma_sems = 14
        self.sem_objs = {}
        self.nuid = 0
        self.psum_banks = []
        self.ps_rr = 0
        self.n_instr = 0
        self.scopes = []

    def _alloc_sem(self, name):
        s = self.nc.alloc_semaphore(name=name)
        self.nuid += 1
        sid = self.nuid
        self.sem_objs[sid] = s
        return sid

    def sb(self, name, shape, dtype=F32):
        st = self.scopes[-1] if self.scopes else self.stack
        self.nuid += 1
        t = st.enter_context(self.nc.sbuf_tensor(f"sb_{name}_{self.nuid}", list(shape), dtype))
        return t

    def barrier(self):
        for e in self.ENGS:
            waits = {}
            for o in self.ENGS:
                if o != e and self.cur_sem[o] is not None and self.waited[e].get(self.cur_sem[o], 0) < self.cur_cnt[o]:
                    waits[self.cur_sem[o]] = self.cur_cnt[o]
            for sid, val in self.dma_pool:
                if val > 0 and self.waited[e].get(sid, 0) < val:
                    waits[sid] = val
            self._emit_waits(e, waits)

    def open_scope(self):
        self.scopes.append(ExitStack())

    def close_scope(self):
        self.barrier()
        self.emit_block()
        self.scopes.pop().close()

    def psum_init(self):
        for i in range(8):
            t = self.stack.enter_context(self.nc.psum_tensor(f"psb{i}", [128, 512], F32))
            self.psum_banks.append(t)

    def ps(self):
        t = self.psum_banks[self.ps_rr % 8]
        self.ps_rr += 1
        return t

    @staticmethod
    def key(x):
        if isinstance(x, (str, tuple)):
            return x
        return x.tensor.name

    def _deps(self, eng, reads, writes, is_dma):
        deps = []
        for r in reads:
            for src, t in self.writers.get(r, {}).items():
                deps.append((src, t, "raw"))
        for w in writes:
            for src, t in self.writers.get(w, {}).items():
                deps.append((src, t, "waw"))
            for src, t in self.readers.get(w, {}).items():
                deps.append((src, t, "war"))
        out = {}
        for src, (sid, val), kind in deps:
            if src == eng and not is_dma and not str(src).startswith("dma"):
                if eng == "pe" or kind != "raw":
                    continue
            if self.waited[eng].get(sid, 0) >= val:
                continue
            if out.get(sid, 0) < val:
                out[sid] = val
        return out

    def _emit_waits(self, eng, waits):
        for sid, val in waits.items():
            s = self.sem_objs[sid]
            self.ops[eng].append(lambda h, s=s, val=val: h.wait_ge(s, val))
            self.waited[eng][sid] = val
            self.n_instr += 1

    def _record(self, src, ticket, reads, writes):
        for r in reads:
            self.readers.setdefault(r, {})[src] = ticket
        for w in writes:
            self.writers[w] = {src: ticket}
            self.readers[w] = {}

    def op(self, eng, fn, reads, writes):
        reads = [self.key(r) for r in reads if r is not None and not isinstance(r, (int, float))]
        writes = [self.key(w) for w in writes]
        waits = self._deps(eng, reads, writes, False)
        self._emit_waits(eng, waits)
        if self.cur_sem[eng] is None or self.cur_cnt[eng] >= SEM_LIMIT:
            self.cur_sem[eng] = self._alloc_sem(f"s_{eng}_{self.nuid}")
            self.cur_cnt[eng] = 0
        self.cur_cnt[eng] += 1
        sid, val = self.cur_sem[eng], self.cur_cnt[eng]
        s = self.sem_objs[sid]
        self.ops[eng].append(lambda h, s=s: fn(h).then_inc(s, 1))
        self.n_instr += 1
        self._record(eng, (sid, val), reads, writes)

    def dma(self, out, in_, reads=None, writes=None, eng="sp", **kw):
        reads = [self.key(r) for r in (reads if reads is not None else [in_])]
        writes = [self.key(w) for w in (writes if writes is not None else [out])]
        waits = self._deps(eng, reads, writes, True)
        if len(self.dma_pool) < self.n_dma_sems:
            self.dma_pool.append([self._alloc_sem(f"s_dma_{self.nuid}"), 0])
        slot = self.dma_pool[self.dma_rr % self.n_dma_sems]
        self.dma_rr += 1
        if slot[1] + 16 > SEM_LIMIT:
            if self.waited[eng].get(slot[0], 0) < slot[1]:
                waits[slot[0]] = max(waits.get(slot[0], 0), slot[1])
            self._emit_waits(eng, waits)
            waits = {}
            slot[0] = self._alloc_sem(f"s_dma_{self.nuid}")
            slot[1] = 0
        sid = slot[0]
        if slot[1] > 0 and self.waited[eng].get(sid, 0) < slot[1]:
            waits[sid] = max(waits.get(sid, 0), slot[1])
        self._emit_waits(eng, waits)
        slot[1] += 16
        val = slot[1]
        s = self.sem_objs[sid]
        self.ops[eng].append(lambda h, s=s: h.dma_start(out=out, in_=in_, allow_slow_non_contiguous=True, **kw).then_inc(s, 16))
        self.n_instr += 1
        self._record(f"dma{sid}", (sid, val), reads, writes)
        return (sid, val)

    def wait_all(self, eng, keys):
        waits = {}
        for k in keys:
            k = self.key(k)
            for src, (sid, val) in self.writers.get(k, {}).items():
                if self.waited[eng].get(sid, 0) < val:
                    waits[sid] = max(waits.get(sid, 0), val)
        self._emit_waits(eng, waits)

    def mm(self, out, lhsT, rhs, start=True, stop=True, extra_reads=()):
        self.op("pe", lambda h: h.matmul(out, lhsT, rhs, start=start, stop=stop),
                [lhsT, rhs] + list(extra_reads) + ([] if start else [out]), [out])

    def tr(self, out, in_, ident):
        self.op("pe", lambda h: h.transpose(out, in_, ident), [in_, ident], [out])

    def act(self, out, in_, func, bias=0.0, scale=1.0, eng="act"):
        rd = [in_]
        if not isinstance(bias, (int, float)):
            rd.append(bias)
        if not isinstance(scale, (int, float)):
            rd.append(scale)
        self.op(eng, lambda h: h.activation(out=out, in_=in_, func=func, bias=bias, scale=scale), rd, [out])

    def tt(self, out, a, b, op, eng="dve"):
        self.op(eng, lambda h: h.tensor_tensor(out=out, in0=a, in1=b, op=op), [a, b], [out])

    def ts(self, out, a, s1, op0, s2=None, op1=None, eng="dve"):
        rd = [a] + [s for s in (s1, s2) if s is not None and not isinstance(s, (int, float))]
        if op1 is None:
            self.op(eng, lambda h: h.tensor_scalar(out=out, in0=a, scalar1=s1, scalar2=None, op0=op0), rd, [out])
        else:
            self.op(eng, lambda h: h.tensor_scalar(out=out, in0=a, scalar1=s1, scalar2=s2, op0=op0, op1=op1), rd, [out])

    def stt(self, out, a, s, b, op0, op1, eng="dve"):
        rd = [a, b] + ([] if isinstance(s, (int, float)) else [s])
        self.op(eng, lambda h: h.scalar_tensor_tensor(out=out, in0=a, scalar=s, in1=b, op0=op0, op1=op1), rd, [out])

    def cp(self, out, in_, eng="dve"):
        if eng == "act":
            self.op("act", lambda h: h.copy(out=out, in_=in_), [in_], [out])
        else:
            self.op(eng, lambda h: h.tensor_copy(out=out, in_=in_), [in_], [out])

    def red(self, out, in_, op=None, eng="dve"):
        op = op or ALU.add
        self.op(eng, lambda h: h.tensor_reduce(out=out, in_=in_, axis=AX.X, op=op), [in_], [out])

    def recip(self, out, in_):
        self.op("dve", lambda h: h.reciprocal(out=out, in_=in_), [in_], [out])

    def memset(self, ap, val, eng="dve"):
        self.op(eng, lambda h: h.memset(ap, val), [], [ap])

    def rsqrt(self, out, in_, scale, eps):
        self.act(out, in_, AF.Sqrt, bias=eps, scale=scale)
        self.recip(out, out)

    def finish(self):
        waits = {}
        for sid, val in self.dma_pool:
            if val > 0 and self.waited["sp"].get(sid, 0) < val:
                waits[sid] = val
        self._emit_waits("sp", waits)

    def emit(self):
        self.emit_block()
        self.stack.close()

    def emit_block(self):
        nc = self.nc
        ops = self.ops
        self.ops = {e: [] for e in self.ENGS}
        with nc.Block() as block:
            @block.tensor
            def _(h):
                for f in ops["pe"]:
                    f(h)

            @block.scalar
            def _(h):
                for f in ops["act"]:
                    f(h)

            @block.vector
            def _(h):
                for f in ops["dve"]:
                    f(h)

            @block.gpsimd
            def _(h):
                for f in ops["pool"]:
                    f(h)

            @block.sync
            def _(h):
                for f in ops["sp"]:
                    f(h)


EPS = 1e-6
D = 2048
NEG = -30000.0


def run_threads(threads):
    alive = [t for t in threads if t is not None]
    while alive:
        for g in list(alive):
            try:
                next(g)
            except StopIteration:
                alive.remove(g)


def build(L):
    nc = bass.Bass("TRN2", target_bir_lowering=False)
    NTOK = L + 64
    tiles = [(0, i * 128, 128, i == 0, i == L // 128 - 1) for i in range(L // 128)]
    tiles += [(1, L, 32, True, True), (2, L + 32, 32, True, True)]

    def din(name, shape, dt=F32):
        return nc.dram_tensor(name, list(shape), dt, kind="ExternalInput").ap()

    def dout(name, shape, dt=F32):
        return nc.dram_tensor(name, list(shape), dt, kind="ExternalOutput").ap()

    xT_d = din("xT", [D, NTOK]); xtok_d = din("xtok", [NTOK, D])
    w_in_ab = din("w_in_ab", [D, 7200]); w_out_ab = din("w_out_ab", [D, D])
    w_in_cd = din("w_in_cd", [D, 6152]); w_out_cd = din("w_out_cd", [D, D])
    gluw_d = din("glu_w", [1024, 1024])
    consts_d = din("consts", [128, 9, 128])
    g0T_d = din("g0T", [128, 16]); g1b_d = din("g1b", [128, D]); gfb_d = din("gfb", [128, D])
    agw_d = din("a_gate_w", [16, 512]); agb_d = din("a_gate_b", [1, 512]); ang_d = din("a_norm_g_b", [128, 1024])
    cw_d = din("conv_w", [128, 24, 4]); alog_d = din("a_log_b", [128, 8]); dtb_d = din("dt_bias_b", [128, 8])
    bng_d = din("b_norm_g_b", [128, 128])
    s5col_d = din("s5col", [128, 3, 32]); s5row_d = din("s5row", [128, 3, 4096])
    bdb_d = din("bd_b", [2, 8, 128, 512]); bdc_d = din("bd_c", [2, 4, 128, 8, 128]); bdd_d = din("bd_d", [4, 128, 2, 128])
    glub_d = din("glu_b_b", [128, 1024]); dib_d = din("d_i_b", [128, 4]); dfb_d = din("d_f_b", [128, 4])
    dng_d = din("d_norm_g_b", [128, 1024])
    si_conv = din("si_conv", [2, 128, 24, 3]); si_gla = din("si_gla", [2, 4, 128, 256]); si_gdn = din("si_gdn", [2, 8, 128, 128])
    si_s5 = din("si_s5", [2, 128, 2, 32]); si_mc = din("si_mc", [2, 4, 128, 256]); si_mn = din("si_mn", [2, 128, 4])
    si_mm = din("si_mm", [2, 128, 4])

    y_d = dout("y", [NTOK, D])
    so_conv = dout("so_conv", [3, 128, 24, 3]); so_gla = dout("so_gla", [3, 4, 128, 256]); so_gdn = dout("so_gdn", [3, 8, 128, 128])
    so_s5 = dout("so_s5", [3, 128, 2, 32]); so_mc = dout("so_mc", [3, 4, 128, 256]); so_mn = dout("so_mn", [3, 128, 4])
    so_mm = dout("so_mm", [3, 128, 4])

    oT_d = nc.dram_tensor("oT_s", [D, NTOK], BF16).ap()
    h1_d = nc.dram_tensor("h1_s", [NTOK, D], F32).ap()
    hnT_d = nc.dram_tensor("hnT_s", [D, NTOK], BF16).ap()
    yT_d = nc.dram_tensor("yT_s", [1024, NTOK], BF16).ap()
    yz_d = nc.dram_tensor("yz_s", [NTOK, 1024], F32).ap()

    k = KB(nc)
    k.psum_init()
    sb = k.sb
    cst = sb("cst", [128, 9, 128])
    k.dma(cst[:], consts_d[:, :, :])
    ident = cst[:, 0, :]; U = cst[:, 1, :]; SU = cst[:, 2, :]; NSL = cst[:, 3, :]; ones = cst[:, 4, :]
    NMU = cst[:, 5, :]; NML = cst[:, 6, :]; IOTA = cst[:, 7, :]; SELS = cst[:, 8, :]
    tcol = cst[:, 8, 0:1]
    sel128 = sb("sel128", [128, 128]); sel32 = sb("sel32", [128, 128])
    k.ts(sel128[:], ones, cst[:, 8, 1:2], ALU.mult)
    k.ts(sel32[:], ones, cst[:, 8, 2:3], ALU.mult)
    epsb = sb("epsb", [128, 1]); k.memset(epsb[:], EPS)
    wbuf = sb("wbuf", [128, 16, 2048], BF16)
    xg = sb("xg", [128, 16, 128], BF16)
    projT = sb("projT", [128, 1160])
    tbuf = [sb(f"tb{i}", [128, 512]) for i in range(12)]
    cols = sb("cols", [128, 64])
    obf = [sb(f"obf{i}", [128, 128], BF16) for i in range(4)]
    obc = [0]
    qT = sb("qT", [128, 128]); kT = sb("kT", [128, 128])
    k.open_scope()
    xt2 = [sb(f"xt{i}", [128, 16, 128]) for i in range(2)]
    xsq = sb("xsq", [128, 16, 128]); xs1 = sb("xs1", [128, 128])
    rbc = sb("rbc", [128, 128]); rcol = sb("rcol", [128, 1])
    g0T = sb("g0T", [128, 16]); k.dma(g0T[:], g0T_d[:, :])
    tctr = [0]

    def tmp():
        t = tbuf[tctr[0] % len(tbuf)]
        tctr[0] += 1
        return t

    def rs_eps(out, in_, scale):
        k.act(out, in_, AF.Ln, bias=epsb[:out.shape[0], :], scale=scale)
        k.act(out, out, AF.Exp, scale=-0.5)

    def load_w(src, pairs):
        for d0, s0, n in pairs:
            k.dma(wbuf[:, :, d0:d0 + n], src[:, s0:s0 + n].rearrange("(kc p) c -> p kc c", p=128), eng="pool")

    def norm_x(ti, t0, T):
        xt = xt2[ti % 2]
        k.dma(xt[:, :, :T], xT_d[:, t0:t0 + T].rearrange("(kc p) t -> p kc t", p=128))
        k.tt(xsq[:, :, :T], xt[:, :, :T], xt[:, :, :T], ALU.mult, eng="pool")
        k.red(xs1[:, :T], xsq[:, :, :T].rearrange("p kc t -> p t kc"))
        p = k.ps(); k.mm(p[:, :T], ones, xs1[:, :T]); rs_eps(rbc[:, :T], p[:, :T], 1.0 / D)
        p = k.ps(); k.mm(p[:T, 0:1], xs1[:, :T], ones[:, 0:1]); rs_eps(rcol[:T, :], p[:T, 0:1], 1.0 / D)
        k.tt(xg[:, :, :T], xt[:, :, :T], g0T[:, :].unsqueeze(2).to_broadcast([128, 16, T]), ALU.mult)

    def load_hn(t0, T):
        k.dma(xg[:, :, :T], hnT_d[:, t0:t0 + T].rearrange("(kc p) t -> p kc t", p=128), reads=[("hnT", t0)])

    def projF(dst, col0, T, ncols=128, scaled=True):
        p = k.ps()
        for kc in range(16):
            k.mm(p[:ncols, :T], wbuf[:, kc, col0:col0 + ncols], xg[:, kc, :T], start=(kc == 0), stop=(kc == 15))
        if scaled:
            k.tt(dst, p[:ncols, :T], rbc[:ncols, :T], ALU.mult)
        else:
            k.cp(dst, p[:ncols, :T], eng="act")

    def projTok(col0, ncols, T, scaled=True, dst0=0):
        for c0 in range(0, ncols, 512):
            n = min(512, ncols - c0)
            p = k.ps()
            for kc in range(16):
                k.mm(p[:T, :n], xg[:, kc, :T], wbuf[:, kc, col0 + c0:col0 + c0 + n], start=(kc == 0), stop=(kc == 15))
            if scaled:
                k.ts(projT[:T, dst0 + c0:dst0 + c0 + n], p[:T, :n], rcol[:T, :], ALU.mult)
            else:
                k.cp(projT[:T, dst0 + c0:dst0 + c0 + n], p[:T, :n], eng="act")

    def store_oT(src, ncols, frow, t0, T):
        for c0 in range(0, ncols, 128):
            p = k.ps(); k.tr(p[:, :T], src[:T, c0:c0 + 128], ident[:T, :T])
            ob = obf[obc[0] % 4]; obc[0] += 1
            k.cp(ob[:, :T], p[:, :T], eng="act")
            k.dma(oT_d[frow + c0:frow + c0 + 128, t0:t0 + T], ob[:, :T], writes=[("oT", frow + c0, t0)])

    def headnorm_rms(o_sb, T, n, gsz, dst):
        sq = tmp(); k.tt(sq[:T, :n], o_sb, o_sb, ALU.mult, eng="pool")
        c = cols[:, 60:61]; k.red(c[:T, :], sq[:T, :n])
        r = cols[:, 61:62]; rs_eps(r[:T, :], c[:T, :], 1.0 / n)
        k.stt(dst, o_sb, r[:T, :], gsz, ALU.mult, ALU.mult)

    S_gla = sb("S_gla", [128, 256]); S_gdn = [sb(f"S_gdn{i}", [128, 128]) for i in range(2)]
    cbuf = sb("cbuf", [128, 6, 131]); cw = sb("cw", [128, 6, 4])
    gw = sb("gw", [16, 128]); gb = sb("gb", [1, 128]); angb = sb("angb", [128, 256]); bngb = sb("bngb", [128, 128])
    alog = sb("alog", [128, 8]); dtb = sb("dtb", [128, 8]); na = sb("na", [128, 8])
    k.dma(alog[:], alog_d[:, :]); k.dma(dtb[:], dtb_d[:, :]); k.dma(bngb[:], bng_d[:, :])
    k.act(na[:], alog[:], AF.Exp); k.ts(na[:], na[:], -1.0, ALU.mult)
    gaT = sb("gaT", [16, 128])
    cvs = sb("cvs", [128, 6, 128]); cacc = sb("cacc", [128, 6, 128]); ctmp = sb("ctmp", [128, 6, 128])
    Rp = [sb(f"Rp{i}", [128, 128]) for i in range(7)]
    Qp = [sb(f"Qp{i}", [128, 128]) for i in range(2)]
    g_vk = sb("g_vk", [128, 256]); g_egrow = sb("g_egrow", [128, 128]); g_AT = sb("g_AT", [128, 128])
    g_X = [sb(f"g_X{i}", [128, 256]) for i in range(2)]
    SC = 128 ** -0.5

    def gla_tile(hg, seq, t0, T):
        kk = projT[:T, 0:128]; v = projT[:T, 128:384]; za = projT[:T, 384:640]
        p = k.ps()
        k.mm(p[:T, :128], gaT[:, :T], gw[:, :], start=True, stop=False)
        k.mm(p[:T, :128], ones[0:1, :T], gb[:, :], start=False, stop=True)
        e = tmp(); k.act(e[:T, :128], p[:T, :128], AF.Exp, scale=-1.0)
        yield
        spl = tmp(); k.act(spl[:T, :128], e[:T, :128], AF.Ln, bias=1.0)
        pc = k.ps(); k.mm(pc[:, :T], spl[:T, :128], U[:T, :T])
        ebT = tmp(); k.act(ebT[:, :T], pc[:, :T], AF.Exp, scale=-1.0 / 16)
        enbT = tmp(); k.act(enbT[:, :T], pc[:, :T], AF.Exp, scale=1.0 / 16)
        yield
        qd = tmp(); k.stt(qd[:, :T], qT[:, :T], SC, ebT[:, :T], ALU.mult, ALU.mult)
        kd = tmp(); k.tt(kd[:, :T], kT[:, :T], enbT[:, :T], ALU.mult, eng="pool")
        pd = k.ps(); k.mm(pd[:T, :128], NSL[:T, :T], spl[:T, :128])
        ekw = tmp(); k.act(ekw[:T, :128], pd[:T, :128], AF.Exp, scale=1.0 / 16)
        yield
        kw = tmp(); k.tt(kw[:T, :128], kk, ekw[:T, :128], ALU.mult, eng="pool")
        psc = k.ps(); k.mm(psc[:T, :T], kd[:, :T], qd[:, :T])
        PT = tmp(); k.tt(PT[:T, :T], psc[:T, :T], U[:T, :T], ALU.mult)
        yield
        po = k.ps()
        k.mm(po[:T, :256], PT[:T, :T], v, start=True, stop=False)
        k.mm(po[:T, :256], qd[:, :T], S_gla[:, :], start=False, stop=True)
        o_sb = tmp(); k.cp(o_sb[:T, :256], po[:T, :256], eng="act")
        pkv = k.ps(); k.mm(pkv[:, :256], kw[:T, :128], v)
        k.stt(S_gla[:, :], S_gla[:, :], ebT[:, T - 1:T], pkv[:, :256], ALU.mult, ALU.add)
        yield
        gsz = tmp(); k.act(gsz[:T, :256], za, AF.Silu)
        k.tt(gsz[:T, :256], gsz[:T, :256], angb[:T, :], ALU.mult, eng="pool")
        og = tmp(); headnorm_rms(o_sb[:T, :256], T, 256, gsz[:T, :256], og[:T, :256])
        yield
        store_oT(og, 256, hg * 256, t0, T)
        yield

    def gdn_pre(T):
        for j in range(4):
            wj = cw[:, :, j:j + 1].to_broadcast([128, 6, T])
            if j == 0:
                k.tt(cacc[:, :, :T], cbuf[:, :, 0:T], wj, ALU.mult)
                yield
            else:
                k.tt(ctmp[:, :, :T], cbuf[:, :, j:j + T], wj, ALU.mult, eng="pool")
                k.tt(cacc[:, :, :T], cacc[:, :, :T], ctmp[:, :, :T], ALU.add)
                yield
        k.act(cvs[:, :, :T], cacc[:, :, :T], AF.Silu)
        yield
        k.tt(ctmp[:, 0:4, :T], cvs[:, 0:4, :T], cvs[:, 0:4, :T], ALU.mult, eng="pool")
        yield
        p = k.ps()
        for b in range(4):
            k.mm(p[:, b * T:(b + 1) * T], ones, ctmp[:, b, :T])
        rs_eps(cacc[:, 0:4, :T], p[:, :4 * T].rearrange("p (a b) -> p a b", a=4), 1.0)
        yield
        k.tt(cvs[:, 0:4, :T], cvs[:, 0:4, :T], cacc[:, 0:4, :T], ALU.mult)
        yield

    def gdn_head(hg, hh, seq, t0, T):
        Sg = S_gdn[hh]
        qTh = cvs[:, 0 + hh, :T]; kTh = cvs[:, 2 + hh, :T]; vTh = cvs[:, 4 + hh, :T]
        zb = projT[:T, 640 + 128 * hh:768 + 128 * hh]
        beta_pre = projT[:T, 896 + hh:897 + hh]; a_pre = projT[:T, 898 + hh:899 + hh]
        gh = 2 * hg + hh
        c = lambda i: cols[:, i:i + 1]
        beta = c(0); k.act(beta[:T, :], beta_pre, AF.Sigmoid)
        e = c(1); k.act(e[:T, :], a_pre, AF.Exp, bias=dtb[:T, gh:gh + 1])
        spg = c(2); k.act(spg[:T, :], e[:T, :], AF.Ln, bias=1.0)
        graw = c(3); k.tt(graw[:T, :], spg[:T, :], na[:T, gh:gh + 1], ALU.mult)
        vk = g_vk
        p = k.ps(); k.tr(p[:T, 0:128], vTh, ident); k.tr(p[:T, 128:256], kTh, ident)
        k.cp(vk[:T, :256], p[:T, :256], eng="act")
        p = k.ps(); k.mm(p[:T, 0:1], U[:T, :T], graw[:T, :])
        gc = c(4); k.cp(gc[:T, :], p[:T, 0:1]); ngc = c(5); k.ts(ngc[:T, :], p[:T, 0:1], -1.0, ALU.mult)
        Ug = tmp(); k.ts(Ug[:T, :T], U[:T, :T], graw[:T, :], ALU.mult)
        pg = k.ps(); k.mm(pg[:, :T], ones[:T, :], Ug[:T, :T])
        Ib = tmp(); k.ts(Ib[:T, :T], ident[:T, :T], beta[:T, :], ALU.mult)
        pb = k.ps(); k.mm(pb[:T, :T], ones[:T, :T], Ib[:T, :T])
        arg = tmp(); k.tt(arg[:T, :T], pg[:T, :T], NMU[:T, :T], ALU.add)
        decT = tmp(); k.act(decT[:T, :T], arg[:T, :T], AF.Exp, bias=ngc[:T, :])
        egrow = g_egrow; k.act(egrow[:, :T], pg[:, :T], AF.Exp)
        glast = c(6); k.cp(glast[:, :], pg[:, T - 1:T])
        eglast = c(7); k.cp(eglast[:, :], egrow[:, T - 1:T], eng="pool")
        egc = c(8); k.act(egc[:T, :], gc[:T, :], AF.Exp)
        ekl = c(9); k.act(ekl[:T, :], gc[:T, :], AF.Exp, scale=-1.0, bias=glast[:T, :])
        pkk = k.ps(); k.mm(pkk[:T, :T], kTh, kTh)
        pqk = k.ps(); k.mm(pqk[:T, :T], kTh, qTh)
        AT = g_AT; k.stt(AT[:T, :T], pqk[:T, :T], SC, decT[:T, :T], ALU.mult, ALU.mult)
        t1 = tmp(); k.tt(t1[:T, :T], pkk[:T, :T], decT[:T, :T], ALU.mult)
        t2 = tmp(); k.tt(t2[:T, :T], pb[:T, :T], SU[:T, :T], ALU.mult)
        LT = Rp[0]; k.tt(LT[:T, :T], t1[:T, :T], t2[:T, :T], ALU.mult, eng="pool")
        X = g_X[0]; xi_ = 0
        k.ts(X[:T, 0:128], vk[:T, 0:128], beta[:T, :], ALU.mult)
        k.ts(X[:T, 128:256], vk[:T, 128:256], beta[:T, :], ALU.mult, egc[:T, :], ALU.mult)
        p = k.ps(); k.tr(p[:T, :T], LT[:T, :T], ident[:T, :T]); k.cp(Qp[0][:T, :T], p[:T, :T], eng="act")
        p = k.ps(); k.mm(p[:T, :256], LT[:T, :T], X[:T, :256])
        xi_ ^= 1; Xn = g_X[xi_]; k.tt(Xn[:T, :256], X[:T, :256], p[:T, :256], ALU.subtract); X = Xn
        nsq = {128: 6, 32: 4}[T]
        for n in range(nsq):
            Q = Qp[n % 2]; R = Rp[n]
            p2 = k.ps(); k.mm(p2[:T, :T], Q[:T, :T], R[:T, :T]); k.cp(Rp[n + 1][:T, :T], p2[:T, :T], eng="act")
            if n < nsq - 1:
                p1 = k.ps(); k.mm(p1[:T, :T], R[:T, :T], Q[:T, :T]); k.cp(Qp[(n + 1) % 2][:T, :T], p1[:T, :T])
            p = k.ps(); k.mm(p[:T, :256], Rp[n + 1][:T, :T], X[:T, :256])
            xi_ ^= 1; Xn = g_X[xi_]; k.tt(Xn[:T, :256], X[:T, :256], p[:T, :256], ALU.add); X = Xn
        p = k.ps(); k.tr(p[:, :T], X[:T, 128:256], ident[:T, :T])
        wT = tmp(); k.cp(wT[:, :T], p[:, :T], eng="act")
        p = k.ps(); k.mm(p[:T, :128], wT[:, :T], Sg[:, :])
        vnew = tmp(); k.tt(vnew[:T, :128], X[:T, 0:128], p[:T, :128], ALU.subtract)
        qg = tmp(); k.stt(qg[:, :T], qTh, SC, egrow[:, :T], ALU.mult, ALU.mult)
        po = k.ps()
        k.mm(po[:T, :128], qg[:, :T], Sg[:, :], start=True, stop=False)
        k.mm(po[:T, :128], AT[:T, :T], vnew[:T, :128], start=False, stop=True)
        o_sb = tmp(); k.cp(o_sb[:T, :128], po[:T, :128], eng="act")
        kg = tmp(); k.ts(kg[:T, :128], vk[:T, 128:256], ekl[:T, :], ALU.mult)
        pkv = k.ps(); k.mm(pkv[:, :128], kg[:T, :128], vnew[:T, :128])
        k.stt(Sg[:, :], Sg[:, :], eglast[:, :], pkv[:, :128], ALU.mult, ALU.add)
        gsz = tmp(); k.act(gsz[:T, :128], zb, AF.Silu)
        k.tt(gsz[:T, :128], gsz[:T, :128], bngb[:T, :], ALU.mult, eng="pool")
        og = tmp(); headnorm_rms(o_sb[:T, :128], T, 128, gsz[:T, :128], og[:T, :128])
        store_oT(og, 128, 1024 + gh * 128, t0, T)

    for hg in range(4):
        TB = 1040
        load_w(w_in_ab, [(0, 128 * hg, 128), (128, 512 + 128 * hg, 128),
                         (256, 3088 + 256 * hg, 256), (512, 4112 + 256 * hg, 256), (768, 5136 + 256 * hg, 256),
                         (1024, 3072, 16),
                         (TB, 512 + 128 * hg, 128), (TB + 128, 1024 + 256 * hg, 256), (TB + 384, 2048 + 256 * hg, 256),
                         (TB + 640, 6160 + 256 * hg, 256), (TB + 896, 7184 + 2 * hg, 2), (TB + 898, 7192 + 2 * hg, 2)])
        k.dma(gw[:], agw_d[:, 128 * hg:128 * hg + 128]); k.dma(gb[:], agb_d[:, 128 * hg:128 * hg + 128])
        k.dma(angb[:], ang_d[:, 256 * hg:256 * hg + 256])
        for typ in range(3):
            k.dma(cw[:, 2 * typ:2 * typ + 2, :], cw_d[:, typ * 8 + 2 * hg:typ * 8 + 2 * hg + 2, :])
        for ti, (seq, t0, T, first, last) in enumerate(tiles):
            if first:
                if seq == 0:
                    k.memset(S_gla[:], 0.0); k.memset(S_gdn[0][:], 0.0); k.memset(S_gdn[1][:], 0.0)
                    k.memset(cbuf[:, :, 0:3], 0.0)
                else:
                    k.dma(S_gla[:], si_gla[seq - 1, hg, :, :])
                    for hh in range(2):
                        k.dma(S_gdn[hh][:], si_gdn[seq - 1, 2 * hg + hh, :, :])
                    for typ in range(3):
                        k.dma(cbuf[:, 2 * typ:2 * typ + 2, 0:3], si_conv[seq - 1, :, typ * 8 + 2 * hg:typ * 8 + 2 * hg + 2, :])
            else:
                k.cp(cbuf[:, :, 0:3], cbuf[:, :, 128:131])
            norm_x(ti, t0, T)
            projF(qT[:, :T], 0, T); projF(kT[:, :T], 128, T)
            for b in range(6):
                projF(cbuf[:, b, 3:3 + T], 256 + 128 * b, T)
            projF(gaT[:, :T], 1024, T, ncols=16)
            projTok(TB, 900, T)
            run_threads([gla_tile(hg, seq, t0, T), gdn_pre(T)])
            for hh in range(2):
                gdn_head(hg, hh, seq, t0, T)
            if last:
                k.dma(so_gla[seq, hg, :, :], S_gla[:])
                for hh in range(2):
                    k.dma(so_gdn[seq, 2 * hg + hh, :, :], S_gdn[hh][:])
                for typ in range(3):
                    k.dma(so_conv[seq, :, typ * 8 + 2 * hg:typ * 8 + 2 * hg + 2, :], cbuf[:, 2 * typ:2 * typ + 2, T:T + 3])

    k.close_scope()
    def outproj_alloc():
        return (sb("ot", [128, 16, 128], BF16), sb("resid", [128, D]), sb("hbuf", [128, D]),
                sb("gnb", [128, D]), sb("hsq", [128, D]), sb("hnT_sb", [128, 16, 128], BF16))

    def outproj_pass(w_out, layer):
        k.open_scope()
        ot, resid, hbuf, gnb, hsq, hnT_sb = outproj_alloc()
        load_w(w_out, [(0, 0, 2048)])
        k.dma(gnb[:], (g1b_d if layer == 0 else gfb_d)[:, :])
        for ti, (seq, t0, T, first, last) in enumerate(tiles):
            k.dma(ot[:, :, :T], oT_d[:, t0:t0 + T].rearrange("(kc p) t -> p kc t", p=128),
                  reads=[("oT", f, t0) for f in range(0, D, 128)])
            if layer == 0:
                k.dma(resid[:T, :], xtok_d[t0:t0 + T, :])
            else:
                k.dma(resid[:T, :], h1_d[t0:t0 + T, :], reads=[("h1", t0)])
            for nb in range(4):
                p = k.ps()
                for kc in range(16):
                    k.mm(p[:T, :512], ot[:, kc, :T], wbuf[:, kc, nb * 512:(nb + 1) * 512], start=(kc == 0), stop=(kc == 15))
                k.tt(hbuf[:T, nb * 512:(nb + 1) * 512], p[:T, :512], resid[:T, nb * 512:(nb + 1) * 512], ALU.add)
            if layer == 0:
                k.dma(h1_d[t0:t0 + T, :], hbuf[:T, :], writes=[("h1", t0)])
            k.tt(hsq[:T, :], hbuf[:T, :], hbuf[:T, :], ALU.mult, eng="pool")
            c = cols[:, 62:63]; k.red(c[:T, :], hsq[:T, :])
            r = cols[:, 63:64]; rs_eps(r[:T, :], c[:T, :], 1.0 / D)
            k.stt(hsq[:T, :], hbuf[:T, :], r[:T, :], gnb[:T, :], ALU.mult, ALU.mult)
            if layer == 0:
                for q4 in range(4):
                    p = k.ps()
                    for j in range(4):
                        kc = q4 * 4 + j
                        k.tr(p[:, j * 128:j * 128 + T], hsq[:T, kc * 128:(kc + 1) * 128], ident[:T, :T])
                    if T == 128:
                        k.cp(hnT_sb[:, q4 * 4:q4 * 4 + 4, :], p[:, :].rearrange("p (a b) -> p a b", a=4), eng="act")
                    else:
                        for j in range(4):
                            k.cp(hnT_sb[:, q4 * 4 + j, :T], p[:, j * 128:j * 128 + T], eng="act")
                k.dma(hnT_d[:, t0:t0 + T].rearrange("(kc p) t -> p kc t", p=128), hnT_sb[:, :, :T], writes=[("hnT", t0)])
            else:
                k.dma(y_d[t0:t0 + T, :], hsq[:T, :])
        k.close_scope()

    outproj_pass(w_out_ab, 0)

    k.open_scope()
    s5c = sb("s5c", [128, 3, 8]); s5r = sb("s5r", [128, 3, 1024])
    ETr = sb("ETr", [128, 8, 128]); ETi = sb("ETi", [128, 8, 128]); EIr = sb("EIr", [128, 1024]); EIi = sb("EIi", [128, 1024])
    big = [sb(f"big{i}", [128, 1024]) for i in range(6)]
    bigi = sb("bigi", [128, 1024], I32)
    BB = sb("BB", [128, 4, 512]); WB = sb("WB", [128, 4, 512])
    CB = sb("CB", [128, 2, 8, 128]); DD = sb("DD", [128, 2, 128])
    uT = sb("uT", [128, 2, 128])
    car = sb("car", [128, 2, 8]); xl = sb("xl", [128, 2, 8]); A1 = sb("A1", [128, 2, 8])
    TWO_PI = 2.0 * np.pi

    def sincos(dst_sin, dst_cos, ang, n, shp=None):
        for dst, off in ((dst_sin, 0.0), (dst_cos, 0.25)):
            tn = big[4][:, :n]; tf = big[5][:, :n]
            k.ts(tn, ang, 1.0 / TWO_PI, ALU.mult, off, ALU.add)
            k.cp(bigi[:, :n], tn)
            k.cp(tf, bigi[:, :n])
            k.tt(tn, tn, tf, ALU.subtract)
            k.act(dst, tn, AF.Sin, scale=TWO_PI)

    def cmul(outr, outi, ar, ai, br, bi, n, t1, t2):
        k.tt(t1, ar, br, ALU.mult); k.tt(t2, ai, bi, ALU.mult, eng="pool"); k.tt(outr, t1, t2, ALU.subtract)
        k.tt(t1, ar, bi, ALU.mult); k.tt(t2, ai, br, ALU.mult, eng="pool"); k.tt(outi, t1, t2, ALU.add)

    def s5_setup(hg):
        k.dma(s5c[:], s5col_d[:, :, 8 * hg:8 * hg + 8]); k.dma(s5r[:], s5row_d[:, :, 1024 * hg:1024 * hg + 1024])
        dtc = cols[:, 16:24]; arc = cols[:, 24:32]; aic = cols[:, 32:40]
        k.act(dtc, s5c[:, 2, :], AF.Exp)
        k.tt(arc, s5c[:, 0, :], dtc, ALU.mult); k.tt(aic, s5c[:, 1, :], dtc, ALU.mult)
        ang = big[0][:, :].rearrange("p (a b) -> p a b", a=8)
        io = IOTA.unsqueeze(1).to_broadcast([128, 8, 128])
        k.tt(ang, io, aic.unsqueeze(2).to_broadcast([128, 8, 128]), ALU.mult)
        sincos(big[1][:, :], big[2][:, :], big[0][:, :], 1024)
        mg = big[3][:, :].rearrange("p (a b) -> p a b", a=8)
        k.tt(mg, io, arc.unsqueeze(2).to_broadcast([128, 8, 128]), ALU.mult)
        k.act(big[3][:, :], big[3][:, :], AF.Exp)
        k.tt(ETi[:, :, :].rearrange("p a b -> p (a b)"), big[3][:, :], big[1][:, :], ALU.mult)
        k.tt(ETr[:, :, :].rearrange("p a b -> p (a b)"), big[3][:, :], big[2][:, :], ALU.mult)
        k.cp(A1[:, 0, :], ETr[:, :, 1]); k.cp(A1[:, 1, :], ETi[:, :, 1])
        dtr = big[0][:, :]; k.act(dtr, s5r[:, 2, :], AF.Exp)
        arr = sb_arr[:, :]; air = sb_air[:, :]
        k.tt(arr, s5r[:, 0, :], dtr, ALU.mult); k.tt(air, s5r[:, 1, :], dtr, ALU.mult)
        sincos(big[1][:, :], big[2][:, :], air, 1024)
        k.act(big[3][:, :], arr, AF.Exp)
        abi = big[1][:, :]; abr = big[2][:, :]
        k.tt(abi, abi, big[3][:, :], ALU.mult); k.tt(abr, abr, big[3][:, :], ALU.mult)
        k.ts(abr, abr, -1.0, ALU.add)
        den = big[3][:, :]; t = big[0][:, :]
        k.tt(den, s5r[:, 0, :], s5r[:, 0, :], ALU.mult); k.tt(t, s5r[:, 1, :], s5r[:, 1, :], ALU.mult)
        k.tt(den, den, t, ALU.add); k.recip(den, den)
        zr = big[4][:, :]; zi = big[5][:, :]
        k.tt(zr, abr, s5r[:, 0, :], ALU.mult); k.tt(t, abi, s5r[:, 1, :], ALU.mult); k.tt(zr, zr, t, ALU.add); k.tt(zr, zr, den, ALU.mult)
        k.tt(zi, abi, s5r[:, 0, :], ALU.mult); k.tt(t, abr, s5r[:, 1, :], ALU.mult); k.tt(zi, zi, t, ALU.subtract); k.tt(zi, zi, den, ALU.mult)
        for h in range(2):
            k.dma(WB[:, 2 * h, :], bdb_d[0, 2 * hg + h, :, :]); k.dma(WB[:, 2 * h + 1, :], bdb_d[1, 2 * hg + h, :, :])
            sl = slice(512 * h, 512 * h + 512)
            t1 = big[0][:, 0:512]; t2 = big[0][:, 512:1024]
            cmul(BB[:, 2 * h, :], BB[:, 2 * h + 1, :], zr[:, sl], zi[:, sl], WB[:, 2 * h, :], WB[:, 2 * h + 1, :], 512, t1, t2)
        k.ts(big[0][:, :], air, tcol, ALU.mult)
        sincos(big[1][:, :], big[2][:, :], big[0][:, :], 1024)
        k.ts(big[3][:, :], arr, tcol, ALU.mult); k.act(big[3][:, :], big[3][:, :], AF.Exp, scale=-1.0)
        k.tt(EIr[:, :], big[3][:, :], big[2][:, :], ALU.mult)
        k.stt(EIi[:, :], big[3][:, :], -1.0, big[1][:, :], ALU.mult, ALU.mult)
        k.dma(CB[:, 0, :, :], bdc_d[0, hg, :, :, :]); k.dma(CB[:, 1, :, :], bdc_d[1, hg, :, :, :])
        k.ts(CB[:, 1, :, :], CB[:, 1, :, :], -1.0, ALU.mult)
        k.dma(DD[:], bdd_d[hg, :, :, :])

    sb_arr = sb("sb_arr", [128, 1024]); sb_air = sb("sb_air", [128, 1024])

    def s5_tile(hg, seq, t0, T):
        zc = projT[:T, 0:256]
        Bu = [big[0], big[1]]
        for h in range(2):
            for cpx in range(2):
                p = k.ps(); k.mm(p[:T, :512], uT[:, h, :T], BB[:, 2 * h + cpx, :])
                k.cp(Bu[cpx][:T, 512 * h:512 * h + 512], p[:T, :512], eng="act")
        Zr = big[2]; Zi = big[3]
        cmul(Zr[:T, :], Zi[:T, :], EIr[:T, :], EIi[:T, :], Bu[0][:T, :], Bu[1][:T, :], 1024, big[4][:T, :], big[5][:T, :])
        Wc = [big[0], big[1]]
        for cpx, Z in enumerate((Zr, Zi)):
            for half in range(2):
                p = k.ps()
                for j in range(4):
                    ch = half * 4 + j
                    k.mm(p[:, j * 128:j * 128 + T], Z[:T, ch * 128:(ch + 1) * 128], U[:T, :T])
                src = p[:, :].rearrange("p (a b) -> p a b", a=4)[:, :, :T]
                dst = Wc[cpx][:, :].rearrange("p (a b) -> p a b", a=8)[:, half * 4:half * 4 + 4, :T]
                k.tt(dst, src, car[:, cpx, half * 4:half * 4 + 4].unsqueeze(2).to_broadcast([128, 4, T]), ALU.add)
        XTr = big[2][:, :].rearrange("p (a b) -> p a b", a=8); XTi = big[3][:, :].rearrange("p (a b) -> p a b", a=8)
        W3 = [w[:, :].rearrange("p (a b) -> p a b", a=8) for w in Wc]
        t1 = big[4][:, :].rearrange("p (a b) -> p a b", a=8); t2 = big[5][:, :].rearrange("p (a b) -> p a b", a=8)
        cmul(XTr[:, :, :T], XTi[:, :, :T], ETr[:, :, :T], ETi[:, :, :T], W3[0][:, :, :T], W3[1][:, :, :T], 0, t1[:, :, :T], t2[:, :, :T])
        k.cp(xl[:, 0, :], XTr[:, :, T - 1]); k.cp(xl[:, 1, :], XTi[:, :, T - 1])
        c1 = cols[:, 40:48]; c2 = cols[:, 48:56]
        cmul(car[:, 0, :], car[:, 1, :], A1[:, 0, :], A1[:, 1, :], xl[:, 0, :], xl[:, 1, :], 8, c1, c2)
        py = k.ps()
        for h in range(2):
            o = py[:T, 128 * h:128 * h + 128]
            k.mm(o, uT[:, h, :T], DD[:, h, :], start=True, stop=False)
            for pc in range(4):
                ch = 4 * h + pc
                k.mm(o, XTr[:, ch, :T], CB[:, 0, ch, :], start=False, stop=False)
                k.mm(o, XTi[:, ch, :T], CB[:, 1, ch, :], start=False, stop=(pc == 3))
        yg = tmp(); k.act(yg[:T, :256], py[:T, :256], AF.Gelu)
        sz = tmp(); k.act(sz[:T, :256], zc, AF.Silu)
        yz = tmp(); k.tt(yz[:T, :256], yg[:T, :256], sz[:T, :256], ALU.mult)
        k.dma(yz_d[t0:t0 + T, 256 * hg:256 * hg + 256], yz[:T, :256], writes=[("yz", hg, t0)])
        for c0 in range(0, 256, 128):
            p = k.ps(); k.tr(p[:, :T], yg[:T, c0:c0 + 128], ident[:T, :T])
            ob = obf[obc[0] % 4]; obc[0] += 1
            k.cp(ob[:, :T], p[:, :T], eng="act")
            k.dma(yT_d[256 * hg + c0:256 * hg + c0 + 128, t0:t0 + T], ob[:, :T], writes=[("yT", 256 * hg + c0, t0)])

    for hg in range(4):
        load_w(w_in_cd, [(0, 256 * hg, 256), (256, 1024 + 256 * hg, 256)])
        s5_setup(hg)
        for ti, (seq, t0, T, first, last) in enumerate(tiles):
            if first:
                if seq == 0:
                    k.memset(car[:], 0.0)
                else:
                    k.dma(xl[:], si_s5[seq - 1, :, :, 8 * hg:8 * hg + 8])
                    cmul(car[:, 0, :], car[:, 1, :], A1[:, 0, :], A1[:, 1, :], xl[:, 0, :], xl[:, 1, :], 8, cols[:, 40:48], cols[:, 48:56])
            load_hn(t0, T)
            projF(uT[:, 0, :T], 0, T, scaled=False); projF(uT[:, 1, :T], 128, T, scaled=False)
            projTok(256, 256, T, scaled=False)
            s5_tile(hg, seq, t0, T)
            if last:
                k.dma(so_s5[seq, :, :, 8 * hg:8 * hg + 8], xl[:])

    k.close_scope()
    k.open_scope()
    Cx = sb("Cx", [128, 257]); mst = sb("mst", [128, 1]); vx = sb("vx", [128, 257])
    gluw = sb("gluw", [128, 8, 256], BF16); ytl = sb("ytl", [128, 8, 128], BF16)
    glub = sb("glub", [128, 256]); dngb = sb("dngb", [128, 256]); dib = sb("dib", [128, 4]); dfb = sb("dfb", [128, 4])
    yzt = sb("yzt", [128, 256])
    k.dma(dib[:], dib_d[:, :]); k.dma(dfb[:], dfb_d[:, :]); k.ts(dfb[:], dfb[:], -1.0, ALU.mult)
    k.memset(vx[:, 256:257], 1.0)

    def mlstm_tile(hg, seq, t0, T):
        kk = projT[:T, 0:128]; od = projT[:T, 384:640]; zd = projT[:T, 640:896]
        c = lambda i: cols[:, i:i + 1]
        k.cp(vx[:T, 0:256], projT[:T, 128:384], eng="pool")
        ip = c(0); k.tt(ip[:T, :], projT[:T, 896:897], dib[:T, hg:hg + 1], ALU.add)
        e = c(1); k.act(e[:T, :], projT[:T, 897:898], AF.Exp, scale=-1.0, bias=dfb[:T, hg:hg + 1])
        sp = c(2); k.act(sp[:T, :], e[:T, :], AF.Ln, bias=1.0)
        p = k.ps(); k.mm(p[:T, 0:1], U[:T, :T], sp[:T, :]); bcol = c(3); k.ts(bcol[:T, :], p[:T, 0:1], -1.0, ALU.mult)
        p = k.ps(); k.mm(p[:, 0:1], ones[:T, :], sp[:T, :]); blast = c(4); k.ts(blast[:, :], p[:, 0:1], -1.0, ALU.mult)
        d = c(5); k.tt(d[:T, :], ip[:T, :], bcol[:T, :], ALU.subtract)
        Dm = tmp(); k.ts(Dm[:T, :T], ident[:T, :T], d[:T, :], ALU.mult)
        pr = k.ps(); k.mm(pr[:T, :T], ones[:T, :T], Dm[:T, :T])
        lw = tmp(); k.stt(lw[:T, :T], pr[:T, :T], bcol[:T, :], NML[:T, :T], ALU.add, ALU.add)
        mi = c(6); k.red(mi[:T, :], lw[:T, :T], ALU.max)
        nmi = c(7); k.ts(nmi[:T, :], mi[:T, :], -1.0, ALU.mult)
        e1 = tmp(); k.act(e1[:T, :T], lw[:T, :T], AF.Exp, bias=nmi[:T, :])
        pqk = k.ps(); k.mm(pqk[:T, :T], qT[:, :T], kT[:, :T])
        pm = tmp(); k.stt(pm[:T, :T], pqk[:T, :T], SC, e1[:T, :T], ALU.mult, ALU.mult)
        p = k.ps(); k.tr(p[:T, :T], pm[:T, :T], ident[:T, :T]); pT = tmp(); k.cp(pT[:T, :T], p[:T, :T], eng="act")
        sel = sel128 if T == 128 else sel32
        p = k.ps(); k.mm(p[:, 0:1], sel[:T, :], mi[:T, :]); mch = c(8); k.cp(mch[:, :], p[:, 0:1])
        bm = c(9); k.tt(bm[:, :], blast[:, :], mch[:, :], ALU.subtract)
        kws = c(10); k.act(kws[:T, :], d[:T, :], AF.Exp, bias=bm[:T, :])
        kw = tmp(); k.ts(kw[:T, :128], kk, kws[:T, :], ALU.mult)
        a = c(11); k.tt(a[:T, :], bcol[:T, :], mst[:T, :], ALU.add)
        mt = c(12); k.tt(mt[:T, :], a[:T, :], mi[:T, :], ALU.max)
        nmt = c(13); k.ts(nmt[:T, :], mt[:T, :], -1.0, ALU.mult)
        wa = c(14); k.act(wa[:T, :], a[:T, :], AF.Exp, bias=nmt[:T, :]); k.ts(wa[:T, :], wa[:T, :], SC, ALU.mult)
        wi = c(15); k.act(wi[:T, :], mi[:T, :], AF.Exp, bias=nmt[:T, :])
        emt = c(56); k.act(emt[:T, :], nmt[:T, :], AF.Exp)
        pA = k.ps(); k.mm(pA[:T, :257], qT[:, :T], Cx[:, :])
        pB = k.ps(); k.mm(pB[:T, :257], pT[:T, :T], vx[:T, :])
        r1 = tmp(); k.ts(r1[:T, :257], pA[:T, :257], wa[:T, :], ALU.mult)
        res = tmp(); k.stt(res[:T, :257], pB[:T, :257], wi[:T, :], r1[:T, :257], ALU.mult, ALU.add)
        dn = c(57); k.act(dn[:T, :], res[:T, 256:257], AF.Abs); k.tt(dn[:T, :], dn[:T, :], emt[:T, :], ALU.max)
        k.recip(dn[:T, :], dn[:T, :])
        sg = tmp(); k.act(sg[:T, :256], od, AF.Sigmoid)
        hd = tmp(); k.stt(hd[:T, :256], res[:T, :256], dn[:T, :], sg[:T, :256], ALU.mult, ALU.mult)
        mu = c(58); k.red(mu[:T, :], hd[:T, :256]); k.ts(mu[:T, :], mu[:T, :], -1.0 / 256, ALU.mult)
        xc = tmp(); k.ts(xc[:T, :256], hd[:T, :256], mu[:T, :], ALU.add)
        gsz = tmp(); k.act(gsz[:T, :256], zd, AF.Silu); k.tt(gsz[:T, :256], gsz[:T, :256], dngb[:T, :], ALU.mult, eng="pool")
        od_ = tmp(); headnorm_rms(xc[:T, :256], T, 256, gsz[:T, :256], od_[:T, :256])
        store_oT(od_, 256, 1024 + 256 * hg, t0, T)
        pkv = k.ps(); k.mm(pkv[:, :257], kw[:T, :128], vx[:T, :])
        bms = c(59); k.tt(bms[:, :], blast[:, :], mst[:, :], ALU.add)
        mnew = c(56); k.tt(mnew[:, :], bms[:, :], mch[:, :], ALU.max)
        nmn = c(57); k.ts(nmn[:, :], mnew[:, :], -1.0, ALU.mult)
        wold = c(58); k.act(wold[:, :], bms[:, :], AF.Exp, bias=nmn[:, :])
        wnew = c(12); k.act(wnew[:, :], mch[:, :], AF.Exp, bias=nmn[:, :])
        t = tmp(); k.ts(t[:, :257], pkv[:, :257], wnew[:, :], ALU.mult)
        k.stt(Cx[:, :], Cx[:, :], wold[:, :], t[:, :257], ALU.mult, ALU.add)
        k.cp(mst[:, :], mnew[:, :])

    def glu_tile(hg, seq, t0, T):
        k.dma(ytl[:, :, :T], yT_d[:, t0:t0 + T].rearrange("(kc p) t -> p kc t", p=128),
              reads=[("yT", f, t0) for f in range(0, 1024, 128)])
        k.dma(yzt[:T, :], yz_d[t0:t0 + T, 256 * hg:256 * hg + 256], reads=[("yz", hg, t0)])
        p = k.ps()
        for kc in range(8):
            k.mm(p[:T, :256], ytl[:, kc, :T], gluw[:, kc, :], start=(kc == 0), stop=(kc == 7))
        g = tmp(); k.tt(g[:T, :256], p[:T, :256], glub[:T, :], ALU.add)
        k.act(g[:T, :256], g[:T, :256], AF.Sigmoid)
        oc = tmp(); k.tt(oc[:T, :256], g[:T, :256], yzt[:T, :], ALU.mult)
        store_oT(oc, 256, 256 * hg, t0, T)

    for hg in range(4):
        load_w(w_in_cd, [(0, 2048 + 128 * hg, 128), (128, 2560 + 128 * hg, 128),
                         (256, 2560 + 128 * hg, 128), (384, 3072 + 256 * hg, 256), (640, 4096 + 256 * hg, 256),
                         (896, 5120 + 256 * hg, 256), (1152, 6144 + hg, 1), (1153, 6148 + hg, 1)])
        k.dma(gluw[:, :, :], gluw_d[:, 256 * hg:256 * hg + 256].rearrange("(kc p) c -> p kc c", p=128), eng="pool")
        k.dma(glub[:], glub_d[:, 256 * hg:256 * hg + 256]); k.dma(dngb[:], dng_d[:, 256 * hg:256 * hg + 256])
        for ti, (seq, t0, T, first, last) in enumerate(tiles):
            if first:
                if seq == 0:
                    k.memset(Cx[:], 0.0); k.memset(mst[:], 0.0)
                else:
                    k.dma(Cx[:, 0:256], si_mc[seq - 1, hg, :, :]); k.dma(Cx[:, 256:257], si_mn[seq - 1, :, hg:hg + 1])
                    k.dma(mst[:], si_mm[seq - 1, :, hg:hg + 1])
            load_hn(t0, T)
            projF(qT[:, :T], 0, T, scaled=False); projF(kT[:, :T], 128, T, scaled=False)
            projTok(256, 898, T, scaled=False)
            mlstm_tile(hg, seq, t0, T)
            glu_tile(hg, seq, t0, T)
            if last:
                k.dma(so_mc[seq, hg, :, :], Cx[:, 0:256]); k.dma(so_mn[seq, :, hg:hg + 1], Cx[:, 256:257])
                k.dma(so_mm[seq, :, hg:hg + 1], mst[:])

    k.close_scope()
    outproj_pass(w_out_cd, 1)
    k.finish()
    k.emit()
    return nc, k


def _consts():
    c = np.zeros((128, 9, 128), np.float32)
    idx = np.arange(128)
    kk, ii = idx[:, None], idx[None, :]
    c[:, 0] = (kk == ii); c[:, 1] = (kk <= ii); c[:, 2] = (kk < ii); c[:, 3] = -(kk > ii).astype(np.float32)
    c[:, 4] = 1.0; c[:, 5] = np.where(kk <= ii, 0.0, NEG); c[:, 6] = np.where(ii <= kk, 0.0, NEG)
    c[:, 7] = np.broadcast_to(idx[None, :], (128, 128))
    c[:, 8, 0] = idx; c[127, 8, 1] = 1.0; c[31, 8, 2] = 1.0
    return c


def _bc(v, n=128):
    v = np.asarray(v, np.float32).reshape(1, -1)
    return np.ascontiguousarray(np.broadcast_to(v, (n, v.shape[1])))


def _s5col(a):
    return np.ascontiguousarray(np.asarray(a, np.float32).reshape(32, 128).T)


def _shared_inputs(inp):
    f = lambda a: np.ascontiguousarray(np.asarray(a, np.float32))
    d = {}
    d["w_in_ab"] = f(inp["w_in_ab"]); d["w_out_ab"] = f(inp["w_out_ab"])
    d["w_in_cd"] = f(inp["w_in_cd"]); d["w_out_cd"] = f(inp["w_out_cd"]); d["glu_w"] = f(inp["c_glu_w"])
    d["consts"] = _consts()
    ng = f(inp["norm_g"])
    d["g0T"] = np.ascontiguousarray(ng[0].reshape(16, 128).T); d["g1b"] = _bc(ng[1]); d["gfb"] = _bc(inp["final_norm_g"])
    d["a_gate_w"] = f(inp["a_gate_w"]); d["a_gate_b"] = f(inp["a_gate_b"]).reshape(1, 512); d["a_norm_g_b"] = _bc(inp["a_norm_g"])
    d["conv_w"] = np.ascontiguousarray(f(inp["b_conv_w"]).reshape(4, 24, 128).transpose(2, 1, 0))
    d["a_log_b"] = _bc(inp["b_a_log"]); d["dt_bias_b"] = _bc(inp["b_dt_bias"]); d["b_norm_g_b"] = _bc(inp["b_norm_g"])
    lre, lim = f(inp["c_lam_re"]), f(inp["c_lam_im"])
    ldt = np.ascontiguousarray(np.broadcast_to(f(inp["c_log_dt"])[:, None], (64, 64)))
    d["s5col"] = np.ascontiguousarray(np.stack([_s5col(lre), _s5col(lim), _s5col(ldt)], axis=1))
    d["s5row"] = np.ascontiguousarray(np.stack([_bc(lre.reshape(-1)), _bc(lim.reshape(-1)), _bc(ldt.reshape(-1))], axis=1))
    bd_b = np.zeros((2, 8, 128, 512), np.float32)
    for ci, b in enumerate((f(inp["c_b_re"]), f(inp["c_b_im"]))):
        for H in range(8):
            for gl in range(8):
                bd_b[ci, H, gl * 16:(gl + 1) * 16, gl * 64:(gl + 1) * 64] = b[8 * H + gl].T
    d["bd_b"] = bd_b
    bd_c = np.zeros((2, 4, 128, 8, 128), np.float32)
    for ci, c in enumerate((f(inp["c_c_re"]), f(inp["c_c_im"]))):
        for hg in range(4):
            for ch in range(8):
                for g2 in range(2):
                    g = 16 * hg + 2 * ch + g2
                    col = (2 * (ch % 4) + g2) * 16
                    bd_c[ci, hg, g2 * 64:(g2 + 1) * 64, ch, col:col + 16] = c[g].T
    d["bd_c"] = bd_c
    bd_d = np.zeros((4, 128, 2, 128), np.float32)
    cd = f(inp["c_d"])
    for hg in range(4):
        for h in range(2):
            for gl in range(8):
                for i in range(16):
                    bd_d[hg, gl * 16 + i, h, gl * 16 + i] = cd[16 * hg + 8 * h + gl, i]
    d["bd_d"] = bd_d
    d["glu_b_b"] = _bc(inp["c_glu_b"]); d["d_i_b"] = _bc(inp["d_i_bias"]); d["d_f_b"] = _bc(inp["d_f_bias"])
    d["d_norm_g_b"] = _bc(inp["d_norm_g"])
    return d


_NC_CACHE = {}


def kernel(**inp):
    f = lambda a: np.ascontiguousarray(np.asarray(a, np.float32))
    xp = f(inp["x_prompt"]); xs = f(inp["x_sample"])
    NB, L, _ = xp.shape
    shared = _shared_inputs(inp)
    conv = f(inp["cache_gdn_conv"]); sgla = f(inp["state_gla"]); sgdn = f(inp["state_gdn"])
    s5re = f(inp["state_s5_re"]); s5im = f(inp["state_s5_im"]); smc = f(inp["state_mlstm_c"])
    smn = f(inp["state_mlstm_n"]); smm = f(inp["state_mlstm_m"])
    in_maps = []
    for c in range(8):
        m = dict(shared)
        sq = [2 * c, 2 * c + 1]
        xtok = np.concatenate([xp[c % NB]] + [xs[s] for s in sq], axis=0)
        m["xtok"] = np.ascontiguousarray(xtok); m["xT"] = np.ascontiguousarray(xtok.T)
        m["si_conv"] = np.ascontiguousarray(np.stack([conv[s].reshape(3, 24, 128).transpose(2, 1, 0) for s in sq]))
        m["si_gla"] = np.ascontiguousarray(sgla[sq]); m["si_gdn"] = np.ascontiguousarray(sgdn[sq])
        m["si_s5"] = np.ascontiguousarray(np.stack([np.stack([_s5col(s5re[s]), _s5col(s5im[s])], axis=1) for s in sq]))
        m["si_mc"] = np.ascontiguousarray(smc[sq])
        m["si_mn"] = np.ascontiguousarray(np.stack([smn[s].T for s in sq]))
        m["si_mm"] = np.ascontiguousarray(np.stack([_bc(smm[s]) for s in sq]))
        in_maps.append(m)
    if L not in _NC_CACHE:
        _NC_CACHE[L] = build(L)[0]
    nc = _NC_CACHE[L]
    res = run_bass_kernel_spmd(nc, in_maps, core_ids=list(range(8)))
    R = res.results
    NS = xs.shape[0]
    y_p = np.stack([R[b]["y"][:L] for b in range(NB)])
    y_s = np.stack([R[s // 2]["y"][L + 32 * (s % 2):L + 32 * (s % 2) + 32] for s in range(NS)])

    def gather(fn):
        p = np.stack([fn(R[b], 0) for b in range(NB)])
        s = np.stack([fn(R[s // 2], 1 + s % 2) for s in range(NS)])
        return p, s

    def s5o(r, i, ci):
        return np.ascontiguousarray(r["so_s5"][i][:, ci, :].T).reshape(64, 64)

    pc, sc = gather(lambda r, i: np.ascontiguousarray(r["so_conv"][i].transpose(2, 1, 0)).reshape(3, 3072))
    pg, sg = gather(lambda r, i: r["so_gla"][i]); pd, sd = gather(lambda r, i: r["so_gdn"][i])
    pr, sr = gather(lambda r, i: s5o(r, i, 0)); pi, si = gather(lambda r, i: s5o(r, i, 1))
    pmc, smc_ = gather(lambda r, i: r["so_mc"][i]); pmn, smn_ = gather(lambda r, i: np.ascontiguousarray(r["so_mn"][i].T))
    pmm, smm_ = gather(lambda r, i: np.ascontiguousarray(r["so_mm"][i][0, :]))
    outs = (y_p, y_s, pc, pg, pd, pr, pi, pmc, pmn, pmm, sc, sg, sd, sr, si, smc_, smn_, smm_)
    return tuple(np.ascontiguousarray(o, dtype=np.float32) for o in outs)
```

```python
import numpy as np
import concourse.bass as bass
import concourse.mybir as mybir
from contextlib import ExitStack
from concourse.bass_utils import run_bass_kernel_spmd

F32 = mybir.dt.float32
BF16 = mybir.dt.bfloat16
I32 = mybir.dt.int32
AF = mybir.ActivationFunctionType
ALU = mybir.AluOpType
AX = mybir.AxisListType

SEM_LIMIT = 24000


class KB:
    ENGS = ("pe", "act", "dve", "pool", "sp")

    def __init__(self, nc):
        self.nc = nc
        self.stack = ExitStack()
        self.ops = {e: [] for e in self.ENGS}
        self.cur_sem = {}
        self.cur_cnt = {}
        for e in self.ENGS:
            self.cur_sem[e] = None
            self.cur_cnt[e] = 0
        self.waited = {e: {} for e in self.ENGS}
        self.writers = {}
        self.readers = {}
        self.dma_pool = []
        self.dma_rr = 0
        self.n_dma_sems = 14
        self.sem_objs = {}
        self.nuid = 0
        self.psum_banks = []
        self.ps_rr = 0
        self.n_instr = 0
        self.scopes = []

    def _alloc_sem(self, name):
        s = self.nc.alloc_semaphore(name=name)
        self.nuid += 1
        sid = self.nuid
        self.sem_objs[sid] = s
        return sid

    def sb(self, name, shape, dtype=F32):
        st = self.scopes[-1] if self.scopes else self.stack
        self.nuid += 1
        t = st.enter_context(self.nc.sbuf_tensor(f"sb_{name}_{self.nuid}", list(shape), dtype))
        return t

    def barrier(self):
        for e in self.ENGS:
            waits = {}
            for o in self.ENGS:
                if o != e and self.cur_sem[o] is not None and self.waited[e].get(self.cur_sem[o], 0) < self.cur_cnt[o]:
                    waits[self.cur_sem[o]] = self.cur_cnt[o]
            for sid, val in self.dma_pool:
                if val > 0 and self.waited[e].get(sid, 0) < val:
                    waits[sid] = val
            self._emit_waits(e, waits)

    def open_scope(self):
        self.scopes.append(ExitStack())

    def close_scope(self):
        self.barrier()
        self.emit_block()
        self.scopes.pop().close()

    def psum_init(self):
        for i in range(8):
            t = self.stack.enter_context(self.nc.psum_tensor(f"psb{i}", [128, 512], F32))
            self.psum_banks.append(t)

    def ps(self):
        t = self.psum_banks[self.ps_rr % 8]
        self.ps_rr += 1
        return t

    @staticmethod
    def key(x):
        if isinstance(x, (str, tuple)):
            return x
        return x.tensor.name

    def _deps(self, eng, reads, writes, is_dma):
        deps = []
        for r in reads:
            for src, t in self.writers.get(r, {}).items():
                deps.append((src, t, "raw"))
        for w in writes:
            for src, t in self.writers.get(w, {}).items():
                deps.append((src, t, "waw"))
            for src, t in self.readers.get(w, {}).items():
                deps.append((src, t, "war"))
        out = {}
        for src, (sid, val), kind in deps:
            if src == eng and not is_dma and not str(src).startswith("dma"):
                if eng == "pe" or kind != "raw":
                    continue
            if self.waited[eng].get(sid, 0) >= val:
                continue
            if out.get(sid, 0) < val:
                out[sid] = val
        return out

    def _emit_waits(self, eng, waits):
        for sid, val in waits.items():
            s = self.sem_objs[sid]
            self.ops[eng].append(lambda h, s=s, val=val: h.wait_ge(s, val))
            self.waited[eng][sid] = val
            self.n_instr += 1

    def _record(self, src, ticket, reads, writes):
        for r in reads:
            self.readers.setdefault(r, {})[src] = ticket
        for w in writes:
            self.writers[w] = {src: ticket}
            self.readers[w] = {}

    def op(self, eng, fn, reads, writes):
        reads = [self.key(r) for r in reads if r is not None and not isinstance(r, (int, float))]
        writes = [self.key(w) for w in writes]
        waits = self._deps(eng, reads, writes, False)
        self._emit_waits(eng, waits)
        if self.cur_sem[eng] is None or self.cur_cnt[eng] >= SEM_LIMIT:
            self.cur_sem[eng] = self._alloc_sem(f"s_{eng}_{self.nuid}")
            self.cur_cnt[eng] = 0
        self.cur_cnt[eng] += 1
        sid, val = self.cur_sem[eng], self.cur_cnt[eng]
        s = self.sem_objs[sid]
        self.ops[eng].append(lambda h, s=s: fn(h).then_inc(s, 1))
        self.n_instr += 1
        self._record(eng, (sid, val), reads, writes)

    def dma(self, out, in_, reads=None, writes=None, eng="sp", **kw):
        reads = [self.key(r) for r in (reads if reads is not None else [in_])]
        writes = [self.key(w) for w in (writes if writes is not None else [out])]
        waits = self._deps(eng, reads, writes, True)
        if len(self.dma_pool) < self.n_dma_sems:
            self.dma_pool.append([self._alloc_sem(f"s_dma_{self.nuid}"), 0])
        slot = self.dma_pool[self.dma_rr % self.n_dma_sems]
        self.dma_rr += 1
        if slot[1] + 16 > SEM_LIMIT:
            if self.waited[eng].get(slot[0], 0) < slot[1]:
                waits[slot[0]] = max(waits.get(slot[0], 0), slot[1])
            self._emit_waits(eng, waits)
            waits = {}
            slot[0] = self._alloc_sem(f"s_dma_{self.nuid}")
            slot[1] = 0
        sid = slot[0]
        if slot[1] > 0 and self.waited[eng].get(sid, 0) < slot[1]:
            waits[sid] = max(waits.get(sid, 0), slot[1])
        self._emit_waits(eng, waits)
        slot[1] += 16
        val = slot[1]
        s = self.sem_objs[sid]
        self.ops[eng].append(lambda h, s=s: h.dma_start(out=out, in_=in_, allow_slow_non_contiguous=True, **kw).then_inc(s, 16))
        self.n_instr += 1
        self._record(f"dma{sid}", (sid, val), reads, writes)
        return (sid, val)

    def wait_all(self, eng, keys):
        waits = {}
        for k in keys:
            k = self.key(k)
            for src, (sid, val) in self.writers.get(k, {}).items():
                if self.waited[eng].get(sid, 0) < val:
                    waits[sid] = max(waits.get(sid, 0), val)
        self._emit_waits(eng, waits)

    def mm(self, out, lhsT, rhs, start=True, stop=True, extra_reads=()):
        self.op("pe", lambda h: h.matmul(out, lhsT, rhs, start=start, stop=stop),
                [lhsT, rhs] + list(extra_reads) + ([] if start else [out]), [out])

    def tr(self, out, in_, ident):
        self.op("pe", lambda h: h.transpose(out, in_, ident), [in_, ident], [out])

    def act(self, out, in_, func, bias=0.0, scale=1.0, eng="act"):
        rd = [in_]
        if not isinstance(bias, (int, float)):
            rd.append(bias)
        if not isinstance(scale, (int, float)):
            rd.append(scale)
        self.op(eng, lambda h: h.activation(out=out, in_=in_, func=func, bias=bias, scale=scale), rd, [out])

    def tt(self, out, a, b, op, eng="dve"):
        self.op(eng, lambda h: h.tensor_tensor(out=out, in0=a, in1=b, op=op), [a, b], [out])

    def ts(self, out, a, s1, op0, s2=None, op1=None, eng="dve"):
        rd = [a] + [s for s in (s1, s2) if s is not None and not isinstance(s, (int, float))]
        if op1 is None:
            self.op(eng, lambda h: h.tensor_scalar(out=out, in0=a, scalar1=s1, scalar2=None, op0=op0), rd, [out])
        else:
            self.op(eng, lambda h: h.tensor_scalar(out=out, in0=a, scalar1=s1, scalar2=s2, op0=op0, op1=op1), rd, [out])

    def stt(self, out, a, s, b, op0, op1, eng="dve"):
        rd = [a, b] + ([] if isinstance(s, (int, float)) else [s])
        self.op(eng, lambda h: h.scalar_tensor_tensor(out=out, in0=a, scalar=s, in1=b, op0=op0, op1=op1), rd, [out])

    def cp(self, out, in_, eng="dve"):
        if eng == "act":
            self.op("act", lambda h: h.copy(out=out, in_=in_), [in_], [out])
        else:
            self.op(eng, lambda h: h.tensor_copy(out=out, in_=in_), [in_], [out])

    def red(self, out, in_, op=None, eng="dve"):
        op = op or ALU.add
        self.op(eng, lambda h: h.tensor_reduce(out=out, in_=in_, axis=AX.X, op=op), [in_], [out])

    def recip(self, out, in_):
        self.op("dve", lambda h: h.reciprocal(out=out, in_=in_), [in_], [out])

    def memset(self, ap, val, eng="dve"):
        self.op(eng, lambda h: h.memset(ap, val), [], [ap])

    def rsqrt(self, out, in_, scale, eps):
        self.act(out, in_, AF.Sqrt, bias=eps, scale=scale)
        self.recip(out, out)

    def finish(self):
        waits = {}
        for sid, val in self.dma_pool:
            if val > 0 and self.waited["sp"].get(sid, 0) < val:
                waits[sid] = val
        self._emit_waits("sp", waits)

    def emit(self):
        self.emit_block()
        self.stack.close()

    def emit_block(self):
        nc = self.nc
        ops = self.ops
        self.ops = {e: [] for e in self.ENGS}
        with nc.Block() as block:
            @block.tensor
            def _(h):
                for f in ops["pe"]:
                    f(h)

            @block.scalar
            def _(h):
                for f in ops["act"]:
                    f(h)

            @block.vector
            def _(h):
                for f in ops["dve"]:
                    f(h)

            @block.gpsimd
            def _(h):
                for f in ops["pool"]:
                    f(h)

            @block.sync
            def _(h):
                for f in ops["sp"]:
                    f(h)


EPS = 1e-6
D = 2048
NEG = -30000.0


def run_threads(threads):
    alive = [t for t in threads if t is not None]
    while alive:
        for g in list(alive):
            try:
                next(g)
            except StopIteration:
                alive.remove(g)


def build(L):
    nc = bass.Bass("TRN2", target_bir_lowering=False)
    NTOK = L + 64
    tiles = [(0, i * 128, 128, i == 0, i == L // 128 - 1) for i in range(L // 128)]
    tiles += [(1, L, 32, True, True), (2, L + 32, 32, True, True)]

    def din(name, shape, dt=F32):
        return nc.dram_tensor(name, list(shape), dt, kind="ExternalInput").ap()

    def dout(name, shape, dt=F32):
        return nc.dram_tensor(name, list(shape), dt, kind="ExternalOutput").ap()

    xT_d = din("xT", [D, NTOK]); xtok_d = din("xtok", [NTOK, D])
    w_in_ab = din("w_in_ab", [D, 7200]); w_out_ab = din("w_out_ab", [D, D])
    w_in_cd = din("w_in_cd", [D, 6152]); w_out_cd = din("w_out_cd", [D, D])
    gluw_d = din("glu_w", [1024, 1024])
    consts_d = din("consts", [128, 9, 128])
    g0T_d = din("g0T", [128, 16]); g1b_d = din("g1b", [128, D]); gfb_d = din("gfb", [128, D])
    agw_d = din("a_gate_w", [16, 512]); agb_d = din("a_gate_b", [1, 512]); ang_d = din("a_norm_g_b", [128, 1024])
    cw_d = din("conv_w", [128, 24, 4]); alog_d = din("a_log_b", [128, 8]); dtb_d = din("dt_bias_b", [128, 8])
    bng_d = din("b_norm_g_b", [128, 128])
    s5col_d = din("s5col", [128, 3, 32]); s5row_d = din("s5row", [128, 3, 4096])
    bdb_d = din("bd_b", [2, 8, 128, 512]); bdc_d = din("bd_c", [2, 4, 128, 8, 128]); bdd_d = din("bd_d", [4, 128, 2, 128])
    glub_d = din("glu_b_b", [128, 1024]); dib_d = din("d_i_b", [128, 4]); dfb_d = din("d_f_b", [128, 4])
    dng_d = din("d_norm_g_b", [128, 1024])
    si_conv = din("si_conv", [2, 128, 24, 3]); si_gla = din("si_gla", [2, 4, 128, 256]); si_gdn = din("si_gdn", [2, 8, 128, 128])
    si_s5 = din("si_s5", [2, 128, 2, 32]); si_mc = din("si_mc", [2, 4, 128, 256]); si_mn = din("si_mn", [2, 128, 4])
    si_mm = din("si_mm", [2, 128, 4])

    y_d = dout("y", [NTOK, D])
    so_conv = dout("so_conv", [3, 128, 24, 3]); so_gla = dout("so_gla", [3, 4, 128, 256]); so_gdn = dout("so_gdn", [3, 8, 128, 128])
    so_s5 = dout("so_s5", [3, 128, 2, 32]); so_mc = dout("so_mc", [3, 4, 128, 256]); so_mn = dout("so_mn", [3, 128, 4])
    so_mm = dout("so_mm", [3, 128, 4])

    oT_d = nc.dram_tensor("oT_s", [D, NTOK], BF16).ap()
    h1_d = nc.dram_tensor("h1_s", [NTOK, D], F32).ap()
    hnT_d = nc.dram_tensor("hnT_s", [D, NTOK], BF16).ap()
    yT_d = nc.dram_tensor("yT_s", [1024, NTOK], BF16).ap()
    yz_d = nc.dram_tensor("yz_s", [NTOK, 1024], F32).ap()

    k = KB(nc)
    k.psum_init()
    sb = k.sb
    cst = sb("cst", [128, 9, 128])
    k.dma(cst[:], consts_d[:, :, :])
    ident = cst[:, 0, :]; U = cst[:, 1, :]; SU = cst[:, 2, :]; NSL = cst[:, 3, :]; ones = cst[:, 4, :]
    NMU = cst[:, 5, :]; NML = cst[:, 6, :]; IOTA = cst[:, 7, :]; SELS = cst[:, 8, :]
    tcol = cst[:, 8, 0:1]
    sel128 = sb("sel128", [128, 128]); sel32 = sb("sel32", [128, 128])
    k.ts(sel128[:], ones, cst[:, 8, 1:2], ALU.mult)
    k.ts(sel32[:], ones, cst[:, 8, 2:3], ALU.mult)
    epsb = sb("epsb", [128, 1]); k.memset(epsb[:], EPS)
    wbuf = sb("wbuf", [128, 16, 2048], BF16)
    xg = sb("xg", [128, 16, 128], BF16)
    projT = sb("projT", [128, 1160])
    tbuf = [sb(f"tb{i}", [128, 512]) for i in range(12)]
    cols = sb("cols", [128, 64])
    obf = [sb(f"obf{i}", [128, 128], BF16) for i in range(4)]
    obc = [0]
    qT = sb("qT", [128, 128]); kT = sb("kT", [128, 128])
    k.open_scope()
    xt2 = [sb(f"xt{i}", [128, 16, 128]) for i in range(2)]
    xsq = sb("xsq", [128, 16, 128]); xs1 = sb("xs1", [128, 128])
    rbc = sb("rbc", [128, 128]); rcol = sb("rcol", [128, 1])
    g0T = sb("g0T", [128, 16]); k.dma(g0T[:], g0T_d[:, :])
    tctr = [0]

    def tmp():
        t = tbuf[tctr[0] % len(tbuf)]
        tctr[0] += 1
        return t

    def rs_eps(out, in_, scale):
        k.act(out, in_, AF.Ln, bias=epsb[:out.shape[0], :], scale=scale)
        k.act(out, out, AF.Exp, scale=-0.5)

    def load_w(src, pairs):
        for d0, s0, n in pairs:
            k.dma(wbuf[:, :, d0:d0 + n], src[:, s0:s0 + n].rearrange("(kc p) c -> p kc c", p=128), eng="pool")

    def norm_x(ti, t0, T):
        xt = xt2[ti % 2]
        k.dma(xt[:, :, :T], xT_d[:, t0:t0 + T].rearrange("(kc p) t -> p kc t", p=128))
        k.tt(xsq[:, :, :T], xt[:, :, :T], xt[:, :, :T], ALU.mult, eng="pool")
        k.red(xs1[:, :T], xsq[:, :, :T].rearrange("p kc t -> p t kc"))
        p = k.ps(); k.mm(p[:, :T], ones, xs1[:, :T]); rs_eps(rbc[:, :T], p[:, :T], 1.0 / D)
        p = k.ps(); k.mm(p[:T, 0:1], xs1[:, :T], ones[:, 0:1]); rs_eps(rcol[:T, :], p[:T, 0:1], 1.0 / D)
        k.tt(xg[:, :, :T], xt[:, :, :T], g0T[:, :].unsqueeze(2).to_broadcast([128, 16, T]), ALU.mult)

    def load_hn(t0, T):
        k.dma(xg[:, :, :T], hnT_d[:, t0:t0 + T].rearrange("(kc p) t -> p kc t", p=128), reads=[("hnT", t0)])

    def projF(dst, col0, T, ncols=128, scaled=True):
        p = k.ps()
        for kc in range(16):
            k.mm(p[:ncols, :T], wbuf[:, kc, col0:col0 + ncols], xg[:, kc, :T], start=(kc == 0), stop=(kc == 15))
        if scaled:
            k.tt(dst, p[:ncols, :T], rbc[:ncols, :T], ALU.mult)
        else:
            k.cp(dst, p[:ncols, :T], eng="act")

    def projTok(col0, ncols, T, scaled=True, dst0=0):
        for c0 in range(0, ncols, 512):
            n = min(512, ncols - c0)
            p = k.ps()
            for kc in range(16):
                k.mm(p[:T, :n], xg[:, kc, :T], wbuf[:, kc, col0 + c0:col0 + c0 + n], start=(kc == 0), stop=(kc == 15))
            if scaled:
                k.ts(projT[:T, dst0 + c0:dst0 + c0 + n], p[:T, :n], rcol[:T, :], ALU.mult)
            else:
                k.cp(projT[:T, dst0 + c0:dst0 + c0 + n], p[:T, :n], eng="act")

    def store_oT(src, ncols, frow, t0, T):
        for c0 in range(0, ncols, 128):
            p = k.ps(); k.tr(p[:, :T], src[:T, c0:c0 + 128], ident[:T, :T])
            ob = obf[obc[0] % 4]; obc[0] += 1
            k.cp(ob[:, :T], p[:, :T], eng="act")
            k.dma(oT_d[frow + c0:frow + c0 + 128, t0:t0 + T], ob[:, :T], writes=[("oT", frow + c0, t0)])

    def headnorm_rms(o_sb, T, n, gsz, dst):
        sq = tmp(); k.tt(sq[:T, :n], o_sb, o_sb, ALU.mult, eng="pool")
        c = cols[:, 60:61]; k.red(c[:T, :], sq[:T, :n])
        r = cols[:, 61:62]; rs_eps(r[:T, :], c[:T, :], 1.0 / n)
        k.stt(dst, o_sb, r[:T, :], gsz, ALU.mult, ALU.mult)

    S_gla = sb("S_gla", [128, 256]); S_gdn = [sb(f"S_gdn{i}", [128, 128]) for i in range(2)]
    cbuf = sb("cbuf", [128, 6, 131]); cw = sb("cw", [128, 6, 4])
    gw = sb("gw", [16, 128]); gb = sb("gb", [1, 128]); angb = sb("angb", [128, 256]); bngb = sb("bngb", [128, 128])
    alog = sb("alog", [128, 8]); dtb = sb("dtb", [128, 8]); na = sb("na", [128, 8])
    k.dma(alog[:], alog_d[:, :]); k.dma(dtb[:], dtb_d[:, :]); k.dma(bngb[:], bng_d[:, :])
    k.act(na[:], alog[:], AF.Exp); k.ts(na[:], na[:], -1.0, ALU.mult)
    gaT = sb("gaT", [16, 128])
    cvs = sb("cvs", [128, 6, 128]); cacc = sb("cacc", [128, 6, 128]); ctmp = sb("ctmp", [128, 6, 128])
    Rp = [sb(f"Rp{i}", [128, 128]) for i in range(7)]
    Qp = [sb(f"Qp{i}", [128, 128]) for i in range(2)]
    g_vk = sb("g_vk", [128, 256]); g_egrow = sb("g_egrow", [128, 128]); g_AT = sb("g_AT", [128, 128])
    g_X = [sb(f"g_X{i}", [128, 256]) for i in range(2)]
    SC = 128 ** -0.5

    def gla_tile(hg, seq, t0, T):
        kk = projT[:T, 0:128]; v = projT[:T, 128:384]; za = projT[:T, 384:640]
        p = k.ps()
        k.mm(p[:T, :128], gaT[:, :T], gw[:, :], start=True, stop=False)
        k.mm(p[:T, :128], ones[0:1, :T], gb[:, :], start=False, stop=True)
        e = tmp(); k.act(e[:T, :128], p[:T, :128], AF.Exp, scale=-1.0)
        yield
        spl = tmp(); k.act(spl[:T, :128], e[:T, :128], AF.Ln, bias=1.0)
        pc = k.ps(); k.mm(pc[:, :T], spl[:T, :128], U[:T, :T])
        ebT = tmp(); k.act(ebT[:, :T], pc[:, :T], AF.Exp, scale=-1.0 / 16)
        enbT = tmp(); k.act(enbT[:, :T], pc[:, :T], AF.Exp, scale=1.0 / 16)
        yield
        qd = tmp(); k.stt(qd[:, :T], qT[:, :T], SC, ebT[:, :T], ALU.mult, ALU.mult)
        kd = tmp(); k.tt(kd[:, :T], kT[:, :T], enbT[:, :T], ALU.mult, eng="pool")
        pd = k.ps(); k.mm(pd[:T, :128], NSL[:T, :T], spl[:T, :128])
        ekw = tmp(); k.act(ekw[:T, :128], pd[:T, :128], AF.Exp, scale=1.0 / 16)
        yield
        kw = tmp(); k.tt(kw[:T, :128], kk, ekw[:T, :128], ALU.mult, eng="pool")
        psc = k.ps(); k.mm(psc[:T, :T], kd[:, :T], qd[:, :T])
        PT = tmp(); k.tt(PT[:T, :T], psc[:T, :T], U[:T, :T], ALU.mult)
        yield
        po = k.ps()
        k.mm(po[:T, :256], PT[:T, :T], v, start=True, stop=False)
        k.mm(po[:T, :256], qd[:, :T], S_gla[:, :], start=False, stop=True)
        o_sb = tmp(); k.cp(o_sb[:T, :256], po[:T, :256], eng="act")
        pkv = k.ps(); k.mm(pkv[:, :256], kw[:T, :128], v)
        k.stt(S_gla[:, :], S_gla[:, :], ebT[:, T - 1:T], pkv[:, :256], ALU.mult, ALU.add)
        yield
        gsz = tmp(); k.act(gsz[:T, :256], za, AF.Silu)
        k.tt(gsz[:T, :256], gsz[:T, :256], angb[:T, :], ALU.mult, eng="pool")
        og = tmp(); headnorm_rms(o_sb[:T, :256], T, 256, gsz[:T, :256], og[:T, :256])
        yield
        store_oT(og, 256, hg * 256, t0, T)
        yield

    def gdn_pre(T):
        for j in range(4):
            wj = cw[:, :, j:j + 1].to_broadcast([128, 6, T])
            if j == 0:
                k.tt(cacc[:, :, :T], cbuf[:, :, 0:T], wj, ALU.mult)
                yield
            else:
                k.tt(ctmp[:, :, :T], cbuf[:, :, j:j + T], wj, ALU.mult, eng="pool")
                k.tt(cacc[:, :, :T], cacc[:, :, :T], ctmp[:, :, :T], ALU.add)
                yield
        k.act(cvs[:, :, :T], cacc[:, :, :T], AF.Silu)
        yield
        k.tt(ctmp[:, 0:4, :T], cvs[:, 0:4, :T], cvs[:, 0:4, :T], ALU.mult, eng="pool")
        yield
        p = k.ps()
        for b in range(4):
            k.mm(p[:, b * T:(b + 1) * T], ones, ctmp[:, b, :T])
        rs_eps(cacc[:, 0:4, :T], p[:, :4 * T].rearrange("p (a b) -> p a b", a=4), 1.0)
        yield
        k.tt(cvs[:, 0:4, :T], cvs[:, 0:4, :T], cacc[:, 0:4, :T], ALU.mult)
        yield

    def gdn_head(hg, hh, seq, t0, T):
        Sg = S_gdn[hh]
        qTh = cvs[:, 0 + hh, :T]; kTh = cvs[:, 2 + hh, :T]; vTh = cvs[:, 4 + hh, :T]
        zb = projT[:T, 640 + 128 * hh:768 + 128 * hh]
        beta_pre = projT[:T, 896 + hh:897 + hh]; a_pre = projT[:T, 898 + hh:899 + hh]
        gh = 2 * hg + hh
        c = lambda i: cols[:, i:i + 1]
        beta = c(0); k.act(beta[:T, :], beta_pre, AF.Sigmoid)
        e = c(1); k.act(e[:T, :], a_pre, AF.Exp, bias=dtb[:T, gh:gh + 1])
        spg = c(2); k.act(spg[:T, :], e[:T, :], AF.Ln, bias=1.0)
        graw = c(3); k.tt(graw[:T, :], spg[:T, :], na[:T, gh:gh + 1], ALU.mult)
        vk = g_vk
        p = k.ps(); k.tr(p[:T, 0:128], vTh, ident); k.tr(p[:T, 128:256], kTh, ident)
        k.cp(vk[:T, :256], p[:T, :256], eng="act")
        p = k.ps(); k.mm(p[:T, 0:1], U[:T, :T], graw[:T, :])
        gc = c(4); k.cp(gc[:T, :], p[:T, 0:1]); ngc = c(5); k.ts(ngc[:T, :], p[:T, 0:1], -1.0, ALU.mult)
        Ug = tmp(); k.ts(Ug[:T, :T], U[:T, :T], graw[:T, :], ALU.mult)
        pg = k.ps(); k.mm(pg[:, :T], ones[:T, :], Ug[:T, :T])
        Ib = tmp(); k.ts(Ib[:T, :T], ident[:T, :T], beta[:T, :], ALU.mult)
        pb = k.ps(); k.mm(pb[:T, :T], ones[:T, :T], Ib[:T, :T])
        arg = tmp(); k.tt(arg[:T, :T], pg[:T, :T], NMU[:T, :T], ALU.add)
        decT = tmp(); k.act(decT[:T, :T], arg[:T, :T], AF.Exp, bias=ngc[:T, :])
        egrow = g_egrow; k.act(egrow[:, :T], pg[:, :T], AF.Exp)
        glast = c(6); k.cp(glast[:, :], pg[:, T - 1:T])
        eglast = c(7); k.cp(eglast[:, :], egrow[:, T - 1:T], eng="pool")
        egc = c(8); k.act(egc[:T, :], gc[:T, :], AF.Exp)
        ekl = c(9); k.act(ekl[:T, :], gc[:T, :], AF.Exp, scale=-1.0, bias=glast[:T, :])
        pkk = k.ps(); k.mm(pkk[:T, :T], kTh, kTh)
        pqk = k.ps(); k.mm(pqk[:T, :T], kTh, qTh)
        AT = g_AT; k.stt(AT[:T, :T], pqk[:T, :T], SC, decT[:T, :T], ALU.mult, ALU.mult)
        t1 = tmp(); k.tt(t1[:T, :T], pkk[:T, :T], decT[:T, :T], ALU.mult)
        t2 = tmp(); k.tt(t2[:T, :T], pb[:T, :T], SU[:T, :T], ALU.mult)
        LT = Rp[0]; k.tt(LT[:T, :T], t1[:T, :T], t2[:T, :T], ALU.mult, eng="pool")
        X = g_X[0]; xi_ = 0
        k.ts(X[:T, 0:128], vk[:T, 0:128], beta[:T, :], ALU.mult)
        k.ts(X[:T, 128:256], vk[:T, 128:256], beta[:T, :], ALU.mult, egc[:T, :], ALU.mult)
        p = k.ps(); k.tr(p[:T, :T], LT[:T, :T], ident[:T, :T]); k.cp(Qp[0][:T, :T], p[:T, :T], eng="act")
        p = k.ps(); k.mm(p[:T, :256], LT[:T, :T], X[:T, :256])
        xi_ ^= 1; Xn = g_X[xi_]; k.tt(Xn[:T, :256], X[:T, :256], p[:T, :256], ALU.subtract); X = Xn
        nsq = {128: 6, 32: 4}[T]
        for n in range(nsq):
            Q = Qp[n % 2]; R = Rp[n]
            p2 = k.ps(); k.mm(p2[:T, :T], Q[:T, :T], R[:T, :T]); k.cp(Rp[n + 1][:T, :T], p2[:T, :T], eng="act")
            if n < nsq - 1:
                p1 = k.ps(); k.mm(p1[:T, :T], R[:T, :T], Q[:T, :T]); k.cp(Qp[(n + 1) % 2][:T, :T], p1[:T, :T])
            p = k.ps(); k.mm(p[:T, :256], Rp[n + 1][:T, :T], X[:T, :256])
            xi_ ^= 1; Xn = g_X[xi_]; k.tt(Xn[:T, :256], X[:T, :256], p[:T, :256], ALU.add); X = Xn
        p = k.ps(); k.tr(p[:, :T], X[:T, 128:256], ident[:T, :T])
        wT = tmp(); k.cp(wT[:, :T], p[:, :T], eng="act")
        p = k.ps(); k.mm(p[:T, :128], wT[:, :T], Sg[:, :])
        vnew = tmp(); k.tt(vnew[:T, :128], X[:T, 0:128], p[:T, :128], ALU.subtract)
        qg = tmp(); k.stt(qg[:, :T], qTh, SC, egrow[:, :T], ALU.mult, ALU.mult)
        po = k.ps()
        k.mm(po[:T, :128], qg[:, :T], Sg[:, :], start=True, stop=False)
        k.mm(po[:T, :128], AT[:T, :T], vnew[:T, :128], start=False, stop=True)
        o_sb = tmp(); k.cp(o_sb[:T, :128], po[:T, :128], eng="act")
        kg = tmp(); k.ts(kg[:T, :128], vk[:T, 128:256], ekl[:T, :], ALU.mult)
        pkv = k.ps(); k.mm(pkv[:, :128], kg[:T, :128], vnew[:T, :128])
        k.stt(Sg[:, :], Sg[:, :], eglast[:, :], pkv[:, :128], ALU.mult, ALU.add)
        gsz = tmp(); k.act(gsz[:T, :128], zb, AF.Silu)
        k.tt(gsz[:T, :128], gsz[:T, :128], bngb[:T, :], ALU.mult, eng="pool")
        og = tmp(); headnorm_rms(o_sb[:T, :128], T, 128, gsz[:T, :128], og[:T, :128])
        store_oT(og, 128, 1024 + gh * 128, t0, T)

    for hg in range(4):
        TB = 1040
        load_w(w_in_ab, [(0, 128 * hg, 128), (128, 512 + 128 * hg, 128),
                         (256, 3088 + 256 * hg, 256), (512, 4112 + 256 * hg, 256), (768, 5136 + 256 * hg, 256),
                         (1024, 3072, 16),
                         (TB, 512 + 128 * hg, 128), (TB + 128, 1024 + 256 * hg, 256), (TB + 384, 2048 + 256 * hg, 256),
                         (TB + 640, 6160 + 256 * hg, 256), (TB + 896, 7184 + 2 * hg, 2), (TB + 898, 7192 + 2 * hg, 2)])
        k.dma(gw[:], agw_d[:, 128 * hg:128 * hg + 128]); k.dma(gb[:], agb_d[:, 128 * hg:128 * hg + 128])
        k.dma(angb[:], ang_d[:, 256 * hg:256 * hg + 256])
        for typ in range(3):
            k.dma(cw[:, 2 * typ:2 * typ + 2, :], cw_d[:, typ * 8 + 2 * hg:typ * 8 + 2 * hg + 2, :])
        for ti, (seq, t0, T, first, last) in enumerate(tiles):
            if first:
                if seq == 0:
                    k.memset(S_gla[:], 0.0); k.memset(S_gdn[0][:], 0.0); k.memset(S_gdn[1][:], 0.0)
                    k.memset(cbuf[:, :, 0:3], 0.0)
                else:
                    k.dma(S_gla[:], si_gla[seq - 1, hg, :, :])
                    for hh in range(2):
                        k.dma(S_gdn[hh][:], si_gdn[seq - 1, 2 * hg + hh, :, :])
                    for typ in range(3):
                        k.dma(cbuf[:, 2 * typ:2 * typ + 2, 0:3], si_conv[seq - 1, :, typ * 8 + 2 * hg:typ * 8 + 2 * hg + 2, :])
            else:
                k.cp(cbuf[:, :, 0:3], cbuf[:, :, 128:131])
            norm_x(ti, t0, T)
            projF(qT[:, :T], 0, T); projF(kT[:, :T], 128, T)
            for b in range(6):
                projF(cbuf[:, b, 3:3 + T], 256 + 128 * b, T)
            projF(gaT[:, :T], 1024, T, ncols=16)
            projTok(TB, 900, T)
            run_threads([gla_tile(hg, seq, t0, T), gdn_pre(T)])
            for hh in range(2):
                gdn_head(hg, hh, seq, t0, T)
            if last:
                k.dma(so_gla[seq, hg, :, :], S_gla[:])
                for hh in range(2):
                    k.dma(so_gdn[seq, 2 * hg + hh, :, :], S_gdn[hh][:])
                for typ in range(3):
                    k.dma(so_conv[seq, :, typ * 8 + 2 * hg:typ * 8 + 2 * hg + 2, :], cbuf[:, 2 * typ:2 * typ + 2, T:T + 3])

    k.close_scope()
    def outproj_alloc():
        return (sb("ot", [128, 16, 128], BF16), sb("resid", [128, D]), sb("hbuf", [128, D]),
                sb("gnb", [128, D]), sb("hsq", [128, D]), sb("hnT_sb", [128, 16, 128], BF16))

    def outproj_pass(w_out, layer):
        k.open_scope()
        ot, resid, hbuf, gnb, hsq, hnT_sb = outproj_alloc()
        load_w(w_out, [(0, 0, 2048)])
        k.dma(gnb[:], (g1b_d if layer == 0 else gfb_d)[:, :])
        for ti, (seq, t0, T, first, last) in enumerate(tiles):
            k.dma(ot[:, :, :T], oT_d[:, t0:t0 + T].rearrange("(kc p) t -> p kc t", p=128),
                  reads=[("oT", f, t0) for f in range(0, D, 128)])
            if layer == 0:
                k.dma(resid[:T, :], xtok_d[t0:t0 + T, :])
            else:
                k.dma(resid[:T, :], h1_d[t0:t0 + T, :], reads=[("h1", t0)])
            for nb in range(4):
                p = k.ps()
                for kc in range(16):
                    k.mm(p[:T, :512], ot[:, kc, :T], wbuf[:, kc, nb * 512:(nb + 1) * 512], start=(kc == 0), stop=(kc == 15))
                k.tt(hbuf[:T, nb * 512:(nb + 1) * 512], p[:T, :512], resid[:T, nb * 512:(nb + 1) * 512], ALU.add)
            if layer == 0:
                k.dma(h1_d[t0:t0 + T, :], hbuf[:T, :], writes=[("h1", t0)])
            k.tt(hsq[:T, :], hbuf[:T, :], hbuf[:T, :], ALU.mult, eng="pool")
            c = cols[:, 62:63]; k.red(c[:T, :], hsq[:T, :])
            r = cols[:, 63:64]; rs_eps(r[:T, :], c[:T, :], 1.0 / D)
            k.stt(hsq[:T, :], hbuf[:T, :], r[:T, :], gnb[:T, :], ALU.mult, ALU.mult)
            if layer == 0:
                for q4 in range(4):
                    p = k.ps()
                    for j in range(4):
                        kc = q4 * 4 + j
                        k.tr(p[:, j * 128:j * 128 + T], hsq[:T, kc * 128:(kc + 1) * 128], ident[:T, :T])
                    if T == 128:
                        k.cp(hnT_sb[:, q4 * 4:q4 * 4 + 4, :], p[:, :].rearrange("p (a b) -> p a b", a=4), eng="act")
                    else:
                        for j in range(4):
                            k.cp(hnT_sb[:, q4 * 4 + j, :T], p[:, j * 128:j * 128 + T], eng="act")
                k.dma(hnT_d[:, t0:t0 + T].rearrange("(kc p) t -> p kc t", p=128), hnT_sb[:, :, :T], writes=[("hnT", t0)])
            else:
                k.dma(y_d[t0:t0 + T, :], hsq[:T, :])
        k.close_scope()

    outproj_pass(w_out_ab, 0)

    k.open_scope()
    s5c = sb("s5c", [128, 3, 8]); s5r = sb("s5r", [128, 3, 1024])
    ETr = sb("ETr", [128, 8, 128]); ETi = sb("ETi", [128, 8, 128]); EIr = sb("EIr", [128, 1024]); EIi = sb("EIi", [128, 1024])
    big = [sb(f"big{i}", [128, 1024]) for i in range(6)]
    bigi = sb("bigi", [128, 1024], I32)
    BB = sb("BB", [128, 4, 512]); WB = sb("WB", [128, 4, 512])
    CB = sb("CB", [128, 2, 8, 128]); DD = sb("DD", [128, 2, 128])
    uT = sb("uT", [128, 2, 128])
    car = sb("car", [128, 2, 8]); xl = sb("xl", [128, 2, 8]); A1 = sb("A1", [128, 2, 8])
    TWO_PI = 2.0 * np.pi

    def sincos(dst_sin, dst_cos, ang, n, shp=None):
        for dst, off in ((dst_sin, 0.0), (dst_cos, 0.25)):
            tn = big[4][:, :n]; tf = big[5][:, :n]
            k.ts(tn, ang, 1.0 / TWO_PI, ALU.mult, off, ALU.add)
            k.cp(bigi[:, :n], tn)
            k.cp(tf, bigi[:, :n])
            k.tt(tn, tn, tf, ALU.subtract)
            k.act(dst, tn, AF.Sin, scale=TWO_PI)

    def cmul(outr, outi, ar, ai, br, bi, n, t1, t2):
        e2 = "dve" if n == 8 else "pool"
        k.tt(t1, ar, br, ALU.mult); k.tt(t2, ai, bi, ALU.mult, eng=e2); k.tt(outr, t1, t2, ALU.subtract)
        k.tt(t1, ar, bi, ALU.mult); k.tt(t2, ai, br, ALU.mult, eng=e2); k.tt(outi, t1, t2, ALU.add)

    def s5_setup(hg):
        k.dma(s5c[:], s5col_d[:, :, 8 * hg:8 * hg + 8]); k.dma(s5r[:], s5row_d[:, :, 1024 * hg:1024 * hg + 1024])
        dtc = cols[:, 16:24]; arc = cols[:, 24:32]; aic = cols[:, 32:40]
        k.act(dtc, s5c[:, 2, :], AF.Exp)
        k.tt(arc, s5c[:, 0, :], dtc, ALU.mult); k.tt(aic, s5c[:, 1, :], dtc, ALU.mult)
        ang = big[0][:, :].rearrange("p (a b) -> p a b", a=8)
        io = IOTA.unsqueeze(1).to_broadcast([128, 8, 128])
        k.tt(ang, io, aic.unsqueeze(2).to_broadcast([128, 8, 128]), ALU.mult)
        sincos(big[1][:, :], big[2][:, :], big[0][:, :], 1024)
        mg = big[3][:, :].rearrange("p (a b) -> p a b", a=8)
        k.tt(mg, io, arc.unsqueeze(2).to_broadcast([128, 8, 128]), ALU.mult)
        k.act(big[3][:, :], big[3][:, :], AF.Exp)
        k.tt(ETi[:, :, :].rearrange("p a b -> p (a b)"), big[3][:, :], big[1][:, :], ALU.mult)
        k.tt(ETr[:, :, :].rearrange("p a b -> p (a b)"), big[3][:, :], big[2][:, :], ALU.mult)
        k.cp(A1[:, 0, :], ETr[:, :, 1]); k.cp(A1[:, 1, :], ETi[:, :, 1])
        dtr = big[0][:, :]; k.act(dtr, s5r[:, 2, :], AF.Exp)
        arr = sb_arr[:, :]; air = sb_air[:, :]
        k.tt(arr, s5r[:, 0, :], dtr, ALU.mult); k.tt(air, s5r[:, 1, :], dtr, ALU.mult)
        sincos(big[1][:, :], big[2][:, :], air, 1024)
        k.act(big[3][:, :], arr, AF.Exp)
        abi = big[1][:, :]; abr = big[2][:, :]
        k.tt(abi, abi, big[3][:, :], ALU.mult); k.tt(abr, abr, big[3][:, :], ALU.mult)
        k.ts(abr, abr, -1.0, ALU.add)
        den = big[3][:, :]; t = big[0][:, :]
        k.tt(den, s5r[:, 0, :], s5r[:, 0, :], ALU.mult); k.tt(t, s5r[:, 1, :], s5r[:, 1, :], ALU.mult)
        k.tt(den, den, t, ALU.add); k.recip(den, den)
        zr = big[4][:, :]; zi = big[5][:, :]
        k.tt(zr, abr, s5r[:, 0, :], ALU.mult); k.tt(t, abi, s5r[:, 1, :], ALU.mult); k.tt(zr, zr, t, ALU.add); k.tt(zr, zr, den, ALU.mult)
        k.tt(zi, abi, s5r[:, 0, :], ALU.mult); k.tt(t, abr, s5r[:, 1, :], ALU.mult); k.tt(zi, zi, t, ALU.subtract); k.tt(zi, zi, den, ALU.mult)
        for h in range(2):
            k.dma(WB[:, 2 * h, :], bdb_d[0, 2 * hg + h, :, :]); k.dma(WB[:, 2 * h + 1, :], bdb_d[1, 2 * hg + h, :, :])
            sl = slice(512 * h, 512 * h + 512)
            t1 = big[0][:, 0:512]; t2 = big[0][:, 512:1024]
            cmul(BB[:, 2 * h, :], BB[:, 2 * h + 1, :], zr[:, sl], zi[:, sl], WB[:, 2 * h, :], WB[:, 2 * h + 1, :], 512, t1, t2)
        k.ts(big[0][:, :], air, tcol, ALU.mult)
        sincos(big[1][:, :], big[2][:, :], big[0][:, :], 1024)
        k.ts(big[3][:, :], arr, tcol, ALU.mult); k.act(big[3][:, :], big[3][:, :], AF.Exp, scale=-1.0)
        k.tt(EIr[:, :], big[3][:, :], big[2][:, :], ALU.mult)
        k.stt(EIi[:, :], big[3][:, :], -1.0, big[1][:, :], ALU.mult, ALU.mult)
        k.dma(CB[:, 0, :, :], bdc_d[0, hg, :, :, :]); k.dma(CB[:, 1, :, :], bdc_d[1, hg, :, :, :])
        k.ts(CB[:, 1, :, :], CB[:, 1, :, :], -1.0, ALU.mult)
        k.dma(DD[:], bdd_d[hg, :, :, :])

    sb_arr = sb("sb_arr", [128, 1024]); sb_air = sb("sb_air", [128, 1024])

    def s5_tile(hg, seq, t0, T):
        zc = projT[:T, 0:256]
        Bu = [big[0], big[1]]
        for h in range(2):
            for cpx in range(2):
                p = k.ps(); k.mm(p[:T, :512], uT[:, h, :T], BB[:, 2 * h + cpx, :])
                k.cp(Bu[cpx][:T, 512 * h:512 * h + 512], p[:T, :512], eng="act")
        Zr = big[2]; Zi = big[3]
        cmul(Zr[:T, :], Zi[:T, :], EIr[:T, :], EIi[:T, :], Bu[0][:T, :], Bu[1][:T, :], 1024, big[4][:T, :], big[5][:T, :])
        Wc = [big[0], big[1]]
        for cpx, Z in enumerate((Zr, Zi)):
            for half in range(2):
                p = k.ps()
                for j in range(4):
                    ch = half * 4 + j
                    k.mm(p[:, j * 128:j * 128 + T], Z[:T, ch * 128:(ch + 1) * 128], U[:T, :T])
                src = p[:, :].rearrange("p (a b) -> p a b", a=4)[:, :, :T]
                dst = Wc[cpx][:, :].rearrange("p (a b) -> p a b", a=8)[:, half * 4:half * 4 + 4, :T]
                k.tt(dst, src, car[:, cpx, half * 4:half * 4 + 4].unsqueeze(2).to_broadcast([128, 4, T]), ALU.add)
        XTr = big[2][:, :].rearrange("p (a b) -> p a b", a=8); XTi = big[3][:, :].rearrange("p (a b) -> p a b", a=8)
        W3 = [w[:, :].rearrange("p (a b) -> p a b", a=8) for w in Wc]
        t1 = big[4][:, :].rearrange("p (a b) -> p a b", a=8); t2 = big[5][:, :].rearrange("p (a b) -> p a b", a=8)
        cmul(XTr[:, :, :T], XTi[:, :, :T], ETr[:, :, :T], ETi[:, :, :T], W3[0][:, :, :T], W3[1][:, :, :T], 0, t1[:, :, :T], t2[:, :, :T])
        k.cp(xl[:, 0, :], XTr[:, :, T - 1]); k.cp(xl[:, 1, :], XTi[:, :, T - 1])
        c1 = cols[:, 40:48]; c2 = cols[:, 48:56]
        cmul(car[:, 0, :], car[:, 1, :], A1[:, 0, :], A1[:, 1, :], xl[:, 0, :], xl[:, 1, :], 8, c1, c2)
        py = k.ps()
        for h in range(2):
            o = py[:T, 128 * h:128 * h + 128]
            k.mm(o, uT[:, h, :T], DD[:, h, :], start=True, stop=False)
            for pc in range(4):
                ch = 4 * h + pc
                k.mm(o, XTr[:, ch, :T], CB[:, 0, ch, :], start=False, stop=False)
                k.mm(o, XTi[:, ch, :T], CB[:, 1, ch, :], start=False, stop=(pc == 3))
        yg = tmp(); k.act(yg[:T, :256], py[:T, :256], AF.Gelu)
        sz = tmp(); k.act(sz[:T, :256], zc, AF.Silu)
        yz = tmp(); k.tt(yz[:T, :256], yg[:T, :256], sz[:T, :256], ALU.mult)
        k.dma(yz_d[t0:t0 + T, 256 * hg:256 * hg + 256], yz[:T, :256], writes=[("yz", hg, t0)])
        for c0 in range(0, 256, 128):
            p = k.ps(); k.tr(p[:, :T], yg[:T, c0:c0 + 128], ident[:T, :T])
            ob = obf[obc[0] % 4]; obc[0] += 1
            k.cp(ob[:, :T], p[:, :T], eng="act")
            k.dma(yT_d[256 * hg + c0:256 * hg + c0 + 128, t0:t0 + T], ob[:, :T], writes=[("yT", 256 * hg + c0, t0)])

    for hg in range(4):
        load_w(w_in_cd, [(0, 256 * hg, 256), (256, 1024 + 256 * hg, 256)])
        s5_setup(hg)
        for ti, (seq, t0, T, first, last) in enumerate(tiles):
            if first:
                if seq == 0:
                    k.memset(car[:], 0.0)
                else:
                    k.dma(xl[:], si_s5[seq - 1, :, :, 8 * hg:8 * hg + 8])
                    cmul(car[:, 0, :], car[:, 1, :], A1[:, 0, :], A1[:, 1, :], xl[:, 0, :], xl[:, 1, :], 8, cols[:, 40:48], cols[:, 48:56])
            load_hn(t0, T)
            projF(uT[:, 0, :T], 0, T, scaled=False); projF(uT[:, 1, :T], 128, T, scaled=False)
            projTok(256, 256, T, scaled=False)
            s5_tile(hg, seq, t0, T)
            if last:
                k.dma(so_s5[seq, :, :, 8 * hg:8 * hg + 8], xl[:])

    k.close_scope()
    k.open_scope()
    Cx = sb("Cx", [128, 257]); mst = sb("mst", [128, 1]); vx = sb("vx", [128, 257])
    gluw = sb("gluw", [128, 8, 256], BF16); ytl = sb("ytl", [128, 8, 128], BF16)
    glub = sb("glub", [128, 256]); dngb = sb("dngb", [128, 256]); dib = sb("dib", [128, 4]); dfb = sb("dfb", [128, 4])
    yzt = sb("yzt", [128, 256])
    k.dma(dib[:], dib_d[:, :]); k.dma(dfb[:], dfb_d[:, :]); k.ts(dfb[:], dfb[:], -1.0, ALU.mult)
    k.memset(vx[:, 256:257], 1.0)

    def mlstm_tile(hg, seq, t0, T):
        kk = projT[:T, 0:128]; od = projT[:T, 384:640]; zd = projT[:T, 640:896]
        c = lambda i: cols[:, i:i + 1]
        k.cp(vx[:T, 0:256], projT[:T, 128:384], eng="pool")
        ip = c(0); k.tt(ip[:T, :], projT[:T, 896:897], dib[:T, hg:hg + 1], ALU.add)
        e = c(1); k.act(e[:T, :], projT[:T, 897:898], AF.Exp, scale=-1.0, bias=dfb[:T, hg:hg + 1])
        sp = c(2); k.act(sp[:T, :], e[:T, :], AF.Ln, bias=1.0)
        p = k.ps(); k.mm(p[:T, 0:1], U[:T, :T], sp[:T, :]); bcol = c(3); k.ts(bcol[:T, :], p[:T, 0:1], -1.0, ALU.mult)
        p = k.ps(); k.mm(p[:, 0:1], ones[:T, :], sp[:T, :]); blast = c(4); k.ts(blast[:, :], p[:, 0:1], -1.0, ALU.mult)
        d = c(5); k.tt(d[:T, :], ip[:T, :], bcol[:T, :], ALU.subtract)
        Dm = tmp(); k.ts(Dm[:T, :T], ident[:T, :T], d[:T, :], ALU.mult)
        pr = k.ps(); k.mm(pr[:T, :T], ones[:T, :T], Dm[:T, :T])
        lw = tmp(); k.stt(lw[:T, :T], pr[:T, :T], bcol[:T, :], NML[:T, :T], ALU.add, ALU.add)
        mi = c(6); k.red(mi[:T, :], lw[:T, :T], ALU.max)
        nmi = c(7); k.ts(nmi[:T, :], mi[:T, :], -1.0, ALU.mult)
        e1 = tmp(); k.act(e1[:T, :T], lw[:T, :T], AF.Exp, bias=nmi[:T, :])
        pqk = k.ps(); k.mm(pqk[:T, :T], qT[:, :T], kT[:, :T])
        pm = tmp(); k.stt(pm[:T, :T], pqk[:T, :T], SC, e1[:T, :T], ALU.mult, ALU.mult)
        p = k.ps(); k.tr(p[:T, :T], pm[:T, :T], ident[:T, :T]); pT = tmp(); k.cp(pT[:T, :T], p[:T, :T], eng="act")
        sel = sel128 if T == 128 else sel32
        p = k.ps(); k.mm(p[:, 0:1], sel[:T, :], mi[:T, :]); mch = c(8); k.cp(mch[:, :], p[:, 0:1])
        bm = c(9); k.tt(bm[:, :], blast[:, :], mch[:, :], ALU.subtract)
        kws = c(10); k.act(kws[:T, :], d[:T, :], AF.Exp, bias=bm[:T, :])
        kw = tmp(); k.ts(kw[:T, :128], kk, kws[:T, :], ALU.mult)
        a = c(11); k.tt(a[:T, :], bcol[:T, :], mst[:T, :], ALU.add)
        mt = c(12); k.tt(mt[:T, :], a[:T, :], mi[:T, :], ALU.max)
        nmt = c(13); k.ts(nmt[:T, :], mt[:T, :], -1.0, ALU.mult)
        wa = c(14); k.act(wa[:T, :], a[:T, :], AF.Exp, bias=nmt[:T, :]); k.ts(wa[:T, :], wa[:T, :], SC, ALU.mult)
        wi = c(15); k.act(wi[:T, :], mi[:T, :], AF.Exp, bias=nmt[:T, :])
        emt = c(56); k.act(emt[:T, :], nmt[:T, :], AF.Exp)
        pA = k.ps(); k.mm(pA[:T, :257], qT[:, :T], Cx[:, :])
        pB = k.ps(); k.mm(pB[:T, :257], pT[:T, :T], vx[:T, :])
        r1 = tmp(); k.ts(r1[:T, :257], pA[:T, :257], wa[:T, :], ALU.mult)
        res = tmp(); k.stt(res[:T, :257], pB[:T, :257], wi[:T, :], r1[:T, :257], ALU.mult, ALU.add)
        dn = c(57); k.act(dn[:T, :], res[:T, 256:257], AF.Abs); k.tt(dn[:T, :], dn[:T, :], emt[:T, :], ALU.max)
        k.recip(dn[:T, :], dn[:T, :])
        sg = tmp(); k.act(sg[:T, :256], od, AF.Sigmoid)
        hd = tmp(); k.stt(hd[:T, :256], res[:T, :256], dn[:T, :], sg[:T, :256], ALU.mult, ALU.mult)
        mu = c(58); k.red(mu[:T, :], hd[:T, :256]); k.ts(mu[:T, :], mu[:T, :], -1.0 / 256, ALU.mult)
        xc = tmp(); k.ts(xc[:T, :256], hd[:T, :256], mu[:T, :], ALU.add)
        gsz = tmp(); k.act(gsz[:T, :256], zd, AF.Silu); k.tt(gsz[:T, :256], gsz[:T, :256], dngb[:T, :], ALU.mult, eng="pool")
        od_ = tmp(); headnorm_rms(xc[:T, :256], T, 256, gsz[:T, :256], od_[:T, :256])
        store_oT(od_, 256, 1024 + 256 * hg, t0, T)
        pkv = k.ps(); k.mm(pkv[:, :257], kw[:T, :128], vx[:T, :])
        bms = c(59); k.tt(bms[:, :], blast[:, :], mst[:, :], ALU.add)
        mnew = c(56); k.tt(mnew[:, :], bms[:, :], mch[:, :], ALU.max)
        nmn = c(57); k.ts(nmn[:, :], mnew[:, :], -1.0, ALU.mult)
        wold = c(58); k.act(wold[:, :], bms[:, :], AF.Exp, bias=nmn[:, :])
        wnew = c(12); k.act(wnew[:, :], mch[:, :], AF.Exp, bias=nmn[:, :])
        t = tmp(); k.ts(t[:, :257], pkv[:, :257], wnew[:, :], ALU.mult)
        k.stt(Cx[:, :], Cx[:, :], wold[:, :], t[:, :257], ALU.mult, ALU.add)
        k.cp(mst[:, :], mnew[:, :])

    def glu_tile(hg, seq, t0, T):
        k.dma(ytl[:, :, :T], yT_d[:, t0:t0 + T].rearrange("(kc p) t -> p kc t", p=128),
              reads=[("yT", f, t0) for f in range(0, 1024, 128)])
        k.dma(yzt[:T, :], yz_d[t0:t0 + T, 256 * hg:256 * hg + 256], reads=[("yz", hg, t0)])
        p = k.ps()
        for kc in range(8):
            k.mm(p[:T, :256], ytl[:, kc, :T], gluw[:, kc, :], start=(kc == 0), stop=(kc == 7))
        g = tmp(); k.tt(g[:T, :256], p[:T, :256], glub[:T, :], ALU.add)
        k.act(g[:T, :256], g[:T, :256], AF.Sigmoid)
        oc = tmp(); k.tt(oc[:T, :256], g[:T, :256], yzt[:T, :], ALU.mult)
        store_oT(oc, 256, 256 * hg, t0, T)

    for hg in range(4):
        load_w(w_in_cd, [(0, 2048 + 128 * hg, 128), (128, 2560 + 128 * hg, 128),
                         (256, 2560 + 128 * hg, 128), (384, 3072 + 256 * hg, 256), (640, 4096 + 256 * hg, 256),
                         (896, 5120 + 256 * hg, 256), (1152, 6144 + hg, 1), (1153, 6148 + hg, 1)])
        k.dma(gluw[:, :, :], gluw_d[:, 256 * hg:256 * hg + 256].rearrange("(kc p) c -> p kc c", p=128), eng="pool")
        k.dma(glub[:], glub_d[:, 256 * hg:256 * hg + 256]); k.dma(dngb[:], dng_d[:, 256 * hg:256 * hg + 256])
        for ti, (seq, t0, T, first, last) in enumerate(tiles):
            if first:
                if seq == 0:
                    k.memset(Cx[:], 0.0); k.memset(mst[:], 0.0)
                else:
                    k.dma(Cx[:, 0:256], si_mc[seq - 1, hg, :, :]); k.dma(Cx[:, 256:257], si_mn[seq - 1, :, hg:hg + 1])
                    k.dma(mst[:], si_mm[seq - 1, :, hg:hg + 1])
            load_hn(t0, T)
            projF(qT[:, :T], 0, T, scaled=False); projF(kT[:, :T], 128, T, scaled=False)
            projTok(256, 898, T, scaled=False)
            mlstm_tile(hg, seq, t0, T)
            glu_tile(hg, seq, t0, T)
            if last:
                k.dma(so_mc[seq, hg, :, :], Cx[:, 0:256]); k.dma(so_mn[seq, :, hg:hg + 1], Cx[:, 256:257])
                k.dma(so_mm[seq, :, hg:hg + 1], mst[:])

    k.close_scope()
    outproj_pass(w_out_cd, 1)
    k.finish()
    k.emit()
    return nc, k


def _consts():
    c = np.zeros((128, 9, 128), np.float32)
    idx = np.arange(128)
    kk, ii = idx[:, None], idx[None, :]
    c[:, 0] = (kk == ii); c[:, 1] = (kk <= ii); c[:, 2] = (kk < ii); c[:, 3] = -(kk > ii).astype(np.float32)
    c[:, 4] = 1.0; c[:, 5] = np.where(kk <= ii, 0.0, NEG); c[:, 6] = np.where(ii <= kk, 0.0, NEG)
    c[:, 7] = np.broadcast_to(idx[None, :], (128, 128))
    c[:, 8, 0] = idx; c[127, 8, 1] = 1.0; c[31, 8, 2] = 1.0
    return c


def _bc(v, n=128):
    v = np.asarray(v, np.float32).reshape(1, -1)
    return np.ascontiguousarray(np.broadcast_to(v, (n, v.shape[1])))


def _s5col(a):
    return np.ascontiguousarray(np.asarray(a, np.float32).reshape(32, 128).T)


def _shared_inputs(inp):
    f = lambda a: np.ascontiguousarray(np.asarray(a, np.float32))
    d = {}
    d["w_in_ab"] = f(inp["w_in_ab"]); d["w_out_ab"] = f(inp["w_out_ab"])
    d["w_in_cd"] = f(inp["w_in_cd"]); d["w_out_cd"] = f(inp["w_out_cd"]); d["glu_w"] = f(inp["c_glu_w"])
    d["consts"] = _consts()
    ng = f(inp["norm_g"])
    d["g0T"] = np.ascontiguousarray(ng[0].reshape(16, 128).T); d["g1b"] = _bc(ng[1]); d["gfb"] = _bc(inp["final_norm_g"])
    d["a_gate_w"] = f(inp["a_gate_w"]); d["a_gate_b"] = f(inp["a_gate_b"]).reshape(1, 512); d["a_norm_g_b"] = _bc(inp["a_norm_g"])
    d["conv_w"] = np.ascontiguousarray(f(inp["b_conv_w"]).reshape(4, 24, 128).transpose(2, 1, 0))
    d["a_log_b"] = _bc(inp["b_a_log"]); d["dt_bias_b"] = _bc(inp["b_dt_bias"]); d["b_norm_g_b"] = _bc(inp["b_norm_g"])
    lre, lim = f(inp["c_lam_re"]), f(inp["c_lam_im"])
    ldt = np.ascontiguousarray(np.broadcast_to(f(inp["c_log_dt"])[:, None], (64, 64)))
    d["s5col"] = np.ascontiguousarray(np.stack([_s5col(lre), _s5col(lim), _s5col(ldt)], axis=1))
    d["s5row"] = np.ascontiguousarray(np.stack([_bc(lre.reshape(-1)), _bc(lim.reshape(-1)), _bc(ldt.reshape(-1))], axis=1))
    bd_b = np.zeros((2, 8, 128, 512), np.float32)
    for ci, b in enumerate((f(inp["c_b_re"]), f(inp["c_b_im"]))):
        for H in range(8):
            for gl in range(8):
                bd_b[ci, H, gl * 16:(gl + 1) * 16, gl * 64:(gl + 1) * 64] = b[8 * H + gl].T
    d["bd_b"] = bd_b
    bd_c = np.zeros((2, 4, 128, 8, 128), np.float32)
    for ci, c in enumerate((f(inp["c_c_re"]), f(inp["c_c_im"]))):
        for hg in range(4):
            for ch in range(8):
                for g2 in range(2):
                    g = 16 * hg + 2 * ch + g2
                    col = (2 * (ch % 4) + g2) * 16
                    bd_c[ci, hg, g2 * 64:(g2 + 1) * 64, ch, col:col + 16] = c[g].T
    d["bd_c"] = bd_c
    bd_d = np.zeros((4, 128, 2, 128), np.float32)
    cd = f(inp["c_d"])
    for hg in range(4):
        for h in range(2):
            for gl in range(8):
                for i in range(16):
                    bd_d[hg, gl * 16 + i, h, gl * 16 + i] = cd[16 * hg + 8 * h + gl, i]
    d["bd_d"] = bd_d
    d["glu_b_b"] = _bc(inp["c_glu_b"]); d["d_i_b"] = _bc(inp["d_i_bias"]); d["d_f_b"] = _bc(inp["d_f_bias"])
    d["d_norm_g_b"] = _bc(inp["d_norm_g"])
    return d


_NC_CACHE = {}


def kernel(**inp):
    f = lambda a: np.ascontiguousarray(np.asarray(a, np.float32))
    xp = f(inp["x_prompt"]); xs = f(inp["x_sample"])
    NB, L, _ = xp.shape
    shared = _shared_inputs(inp)
    conv = f(inp["cache_gdn_conv"]); sgla = f(inp["state_gla"]); sgdn = f(inp["state_gdn"])
    s5re = f(inp["state_s5_re"]); s5im = f(inp["state_s5_im"]); smc = f(inp["state_mlstm_c"])
    smn = f(inp["state_mlstm_n"]); smm = f(inp["state_mlstm_m"])
    in_maps = []
    for c in range(8):
        m = dict(shared)
        sq = [2 * c, 2 * c + 1]
        xtok = np.concatenate([xp[c % NB]] + [xs[s] for s in sq], axis=0)
        m["xtok"] = np.ascontiguousarray(xtok); m["xT"] = np.ascontiguousarray(xtok.T)
        m["si_conv"] = np.ascontiguousarray(np.stack([conv[s].reshape(3, 24, 128).transpose(2, 1, 0) for s in sq]))
        m["si_gla"] = np.ascontiguousarray(sgla[sq]); m["si_gdn"] = np.ascontiguousarray(sgdn[sq])
        m["si_s5"] = np.ascontiguousarray(np.stack([np.stack([_s5col(s5re[s]), _s5col(s5im[s])], axis=1) for s in sq]))
        m["si_mc"] = np.ascontiguousarray(smc[sq])
        m["si_mn"] = np.ascontiguousarray(np.stack([smn[s].T for s in sq]))
        m["si_mm"] = np.ascontiguousarray(np.stack([_bc(smm[s]) for s in sq]))
        in_maps.append(m)
    if L not in _NC_CACHE:
        _NC_CACHE[L] = build(L)[0]
    nc = _NC_CACHE[L]
    res = run_bass_kernel_spmd(nc, in_maps, core_ids=list(range(8)))
    R = res.results
    NS = xs.shape[0]
    y_p = np.stack([R[b]["y"][:L] for b in range(NB)])
    y_s = np.stack([R[s // 2]["y"][L + 32 * (s % 2):L + 32 * (s % 2) + 32] for s in range(NS)])

    def gather(fn):
        p = np.stack([fn(R[b], 0) for b in range(NB)])
        s = np.stack([fn(R[s // 2], 1 + s % 2) for s in range(NS)])
        return p, s

    def s5o(r, i, ci):
        return np.ascontiguousarray(r["so_s5"][i][:, ci, :].T).reshape(64, 64)

    pc, sc = gather(lambda r, i: np.ascontiguousarray(r["so_conv"][i].transpose(2, 1, 0)).reshape(3, 3072))
    pg, sg = gather(lambda r, i: r["so_gla"][i]); pd, sd = gather(lambda r, i: r["so_gdn"][i])
    pr, sr = gather(lambda r, i: s5o(r, i, 0)); pi, si = gather(lambda r, i: s5o(r, i, 1))
    pmc, smc_ = gather(lambda r, i: r["so_mc"][i]); pmn, smn_ = gather(lambda r, i: np.ascontiguousarray(r["so_mn"][i].T))
    pmm, smm_ = gather(lambda r, i: np.ascontiguousarray(r["so_mm"][i][0, :]))
    outs = (y_p, y_s, pc, pg, pd, pr, pi, pmc, pmn, pmm, sc, sg, sd, sr, si, smc_, smn_, smm_)
    return tuple(np.ascontiguousarray(o, dtype=np.float32) for o in outs)
```
